# Optimizing a Trainium2 kernel written in Bass

```python
import jax, jax.numpy as jnp
from jax import lax
import numpy as np

D_MODEL = 1024
BATCH = 4
SEQ = 8192
DEPTH = 1
DEC_BATCH = 16
DEC_SEQ = 16
PAST_LEN = 2048

CHUNK = 64
D_MIX = D_MODEL
GDN_HEADS = 8
GDN_DK = 64
GDN_DV = 64
GDN_QK = GDN_HEADS * GDN_DK
GDN_WIDTH = GDN_HEADS * GDN_DV
GDN_CONV_CH = 2 * GDN_QK + GDN_WIDTH
CONV_W = 4
HG_HEADS = 4
HG_DK = 128
HG_DV = 128
HG_QK = HG_HEADS * HG_DK
HG_WIDTH = HG_HEADS * HG_DV
D_IN = GDN_CONV_CH + GDN_WIDTH + 2 * GDN_HEADS + 2 * HG_QK + 2 * HG_WIDTH
EPS = 1e-6

kernel_name = "hybrid_gdn_hgrn2_stream_step"


def _rmsnorm(x, w):
    xf = x.astype(jnp.float32)
    return xf * lax.rsqrt(jnp.mean(xf * xf, axis=-1, keepdims=True) + EPS) * w.astype(jnp.float32)


def _l2norm(x):
    return x * lax.rsqrt(jnp.sum(x * x, axis=-1, keepdims=True) + EPS)


def _to_chunks(t, c):
    B, L, H, d = t.shape
    return t.reshape(B, L // c, c, H, d).transpose(1, 0, 3, 2, 4)


def _from_chunks(t):
    n, B, H, c, d = t.shape
    return t.transpose(1, 0, 3, 2, 4).reshape(B, n * c, H, d)


def _causal_conv(u, buf, w):
    full = jnp.concatenate([buf.astype(jnp.float32), u], axis=1)
    L = u.shape[1]
    wf = w.astype(jnp.float32)
    out = full[:, 0:L] * wf[0]
    for j in range(1, CONV_W):
        out = out + full[:, j:j + L] * wf[j]
    return jax.nn.silu(out), full[:, full.shape[1] - (CONV_W - 1):]


def _gated_delta(q, k, v, g, beta, S0, c):
    dv = v.shape[-1]
    qc, kc, vc = _to_chunks(q, c), _to_chunks(k, c), _to_chunks(v, c)
    gc = _to_chunks(g[..., None], c)[..., 0]
    bc = _to_chunks(beta[..., None], c)[..., 0]
    idx = jnp.arange(c)
    causal = idx[:, None] >= idx[None, :]
    strict = idx[:, None] > idx[None, :]
    eye = jnp.eye(c, dtype=jnp.float32)

    def step(S, inp):
        qi, ki, vi, gi, bi = inp
        gcum = jnp.cumsum(gi, axis=-1)
        diff = gcum[..., :, None] - gcum[..., None, :]
        decay = jnp.where(causal, jnp.exp(jnp.where(causal, diff, 0.0)), 0.0)
        kb = ki * bi[..., None]
        lower = jnp.where(strict, jnp.einsum("bhid,bhjd->bhij", kb, ki) * decay, 0.0)
        rhs = jnp.concatenate([vi * bi[..., None], kb * jnp.exp(gcum)[..., None]], axis=-1)
        a = jnp.broadcast_to(eye, lower.shape) + lower
        sol = lax.linalg.triangular_solve(a, rhs, left_side=True, lower=True, unit_diagonal=True)
        u, w = sol[..., :dv], sol[..., dv:]
        v_new = u - jnp.einsum("bhid,bhde->bhie", w, S)
        attn = jnp.einsum("bhid,bhjd->bhij", qi, ki) * decay
        o = (jnp.einsum("bhid,bhde->bhie", qi * jnp.exp(gcum)[..., None], S)
             + jnp.einsum("bhij,bhje->bhie", attn, v_new))
        glast = gcum[..., -1]
        S_new = (S * jnp.exp(glast)[..., None, None]
                 + jnp.einsum("bhid,bhie->bhde", ki * jnp.exp(glast[..., None] - gcum)[..., None], v_new))
        return S_new, o

    S_fin, o = lax.scan(step, S0.astype(jnp.float32), (qc, kc, vc, gc, bc))
    return _from_chunks(o), S_fin


def _hgrn2(q, k, v, logf, S0, c):
    qc, kc, vc, fc = _to_chunks(q, c), _to_chunks(k, c), _to_chunks(v, c), _to_chunks(logf, c)
    idx = jnp.arange(c)
    causal = (idx[:, None] >= idx[None, :])[..., None]

    def step(S, inp):
        qi, ki, vi, lfi = inp
        b = jnp.cumsum(lfi, axis=2)
        diff = b[:, :, :, None, :] - b[:, :, None, :, :]
        dec = jnp.where(causal, jnp.exp(jnp.where(causal, diff, 0.0)), 0.0)
        attn = jnp.einsum("bhid,bhijd,bhjd->bhij", qi, dec, ki)
        o = (jnp.einsum("bhid,bhde->bhie", qi * jnp.exp(b), S)
             + jnp.einsum("bhij,bhje->bhie", attn, vi))
        blast = b[:, :, -1]
        S_new = (jnp.exp(blast)[..., None] * S
                 + jnp.einsum("bhid,bhie->bhde", ki * jnp.exp(blast[:, :, None] - b), vi))
        return S_new, o

    S_fin, o = lax.scan(step, S0.astype(jnp.float32), (qc, kc, vc, fc))
    return _from_chunks(o), S_fin


def _layer(x, conv_buf, S_gdn, S_hg, c, lb, norm_w, w_in, conv_w, A_log, dt_bias, gdn_norm_w, hg_norm_w, w_out):
    B, L, _ = x.shape
    h = _rmsnorm(x, norm_w)
    proj = h @ w_in.astype(jnp.float32)
    sizes = [GDN_CONV_CH, GDN_WIDTH, GDN_HEADS, GDN_HEADS, HG_QK, HG_QK, HG_WIDTH, HG_WIDTH]
    offs = np.cumsum(sizes)[:-1].tolist()
    qkv, z_a, b_a, a_a, hq, hf, hi, z_b = jnp.split(proj, offs, axis=-1)
    qkv_c, new_buf = _causal_conv(qkv, conv_buf, conv_w)
    q_a, k_a, v_a = jnp.split(qkv_c, [GDN_QK, 2 * GDN_QK], axis=-1)
    q_a = _l2norm(q_a.reshape(B, L, GDN_HEADS, GDN_DK)) * (GDN_DK ** -0.5)
    k_a = _l2norm(k_a.reshape(B, L, GDN_HEADS, GDN_DK))
    v_a = v_a.reshape(B, L, GDN_HEADS, GDN_DV)
    beta = jax.nn.sigmoid(b_a)
    g = -jnp.exp(A_log.astype(jnp.float32)) * jax.nn.softplus(a_a + dt_bias.astype(jnp.float32))
    o_a, S_gdn_new = _gated_delta(q_a, k_a, v_a, g, beta, S_gdn, c)
    o_a = _rmsnorm(o_a, gdn_norm_w) * jax.nn.silu(z_a.reshape(B, L, GDN_HEADS, GDN_DV))
    f = lb + (1.0 - lb) * jax.nn.sigmoid(hf)
    logf = jnp.log(f).reshape(B, L, HG_HEADS, HG_DK)
    k_b = (1.0 - f).reshape(B, L, HG_HEADS, HG_DK)
    q_b = jax.nn.silu(hq).reshape(B, L, HG_HEADS, HG_DK)
    v_b = hi.reshape(B, L, HG_HEADS, HG_DV)
    o_b, S_hg_new = _hgrn2(q_b, k_b, v_b, logf, S_hg, c)
    o_b = _rmsnorm(o_b, hg_norm_w) * jax.nn.silu(z_b.reshape(B, L, HG_HEADS, HG_DV))
    o = jnp.concatenate([o_a.reshape(B, L, GDN_WIDTH), o_b.reshape(B, L, HG_WIDTH)], axis=-1)
    y = x.astype(jnp.float32) + o @ w_out.astype(jnp.float32)
    return y, new_buf, S_gdn_new, S_hg_new


def setup_inputs(seed: int = 0) -> dict:
    key = jax.random.key(seed)
    ks = jax.random.split(key, 16)
    f32 = jnp.float32
    x_prompt = jax.random.normal(ks[0], (BATCH, SEQ, D_MODEL), f32)
    x_sample = jax.random.normal(ks[1], (DEC_BATCH, DEC_SEQ, D_MODEL), f32)
    state_conv = jax.random.normal(ks[2], (DEPTH, DEC_BATCH, CONV_W - 1, GDN_CONV_CH), f32)
    state_gdn = 0.5 * jax.random.normal(ks[3], (DEPTH, DEC_BATCH, GDN_HEADS, GDN_DK, GDN_DV), f32)
    state_hgrn = jax.random.normal(ks[4], (DEPTH, DEC_BATCH, HG_HEADS, HG_DK, HG_DV), f32)
    norm_w = 1.0 + 0.02 * jax.random.normal(ks[5], (DEPTH, D_MODEL), f32)
    w_in = jax.random.normal(ks[6], (DEPTH, D_MODEL, D_IN), f32) * D_MODEL ** -0.5
    conv_w = jax.random.normal(ks[7], (DEPTH, CONV_W, GDN_CONV_CH), f32) * CONV_W ** -0.5
    gdn_A_log = jnp.log(jax.random.uniform(ks[8], (DEPTH, GDN_HEADS), f32, 1.0, 16.0))
    dt = jnp.exp(jax.random.uniform(ks[9], (DEPTH, GDN_HEADS), f32, float(np.log(1e-3)), float(np.log(1e-1))))
    gdn_dt_bias = dt + jnp.log(-jnp.expm1(-dt))
    gdn_norm_w = 1.0 + 0.02 * jax.random.normal(ks[10], (DEPTH, GDN_DV), f32)
    hgrn_lb_logits = 0.1 * jax.random.normal(ks[11], (DEPTH + 1, HG_QK), f32)
    hgrn_norm_w = 1.0 + 0.02 * jax.random.normal(ks[12], (DEPTH, HG_DV), f32)
    w_out = jax.random.normal(ks[13], (DEPTH, D_MIX, D_MODEL), f32) * D_MIX ** -0.5
    final_norm_w = 1.0 + 0.02 * jax.random.normal(ks[14], (D_MODEL,), f32)
    return {"x_prompt": x_prompt, "x_sample": x_sample, "state_conv": state_conv,
            "state_gdn": state_gdn, "state_hgrn": state_hgrn, "norm_w": norm_w, "w_in": w_in,
            "conv_w": conv_w, "gdn_A_log": gdn_A_log, "gdn_dt_bias": gdn_dt_bias,
            "gdn_norm_w": gdn_norm_w, "hgrn_lb_logits": hgrn_lb_logits, "hgrn_norm_w": hgrn_norm_w,
            "w_out": w_out, "final_norm_w": final_norm_w}


def reference(x_prompt, x_sample, state_conv, state_gdn, state_hgrn, norm_w, w_in, conv_w, gdn_A_log,
              gdn_dt_bias, gdn_norm_w, hgrn_lb_logits, hgrn_norm_w, w_out, final_norm_w):
    f32 = jnp.float32
    lb_all = jnp.cumsum(jax.nn.softmax(hgrn_lb_logits.astype(f32), axis=0), axis=0)
    Bp, Ls = x_prompt.shape[0], x_sample.shape[1]
    hp = x_prompt.astype(f32)
    hs = x_sample.astype(f32)
    pc, pg, ph, sc, sg, sh = [], [], [], [], [], []
    for l in range(DEPTH):
        params = (lb_all[l], norm_w[l], w_in[l], conv_w[l], gdn_A_log[l], gdn_dt_bias[l],
                  gdn_norm_w[l], hgrn_norm_w[l], w_out[l])
        hp, c1, g1, r1 = _layer(hp, jnp.zeros((Bp, CONV_W - 1, GDN_CONV_CH), f32),
                                jnp.zeros((Bp, GDN_HEADS, GDN_DK, GDN_DV), f32),
                                jnp.zeros((Bp, HG_HEADS, HG_DK, HG_DV), f32), CHUNK, *params)
        hs, c2, g2, r2 = _layer(hs, state_conv[l], state_gdn[l], state_hgrn[l], Ls, *params)
        pc.append(c1); pg.append(g1); ph.append(r1)
        sc.append(c2); sg.append(g2); sh.append(r2)
    y_prompt = _rmsnorm(hp, final_norm_w).astype(x_prompt.dtype)
    y_sample = _rmsnorm(hs, final_norm_w).astype(x_sample.dtype)
    new_conv_prompt = jnp.stack(pc).astype(x_prompt.dtype)
    new_gdn_prompt = jnp.stack(pg).astype(x_prompt.dtype)
    new_hgrn_prompt = jnp.stack(ph).astype(x_prompt.dtype)
    new_conv_sample = jnp.stack(sc).astype(state_conv.dtype)
    new_gdn_sample = jnp.stack(sg).astype(state_gdn.dtype)
    new_hgrn_sample = jnp.stack(sh).astype(state_hgrn.dtype)
    return (y_prompt, y_sample, new_conv_prompt, new_gdn_prompt, new_hgrn_prompt,
            new_conv_sample, new_gdn_sample, new_hgrn_sample)
```

```python
import numpy as np
import concourse.bass as bass
import concourse.mybir as mybir
from concourse.bass_utils import run_bass_kernel_spmd

F32 = mybir.dt.float32
BF16 = mybir.dt.bfloat16
AF = mybir.ActivationFunctionType
ALU = mybir.AluOpType

D = 1024
HA, HB = 8, 4
NP = HA // 2
NCOL = 4112
EPS = 1e-6
C_HI = 3584
C_G = 4096


PSUM_KEYS = {"B0", "B1", "B2", "B3", "B4", "B5", "B6", "BT"}


class Sched:
    def __init__(self, nc):
        self.nc = nc
        self.eng = {"pe": nc.tensor, "dve": nc.vector, "act": nc.scalar, "pool": nc.gpsimd, "sp": nc.sync}
        self.sem = {k: nc.alloc_semaphore(name=f"s_{k}") for k in self.eng}
        self.cnt = {k: 0 for k in self.eng}
        self.seen = {k: {} for k in self.eng}
        self.last_w = {}
        self.readers = {}
        self.dma_sems = {}
        self.n_wait = 0
        self.n_ops = 0

    def _wait(self, e, tok):
        name, sem, val = tok
        if name == "pe" and e == "pe":
            return
        if self.seen[e].get(name, 0) >= val:
            return
        self.eng[e].wait_ge(sem, val)
        self.seen[e][name] = val
        self.n_wait += 1

    def _deps(self, e, reads, writes):
        for k in reads:
            t = self.last_w.get(k)
            if t is not None:
                self._wait(e, t)
        for k in writes:
            t = self.last_w.get(k)
            if t is not None:
                self._wait(e, t)
            for t in self.readers.get(k, {}).values():
                self._wait(e, t)

    def _commit(self, tok, reads, writes):
        for k in reads:
            self.readers.setdefault(k, {})[tok[0]] = tok
        for k in writes:
            self.last_w[k] = tok
            self.readers[k] = {}

    def op(self, e, fn, reads=(), writes=()):
        ex = [k for k in reads if k in PSUM_KEYS]
        if ex:
            writes = list(writes) + ex
        self._deps(e, reads, writes)
        ins = fn(self.eng[e])
        self.cnt[e] += 1
        ins.then_inc(self.sem[e], 1)
        tok = (e, self.sem[e], self.cnt[e])
        self._commit(tok, reads, writes)
        self.n_ops += 1
        return tok

    def dma(self, q, out, in_, reads=(), writes=(), slot=None):
        self._deps(q, reads, writes)
        slot = slot or (writes[0] if writes else reads[0])
        sname = f"d_{slot}"
        if sname not in self.dma_sems:
            self.dma_sems[sname] = [self.nc.alloc_semaphore(name=sname), 0]
        ent = self.dma_sems[sname]
        ent[1] += 16
        self.eng[q].dma_start(out=out, in_=in_).then_inc(ent[0], 16)
        tok = (sname, ent[0], ent[1])
        self._commit(tok, reads, writes)
        self.n_ops += 1
        return tok

    def finish(self, e="sp"):
        for k, t in list(self.last_w.items()):
            self._wait(e, t)


def bc(ap, shape):
    return ap.to_broadcast(list(shape))


def build(T, TB=256):
    nc = bass.Bass("TRN2", target_bir_lowering=False)
    s = Sched(nc)
    dt_in = lambda n, sh: nc.dram_tensor(n, list(sh), F32, kind="ExternalInput").ap()
    dt_out = lambda n, sh: nc.dram_tensor(n, list(sh), F32, kind="ExternalOutput").ap()
    xp = dt_in("xp", [T, D]); xs = dt_in("xs", [64, D])
    w_in = dt_in("w_in", [D, NCOL]); w_out = dt_in("w_out", [D, D])
    nw_d = dt_in("nw", [128, 8]); cw_d = dt_in("cw", [128, 12, 4])
    alog_d = dt_in("alog", [128, HA]); dtb_d = dt_in("dtb", [128, HA])
    gnw_d = dt_in("gnw", [128, 1]); hnw_d = dt_in("hnw", [128, 1])
    lbl_d = dt_in("lbl", [128, HB, 2]); fnw_d = dt_in("fnw", [128, D])
    sc_d = dt_in("sc", [128, 12, 4, 3])
    sg_d = dt_in("sg", [4, NP, 128, 64]); sh_d = dt_in("sh", [4, HB, 128, 128])
    yp = dt_out("yp", [T, D]); ys = dt_out("ys", [64, D])
    ncp = dt_out("ncp", [128, 12, 1, 3]); ngp = dt_out("ngp", [1, NP, 128, 64]); nhp = dt_out("nhp", [1, HB, 128, 128])
    ncs = dt_out("ncs", [128, 12, 4, 3]); ngs = dt_out("ngs", [4, NP, 128, 64]); nhs = dt_out("nhs", [4, HB, 128, 128])

    sb = lambda n, sh, d=F32: nc.alloc_sbuf_tensor(n, list(sh), d)
    Wb = sb("Wb", [128, 8, NCOL], BF16)
    WOb = sb("WOb", [128, 8, D], BF16)
    nw = sb("nw_t", [128, 8]); cw = sb("cw_t", [128, 12, 4])
    alog = sb("alog_t", [128, HA]); dtb = sb("dtb_t", [128, HA]); negA = sb("negA", [128, HA])
    gnw = sb("gnw_t", [128, 1]); hnw = sb("hnw_t", [128, 1])
    lbl = sb("lbl_t", [128, HB, 2]); lb = sb("lb", [128, HB]); oml = sb("oml", [128, HB])
    fnw = sb("fnw_t", [128, D])
    identb = sb("identb", [128, 128], BF16); identf = sb("identf", [128, 128])
    ones = sb("ones", [128, 128]); ob2 = sb("ob2", [128, 128])
    fgt = sb("fgt", [128, 128]); fle = sb("fle", [128, 128])
    I_s = sb("I_s", [128, 64]); U_s = sb("U_s", [128, 64]); Tri_s = sb("Tri_s", [128, 64]); Mc_s = sb("Mc_s", [128, 64])
    halo = sb("halo", [128, 12, 4, 3])
    Sg = sb("Sg", [128, NP, 64]); Sh = sb("Sh", [128, HB, 128])
    W_ = TB
    xt = [sb(f"xt{i}", [128, D]) for i in range(2)]
    sqj = sb("sqj", [128, D], BF16)
    xb = sb("xb", [128, D], BF16)
    hT = sb("hT", [128, 8, W_], BF16)
    ss = sb("ss", [128, 1]); rr = sb("rr", [128, 1])
    raw = sb("raw", [128, 3, W_ + 12])
    cv = sb("cv", [128, 3, W_])
    tmp = [sb(f"tmp{i}", [128, W_]) for i in range(4)]
    za = sb("za", [128, W_])
    G = sb("G", [128, 4, 16]); Gb = sb("Gb", [128, 4, HA]); Gg = sb("Gg", [128, 4, HA])
    gs = sb("gs", [128, NP, 4]); bs = sb("bs", [128, NP, 4]); nbs = sb("nbs", [128, NP, 4])
    gc = sb("gc", [128, NP, 4]); gl = sb("gl", [128, NP, 4]); egc = sb("egc", [128, NP, 4])
    dk = sb("dk", [128, NP, 4]); bge = sb("bge", [128, NP, 4])
    rhsD = sb("rhsD", [128, W_]); Dg = sb("Dg", [128, W_])
    Ee = sb("Ee", [128, W_]); Dm = sb("Dm", [128, W_]); Ds = sb("Ds", [128, W_])
    EBs = sb("EBs", [128, W_])
    P = [sb(f"P{i}", [128, W_]) for i in range(2)]
    PT = [sb(f"PT{i}", [128, W_]) for i in range(2)]
    R = [sb(f"R{i}", [128, W_]) for i in range(2)]
    attn = sb("attn", [128, W_]); attnT = sb("attnT", [128, W_])
    Kbe = sb("Kbe", [128, 4, 64]); Kd = sb("Kd", [128, 4, 64]); bV = sb("bV", [128, 4, 64])
    u = sb("u", [128, 4, 64]); wT = sb("wT", [128, W_]); QeT = sb("QeT", [128, W_])
    vn = sb("vn", [128, 64])
    oTf = sb("oTf", [128, W_])
    oTn = sb("oTn", [128, 8, W_], BF16)
    qb = sb("qb", [128, W_]); ff = sb("ff", [128, W_]); lf = sb("lf", [128, W_]); kb = sb("kb", [128, W_])
    bb = sb("bb", [128, W_]); bl = sb("bl", [128, W_])
    Qe = sb("Qe", [128, W_]); Qx = sb("Qx", [128, W_]); Kdh = sb("Kdh", [128, W_])
    ebl = sb("ebl", [128, 4]); zb = sb("zb", [128, W_])
    vtok = sb("vtok", [64, 4, 128]); Kdt = sb("Kdt", [64, 4, 128]); aTh = sb("aTh", [64, W_])
    smask = sb("smask", [128, W_])
    yo = sb("yo", [128, D]); yo2 = sb("yo2", [128, D])
    stg = sb("stg", [128, NCOL])
    pb = [nc.alloc_psum_tensor(f"pb{i}", [128, 512], F32) for i in range(7)]
    pT = nc.alloc_psum_tensor("BT", [128, 8, 128], BF16)

    def aff(out, cmp, fill_in, step=-1, cm=1, base=0):
        s.op("pool", lambda e: e.memset(out[:], fill_in), writes=[out.name])
        s.op("pool", lambda e: e.affine_select(out=out[:], in_=out[:], pattern=[[step, 128]], compare_op=cmp,
                                               fill=0.0, base=base, channel_multiplier=cm), reads=[out.name], writes=[out.name])
    aff(identf, ALU.is_equal, 1.0)
    aff(fgt, ALU.is_gt, 1.0)
    aff(fle, ALU.is_gt, 1.0, step=1, cm=-1, base=1)
    s.op("pool", lambda e: e.memset(ones[:], 1.0), writes=["ones"])
    s.op("pool", lambda e: e.memset(ob2[:], 0.0), writes=["ob2"])
    for h in range(2):
        sl = slice(64 * h, 64 * h + 64)
        s.op("pool", lambda e: e.memset(ob2[sl, sl], 1.0), reads=["ob2"], writes=["ob2"])
    s.op("dve", lambda e: e.tensor_copy(out=identb[:], in_=identf[:]), reads=["identf"], writes=["identb"])
    for (dst, src) in ((I_s, identf), (U_s, fgt), (Tri_s, fle)):
        for h in range(2):
            sl = slice(64 * h, 64 * h + 64)
            s.op("dve", lambda e: e.tensor_copy(out=dst[sl, :], in_=src[sl, sl]), reads=[src.name], writes=[dst.name])
    s.op("dve", lambda e: e.tensor_tensor(out=Mc_s[:], in0=U_s[:], in1=I_s[:], op=ALU.add), reads=["U_s", "I_s"], writes=["Mc_s"])
    for t_, d_ in ((nw, nw_d), (cw, cw_d), (alog, alog_d), (dtb, dtb_d), (gnw, gnw_d), (hnw, hnw_d), (lbl, lbl_d), (fnw, fnw_d)):
        s.dma("sp", t_[:], d_, writes=[t_.name])
    s.op("act", lambda e: e.activation(out=negA[:], in_=alog[:], func=AF.Exp), reads=["alog_t"], writes=["negA"])
    s.op("dve", lambda e: e.tensor_scalar(out=negA[:], in0=negA[:], scalar1=-1.0, scalar2=None, op0=ALU.mult), reads=["negA"], writes=["negA"])
    s.op("dve", lambda e: e.tensor_tensor(out=lb[:], in0=lbl[:, :, 1], in1=lbl[:, :, 0], op=ALU.subtract), reads=["lbl_t"], writes=["lb"])
    s.op("act", lambda e: e.activation(out=lb[:], in_=lb[:], func=AF.Exp), reads=["lb"], writes=["lb"])
    s.op("dve", lambda e: e.tensor_scalar(out=lb[:], in0=lb[:], scalar1=1.0, scalar2=None, op0=ALU.add), reads=["lb"], writes=["lb"])
    s.op("dve", lambda e: e.reciprocal(out=lb[:], in_=lb[:]), reads=["lb"], writes=["lb"])
    s.op("dve", lambda e: e.tensor_scalar(out=oml[:], in0=lb[:], scalar1=-1.0, scalar2=1.0, op0=ALU.mult, op1=ALU.add), reads=["lb"], writes=["oml"])
    w_in_v = w_in.rearrange("(k p) n -> p k n", p=128)
    for k in range(8):
        s.dma("sp", stg[:], w_in_v[:, k, :], writes=["stg"])
        eng = "dve" if k % 2 == 0 else "pool"
        s.op(eng, lambda e: e.tensor_scalar(out=Wb[:, k, :], in0=stg[:], scalar1=nw[:, k:k + 1], scalar2=None, op0=ALU.mult),
             reads=["stg", "nw_t"], writes=["Wb"])
    w_out_v = w_out.rearrange("(k p) n -> p k n", p=128)
    for k in range(8):
        s.dma("sp", stg[:, 0:D], w_out_v[:, k, :], writes=["stg"])
        eng = "dve" if k % 2 == 0 else "pool"
        s.op(eng, lambda e: e.tensor_copy(out=WOb[:, k, :], in_=stg[:, 0:D]), reads=["stg"], writes=["WOb"])

    def block(x_src, y_dst, t0, nseg, seglen, c, first, is_sample):
        TBk = nseg * seglen
        nch = TBk // c
        cps = seglen // c
        TT = min(128, TBk)
        ntt = TBk // TT
        nlev = {64: 5, 16: 3}[c]
        v3 = lambda ap: ap.rearrange("p (n c) -> p n c", c=c)

        for tt in range(ntt):
            X = xt[tt]
            s.dma("sp", X[0:TT, :], x_src[t0 + tt * TT: t0 + (tt + 1) * TT, :], writes=[X.name])
            s.op("act", lambda e: e.activation(out=sqj[0:TT, :], in_=X[0:TT, :], func=AF.Square, accum_out=ss[0:TT, :]),
                 reads=[X.name], writes=["sqj", "ss"])
            s.op("act", lambda e: e.activation(out=rr[0:TT, :], in_=ss[0:TT, :], func=AF.Ln, scale=1.0 / D, bias=EPS), reads=["ss"], writes=["rr"])
            s.op("act", lambda e: e.activation(out=rr[0:TT, :], in_=rr[0:TT, :], func=AF.Exp, scale=-0.5), reads=["rr"], writes=["rr"])
            s.op("pool", lambda e: e.tensor_scalar(out=xb[0:TT, :], in0=X[0:TT, :], scalar1=rr[0:TT, :], scalar2=None, op0=ALU.mult),
                 reads=[X.name, "rr"], writes=["xb"])
            for k in range(8):
                s.op("pe", lambda e: e.transpose(out=pT[:, k, 0:TT], in_=xb[0:TT, k * 128:(k + 1) * 128], identity=identb[0:TT, 0:TT]),
                     reads=["xb", "identb"], writes=["BT"])
            s.op("dve", lambda e: e.tensor_copy(out=hT[:, :, tt * TT:(tt + 1) * TT], in_=pT[:, :, 0:TT]), reads=["BT"], writes=["hT"])

        pi_state = [0]

        def inproj_fm(ct):
            i = pi_state[0] % 2
            pi_state[0] += 1
            key = "B0"
            out = pb[0][:, i * 256: i * 256 + TBk]
            for k in range(8):
                s.op("pe", lambda e: e.matmul(out, lhsT=Wb[:, k, ct * 128:(ct + 1) * 128], rhs=hT[:, k, 0:TBk], start=(k == 0), stop=(k == 7)),
                     reads=["Wb", "hT"], writes=[key])
            return out, key

        for ch in range(nch):
            for h in range(2):
                out = pb[2][64 * h:64 * h + c, ch * 16:(ch + 1) * 16]
                for k in range(8):
                    s.op("pe", lambda e: e.matmul(out, lhsT=hT[:, k, ch * c:(ch + 1) * c], rhs=Wb[:, k, C_G:C_G + 16], start=(k == 0), stop=(k == 7)),
                         reads=["Wb", "hT"], writes=["B2"])
        Gv = G[:, 0:nch, :]
        s.op("dve", lambda e: e.tensor_copy(out=Gv, in_=pb[2][:, 0:nch * 16].rearrange("p (n g) -> p n g", g=16)), reads=["B2"], writes=["G"])
        Gbv = Gb[:, 0:nch, :]; Ggv = Gg[:, 0:nch, :]
        s.op("act", lambda e: e.activation(out=Gbv, in_=Gv[:, :, 0:HA], func=AF.Exp, scale=-1.0), reads=["G"], writes=["Gb"])
        s.op("dve", lambda e: e.tensor_scalar(out=Gbv, in0=Gbv, scalar1=1.0, scalar2=None, op0=ALU.add), reads=["Gb"], writes=["Gb"])
        s.op("dve", lambda e: e.reciprocal(out=Gbv, in_=Gbv), reads=["Gb"], writes=["Gb"])
        s.op("dve", lambda e: e.tensor_tensor(out=Ggv, in0=Gv[:, :, HA:2 * HA], in1=bc(dtb[:, None, :], [128, nch, HA]), op=ALU.add),
             reads=["G", "dtb_t"], writes=["Gg"])
        s.op("act", lambda e: e.activation(out=Ggv, in_=Ggv, func=AF.Exp), reads=["Gg"], writes=["Gg"])
        s.op("act", lambda e: e.activation(out=Ggv, in_=Ggv, func=AF.Ln, bias=1.0), reads=["Gg"], writes=["Gg"])
        s.op("dve", lambda e: e.tensor_tensor(out=Ggv, in0=Ggv, in1=bc(negA[:, None, :], [128, nch, HA]), op=ALU.mult),
             reads=["Gg", "negA"], writes=["Gg"])
        gsv = gs[:, :, 0:nch]; bsv = bs[:, :, 0:nch]; nbsv = nbs[:, :, 0:nch]
        gcv = gc[:, :, 0:nch]; glv = gl[:, :, 0:nch]; egcv = egc[:, :, 0:nch]; dkv = dk[:, :, 0:nch]; bgev = bge[:, :, 0:nch]
        for h in range(2):
            sl = slice(64 * h, 64 * h + 64)
            for (dst, src, kd, ks) in ((gs, Gg, "gs", "Gg"), (bs, Gb, "bs", "Gb")):
                for p in range(NP):
                    s.op("dve", lambda e: e.tensor_copy(out=dst[sl, p, 0:nch], in_=src[sl, 0:nch, 2 * p + h]), reads=[ks], writes=[kd])
        s.op("dve", lambda e: e.tensor_scalar(out=nbsv, in0=bsv, scalar1=-1.0, scalar2=None, op0=ALU.mult), reads=["bs"], writes=["nbs"])
        for h in range(2):
            rs = slice(64 * h, 64 * h + c)
            s.op("pe", lambda e: e.matmul(pb[2][rs, 64:64 + NP * nch], lhsT=Tri_s[rs, 0:c], rhs=gs[rs, :, 0:nch], start=True, stop=True),
                 reads=["Tri_s", "gs"], writes=["B2"])
            s.op("pe", lambda e: e.matmul(pb[2][rs, 96:96 + NP * nch], lhsT=ones[rs, 0:c], rhs=gs[rs, :, 0:nch], start=True, stop=True),
                 reads=["ones", "gs"], writes=["B2"])
        s.op("dve", lambda e: e.tensor_copy(out=gcv, in_=pb[2][:, 64:64 + NP * nch].rearrange("p (a n) -> p a n", n=nch)), reads=["B2"], writes=["gc"])
        s.op("dve", lambda e: e.tensor_copy(out=glv, in_=pb[2][:, 96:96 + NP * nch].rearrange("p (a n) -> p a n", n=nch)), reads=["B2"], writes=["gl"])
        s.op("act", lambda e: e.activation(out=egcv, in_=gcv, func=AF.Exp), reads=["gc"], writes=["egc"])
        s.op("dve", lambda e: e.tensor_tensor(out=dkv, in0=glv, in1=gcv, op=ALU.subtract), reads=["gl", "gc"], writes=["dk"])
        s.op("act", lambda e: e.activation(out=dkv, in_=dkv, func=AF.Exp), reads=["dk"], writes=["dk"])
        s.op("dve", lambda e: e.tensor_tensor(out=bgev, in0=bsv, in1=egcv, op=ALU.mult), reads=["bs", "egc"], writes=["bge"])

        for p in range(NP):
            for i3 in range(3):
                ct = 4 * i3 + p
                pp, pk = inproj_fm(ct)
                rv = raw[:, i3, 0:nseg * (seglen + 3)].rearrange("p (n c) -> p n c", c=seglen + 3)
                if first and not is_sample:
                    s.op("pool", lambda e: e.memset(rv[:, :, 0:3], 0.0), reads=["raw"], writes=["raw"])
                else:
                    s.op("pool", lambda e: e.tensor_copy(out=rv[:, :, 0:3], in_=halo[:, ct, 0:nseg, :]), reads=["halo"], writes=["raw"])
                s.op("act", lambda e: e.activation(out=rv[:, :, 3:3 + seglen], in_=pp.rearrange("p (n c) -> p n c", c=seglen), func=AF.Copy),
                     reads=[pk], writes=["raw"])
                s.op("pool", lambda e: e.tensor_copy(out=halo[:, ct, 0:nseg, :], in_=rv[:, :, seglen:seglen + 3]), reads=["raw"], writes=["halo"])
                cvv = cv[:, i3, 0:TBk].rearrange("p (n c) -> p n c", c=seglen)
                s.op("dve", lambda e: e.tensor_scalar(out=cvv, in0=rv[:, :, 0:seglen], scalar1=cw[:, ct, 0:1], scalar2=None, op0=ALU.mult),
                     reads=["raw", "cw_t"], writes=["cv"])
                for j in range(1, 4):
                    s.op("dve", lambda e: e.scalar_tensor_tensor(out=cvv, in0=rv[:, :, j:j + seglen], scalar=cw[:, ct, j:j + 1], in1=cvv,
                                                                 op0=ALU.mult, op1=ALU.add), reads=["raw", "cw_t", "cv"], writes=["cv"])
                s.op("act", lambda e: e.activation(out=cv[:, i3, 0:TBk], in_=cv[:, i3, 0:TBk], func=AF.Silu), reads=["cv"], writes=["cv"])
            for i3 in range(2):
                src = cv[:, i3, 0:TBk]
                s.op("pool", lambda e: e.tensor_tensor(out=tmp[0][:, 0:TBk], in0=src, in1=src, op=ALU.mult), reads=["cv"], writes=["tmp0"])
                s.op("pe", lambda e: e.matmul(pb[3][:, 0:TBk], lhsT=ob2[:], rhs=tmp[0][:, 0:TBk], start=True, stop=True), reads=["ob2", "tmp0"], writes=["B3"])
                s.op("act", lambda e: e.activation(out=tmp[1][:, 0:TBk], in_=pb[3][:, 0:TBk], func=AF.Ln, bias=EPS), reads=["B3"], writes=["tmp1"])
                s.op("act", lambda e: e.activation(out=tmp[1][:, 0:TBk], in_=tmp[1][:, 0:TBk], func=AF.Exp, scale=-0.5), reads=["tmp1"], writes=["tmp1"])
                if i3 == 0:
                    s.op("dve", lambda e: e.scalar_tensor_tensor(out=src, in0=src, scalar=0.125, in1=tmp[1][:, 0:TBk], op0=ALU.mult, op1=ALU.mult),
                         reads=["cv", "tmp1"], writes=["cv"])
                else:
                    s.op("dve", lambda e: e.tensor_tensor(out=src, in0=src, in1=tmp[1][:, 0:TBk], op=ALU.mult), reads=["cv", "tmp1"], writes=["cv"])
            qn = cv[:, 0, 0:TBk]; kn = cv[:, 1, 0:TBk]; vs = cv[:, 2, 0:TBk]
            pp, pk = inproj_fm(12 + p)
            s.op("act", lambda e: e.activation(out=za[:, 0:TBk], in_=pp, func=AF.Silu), reads=[pk], writes=["za"])
            s.op("dve", lambda e: e.tensor_tensor(out=v3(rhsD[:, 0:TBk]), in0=bc(gs[:, p, 0:nch, None], [128, nch, c]), in1=bc(U_s[:, None, 0:c], [128, nch, c]), op=ALU.mult),
                 reads=["gs", "U_s"], writes=["rhsD"])
            s.op("pool", lambda e: e.tensor_tensor(out=v3(Dg[:, 0:TBk]), in0=bc(egc[:, p, 0:nch, None], [128, nch, c]), in1=bc(I_s[:, None, 0:c], [128, nch, c]), op=ALU.mult),
                 reads=["egc", "I_s"], writes=["Dg"])
            for h in range(2):
                rs = slice(64 * h, 64 * h + c)
                s.op("pe", lambda e: e.matmul(pb[3][rs, 256:256 + TBk], lhsT=Tri_s[rs, 0:c], rhs=rhsD[rs, 0:TBk], start=True, stop=True),
                     reads=["Tri_s", "rhsD"], writes=["B3"])
            s.op("act", lambda e: e.activation(out=Ee[:, 0:TBk], in_=pb[3][:, 256:256 + TBk], func=AF.Exp), reads=["B3"], writes=["Ee"])
            s.op("pool", lambda e: e.tensor_tensor(out=v3(Dm[:, 0:TBk]), in0=v3(Ee[:, 0:TBk]), in1=bc(Mc_s[:, None, 0:c], [128, nch, c]), op=ALU.mult),
                 reads=["Ee", "Mc_s"], writes=["Dm"])
            s.op("pool", lambda e: e.tensor_tensor(out=v3(Ds[:, 0:TBk]), in0=v3(Ee[:, 0:TBk]), in1=bc(U_s[:, None, 0:c], [128, nch, c]), op=ALU.mult),
                 reads=["Ee", "U_s"], writes=["Ds"])
            for h in range(2):
                rs = slice(64 * h, 64 * h + c)
                s.op("pe", lambda e: e.matmul(pb[3][64 * h:64 * h + 64, 0:TBk], lhsT=ones[rs, 0:64], rhs=Dg[rs, 0:TBk], start=True, stop=True),
                     reads=["ones", "Dg"], writes=["B3"])
            s.op("act", lambda e: e.activation(out=EBs[:, 0:TBk], in_=pb[3][:, 0:TBk], func=AF.Copy), reads=["B3"], writes=["EBs"])
            s.op("dve", lambda e: e.tensor_tensor(out=QeT[:, 0:TBk], in0=qn, in1=EBs[:, 0:TBk], op=ALU.mult), reads=["cv", "EBs"], writes=["QeT"])
            for ch in range(nch):
                cs = slice(ch * c, (ch + 1) * c)
                for h in range(2):
                    fs = slice(64 * h, 64 * h + 64); rs = slice(64 * h, 64 * h + c)
                    s.op("pe", lambda e: e.matmul(pb[4][rs, cs], lhsT=kn[fs, cs], rhs=kn[fs, cs], start=True, stop=True), reads=["cv"], writes=["B4"])
                    s.op("pe", lambda e: e.matmul(pb[4][rs, 256 + ch * c:256 + (ch + 1) * c], lhsT=qn[fs, cs], rhs=kn[fs, cs], start=True, stop=True),
                         reads=["cv"], writes=["B4"])
            s.op("dve", lambda e: e.tensor_tensor(out=v3(tmp[2][:, 0:TBk]), in0=v3(pb[4][:, 0:TBk]), in1=bc(nbs[:, p, 0:nch, None], [128, nch, c]), op=ALU.mult),
                 reads=["B4", "nbs"], writes=["tmp2"])
            s.op("pool", lambda e: e.tensor_tensor(out=PT[0][:, 0:TBk], in0=tmp[2][:, 0:TBk], in1=Ds[:, 0:TBk], op=ALU.mult), reads=["tmp2", "Ds"], writes=["PT0"])
            s.op("dve", lambda e: e.tensor_tensor(out=attn[:, 0:TBk], in0=pb[4][:, 256:256 + TBk], in1=Dm[:, 0:TBk], op=ALU.mult), reads=["B4", "Dm"], writes=["attn"])

            for ch in range(nch):
                cs = slice(ch * c, (ch + 1) * c)
                for h in range(2):
                    fs = slice(64 * h, 64 * h + 64); rs = slice(64 * h, 64 * h + c)
                    s.op("pe", lambda e: e.matmul(pb[5][rs, cs], lhsT=PT[0][rs, cs], rhs=I_s[rs, 0:c], start=True, stop=True), reads=["PT0", "I_s"], writes=["B5"])
                    s.op("pe", lambda e: e.matmul(pb[5][rs, 256 + ch * c:256 + (ch + 1) * c], lhsT=attn[rs, cs], rhs=I_s[rs, 0:c], start=True, stop=True),
                         reads=["attn", "I_s"], writes=["B5"])
                    s.op("pe", lambda e: e.matmul(pb[6][rs, ch * 64:(ch + 1) * 64], lhsT=kn[fs, cs], rhs=I_s[fs, 0:64], start=True, stop=True), reads=["cv", "I_s"], writes=["B6"])
                    s.op("pe", lambda e: e.matmul(pb[6][rs, 256 + ch * 64:256 + (ch + 1) * 64], lhsT=vs[fs, cs], rhs=I_s[fs, 0:64], start=True, stop=True),
                         reads=["cv", "I_s"], writes=["B6"])
            s.op("act", lambda e: e.activation(out=P[0][:, 0:TBk], in_=pb[5][:, 0:TBk], func=AF.Copy), reads=["B5"], writes=["P0"])
            s.op("dve", lambda e: e.tensor_tensor(out=v3(R[0][:, 0:TBk]), in0=v3(pb[5][:, 0:TBk]), in1=bc(I_s[:, None, 0:c], [128, nch, c]), op=ALU.add),
                 reads=["B5", "I_s"], writes=["R0"])
            s.op("act", lambda e: e.activation(out=attnT[:, 0:TBk], in_=pb[5][:, 256:256 + TBk], func=AF.Copy), reads=["B5"], writes=["attnT"])
            k4 = pb[6][:, 0:nch * 64].rearrange("p (n d) -> p n d", d=64)
            v4 = pb[6][:, 256:256 + nch * 64].rearrange("p (n d) -> p n d", d=64)
            s.op("dve", lambda e: e.tensor_tensor(out=Kbe[:, 0:nch, :], in0=k4, in1=bc(bge[:, p, 0:nch, None], [128, nch, 64]), op=ALU.mult), reads=["B6", "bge"], writes=["Kbe"])
            s.op("dve", lambda e: e.tensor_tensor(out=Kd[:, 0:nch, :], in0=k4, in1=bc(dk[:, p, 0:nch, None], [128, nch, 64]), op=ALU.mult), reads=["B6", "dk"], writes=["Kd"])
            s.op("dve", lambda e: e.tensor_tensor(out=bV[:, 0:nch, :], in0=v4, in1=bc(bs[:, p, 0:nch, None], [128, nch, 64]), op=ALU.mult), reads=["B6", "bs"], writes=["bV"])
            cur = 0
            for lev in range(1, nlev + 1):
                nxt = 1 - cur
                last = lev == nlev
                for ch in range(nch):
                    cs = slice(ch * c, (ch + 1) * c)
                    for h in range(2):
                        rs = slice(64 * h, 64 * h + c)
                        if not last:
                            s.op("pe", lambda e: e.matmul(pb[5][rs, cs], lhsT=PT[cur][rs, cs], rhs=P[cur][rs, cs], start=True, stop=True),
                                 reads=[f"PT{cur}", f"P{cur}"], writes=["B5"])
                        s.op("pe", lambda e: e.matmul(pb[5][rs, 256 + ch * c:256 + (ch + 1) * c], lhsT=P[cur][rs, cs], rhs=PT[cur][rs, cs], start=True, stop=True),
                             reads=[f"PT{cur}", f"P{cur}"], writes=["B5"])
                if not last:
                    s.op("act", lambda e: e.activation(out=P[nxt][:, 0:TBk], in_=pb[5][:, 0:TBk], func=AF.Copy), reads=["B5"], writes=[f"P{nxt}"])
                s.op("dve", lambda e: e.tensor_copy(out=PT[nxt][:, 0:TBk], in_=pb[5][:, 256:256 + TBk]), reads=["B5"], writes=[f"PT{nxt}"])
                for ch in range(nch):
                    cs = slice(ch * c, (ch + 1) * c)
                    for h in range(2):
                        rs = slice(64 * h, 64 * h + c)
                        s.op("pe", lambda e: e.matmul(pb[4][rs, cs], lhsT=PT[nxt][rs, cs], rhs=R[cur][rs, cs], start=True, stop=True),
                             reads=[f"PT{nxt}", f"R{cur}"], writes=["B4"])
                s.op("dve", lambda e: e.tensor_tensor(out=R[nxt][:, 0:TBk], in0=pb[4][:, 0:TBk], in1=R[cur][:, 0:TBk], op=ALU.add),
                     reads=["B4", f"R{cur}"], writes=[f"R{nxt}"])
                cur = nxt
            Rf = R[cur]; rk = f"R{cur}"
            for ch in range(nch):
                cs = slice(ch * c, (ch + 1) * c)
                for h in range(2):
                    rs = slice(64 * h, 64 * h + c)
                    s.op("pe", lambda e: e.matmul(pb[6][rs, ch * 64:(ch + 1) * 64], lhsT=Rf[rs, cs], rhs=bV[rs, ch, :], start=True, stop=True),
                         reads=[rk, "bV"], writes=["B6"])
                    s.op("pe", lambda e: e.matmul(pb[6][64 * h:64 * h + 64, 256 + ch * c:256 + (ch + 1) * c], lhsT=Kbe[rs, ch, :], rhs=Rf[rs, cs], start=True, stop=True),
                         reads=[rk, "Kbe"], writes=["B6"])
            s.op("act", lambda e: e.activation(out=u[:, 0:nch, :], in_=k4, func=AF.Copy), reads=["B6"], writes=["u"])
            s.op("dve", lambda e: e.tensor_copy(out=wT[:, 0:TBk], in_=pb[6][:, 256:256 + TBk]), reads=["B6"], writes=["wT"])
            skey = f"Sg{p}"
            for ch in range(nch):
                cs = slice(ch * c, (ch + 1) * c)
                seg = ch // cps
                if ch % cps == 0:
                    if is_sample:
                        s.dma("sp", Sg[:, p, :], sg_d[seg, p], writes=[skey])
                    elif first:
                        s.op("pool", lambda e: e.memset(Sg[:, p, :], 0.0), writes=[skey])
                for h in range(2):
                    fs = slice(64 * h, 64 * h + 64); rs = slice(64 * h, 64 * h + c)
                    s.op("pe", lambda e: e.matmul(pb[1][rs, 0:64], lhsT=wT[fs, cs], rhs=Sg[fs, p, :], start=True, stop=True), reads=["wT", skey], writes=["B1"])
                s.op("dve", lambda e: e.tensor_tensor(out=vn[:], in0=u[:, ch, :], in1=pb[1][:, 0:64], op=ALU.subtract), reads=["u", "B1"], writes=["vn"])
                for h in range(2):
                    fs = slice(64 * h, 64 * h + 64); rs = slice(64 * h, 64 * h + c)
                    s.op("pe", lambda e: e.matmul(pb[1][fs, 256 + ch * c:256 + (ch + 1) * c], lhsT=Sg[fs, p, :], rhs=QeT[fs, cs], start=True, stop=False),
                         reads=[skey, "QeT"], writes=["B1"])
                    s.op("pe", lambda e: e.matmul(pb[1][fs, 256 + ch * c:256 + (ch + 1) * c], lhsT=vn[rs, :], rhs=attnT[rs, cs], start=False, stop=True),
                         reads=["vn", "attnT"], writes=["B1"])
                for h in range(2):
                    fs = slice(64 * h, 64 * h + 64); rs = slice(64 * h, 64 * h + c)
                    s.op("pe", lambda e: e.matmul(pb[1][fs, 64:128], lhsT=Kd[rs, ch, :], rhs=vn[rs, :], start=True, stop=True), reads=["Kd", "vn"], writes=["B1"])
                s.op("dve", lambda e: e.scalar_tensor_tensor(out=Sg[:, p, :], in0=Sg[:, p, :], scalar=EBs[:, (ch + 1) * c - 1:(ch + 1) * c], in1=pb[1][:, 64:128],
                                                             op0=ALU.mult, op1=ALU.add), reads=[skey, "EBs", "B1"], writes=[skey])
                if is_sample and (ch + 1) % cps == 0:
                    s.dma("sp", ngs[seg, p], Sg[:, p, :], reads=[skey], writes=[f"o_ngs{seg}_{p}"])
            s.op("act", lambda e: e.activation(out=oTf[:, 0:TBk], in_=pb[1][:, 256:256 + TBk], func=AF.Copy), reads=["B1"], writes=["oTf"])
            s.op("pool", lambda e: e.tensor_tensor(out=tmp[0][:, 0:TBk], in0=oTf[:, 0:TBk], in1=oTf[:, 0:TBk], op=ALU.mult), reads=["oTf"], writes=["tmp0"])
            s.op("pe", lambda e: e.matmul(pb[3][:, 0:TBk], lhsT=ob2[:], rhs=tmp[0][:, 0:TBk], start=True, stop=True), reads=["ob2", "tmp0"], writes=["B3"])
            s.op("act", lambda e: e.activation(out=tmp[1][:, 0:TBk], in_=pb[3][:, 0:TBk], func=AF.Ln, scale=1.0 / 64, bias=EPS), reads=["B3"], writes=["tmp1"])
            s.op("act", lambda e: e.activation(out=tmp[1][:, 0:TBk], in_=tmp[1][:, 0:TBk], func=AF.Exp, scale=-0.5), reads=["tmp1"], writes=["tmp1"])
            s.op("dve", lambda e: e.tensor_tensor(out=oTf[:, 0:TBk], in0=oTf[:, 0:TBk], in1=tmp[1][:, 0:TBk], op=ALU.mult), reads=["oTf", "tmp1"], writes=["oTf"])
            s.op("dve", lambda e: e.scalar_tensor_tensor(out=oTn[:, p, 0:TBk], in0=oTf[:, 0:TBk], scalar=gnw[:, 0:1], in1=za[:, 0:TBk], op0=ALU.mult, op1=ALU.mult),
                 reads=["oTf", "gnw_t", "za"], writes=["oTn"])

        s.op("pool", lambda e: e.memset(smask[:, 0:TBk], 1.0), writes=["smask"])
        s.op("pool", lambda e: e.memset(v3(smask[:, 0:TBk])[:, :, 0:1], 0.0), reads=["smask"], writes=["smask"])
        for h in range(HB):
            pp, pk = inproj_fm(16 + h)
            s.op("act", lambda e: e.activation(out=qb[:, 0:TBk], in_=pp, func=AF.Silu), reads=[pk], writes=["qb"])
            pp, pk = inproj_fm(24 + h)
            s.op("act", lambda e: e.activation(out=zb[:, 0:TBk], in_=pp, func=AF.Silu), reads=[pk], writes=["zb"])
            pp, pk = inproj_fm(20 + h)
            s.op("act", lambda e: e.activation(out=ff[:, 0:TBk], in_=pp, func=AF.Exp, scale=-1.0), reads=[pk], writes=["ff"])
            s.op("dve", lambda e: e.tensor_scalar(out=ff[:, 0:TBk], in0=ff[:, 0:TBk], scalar1=1.0, scalar2=None, op0=ALU.add), reads=["ff"], writes=["ff"])
            s.op("dve", lambda e: e.reciprocal(out=ff[:, 0:TBk], in_=ff[:, 0:TBk]), reads=["ff"], writes=["ff"])
            s.op("dve", lambda e: e.tensor_scalar(out=ff[:, 0:TBk], in0=ff[:, 0:TBk], scalar1=oml[:, h:h + 1], scalar2=lb[:, h:h + 1], op0=ALU.mult, op1=ALU.add),
                 reads=["ff", "oml", "lb"], writes=["ff"])
            s.op("act", lambda e: e.activation(out=lf[:, 0:TBk], in_=ff[:, 0:TBk], func=AF.Ln), reads=["ff"], writes=["lf"])
            s.op("pool", lambda e: e.tensor_scalar(out=kb[:, 0:TBk], in0=ff[:, 0:TBk], scalar1=-1.0, scalar2=1.0, op0=ALU.mult, op1=ALU.add), reads=["ff"], writes=["kb"])
            s.op("dve", lambda e: e.tensor_tensor_scan(out=bb[:, 0:TBk], data0=smask[:, 0:TBk], data1=lf[:, 0:TBk], initial=0.0, op0=ALU.mult, op1=ALU.add),
                 reads=["smask", "lf"], writes=["bb"])
            b3 = v3(bb[:, 0:TBk])
            s.op("pool", lambda e: e.tensor_tensor(out=v3(bl[:, 0:TBk]), in0=b3, in1=bc(b3[:, :, c - 1:c], [128, nch, c]), op=ALU.subtract), reads=["bb"], writes=["bl"])
            s.op("act", lambda e: e.activation(out=Qe[:, 0:TBk], in_=bb[:, 0:TBk], func=AF.Exp), reads=["bb"], writes=["Qe"])
            s.op("act", lambda e: e.activation(out=Qx[:, 0:TBk], in_=bl[:, 0:TBk], func=AF.Exp), reads=["bl"], writes=["Qx"])
            s.op("act", lambda e: e.activation(out=Kdh[:, 0:TBk], in_=bl[:, 0:TBk], func=AF.Exp, scale=-1.0), reads=["bl"], writes=["Kdh"])
            s.op("act", lambda e: e.activation(out=ebl[:, 0:nch], in_=b3[:, :, c - 1], func=AF.Exp), reads=["bb"], writes=["ebl"])
            s.op("dve", lambda e: e.tensor_tensor(out=Qe[:, 0:TBk], in0=Qe[:, 0:TBk], in1=qb[:, 0:TBk], op=ALU.mult), reads=["Qe", "qb"], writes=["Qe"])
            s.op("pool", lambda e: e.tensor_tensor(out=Qx[:, 0:TBk], in0=Qx[:, 0:TBk], in1=qb[:, 0:TBk], op=ALU.mult), reads=["Qx", "qb"], writes=["Qx"])
            s.op("dve", lambda e: e.tensor_tensor(out=Kdh[:, 0:TBk], in0=Kdh[:, 0:TBk], in1=kb[:, 0:TBk], op=ALU.mult), reads=["Kdh", "kb"], writes=["Kdh"])
            for ch in range(nch):
                cs = slice(ch * c, (ch + 1) * c)
                i = pi_state[0] % 2; pi_state[0] += 1
                outp = pb[0][0:c, i * 256:i * 256 + 128]
                for k in range(8):
                    s.op("pe", lambda e: e.matmul(outp, lhsT=hT[:, k, cs], rhs=Wb[:, k, C_HI + 128 * h:C_HI + 128 * (h + 1)], start=(k == 0), stop=(k == 7)),
                         reads=["Wb", "hT"], writes=["B0"])
                s.op("act", lambda e: e.activation(out=vtok[0:c, ch, :], in_=outp, func=AF.Copy), reads=["B0"], writes=["vtok"])
                s.op("pe", lambda e: e.matmul(pb[4][0:c, ch * 128:(ch + 1) * 128], lhsT=Kdh[:, cs], rhs=identf[:], start=True, stop=True), reads=["Kdh", "identf"], writes=["B4", "B4"])
                s.op("pe", lambda e: e.matmul(pb[5][0:c, cs], lhsT=Kdh[:, cs], rhs=Qx[:, cs], start=True, stop=True), reads=["Kdh", "Qx"], writes=["B5"])
            s.op("dve", lambda e: e.tensor_copy(out=Kdt[0:c, 0:nch, :], in_=pb[4][0:c, 0:nch * 128].rearrange("p (n d) -> p n d", d=128)), reads=["B4", "B4"], writes=["Kdt"])
            s.op("dve", lambda e: e.tensor_tensor(out=v3(aTh[0:c, 0:TBk]), in0=v3(pb[5][0:c, 0:TBk]), in1=bc(Tri_s[0:c, None, 0:c], [c, nch, c]), op=ALU.mult),
                 reads=["B5", "Tri_s"], writes=["aTh"])
            skey = f"Sh{h}"
            for ch in range(nch):
                cs = slice(ch * c, (ch + 1) * c)
                seg = ch // cps
                if ch % cps == 0:
                    if is_sample:
                        s.dma("sp", Sh[:, h, :], sh_d[seg, h], writes=[skey])
                    elif first:
                        s.op("pool", lambda e: e.memset(Sh[:, h, :], 0.0), writes=[skey])
                s.op("pe", lambda e: e.matmul(pb[1][:, 256 + ch * c:256 + (ch + 1) * c], lhsT=Sh[:, h, :], rhs=Qe[:, cs], start=True, stop=False), reads=[skey, "Qe"], writes=["B1"])
                s.op("pe", lambda e: e.matmul(pb[1][:, 256 + ch * c:256 + (ch + 1) * c], lhsT=vtok[0:c, ch, :], rhs=aTh[0:c, cs], start=False, stop=True),
                     reads=["vtok", "aTh"], writes=["B1"])
                s.op("pe", lambda e: e.matmul(pb[1][:, 0:128], lhsT=Kdt[0:c, ch, :], rhs=vtok[0:c, ch, :], start=True, stop=True), reads=["Kdt", "vtok"], writes=["B1", "B1"])
                s.op("dve", lambda e: e.scalar_tensor_tensor(out=Sh[:, h, :], in0=Sh[:, h, :], scalar=ebl[:, ch:ch + 1], in1=pb[1][:, 0:128], op0=ALU.mult, op1=ALU.add),
                     reads=[skey, "ebl", "B1", "B1"], writes=[skey])
                if is_sample and (ch + 1) % cps == 0:
                    s.dma("sp", nhs[seg, h], Sh[:, h, :], reads=[skey], writes=[f"o_nhs{seg}_{h}"])
            s.op("act", lambda e: e.activation(out=oTf[:, 0:TBk], in_=pb[1][:, 256:256 + TBk], func=AF.Copy), reads=["B1"], writes=["oTf"])
            s.op("pool", lambda e: e.tensor_tensor(out=tmp[0][:, 0:TBk], in0=oTf[:, 0:TBk], in1=oTf[:, 0:TBk], op=ALU.mult), reads=["oTf"], writes=["tmp0"])
            s.op("pe", lambda e: e.matmul(pb[3][:, 0:TBk], lhsT=ones[:], rhs=tmp[0][:, 0:TBk], start=True, stop=True), reads=["ones", "tmp0"], writes=["B3"])
            s.op("act", lambda e: e.activation(out=tmp[1][:, 0:TBk], in_=pb[3][:, 0:TBk], func=AF.Ln, scale=1.0 / 128, bias=EPS), reads=["B3"], writes=["tmp1"])
            s.op("act", lambda e: e.activation(out=tmp[1][:, 0:TBk], in_=tmp[1][:, 0:TBk], func=AF.Exp, scale=-0.5), reads=["tmp1"], writes=["tmp1"])
            s.op("dve", lambda e: e.tensor_tensor(out=oTf[:, 0:TBk], in0=oTf[:, 0:TBk], in1=tmp[1][:, 0:TBk], op=ALU.mult), reads=["oTf", "tmp1"], writes=["oTf"])
            s.op("dve", lambda e: e.scalar_tensor_tensor(out=oTn[:, 4 + h, 0:TBk], in0=oTf[:, 0:TBk], scalar=hnw[:, 0:1], in1=zb[:, 0:TBk], op0=ALU.mult, op1=ALU.mult),
                 reads=["oTf", "hnw_t", "zb"], writes=["oTn"])

        for tt in range(ntt):
            X = xt[tt]
            for half in range(2):
                bank = pb[0] if half == 0 else pb[2]
                key = "pi_full" if half == 0 else "pg_full"
                for k in range(8):
                    s.op("pe", lambda e: e.matmul(bank[0:TT, :], lhsT=oTn[:, k, tt * TT:(tt + 1) * TT], rhs=WOb[:, k, half * 512:(half + 1) * 512], start=(k == 0), stop=(k == 7)),
                         reads=["oTn", "WOb"], writes=(["B0", "B0"] if half == 0 else ["B2", "B2", "B2"]))
                s.op("dve", lambda e: e.tensor_tensor(out=yo[0:TT, half * 512:(half + 1) * 512], in0=bank[0:TT, :], in1=X[0:TT, half * 512:(half + 1) * 512], op=ALU.add),
                     reads=(["B0", "B0"] if half == 0 else ["B2", "B2", "B2"]) + [X.name], writes=["yo"])
            s.op("act", lambda e: e.activation(out=sqj[0:TT, :], in_=yo[0:TT, :], func=AF.Square, accum_out=ss[0:TT, :]), reads=["yo"], writes=["sqj", "ss"])
            s.op("act", lambda e: e.activation(out=rr[0:TT, :], in_=ss[0:TT, :], func=AF.Ln, scale=1.0 / D, bias=EPS), reads=["ss"], writes=["rr"])
            s.op("act", lambda e: e.activation(out=rr[0:TT, :], in_=rr[0:TT, :], func=AF.Exp, scale=-0.5), reads=["rr"], writes=["rr"])
            s.op("dve", lambda e: e.scalar_tensor_tensor(out=yo2[0:TT, :], in0=yo[0:TT, :], scalar=rr[0:TT, :], in1=fnw[0:TT, :], op0=ALU.mult, op1=ALU.mult),
                 reads=["yo", "rr", "fnw_t"], writes=["yo2"])
            s.dma("sp", y_dst[t0 + tt * TT: t0 + (tt + 1) * TT, :], yo2[0:TT, :], reads=["yo2"], writes=[f"o_y{id(y_dst)}"], slot="yout")

    nblk = T // TB
    for b in range(nblk):
        block(xp, yp, b * TB, 1, TB, 64, b == 0, False)
    s.dma("sp", ncp[:, :, 0, :], halo[:, :, 0, :], reads=["halo"], writes=["o_ncp"])
    for p in range(NP):
        s.dma("sp", ngp[0, p], Sg[:, p, :], reads=[f"Sg{p}"], writes=[f"o_ngp{p}"])
    for h in range(HB):
        s.dma("sp", nhp[0, h], Sh[:, h, :], reads=[f"Sh{h}"], writes=[f"o_nhp{h}"])
    s.dma("sp", halo[:], sc_d, reads=["halo"], writes=["halo"])
    block(xs, ys, 0, 4, 16, 16, True, True)
    s.dma("sp", ncs, halo[:], reads=["halo"], writes=["o_ncs"])
    s.finish("sp")
    return nc, s


_PERM = np.concatenate([np.arange(0, 1536), np.arange(1536, 2048), np.arange(2064, 2576), np.arange(2576, 3088),
                        np.arange(3600, 4112), np.arange(3088, 3600), np.arange(2048, 2064)])


def _core_inputs(b, inp):
    f = lambda a: np.ascontiguousarray(a, dtype=np.float32)
    cwv = inp["conv_w"][0]
    scv = inp["state_conv"][0][4 * b:4 * b + 4]
    return {
        "xp": f(inp["x_prompt"][b]),
        "xs": f(inp["x_sample"][4 * b:4 * b + 4].reshape(64, D)),
        "w_in": f(inp["w_in"][0][:, _PERM]),
        "w_out": f(inp["w_out"][0]),
        "nw": f(inp["norm_w"][0].reshape(8, 128).T),
        "cw": f(cwv.reshape(4, 12, 128).transpose(2, 1, 0)),
        "alog": f(np.broadcast_to(inp["gdn_A_log"][0][None, :], (128, HA))),
        "dtb": f(np.broadcast_to(inp["gdn_dt_bias"][0][None, :], (128, HA))),
        "gnw": f(np.tile(inp["gdn_norm_w"][0], 2).reshape(128, 1)),
        "hnw": f(inp["hgrn_norm_w"][0].reshape(128, 1)),
        "lbl": f(inp["hgrn_lb_logits"].reshape(2, HB, 128).transpose(2, 1, 0)),
        "fnw": f(np.broadcast_to(inp["final_norm_w"][None, :], (128, D))),
        "sc": f(scv.reshape(4, 3, 12, 128).transpose(3, 2, 0, 1)),
        "sg": f(inp["state_gdn"][0][4 * b:4 * b + 4].reshape(4, NP, 128, 64)),
        "sh": f(inp["state_hgrn"][0][4 * b:4 * b + 4]),
    }


_CACHE = {}


def kernel(**inputs):
    inp = {k: np.asarray(v) for k, v in inputs.items()}
    Bp, T, _ = inp["x_prompt"].shape
    assert Bp == 4 and inp["x_sample"].shape[:2] == (16, 16)
    if T not in _CACHE:
        _CACHE[T] = build(T)[0]
    nc = _CACHE[T]
    in_maps = [_core_inputs(c % 4, inp) for c in range(8)]
    res = run_bass_kernel_spmd(nc, in_maps, core_ids=list(range(8)))
    r = res.results
    y_prompt = np.stack([r[b]["yp"] for b in range(4)]).astype(np.float32)
    y_sample = np.concatenate([r[b]["ys"].reshape(4, 16, D) for b in range(4)]).astype(np.float32)
    cvt = lambda a: a.transpose(2, 3, 1, 0).reshape(a.shape[2], 3, 1536)
    ncp_ = np.stack([cvt(r[b]["ncp"])[0] for b in range(4)])[None].astype(np.float32)
    ngp_ = np.stack([r[b]["ngp"].reshape(HA, 64, 64) for b in range(4)])[None].astype(np.float32)
    nhp_ = np.stack([r[b]["nhp"].reshape(HB, 128, 128) for b in range(4)])[None].astype(np.float32)
    ncs_ = np.concatenate([cvt(r[b]["ncs"]) for b in range(4)])[None].astype(np.float32)
    ngs_ = np.concatenate([r[b]["ngs"].reshape(4, HA, 64, 64) for b in range(4)])[None].astype(np.float32)
    nhs_ = np.concatenate([r[b]["nhs"].reshape(4, HB, 128, 128) for b in range(4)])[None].astype(np.float32)
    return (y_prompt, y_sample, ncp_, ngp_, nhp_, ncs_, ngs_, nhs_)
```

```python
import numpy as np
import concourse.bass as bass
import concourse.mybir as mybir
from concourse.bass_utils import run_bass_kernel_spmd

F32 = mybir.dt.float32
BF16 = mybir.dt.bfloat16
AF = mybir.ActivationFunctionType
ALU = mybir.AluOpType

D = 1024
HA, HB = 8, 4
NP = HA // 2
NCOL = 4112
EPS = 1e-6
C_HI = 3584
C_G = 4096


SAME_ENGINE_WAIT = True


class _Proxy:
    def __getattr__(self, name):
        return lambda *a, **k: (name, a, k)


_PROXY = _Proxy()
PSUM_KEYS = {"B0", "B1", "B2", "B3", "B4", "B5", "B6", "BT"}


class Sched:
    def __init__(self, nc):
        self.nc = nc
        self.eng = {"pe": nc.tensor, "dve": nc.vector, "act": nc.scalar, "pool": nc.gpsimd, "sp": nc.sync}
        self.sem = {k: nc.alloc_semaphore(name=f"s_{k}") for k in self.eng}
        self.cnt = {k: 0 for k in self.eng}
        self.seen = {k: {} for k in self.eng}
        self.last_w = {}
        self.readers = {}
        self.dma_sems = {}
        self.n_wait = 0
        self.n_ops = 0
        self.rec = None

    def emit(self, r):
        if r[0] == "op":
            _, e, call, reads, writes = r
            self.op(e, lambda eng: getattr(eng, call[0])(*call[1], **call[2]), reads, writes)
        else:
            _, q, out, in_, reads, writes, slot = r
            self.dma(q, out, in_, reads, writes, slot)

    def merge_emit(self, streams):
        units = []
        for st in streams:
            u = []
            for r in st:
                glued = (r[0] == "op" and r[2][0] == "matmul" and r[2][2].get("start") is False)
                if glued and u:
                    u[-1].append(r)
                else:
                    u.append([r])
            units.append(u)
        pos = [0] * len(units)
        tot = [max(1, len(u)) for u in units]
        while True:
            best, bi = None, -1
            for i, u in enumerate(units):
                if pos[i] < len(u):
                    f = pos[i] / tot[i]
                    if best is None or f < best:
                        best, bi = f, i
            if bi < 0:
                break
            for r in units[bi][pos[bi]]:
                self.emit(r)
            pos[bi] += 1

    def _wait(self, e, tok):
        name, sem, val = tok
        if name == "pe" and e == "pe":
            return
        if name == e and not SAME_ENGINE_WAIT:
            return
        if self.seen[e].get(name, 0) >= val:
            return
        self.eng[e].wait_ge(sem, val)
        self.seen[e][name] = val
        self.n_wait += 1

    def _deps(self, e, reads, writes):
        for k in reads:
            t = self.last_w.get(k)
            if t is not None:
                self._wait(e, t)
        for k in writes:
            t = self.last_w.get(k)
            if t is not None:
                self._wait(e, t)
            for t in self.readers.get(k, {}).values():
                self._wait(e, t)

    def _commit(self, tok, reads, writes):
        for k in reads:
            self.readers.setdefault(k, {})[tok[0]] = tok
        for k in writes:
            self.last_w[k] = tok
            self.readers[k] = {}

    def op(self, e, fn, reads=(), writes=()):
        if self.rec is not None:
            self.rec.append(("op", e, fn(_PROXY), tuple(reads), tuple(writes)))
            return None
        ex = [k for k in reads if k in PSUM_KEYS]
        if ex:
            writes = list(writes) + ex
        self._deps(e, reads, writes)
        ins = fn(self.eng[e])
        self.cnt[e] += 1
        ins.then_inc(self.sem[e], 1)
        tok = (e, self.sem[e], self.cnt[e])
        self._commit(tok, reads, writes)
        self.n_ops += 1
        return tok

    def dma(self, q, out, in_, reads=(), writes=(), slot=None):
        if self.rec is not None:
            self.rec.append(("dma", q, out, in_, tuple(reads), tuple(writes), slot))
            return None
        self._deps(q, reads, writes)
        slot = slot or (writes[0] if writes else reads[0])
        sname = f"d_{slot}"
        if sname not in self.dma_sems:
            self.dma_sems[sname] = [self.nc.alloc_semaphore(name=sname), 0]
        ent = self.dma_sems[sname]
        ent[1] += 16
        self.eng[q].dma_start(out=out, in_=in_).then_inc(ent[0], 16)
        tok = (sname, ent[0], ent[1])
        self._commit(tok, reads, writes)
        self.n_ops += 1
        return tok

    def finish(self, e="sp"):
        for k, t in list(self.last_w.items()):
            self._wait(e, t)


def bc(ap, shape):
    return ap.to_broadcast(list(shape))


def build(T, TB=256):
    nc = bass.Bass("TRN2", target_bir_lowering=False)
    s = Sched(nc)
    dt_in = lambda n, sh: nc.dram_tensor(n, list(sh), F32, kind="ExternalInput").ap()
    dt_out = lambda n, sh: nc.dram_tensor(n, list(sh), F32, kind="ExternalOutput").ap()
    xp = dt_in("xp", [T, D]); xs = dt_in("xs", [64, D])
    w_in = dt_in("w_in", [D, NCOL]); w_out = dt_in("w_out", [D, D])
    nw_d = dt_in("nw", [128, 8]); cw_d = dt_in("cw", [128, 12, 4])
    alog_d = dt_in("alog", [128, HA]); dtb_d = dt_in("dtb", [128, HA])
    gnw_d = dt_in("gnw", [128, 1]); hnw_d = dt_in("hnw", [128, 1])
    lbl_d = dt_in("lbl", [128, HB, 2]); fnw_d = dt_in("fnw", [128, D])
    sc_d = dt_in("sc", [128, 12, 4, 3])
    sg_d = dt_in("sg", [4, NP, 128, 64]); sh_d = dt_in("sh", [4, HB, 128, 128])
    yp = dt_out("yp", [T, D]); ys = dt_out("ys", [64, D])
    ncp = dt_out("ncp", [128, 12, 1, 3]); ngp = dt_out("ngp", [1, NP, 128, 64]); nhp = dt_out("nhp", [1, HB, 128, 128])
    ncs = dt_out("ncs", [128, 12, 4, 3]); ngs = dt_out("ngs", [4, NP, 128, 64]); nhs = dt_out("nhs", [4, HB, 128, 128])

    sb = lambda n, sh, d=F32: nc.alloc_sbuf_tensor(n, list(sh), d)
    Wb = sb("Wb", [128, 8, NCOL], BF16)
    WOb = sb("WOb", [128, 8, D], BF16)
    nw = sb("nw_t", [128, 8]); cw = sb("cw_t", [128, 12, 4])
    alog = sb("alog_t", [128, HA]); dtb = sb("dtb_t", [128, HA]); negA = sb("negA", [128, HA])
    gnw = sb("gnw_t", [128, 1]); hnw = sb("hnw_t", [128, 1])
    lbl = sb("lbl_t", [128, HB, 2]); lb = sb("lb", [128, HB]); oml = sb("oml", [128, HB])
    fnw = sb("fnw_t", [128, D])
    identb = sb("identb", [128, 128], BF16); identf = sb("identf", [128, 128])
    ones = sb("ones", [128, 128]); ob2 = sb("ob2", [128, 128])
    fgt = sb("fgt", [128, 128]); fle = sb("fle", [128, 128])
    I_s = sb("I_s", [128, 64]); U_s = sb("U_s", [128, 64]); Tri_s = sb("Tri_s", [128, 64]); Mc_s = sb("Mc_s", [128, 64])
    halo = sb("halo", [128, 12, 4, 3])
    Sg = sb("Sg", [128, NP, 64]); Sh = sb("Sh", [128, HB, 128])
    W_ = TB
    xt = [sb(f"xt{i}", [128, D]) for i in range(2)]
    sqj = sb("sqj", [128, D], BF16)
    xb = sb("xb", [128, D], BF16)
    hT = sb("hT", [128, 8, W_], BF16)
    ss = sb("ss", [128, 1]); rr = sb("rr", [128, 1])
    raw = sb("raw", [128, 3, W_ + 12])
    cv = sb("cv", [128, 3, W_])
    tmp = [sb(f"tmp{i}", [128, W_]) for i in range(4)]
    za = sb("za", [128, W_])
    cvb = sb("cvb", [128, 3, W_], BF16)
    I_sb = sb("I_sb", [128, 64], BF16); ob2b = sb("ob2b", [128, 128], BF16); onesb = sb("onesb", [128, 128], BF16)
    Sgb = sb("Sgb", [128, NP, 64], BF16); Shb = sb("Shb", [128, HB, 128], BF16)
    sqb = sb("sqb", [128, W_], BF16); sqbH = sb("sqbH", [128, W_], BF16)
    G = sb("G", [128, 4, 16]); Gb = sb("Gb", [128, 4, HA]); Gg = sb("Gg", [128, 4, HA])
    gs = sb("gs", [128, NP, 4]); bs = sb("bs", [128, NP, 4]); nbs = sb("nbs", [128, NP, 4])
    gc = sb("gc", [128, NP, 4]); gl = sb("gl", [128, NP, 4]); egc = sb("egc", [128, NP, 4])
    dk = sb("dk", [128, NP, 4]); bge = sb("bge", [128, NP, 4])
    rhsD = sb("rhsD", [128, W_]); Dg = sb("Dg", [128, W_])
    Ee = sb("Ee", [128, W_]); Dm = sb("Dm", [128, W_]); Ds = sb("Ds", [128, W_])
    EBs = sb("EBs", [128, W_])
    P = [sb(f"P{i}", [128, W_], BF16) for i in range(2)]
    PT = [sb(f"PT{i}", [128, W_], BF16) for i in range(2)]
    R = [sb(f"R{i}", [128, W_], BF16) for i in range(2)]
    attn = sb("attn", [128, W_], BF16); attnT = sb("attnT", [128, W_], BF16)
    Kbe = sb("Kbe", [128, 4, 64], BF16); Kd = sb("Kd", [128, 4, 64], BF16); bV = sb("bV", [128, 4, 64], BF16)
    u = sb("u", [128, 4, 64]); wT = sb("wT", [128, W_], BF16); QeT = sb("QeT", [128, W_], BF16)
    vn = sb("vn", [128, 64], BF16)
    oTf = sb("oTf", [128, W_])
    oTn = sb("oTn", [128, 8, W_], BF16)
    qb = sb("qb", [128, W_]); ff = sb("ff", [128, W_]); lf = sb("lf", [128, W_]); kb = sb("kb", [128, W_])
    bb = sb("bb", [128, W_]); bl = sb("bl", [128, W_])
    Qe = sb("Qe", [128, W_], BF16); Qx = sb("Qx", [128, W_], BF16); Kdh = sb("Kdh", [128, W_], BF16)
    Qef = sb("Qef", [128, W_]); Qxf = sb("Qxf", [128, W_]); Kdf = sb("Kdf", [128, W_])
    ebl = sb("ebl", [128, 4]); zb = sb("zb", [128, W_])
    vtok = sb("vtok", [64, 4, 128], BF16); Kdt = sb("Kdt", [64, 4, 128], BF16); aTh = sb("aTh", [64, W_], BF16)
    smask = sb("smask", [128, W_])
    tmpH = [sb(f"tmpH{i}", [128, W_]) for i in range(2)]; oTfH = sb("oTfH", [128, W_])
    yo = sb("yo", [128, D]); yo2 = sb("yo2", [128, D])
    stg = sb("stg", [128, NCOL])
    pb = [nc.alloc_psum_tensor(f"pb{i}", [128, 512], F32) for i in range(7)]
    pT = nc.alloc_psum_tensor("BT", [128, 8, 128], BF16)

    def aff(out, cmp, fill_in, step=-1, cm=1, base=0):
        s.op("pool", lambda e: e.memset(out[:], fill_in), writes=[out.name])
        s.op("pool", lambda e: e.affine_select(out=out[:], in_=out[:], pattern=[[step, 128]], compare_op=cmp,
                                               fill=0.0, base=base, channel_multiplier=cm), reads=[out.name], writes=[out.name])
    aff(identf, ALU.is_equal, 1.0)
    aff(fgt, ALU.is_gt, 1.0)
    aff(fle, ALU.is_gt, 1.0, step=1, cm=-1, base=1)
    s.op("pool", lambda e: e.memset(ones[:], 1.0), writes=["ones"])
    s.op("pool", lambda e: e.memset(ob2[:], 0.0), writes=["ob2"])
    for h in range(2):
        sl = slice(64 * h, 64 * h + 64)
        s.op("pool", lambda e: e.memset(ob2[sl, sl], 1.0), reads=["ob2"], writes=["ob2"])
    s.op("dve", lambda e: e.tensor_copy(out=identb[:], in_=identf[:]), reads=["identf"], writes=["identb"])
    for (dst, src) in ((I_s, identf), (U_s, fgt), (Tri_s, fle)):
        for h in range(2):
            sl = slice(64 * h, 64 * h + 64)
            s.op("dve", lambda e: e.tensor_copy(out=dst[sl, :], in_=src[sl, sl]), reads=[src.name], writes=[dst.name])
    s.op("dve", lambda e: e.tensor_tensor(out=Mc_s[:], in0=U_s[:], in1=I_s[:], op=ALU.add), reads=["U_s", "I_s"], writes=["Mc_s"])
    s.op("dve", lambda e: e.tensor_copy(out=I_sb[:], in_=I_s[:]), reads=["I_s"], writes=["I_sb"])
    s.op("dve", lambda e: e.tensor_copy(out=ob2b[:], in_=ob2[:]), reads=["ob2"], writes=["ob2b"])
    s.op("dve", lambda e: e.tensor_copy(out=onesb[:], in_=ones[:]), reads=["ones"], writes=["onesb"])
    for t_, d_ in ((nw, nw_d), (cw, cw_d), (alog, alog_d), (dtb, dtb_d), (gnw, gnw_d), (hnw, hnw_d), (lbl, lbl_d), (fnw, fnw_d)):
        s.dma("sp", t_[:], d_, writes=[t_.name])
    s.op("act", lambda e: e.activation(out=negA[:], in_=alog[:], func=AF.Exp), reads=["alog_t"], writes=["negA"])
    s.op("dve", lambda e: e.tensor_scalar(out=negA[:], in0=negA[:], scalar1=-1.0, scalar2=None, op0=ALU.mult), reads=["negA"], writes=["negA"])
    s.op("dve", lambda e: e.tensor_tensor(out=lb[:], in0=lbl[:, :, 1], in1=lbl[:, :, 0], op=ALU.subtract), reads=["lbl_t"], writes=["lb"])
    s.op("act", lambda e: e.activation(out=lb[:], in_=lb[:], func=AF.Exp), reads=["lb"], writes=["lb"])
    s.op("dve", lambda e: e.tensor_scalar(out=lb[:], in0=lb[:], scalar1=1.0, scalar2=None, op0=ALU.add), reads=["lb"], writes=["lb"])
    s.op("dve", lambda e: e.reciprocal(out=lb[:], in_=lb[:]), reads=["lb"], writes=["lb"])
    s.op("dve", lambda e: e.tensor_scalar(out=oml[:], in0=lb[:], scalar1=-1.0, scalar2=1.0, op0=ALU.mult, op1=ALU.add), reads=["lb"], writes=["oml"])
    w_in_v = w_in.rearrange("(k p) n -> p k n", p=128)
    for k in range(8):
        s.dma("sp", stg[:], w_in_v[:, k, :], writes=["stg"])
        if k % 2 == 0:
            s.op("dve", lambda e: e.tensor_scalar(out=Wb[:, k, :], in0=stg[:], scalar1=nw[:, k:k + 1], scalar2=None, op0=ALU.mult),
                 reads=["stg", "nw_t"], writes=["Wb"])
        else:
            s.op("act", lambda e: e.activation(out=Wb[:, k, :], in_=stg[:], func=AF.Copy, scale=nw[:, k:k + 1]),
                 reads=["stg", "nw_t"], writes=["Wb"])
    w_out_v = w_out.rearrange("(k p) n -> p k n", p=128)
    for k in range(8):
        s.dma("sp", stg[:, 0:D], w_out_v[:, k, :], writes=["stg"])
        if k % 2 == 0:
            s.op("dve", lambda e: e.tensor_copy(out=WOb[:, k, :], in_=stg[:, 0:D]), reads=["stg"], writes=["WOb"])
        else:
            s.op("act", lambda e: e.activation(out=WOb[:, k, :], in_=stg[:, 0:D], func=AF.Copy), reads=["stg"], writes=["WOb"])

    def block(x_src, y_dst, t0, nseg, seglen, c, first, is_sample):
        TBk = nseg * seglen
        nch = TBk // c
        cps = seglen // c
        TT = min(128, TBk)
        ntt = TBk // TT
        nlev = {64: 5, 16: 3}[c]
        v3 = lambda ap: ap.rearrange("p (n c) -> p n c", c=c)

        for tt in range(ntt):
            X = xt[tt]
            s.dma("sp", X[0:TT, :], x_src[t0 + tt * TT: t0 + (tt + 1) * TT, :], writes=[X.name])
            s.op("act", lambda e: e.activation(out=sqj[0:TT, :], in_=X[0:TT, :], func=AF.Square, accum_out=ss[0:TT, :]),
                 reads=[X.name], writes=["sqj", "ss"])
            s.op("act", lambda e: e.activation(out=rr[0:TT, :], in_=ss[0:TT, :], func=AF.Ln, scale=1.0 / D, bias=EPS), reads=["ss"], writes=["rr"])
            s.op("act", lambda e: e.activation(out=rr[0:TT, :], in_=rr[0:TT, :], func=AF.Exp, scale=-0.5), reads=["rr"], writes=["rr"])
            s.op("dve", lambda e: e.tensor_scalar(out=xb[0:TT, :], in0=X[0:TT, :], scalar1=rr[0:TT, :], scalar2=None, op0=ALU.mult),
                 reads=[X.name, "rr"], writes=["xb"])
            for k in range(8):
                s.op("pe", lambda e: e.transpose(out=pT[:, k, 0:TT], in_=xb[0:TT, k * 128:(k + 1) * 128], identity=identb[0:TT, 0:TT]),
                     reads=["xb", "identb"], writes=["BT"])
            s.op("dve", lambda e: e.tensor_copy(out=hT[:, :, tt * TT:(tt + 1) * TT], in_=pT[:, :, 0:TT]), reads=["BT"], writes=["hT"])

        def silu_from(src, srckeys, dst, dstkey, scr, scrkey, W, outdt_note=None):
            s.op("act", lambda e: e.activation(out=scr, in_=src, func=AF.Exp, scale=-1.0), reads=srckeys, writes=[scrkey])
            s.op("act", lambda e: e.activation(out=scr, in_=scr, func=AF.Ln, bias=1.0), reads=[scrkey], writes=[scrkey])
            s.op("act", lambda e: e.activation(out=scr, in_=scr, func=AF.Exp, scale=-1.0), reads=[scrkey], writes=[scrkey])
            s.op("dve", lambda e: e.tensor_tensor(out=dst, in0=src, in1=scr, op=ALU.mult), reads=list(srckeys) + [scrkey], writes=[dstkey])


        def inproj_fm(ct, i=0):
            key = "B0"
            out = pb[0][:, i * 256: i * 256 + TBk]
            for k in range(8):
                s.op("pe", lambda e: e.matmul(out, lhsT=Wb[:, k, ct * 128:(ct + 1) * 128], rhs=hT[:, k, 0:TBk], start=(k == 0), stop=(k == 7)),
                     reads=["Wb", "hT"], writes=[key])
            return out, key

        for ch in range(nch):
            for h in range(2):
                out = pb[2][64 * h:64 * h + c, ch * 16:(ch + 1) * 16]
                for k in range(8):
                    s.op("pe", lambda e: e.matmul(out, lhsT=hT[:, k, ch * c:(ch + 1) * c], rhs=Wb[:, k, C_G:C_G + 16], start=(k == 0), stop=(k == 7)),
                         reads=["Wb", "hT"], writes=["B2"])
        Gv = G[:, 0:nch, :]
        s.op("dve", lambda e: e.tensor_copy(out=Gv, in_=pb[2][:, 0:nch * 16].rearrange("p (n g) -> p n g", g=16)), reads=["B2"], writes=["G"])
        Gbv = Gb[:, 0:nch, :]; Ggv = Gg[:, 0:nch, :]
        s.op("act", lambda e: e.activation(out=Gbv, in_=Gv[:, :, 0:HA], func=AF.Exp, scale=-1.0), reads=["G"], writes=["Gb"])
        s.op("act", lambda e: e.activation(out=Gbv, in_=Gbv, func=AF.Ln, bias=1.0), reads=["Gb"], writes=["Gb"])
        s.op("act", lambda e: e.activation(out=Gbv, in_=Gbv, func=AF.Exp, scale=-1.0), reads=["Gb"], writes=["Gb"])
        s.op("dve", lambda e: e.tensor_tensor(out=Ggv, in0=Gv[:, :, HA:2 * HA], in1=bc(dtb[:, None, :], [128, nch, HA]), op=ALU.add),
             reads=["G", "dtb_t"], writes=["Gg"])
        s.op("act", lambda e: e.activation(out=Ggv, in_=Ggv, func=AF.Exp), reads=["Gg"], writes=["Gg"])
        s.op("act", lambda e: e.activation(out=Ggv, in_=Ggv, func=AF.Ln, bias=1.0), reads=["Gg"], writes=["Gg"])
        s.op("dve", lambda e: e.tensor_tensor(out=Ggv, in0=Ggv, in1=bc(negA[:, None, :], [128, nch, HA]), op=ALU.mult),
             reads=["Gg", "negA"], writes=["Gg"])
        gsv = gs[:, :, 0:nch]; bsv = bs[:, :, 0:nch]; nbsv = nbs[:, :, 0:nch]
        gcv = gc[:, :, 0:nch]; glv = gl[:, :, 0:nch]; egcv = egc[:, :, 0:nch]; dkv = dk[:, :, 0:nch]; bgev = bge[:, :, 0:nch]
        for h in range(2):
            sl = slice(64 * h, 64 * h + 64)
            for (dst, src, kd, ks) in ((gs, Gg, "gs", "Gg"), (bs, Gb, "bs", "Gb")):
                for p in range(NP):
                    s.op("dve", lambda e: e.tensor_copy(out=dst[sl, p, 0:nch], in_=src[sl, 0:nch, 2 * p + h]), reads=[ks], writes=[kd])
        s.op("dve", lambda e: e.tensor_scalar(out=nbsv, in0=bsv, scalar1=-1.0, scalar2=None, op0=ALU.mult), reads=["bs"], writes=["nbs"])
        for h in range(2):
            rs = slice(64 * h, 64 * h + c)
            s.op("pe", lambda e: e.matmul(pb[2][rs, 64:64 + NP * nch], lhsT=Tri_s[rs, 0:c], rhs=gs[rs, :, 0:nch], start=True, stop=True),
                 reads=["Tri_s", "gs"], writes=["B2"])
            s.op("pe", lambda e: e.matmul(pb[2][rs, 96:96 + NP * nch], lhsT=ones[rs, 0:c], rhs=gs[rs, :, 0:nch], start=True, stop=True),
                 reads=["ones", "gs"], writes=["B2"])
        s.op("dve", lambda e: e.tensor_copy(out=gcv, in_=pb[2][:, 64:64 + NP * nch].rearrange("p (a n) -> p a n", n=nch)), reads=["B2"], writes=["gc"])
        s.op("dve", lambda e: e.tensor_copy(out=glv, in_=pb[2][:, 96:96 + NP * nch].rearrange("p (a n) -> p a n", n=nch)), reads=["B2"], writes=["gl"])
        s.op("act", lambda e: e.activation(out=egcv, in_=gcv, func=AF.Exp), reads=["gc"], writes=["egc"])
        s.op("dve", lambda e: e.tensor_tensor(out=dkv, in0=glv, in1=gcv, op=ALU.subtract), reads=["gl", "gc"], writes=["dk"])
        s.op("act", lambda e: e.activation(out=dkv, in_=dkv, func=AF.Exp), reads=["dk"], writes=["dk"])
        s.op("dve", lambda e: e.tensor_tensor(out=bgev, in0=bsv, in1=egcv, op=ALU.mult), reads=["bs", "egc"], writes=["bge"])

        recA = []
        s.rec = recA
        for p in range(NP):
            for i3 in range(3):
                ct = 4 * i3 + p
                pp, pk = inproj_fm(ct)
                rv = raw[:, i3, 0:nseg * (seglen + 3)].rearrange("p (n c) -> p n c", c=seglen + 3)
                if first and not is_sample:
                    s.op("pool", lambda e: e.memset(rv[:, :, 0:3], 0.0), reads=["raw"], writes=["raw"])
                else:
                    s.op("pool", lambda e: e.tensor_copy(out=rv[:, :, 0:3], in_=halo[:, ct, 0:nseg, :]), reads=["halo"], writes=["raw"])
                s.op("act", lambda e: e.activation(out=rv[:, :, 3:3 + seglen], in_=pp.rearrange("p (n c) -> p n c", c=seglen), func=AF.Copy),
                     reads=[pk], writes=["raw"])
                s.op("pool", lambda e: e.tensor_copy(out=halo[:, ct, 0:nseg, :], in_=rv[:, :, seglen:seglen + 3]), reads=["raw"], writes=["halo"])
                cvv = cv[:, i3, 0:TBk].rearrange("p (n c) -> p n c", c=seglen)
                s.op("dve", lambda e: e.tensor_scalar(out=cvv, in0=rv[:, :, 0:seglen], scalar1=cw[:, ct, 0:1], scalar2=None, op0=ALU.mult),
                     reads=["raw", "cw_t"], writes=["cv"])
                for j in range(1, 4):
                    s.op("dve", lambda e: e.scalar_tensor_tensor(out=cvv, in0=rv[:, :, j:j + seglen], scalar=cw[:, ct, j:j + 1], in1=cvv,
                                                                 op0=ALU.mult, op1=ALU.add), reads=["raw", "cw_t", "cv"], writes=["cv"])
                if i3 < 2:
                    silu_from(cv[:, i3, 0:TBk], ["cv"], cv[:, i3, 0:TBk], "cv", tmp[3][:, 0:TBk], "tmp3", TBk)
                else:
                    silu_from(cv[:, i3, 0:TBk], ["cv"], cvb[:, 2, 0:TBk], "cvb", tmp[3][:, 0:TBk], "tmp3", TBk)
            for i3 in range(2):
                src = cv[:, i3, 0:TBk]
                s.op("pool", lambda e: e.tensor_tensor(out=sqb[:, 0:TBk], in0=src, in1=src, op=ALU.mult), reads=["cv"], writes=["sqb"])
                s.op("pe", lambda e: e.matmul(pb[3][:, 0:TBk], lhsT=ob2b[:], rhs=sqb[:, 0:TBk], start=True, stop=True), reads=["ob2b", "sqb"], writes=["B3"])
                s.op("act", lambda e: e.activation(out=tmp[1][:, 0:TBk], in_=pb[3][:, 0:TBk], func=AF.Ln, bias=EPS), reads=["B3"], writes=["tmp1"])
                s.op("act", lambda e: e.activation(out=tmp[1][:, 0:TBk], in_=tmp[1][:, 0:TBk], func=AF.Exp, scale=-0.5), reads=["tmp1"], writes=["tmp1"])
                if i3 == 0:
                    s.op("dve", lambda e: e.scalar_tensor_tensor(out=cvb[:, 0, 0:TBk], in0=src, scalar=0.125, in1=tmp[1][:, 0:TBk], op0=ALU.mult, op1=ALU.mult),
                         reads=["cv", "tmp1"], writes=["cvb"])
                else:
                    s.op("dve", lambda e: e.tensor_tensor(out=cvb[:, 1, 0:TBk], in0=src, in1=tmp[1][:, 0:TBk], op=ALU.mult), reads=["cv", "tmp1"], writes=["cvb"])
            qn = cvb[:, 0, 0:TBk]; kn = cvb[:, 1, 0:TBk]; vs = cvb[:, 2, 0:TBk]
            pp, pk = inproj_fm(12 + p)
            s.op("act", lambda e: e.activation(out=za[:, 0:TBk], in_=pp, func=AF.Copy), reads=[pk], writes=["za"])
            silu_from(za[:, 0:TBk], ["za"], za[:, 0:TBk], "za", tmp[3][:, 0:TBk], "tmp3", TBk)
            s.op("dve", lambda e: e.tensor_tensor(out=v3(rhsD[:, 0:TBk]), in0=bc(gs[:, p, 0:nch, None], [128, nch, c]), in1=bc(U_s[:, None, 0:c], [128, nch, c]), op=ALU.mult),
                 reads=["gs", "U_s"], writes=["rhsD"])
            s.op("pool", lambda e: e.tensor_tensor(out=v3(Dg[:, 0:TBk]), in0=bc(egc[:, p, 0:nch, None], [128, nch, c]), in1=bc(I_s[:, None, 0:c], [128, nch, c]), op=ALU.mult),
                 reads=["egc", "I_s"], writes=["Dg"])
            for h in range(2):
                rs = slice(64 * h, 64 * h + c)
                s.op("pe", lambda e: e.matmul(pb[3][rs, 0:TBk], lhsT=Tri_s[rs, 0:c], rhs=rhsD[rs, 0:TBk], start=True, stop=True),
                     reads=["Tri_s", "rhsD"], writes=["B3"])
            s.op("act", lambda e: e.activation(out=Ee[:, 0:TBk], in_=pb[3][:, 0:TBk], func=AF.Exp), reads=["B3"], writes=["Ee"])
            s.op("pool", lambda e: e.tensor_tensor(out=v3(Dm[:, 0:TBk]), in0=v3(Ee[:, 0:TBk]), in1=bc(Mc_s[:, None, 0:c], [128, nch, c]), op=ALU.mult),
                 reads=["Ee", "Mc_s"], writes=["Dm"])
            s.op("pool", lambda e: e.tensor_tensor(out=v3(Ds[:, 0:TBk]), in0=v3(Ee[:, 0:TBk]), in1=bc(U_s[:, None, 0:c], [128, nch, c]), op=ALU.mult),
                 reads=["Ee", "U_s"], writes=["Ds"])
            for h in range(2):
                rs = slice(64 * h, 64 * h + c)
                s.op("pe", lambda e: e.matmul(pb[3][64 * h:64 * h + 64, 0:TBk], lhsT=ones[rs, 0:64], rhs=Dg[rs, 0:TBk], start=True, stop=True),
                     reads=["ones", "Dg"], writes=["B3"])
            s.op("act", lambda e: e.activation(out=EBs[:, 0:TBk], in_=pb[3][:, 0:TBk], func=AF.Copy), reads=["B3"], writes=["EBs"])
            s.op("dve", lambda e: e.tensor_tensor(out=QeT[:, 0:TBk], in0=qn, in1=EBs[:, 0:TBk], op=ALU.mult), reads=["cvb", "EBs"], writes=["QeT"])
            for ch in range(nch):
                cs = slice(ch * c, (ch + 1) * c)
                for h in range(2):
                    fs = slice(64 * h, 64 * h + 64); rs = slice(64 * h, 64 * h + c)
                    s.op("pe", lambda e: e.matmul(pb[4][rs, cs], lhsT=kn[fs, cs], rhs=kn[fs, cs], start=True, stop=True), reads=["cvb"], writes=["B4"])
                    s.op("pe", lambda e: e.matmul(pb[4][rs, 256 + ch * c:256 + (ch + 1) * c], lhsT=qn[fs, cs], rhs=kn[fs, cs], start=True, stop=True),
                         reads=["cvb"], writes=["B4"])
            s.op("dve", lambda e: e.tensor_tensor(out=v3(tmp[2][:, 0:TBk]), in0=v3(pb[4][:, 0:TBk]), in1=bc(nbs[:, p, 0:nch, None], [128, nch, c]), op=ALU.mult),
                 reads=["B4", "nbs"], writes=["tmp2"])
            s.op("pool", lambda e: e.tensor_tensor(out=PT[0][:, 0:TBk], in0=tmp[2][:, 0:TBk], in1=Ds[:, 0:TBk], op=ALU.mult), reads=["tmp2", "Ds"], writes=["PT0"])
            s.op("dve", lambda e: e.tensor_tensor(out=attn[:, 0:TBk], in0=pb[4][:, 256:256 + TBk], in1=Dm[:, 0:TBk], op=ALU.mult), reads=["B4", "Dm"], writes=["attn"])

            for ch in range(nch):
                cs = slice(ch * c, (ch + 1) * c)
                for h in range(2):
                    fs = slice(64 * h, 64 * h + 64); rs = slice(64 * h, 64 * h + c)
                    s.op("pe", lambda e: e.matmul(pb[5][rs, cs], lhsT=PT[0][rs, cs], rhs=I_sb[rs, 0:c], start=True, stop=True), reads=["PT0", "I_sb"], writes=["B5"])
                    s.op("pe", lambda e: e.matmul(pb[5][rs, 256 + ch * c:256 + (ch + 1) * c], lhsT=attn[rs, cs], rhs=I_sb[rs, 0:c], start=True, stop=True),
                         reads=["attn", "I_sb"], writes=["B5"])
                    s.op("pe", lambda e: e.matmul(pb[6][rs, ch * 64:(ch + 1) * 64], lhsT=kn[fs, cs], rhs=I_sb[fs, 0:64], start=True, stop=True), reads=["cvb", "I_sb"], writes=["B6"])
                    s.op("pe", lambda e: e.matmul(pb[6][rs, 256 + ch * 64:256 + (ch + 1) * 64], lhsT=vs[fs, cs], rhs=I_sb[fs, 0:64], start=True, stop=True),
                         reads=["cvb", "I_sb"], writes=["B6"])
            s.op("act", lambda e: e.activation(out=P[0][:, 0:TBk], in_=pb[5][:, 0:TBk], func=AF.Copy), reads=["B5"], writes=["P0"])
            s.op("dve", lambda e: e.tensor_tensor(out=v3(R[0][:, 0:TBk]), in0=v3(pb[5][:, 0:TBk]), in1=bc(I_s[:, None, 0:c], [128, nch, c]), op=ALU.add),
                 reads=["B5", "I_s"], writes=["R0"])
            s.op("act", lambda e: e.activation(out=attnT[:, 0:TBk], in_=pb[5][:, 256:256 + TBk], func=AF.Copy), reads=["B5"], writes=["attnT"])
            k4 = pb[6][:, 0:nch * 64].rearrange("p (n d) -> p n d", d=64)
            v4 = pb[6][:, 256:256 + nch * 64].rearrange("p (n d) -> p n d", d=64)
            s.op("dve", lambda e: e.tensor_tensor(out=Kbe[:, 0:nch, :], in0=k4, in1=bc(bge[:, p, 0:nch, None], [128, nch, 64]), op=ALU.mult), reads=["B6", "bge"], writes=["Kbe"])
            s.op("dve", lambda e: e.tensor_tensor(out=Kd[:, 0:nch, :], in0=k4, in1=bc(dk[:, p, 0:nch, None], [128, nch, 64]), op=ALU.mult), reads=["B6", "dk"], writes=["Kd"])
            s.op("dve", lambda e: e.tensor_tensor(out=bV[:, 0:nch, :], in0=v4, in1=bc(bs[:, p, 0:nch, None], [128, nch, 64]), op=ALU.mult), reads=["B6", "bs"], writes=["bV"])
            cur = 0
            for lev in range(1, nlev + 1):
                nxt = 1 - cur
                last = lev == nlev
                for ch in range(nch):
                    cs = slice(ch * c, (ch + 1) * c)
                    for h in range(2):
                        rs = slice(64 * h, 64 * h + c)
                        if not last:
                            s.op("pe", lambda e: e.matmul(pb[5][rs, cs], lhsT=PT[cur][rs, cs], rhs=P[cur][rs, cs], start=True, stop=True),
                                 reads=[f"PT{cur}", f"P{cur}"], writes=["B5"])
                        s.op("pe", lambda e: e.matmul(pb[5][rs, 256 + ch * c:256 + (ch + 1) * c], lhsT=P[cur][rs, cs], rhs=PT[cur][rs, cs], start=True, stop=True),
                             reads=[f"PT{cur}", f"P{cur}"], writes=["B5"])
                if not last:
                    s.op("act", lambda e: e.activation(out=P[nxt][:, 0:TBk], in_=pb[5][:, 0:TBk], func=AF.Copy), reads=["B5"], writes=[f"P{nxt}"])
                s.op("dve", lambda e: e.tensor_copy(out=PT[nxt][:, 0:TBk], in_=pb[5][:, 256:256 + TBk]), reads=["B5"], writes=[f"PT{nxt}"])
                for ch in range(nch):
                    cs = slice(ch * c, (ch + 1) * c)
                    for h in range(2):
                        rs = slice(64 * h, 64 * h + c)
                        s.op("pe", lambda e: e.matmul(pb[4][rs, cs], lhsT=PT[nxt][rs, cs], rhs=R[cur][rs, cs], start=True, stop=True),
                             reads=[f"PT{nxt}", f"R{cur}"], writes=["B4"])
                s.op("dve", lambda e: e.tensor_tensor(out=R[nxt][:, 0:TBk], in0=pb[4][:, 0:TBk], in1=R[cur][:, 0:TBk], op=ALU.add),
                     reads=["B4", f"R{cur}"], writes=[f"R{nxt}"])
                cur = nxt
            Rf = R[cur]; rk = f"R{cur}"
            for ch in range(nch):
                cs = slice(ch * c, (ch + 1) * c)
                for h in range(2):
                    rs = slice(64 * h, 64 * h + c)
                    s.op("pe", lambda e: e.matmul(pb[6][rs, ch * 64:(ch + 1) * 64], lhsT=Rf[rs, cs], rhs=bV[rs, ch, :], start=True, stop=True),
                         reads=[rk, "bV"], writes=["B6"])
                    s.op("pe", lambda e: e.matmul(pb[6][64 * h:64 * h + 64, 256 + ch * c:256 + (ch + 1) * c], lhsT=Kbe[rs, ch, :], rhs=Rf[rs, cs], start=True, stop=True),
                         reads=[rk, "Kbe"], writes=["B6"])
            s.op("act", lambda e: e.activation(out=u[:, 0:nch, :], in_=k4, func=AF.Copy), reads=["B6"], writes=["u"])
            s.op("dve", lambda e: e.tensor_copy(out=wT[:, 0:TBk], in_=pb[6][:, 256:256 + TBk]), reads=["B6"], writes=["wT"])
            skey = f"Sg{p}"
            for ch in range(nch):
                cs = slice(ch * c, (ch + 1) * c)
                seg = ch // cps
                if ch % cps == 0:
                    if is_sample:
                        s.dma("sp", Sg[:, p, :], sg_d[seg, p], writes=[skey])
                        s.op("dve", lambda e: e.tensor_copy(out=Sgb[:, p, :], in_=Sg[:, p, :]), reads=[skey], writes=[skey + "b"])
                    elif first:
                        s.op("pool", lambda e: e.memset(Sg[:, p, :], 0.0), writes=[skey])
                        s.op("pool", lambda e: e.memset(Sgb[:, p, :], 0.0), writes=[skey + "b"])
                for h in range(2):
                    fs = slice(64 * h, 64 * h + 64); rs = slice(64 * h, 64 * h + c)
                    s.op("pe", lambda e: e.matmul(pb[1][rs, 0:64], lhsT=wT[fs, cs], rhs=Sgb[fs, p, :], start=True, stop=True), reads=["wT", skey + "b"], writes=["B1"])
                s.op("dve", lambda e: e.tensor_tensor(out=vn[:], in0=u[:, ch, :], in1=pb[1][:, 0:64], op=ALU.subtract), reads=["u", "B1"], writes=["vn"])
                for h in range(2):
                    fs = slice(64 * h, 64 * h + 64); rs = slice(64 * h, 64 * h + c)
                    s.op("pe", lambda e: e.matmul(pb[1][fs, 256 + ch * c:256 + (ch + 1) * c], lhsT=Sgb[fs, p, :], rhs=QeT[fs, cs], start=True, stop=False),
                         reads=[skey + "b", "QeT"], writes=["B1"])
                    s.op("pe", lambda e: e.matmul(pb[1][fs, 256 + ch * c:256 + (ch + 1) * c], lhsT=vn[rs, :], rhs=attnT[rs, cs], start=False, stop=True),
                         reads=["vn", "attnT"], writes=["B1"])
                for h in range(2):
                    fs = slice(64 * h, 64 * h + 64); rs = slice(64 * h, 64 * h + c)
                    s.op("pe", lambda e: e.matmul(pb[1][fs, 64:128], lhsT=Kd[rs, ch, :], rhs=vn[rs, :], start=True, stop=True), reads=["Kd", "vn"], writes=["B1"])
                s.op("dve", lambda e: e.scalar_tensor_tensor(out=Sgb[:, p, :], in0=Sg[:, p, :], scalar=EBs[:, (ch + 1) * c - 1:(ch + 1) * c], in1=pb[1][:, 64:128],
                                                             op0=ALU.mult, op1=ALU.add), reads=[skey, "EBs", "B1"], writes=[skey + "b"])
                s.op("dve", lambda e: e.scalar_tensor_tensor(out=Sg[:, p, :], in0=Sg[:, p, :], scalar=EBs[:, (ch + 1) * c - 1:(ch + 1) * c], in1=pb[1][:, 64:128],
                                                             op0=ALU.mult, op1=ALU.add), reads=[skey, "EBs", "B1"], writes=[skey])
                if is_sample and (ch + 1) % cps == 0:
                    s.dma("sp", ngs[seg, p], Sg[:, p, :], reads=[skey], writes=[f"o_ngs{seg}_{p}"])
            s.op("act", lambda e: e.activation(out=oTf[:, 0:TBk], in_=pb[1][:, 256:256 + TBk], func=AF.Copy), reads=["B1"], writes=["oTf"])
            s.op("pool", lambda e: e.tensor_tensor(out=sqb[:, 0:TBk], in0=oTf[:, 0:TBk], in1=oTf[:, 0:TBk], op=ALU.mult), reads=["oTf"], writes=["sqb"])
            s.op("pe", lambda e: e.matmul(pb[3][:, 0:TBk], lhsT=ob2b[:], rhs=sqb[:, 0:TBk], start=True, stop=True), reads=["ob2b", "sqb"], writes=["B3"])
            s.op("act", lambda e: e.activation(out=tmp[1][:, 0:TBk], in_=pb[3][:, 0:TBk], func=AF.Ln, scale=1.0 / 64, bias=EPS), reads=["B3"], writes=["tmp1"])
            s.op("act", lambda e: e.activation(out=tmp[1][:, 0:TBk], in_=tmp[1][:, 0:TBk], func=AF.Exp, scale=-0.5), reads=["tmp1"], writes=["tmp1"])
            s.op("dve", lambda e: e.tensor_tensor(out=oTf[:, 0:TBk], in0=oTf[:, 0:TBk], in1=tmp[1][:, 0:TBk], op=ALU.mult), reads=["oTf", "tmp1"], writes=["oTf"])
            s.op("dve", lambda e: e.scalar_tensor_tensor(out=oTn[:, p, 0:TBk], in0=oTf[:, 0:TBk], scalar=gnw[:, 0:1], in1=za[:, 0:TBk], op0=ALU.mult, op1=ALU.mult),
                 reads=["oTf", "gnw_t", "za"], writes=[f"oTn{p}"])

        recB = []
        s.rec = recB
        s.op("pool", lambda e: e.memset(smask[:, 0:TBk], 1.0), writes=["smask"])
        s.op("pool", lambda e: e.memset(v3(smask[:, 0:TBk])[:, :, 0:1], 0.0), reads=["smask"], writes=["smask"])
        for h in range(HB):
            pp, pk = inproj_fm(16 + h, 1)
            s.op("act", lambda e: e.activation(out=qb[:, 0:TBk], in_=pp, func=AF.Copy), reads=[pk], writes=["qb"])
            silu_from(qb[:, 0:TBk], ["qb"], qb[:, 0:TBk], "qb", bl[:, 0:TBk], "bl", TBk)
            pp, pk = inproj_fm(24 + h, 1)
            s.op("act", lambda e: e.activation(out=zb[:, 0:TBk], in_=pp, func=AF.Copy), reads=[pk], writes=["zb"])
            silu_from(zb[:, 0:TBk], ["zb"], zb[:, 0:TBk], "zb", bl[:, 0:TBk], "bl", TBk)
            pp, pk = inproj_fm(20 + h, 1)
            s.op("act", lambda e: e.activation(out=ff[:, 0:TBk], in_=pp, func=AF.Exp, scale=-1.0), reads=[pk], writes=["ff"])
            s.op("act", lambda e: e.activation(out=ff[:, 0:TBk], in_=ff[:, 0:TBk], func=AF.Ln, bias=1.0), reads=["ff"], writes=["ff"])
            s.op("act", lambda e: e.activation(out=ff[:, 0:TBk], in_=ff[:, 0:TBk], func=AF.Exp, scale=-1.0), reads=["ff"], writes=["ff"])
            s.op("dve", lambda e: e.tensor_scalar(out=ff[:, 0:TBk], in0=ff[:, 0:TBk], scalar1=oml[:, h:h + 1], scalar2=lb[:, h:h + 1], op0=ALU.mult, op1=ALU.add),
                 reads=["ff", "oml", "lb"], writes=["ff"])
            s.op("act", lambda e: e.activation(out=lf[:, 0:TBk], in_=ff[:, 0:TBk], func=AF.Ln), reads=["ff"], writes=["lf"])
            s.op("dve", lambda e: e.tensor_scalar(out=kb[:, 0:TBk], in0=ff[:, 0:TBk], scalar1=-1.0, scalar2=1.0, op0=ALU.mult, op1=ALU.add), reads=["ff"], writes=["kb"])
            s.op("dve", lambda e: e.tensor_tensor_scan(out=bb[:, 0:TBk], data0=smask[:, 0:TBk], data1=lf[:, 0:TBk], initial=0.0, op0=ALU.mult, op1=ALU.add),
                 reads=["smask", "lf"], writes=["bb"])
            b3 = v3(bb[:, 0:TBk])
            s.op("pool", lambda e: e.tensor_tensor(out=v3(bl[:, 0:TBk]), in0=b3, in1=bc(b3[:, :, c - 1:c], [128, nch, c]), op=ALU.subtract), reads=["bb"], writes=["bl"])
            s.op("act", lambda e: e.activation(out=Qef[:, 0:TBk], in_=bb[:, 0:TBk], func=AF.Exp), reads=["bb"], writes=["Qef"])
            s.op("act", lambda e: e.activation(out=Qxf[:, 0:TBk], in_=bl[:, 0:TBk], func=AF.Exp), reads=["bl"], writes=["Qxf"])
            s.op("act", lambda e: e.activation(out=Kdf[:, 0:TBk], in_=bl[:, 0:TBk], func=AF.Exp, scale=-1.0), reads=["bl"], writes=["Kdf"])
            s.op("act", lambda e: e.activation(out=ebl[:, 0:nch], in_=b3[:, :, c - 1], func=AF.Exp), reads=["bb"], writes=["ebl"])
            s.op("dve", lambda e: e.tensor_tensor(out=Qe[:, 0:TBk], in0=Qef[:, 0:TBk], in1=qb[:, 0:TBk], op=ALU.mult), reads=["Qef", "qb"], writes=["Qe"])
            s.op("pool", lambda e: e.tensor_tensor(out=Qx[:, 0:TBk], in0=Qxf[:, 0:TBk], in1=qb[:, 0:TBk], op=ALU.mult), reads=["Qxf", "qb"], writes=["Qx"])
            s.op("dve", lambda e: e.tensor_tensor(out=Kdh[:, 0:TBk], in0=Kdf[:, 0:TBk], in1=kb[:, 0:TBk], op=ALU.mult), reads=["Kdf", "kb"], writes=["Kdh"])
            for ch in range(nch):
                cs = slice(ch * c, (ch + 1) * c)
                outp = pb[2][0:c, 128:256]
                for k in range(8):
                    s.op("pe", lambda e: e.matmul(outp, lhsT=hT[:, k, cs], rhs=Wb[:, k, C_HI + 128 * h:C_HI + 128 * (h + 1)], start=(k == 0), stop=(k == 7)),
                         reads=["Wb", "hT"], writes=["B2"])
                s.op("pe", lambda e: e.matmul(pb[2][0:c, 256:384], lhsT=Kdh[:, cs], rhs=identb[:], start=True, stop=True), reads=["Kdh", "identb"], writes=["B2"])
                s.op("pe", lambda e: e.matmul(pb[2][0:c, 384:384 + c], lhsT=Kdh[:, cs], rhs=Qx[:, cs], start=True, stop=True), reads=["Kdh", "Qx"], writes=["B2"])
                s.op("act", lambda e: e.activation(out=vtok[0:c, ch, :], in_=outp, func=AF.Copy), reads=["B2"], writes=["vtok"])
                s.op("dve", lambda e: e.tensor_copy(out=Kdt[0:c, ch, :], in_=pb[2][0:c, 256:384]), reads=["B2"], writes=["Kdt"])
                s.op("dve", lambda e: e.tensor_tensor(out=aTh[0:c, cs], in0=pb[2][0:c, 384:384 + c], in1=Tri_s[0:c, 0:c], op=ALU.mult),
                     reads=["B2", "Tri_s"], writes=["aTh"])
            skey = f"Sh{h}"
            for ch in range(nch):
                cs = slice(ch * c, (ch + 1) * c)
                seg = ch // cps
                if ch % cps == 0:
                    if is_sample:
                        s.dma("sp", Sh[:, h, :], sh_d[seg, h], writes=[skey])
                        s.op("dve", lambda e: e.tensor_copy(out=Shb[:, h, :], in_=Sh[:, h, :]), reads=[skey], writes=[skey + "b"])
                    elif first:
                        s.op("pool", lambda e: e.memset(Sh[:, h, :], 0.0), writes=[skey])
                        s.op("pool", lambda e: e.memset(Shb[:, h, :], 0.0), writes=[skey + "b"])
                s.op("pe", lambda e: e.matmul(pb[3][:, 256 + ch * c:256 + (ch + 1) * c], lhsT=Shb[:, h, :], rhs=Qe[:, cs], start=True, stop=False), reads=[skey + "b", "Qe"], writes=["B3"])
                s.op("pe", lambda e: e.matmul(pb[3][:, 256 + ch * c:256 + (ch + 1) * c], lhsT=vtok[0:c, ch, :], rhs=aTh[0:c, cs], start=False, stop=True),
                     reads=["vtok", "aTh"], writes=["B3"])
                s.op("pe", lambda e: e.matmul(pb[1][:, 128:256], lhsT=Kdt[0:c, ch, :], rhs=vtok[0:c, ch, :], start=True, stop=True), reads=["Kdt", "vtok"], writes=["B1"])
                s.op("dve", lambda e: e.scalar_tensor_tensor(out=Shb[:, h, :], in0=Sh[:, h, :], scalar=ebl[:, ch:ch + 1], in1=pb[1][:, 128:256], op0=ALU.mult, op1=ALU.add),
                     reads=[skey, "ebl", "B1"], writes=[skey + "b"])
                s.op("dve", lambda e: e.scalar_tensor_tensor(out=Sh[:, h, :], in0=Sh[:, h, :], scalar=ebl[:, ch:ch + 1], in1=pb[1][:, 128:256], op0=ALU.mult, op1=ALU.add),
                     reads=[skey, "ebl", "B1"], writes=[skey])
                if is_sample and (ch + 1) % cps == 0:
                    s.dma("sp", nhs[seg, h], Sh[:, h, :], reads=[skey], writes=[f"o_nhs{seg}_{h}"])
            s.op("act", lambda e: e.activation(out=oTfH[:, 0:TBk], in_=pb[3][:, 256:256 + TBk], func=AF.Copy), reads=["B3"], writes=["oTfH"])
            s.op("pool", lambda e: e.tensor_tensor(out=sqbH[:, 0:TBk], in0=oTfH[:, 0:TBk], in1=oTfH[:, 0:TBk], op=ALU.mult), reads=["oTfH"], writes=["sqbH"])
            s.op("pe", lambda e: e.matmul(pb[3][:, 256:256 + TBk], lhsT=onesb[:], rhs=sqbH[:, 0:TBk], start=True, stop=True), reads=["onesb", "sqbH"], writes=["B3"])
            s.op("act", lambda e: e.activation(out=tmpH[1][:, 0:TBk], in_=pb[3][:, 256:256 + TBk], func=AF.Ln, scale=1.0 / 128, bias=EPS), reads=["B3"], writes=["tmpH1"])
            s.op("act", lambda e: e.activation(out=tmpH[1][:, 0:TBk], in_=tmpH[1][:, 0:TBk], func=AF.Exp, scale=-0.5), reads=["tmpH1"], writes=["tmpH1"])
            s.op("dve", lambda e: e.tensor_tensor(out=oTfH[:, 0:TBk], in0=oTfH[:, 0:TBk], in1=tmpH[1][:, 0:TBk], op=ALU.mult), reads=["oTfH", "tmpH1"], writes=["oTfH"])
            s.op("dve", lambda e: e.scalar_tensor_tensor(out=oTn[:, 4 + h, 0:TBk], in0=oTfH[:, 0:TBk], scalar=hnw[:, 0:1], in1=zb[:, 0:TBk], op0=ALU.mult, op1=ALU.mult),
                 reads=["oTfH", "hnw_t", "zb"], writes=[f"oTn{4 + h}"])

        s.rec = None
        s.merge_emit([recA, recB])
        for tt in range(ntt):
            X = xt[tt]
            for half in range(2):
                bank = pb[0] if half == 0 else pb[2]
                key = "pi_full" if half == 0 else "pg_full"
                for k in range(8):
                    s.op("pe", lambda e: e.matmul(bank[0:TT, :], lhsT=oTn[:, k, tt * TT:(tt + 1) * TT], rhs=WOb[:, k, half * 512:(half + 1) * 512], start=(k == 0), stop=(k == 7)),
                         reads=[f"oTn{k}", "WOb"], writes=(["B0", "B0"] if half == 0 else ["B2", "B2", "B2"]))
                s.op("dve", lambda e: e.tensor_tensor(out=yo[0:TT, half * 512:(half + 1) * 512], in0=bank[0:TT, :], in1=X[0:TT, half * 512:(half + 1) * 512], op=ALU.add),
                     reads=(["B0", "B0"] if half == 0 else ["B2", "B2", "B2"]) + [X.name], writes=["yo"])
            s.op("act", lambda e: e.activation(out=sqj[0:TT, :], in_=yo[0:TT, :], func=AF.Square, accum_out=ss[0:TT, :]), reads=["yo"], writes=["sqj", "ss"])
            s.op("act", lambda e: e.activation(out=rr[0:TT, :], in_=ss[0:TT, :], func=AF.Ln, scale=1.0 / D, bias=EPS), reads=["ss"], writes=["rr"])
            s.op("act", lambda e: e.activation(out=rr[0:TT, :], in_=rr[0:TT, :], func=AF.Exp, scale=-0.5), reads=["rr"], writes=["rr"])
            s.op("dve", lambda e: e.scalar_tensor_tensor(out=yo2[0:TT, :], in0=yo[0:TT, :], scalar=rr[0:TT, :], in1=fnw[0:TT, :], op0=ALU.mult, op1=ALU.mult),
                 reads=["yo", "rr", "fnw_t"], writes=["yo2"])
            s.dma("sp", y_dst[t0 + tt * TT: t0 + (tt + 1) * TT, :], yo2[0:TT, :], reads=["yo2"], writes=[f"o_y{id(y_dst)}"], slot="yout")

    nblk = T // TB
    for b in range(nblk):
        block(xp, yp, b * TB, 1, TB, 64, b == 0, False)
    s.dma("sp", ncp[:, :, 0, :], halo[:, :, 0, :], reads=["halo"], writes=["o_ncp"])
    for p in range(NP):
        s.dma("sp", ngp[0, p], Sg[:, p, :], reads=[f"Sg{p}"], writes=[f"o_ngp{p}"])
    for h in range(HB):
        s.dma("sp", nhp[0, h], Sh[:, h, :], reads=[f"Sh{h}"], writes=[f"o_nhp{h}"])
    s.dma("sp", halo[:], sc_d, reads=["halo"], writes=["halo"])
    block(xs, ys, 0, 4, 16, 16, True, True)
    s.dma("sp", ncs, halo[:], reads=["halo"], writes=["o_ncs"])
    s.finish("sp")
    return nc, s


_PERM = np.concatenate([np.arange(0, 1536), np.arange(1536, 2048), np.arange(2064, 2576), np.arange(2576, 3088),
                        np.arange(3600, 4112), np.arange(3088, 3600), np.arange(2048, 2064)])


def _core_inputs(b, inp):
    f = lambda a: np.ascontiguousarray(a, dtype=np.float32)
    cwv = inp["conv_w"][0]
    scv = inp["state_conv"][0][4 * b:4 * b + 4]
    return {
        "xp": f(inp["x_prompt"][b]),
        "xs": f(inp["x_sample"][4 * b:4 * b + 4].reshape(64, D)),
        "w_in": f(inp["w_in"][0][:, _PERM]),
        "w_out": f(inp["w_out"][0]),
        "nw": f(inp["norm_w"][0].reshape(8, 128).T),
        "cw": f(cwv.reshape(4, 12, 128).transpose(2, 1, 0)),
        "alog": f(np.broadcast_to(inp["gdn_A_log"][0][None, :], (128, HA))),
        "dtb": f(np.broadcast_to(inp["gdn_dt_bias"][0][None, :], (128, HA))),
        "gnw": f(np.tile(inp["gdn_norm_w"][0], 2).reshape(128, 1)),
        "hnw": f(inp["hgrn_norm_w"][0].reshape(128, 1)),
        "lbl": f(inp["hgrn_lb_logits"].reshape(2, HB, 128).transpose(2, 1, 0)),
        "fnw": f(np.broadcast_to(inp["final_norm_w"][None, :], (128, D))),
        "sc": f(scv.reshape(4, 3, 12, 128).transpose(3, 2, 0, 1)),
        "sg": f(inp["state_gdn"][0][4 * b:4 * b + 4].reshape(4, NP, 128, 64)),
        "sh": f(inp["state_hgrn"][0][4 * b:4 * b + 4]),
    }


_CACHE = {}


def kernel(**inputs):
    inp = {k: np.asarray(v) for k, v in inputs.items()}
    Bp, T, _ = inp["x_prompt"].shape
    assert Bp == 4 and inp["x_sample"].shape[:2] == (16, 16)
    if T not in _CACHE:
        _CACHE[T] = build(T)[0]
    nc = _CACHE[T]
    in_maps = [_core_inputs(c % 4, inp) for c in range(8)]
    res = run_bass_kernel_spmd(nc, in_maps, core_ids=list(range(8)))
    r = res.results
    y_prompt = np.stack([r[b]["yp"] for b in range(4)]).astype(np.float32)
    y_sample = np.concatenate([r[b]["ys"].reshape(4, 16, D) for b in range(4)]).astype(np.float32)
    cvt = lambda a: a.transpose(2, 3, 1, 0).reshape(a.shape[2], 3, 1536)
    ncp_ = np.stack([cvt(r[b]["ncp"])[0] for b in range(4)])[None].astype(np.float32)
    ngp_ = np.stack([r[b]["ngp"].reshape(HA, 64, 64) for b in range(4)])[None].astype(np.float32)
    nhp_ = np.stack([r[b]["nhp"].reshape(HB, 128, 128) for b in range(4)])[None].astype(np.float32)
    ncs_ = np.concatenate([cvt(r[b]["ncs"]) for b in range(4)])[None].astype(np.float32)
    ngs_ = np.concatenate([r[b]["ngs"].reshape(4, HA, 64, 64) for b in range(4)])[None].astype(np.float32)
    nhs_ = np.concatenate([r[b]["nhs"].reshape(4, HB, 128, 128) for b in range(4)])[None].astype(np.float32)
    return (y_prompt, y_sample, ncp_, ngp_, nhp_, ncs_, ngs_, nhs_)
```

```python
import numpy as np
import concourse.bass as bass
import concourse.mybir as mybir
from concourse.bass_utils import run_bass_kernel_spmd

F32 = mybir.dt.float32
BF16 = mybir.dt.bfloat16
AF = mybir.ActivationFunctionType
ALU = mybir.AluOpType

D = 1024
HA, HB = 8, 4
NP = HA // 2
NCOL = 4112
EPS = 1e-6
C_HI = 3584
C_G = 4096


SAME_ENGINE_WAIT = True


class _Proxy:
    def __getattr__(self, name):
        return lambda *a, **k: (name, a, k)


_PROXY = _Proxy()
PSUM_KEYS = {"B0", "B1", "B2", "B3", "B4", "B5", "B6", "BT"}


class Sched:
    def __init__(self, nc):
        self.nc = nc
        self.eng = {"pe": nc.tensor, "dve": nc.vector, "act": nc.scalar, "pool": nc.gpsimd, "sp": nc.sync}
        self.sem = {k: nc.alloc_semaphore(name=f"s_{k}") for k in self.eng}
        self.cnt = {k: 0 for k in self.eng}
        self.seen = {k: {} for k in self.eng}
        self.last_w = {}
        self.readers = {}
        self.dma_sems = {}
        self.n_wait = 0
        self.n_ops = 0
        self.rec = None

    def emit(self, r):
        if r[0] == "op":
            _, e, call, reads, writes = r
            self.op(e, lambda eng: getattr(eng, call[0])(*call[1], **call[2]), reads, writes)
        else:
            _, q, out, in_, reads, writes, slot = r
            self.dma(q, out, in_, reads, writes, slot)

    def _cost(self, r):
        if r[0] == "dma":
            return 2500.0
        _, e, call, reads, writes = r
        name, args, kw = call
        def nfree(ap):
            try:
                sh = list(ap.shape)
                n = 1
                for d in sh[1:]:
                    n *= int(d)
                return n
            except Exception:
                return 256
        if e == "pe":
            ap = kw.get("rhs", None) if name == "matmul" else kw.get("in_", None)
            n = nfree(ap) if ap is not None else 64
            c = 45.0 + 0.45 * n
            try:
                if name == "matmul" and kw["rhs"].dtype == F32:
                    c *= 3.0
            except Exception:
                pass
            return c
        ap = kw.get("out", None)
        n = nfree(ap) if ap is not None else 256
        if e == "dve":
            return 70.0 + 1.0 * n
        if e == "act":
            return 130.0 + 0.9 * n
        return 110.0 + 1.8 * n

    def _est_start(self, r):
        if r[0] == "dma":
            e, reads, writes = r[1], r[4], r[5]
        else:
            e, reads, writes = r[1], r[3], r[4]
        m = self.model
        t = m["eng"].get(e, 0.0)
        ex = [k for k in reads if k in PSUM_KEYS]
        for k in reads:
            w = m["w"].get(k)
            if w is not None:
                t = max(t, w[0] + (0.0 if w[1] == e else 160.0))
        for k in list(writes) + ex:
            w = m["w"].get(k)
            if w is not None:
                t = max(t, w[0] + (0.0 if w[1] == e else 160.0))
            for (tt, ee) in m["r"].get(k, {}).values():
                t = max(t, tt + (0.0 if ee == e else 160.0))
        return t

    def _model_commit(self, r, t0):
        if r[0] == "dma":
            e, reads, writes = r[1], r[4], r[5]
            eng_busy = 60.0
        else:
            e, reads, writes = r[1], r[3], r[4]
            eng_busy = None
        m = self.model
        c = self._cost(r)
        t1 = t0 + c
        m["eng"][e] = t0 + (eng_busy if eng_busy is not None else c)
        ex = [k for k in reads if k in PSUM_KEYS]
        who = e if r[0] == "op" else "dma"
        for k in reads:
            m["r"].setdefault(k, {})[who] = (t1, who)
        for k in list(writes) + ex:
            m["w"][k] = (t1, who)
            m["r"][k] = {}
        m["t"] = max(m.get("t", 0.0), t1)

    def merge_emit(self, streams):
        if not hasattr(self, "model"):
            self.model = {"eng": {}, "w": {}, "r": {}, "t": 0.0}
        units = []
        for st in streams:
            u = []
            for r in st:
                glued = (r[0] == "op" and r[2][0] == "matmul" and r[2][2].get("start") is False)
                if glued and u:
                    u[-1].append(r)
                else:
                    u.append([r])
            units.append(u)
        pos = [0] * len(units)
        while True:
            best, bi = None, -1
            for i, u in enumerate(units):
                if pos[i] < len(u):
                    t = self._est_start(u[pos[i]][0])
                    key = (t, -(len(u) - pos[i]))
                    if best is None or key < best:
                        best, bi = key, i
            if bi < 0:
                break
            for r in units[bi][pos[bi]]:
                t0 = self._est_start(r)
                self._model_commit(r, t0)
                self.emit(r)
            pos[bi] += 1


    def _wait(self, e, tok):
        name, sem, val = tok
        if name == "pe" and e == "pe":
            return
        if name == e and not SAME_ENGINE_WAIT:
            return
        if self.seen[e].get(name, 0) >= val:
            return
        self.eng[e].wait_ge(sem, val)
        self.seen[e][name] = val
        self.n_wait += 1

    def _deps(self, e, reads, writes):
        for k in reads:
            t = self.last_w.get(k)
            if t is not None:
                self._wait(e, t)
        for k in writes:
            t = self.last_w.get(k)
            if t is not None:
                self._wait(e, t)
            for t in self.readers.get(k, {}).values():
                self._wait(e, t)

    def _commit(self, tok, reads, writes):
        for k in reads:
            self.readers.setdefault(k, {})[tok[0]] = tok
        for k in writes:
            self.last_w[k] = tok
            self.readers[k] = {}

    def op(self, e, fn, reads=(), writes=()):
        if self.rec is not None:
            self.rec.append(("op", e, fn(_PROXY), tuple(reads), tuple(writes)))
            return None
        ex = [k for k in reads if k in PSUM_KEYS]
        if ex:
            writes = list(writes) + ex
        self._deps(e, reads, writes)
        ins = fn(self.eng[e])
        self.cnt[e] += 1
        ins.then_inc(self.sem[e], 1)
        tok = (e, self.sem[e], self.cnt[e])
        self._commit(tok, reads, writes)
        self.n_ops += 1
        return tok

    def dma(self, q, out, in_, reads=(), writes=(), slot=None):
        if self.rec is not None:
            self.rec.append(("dma", q, out, in_, tuple(reads), tuple(writes), slot))
            return None
        self._deps(q, reads, writes)
        slot = slot or (writes[0] if writes else reads[0])
        sname = f"d_{slot}"
        if sname not in self.dma_sems:
            self.dma_sems[sname] = [self.nc.alloc_semaphore(name=sname), 0]
        ent = self.dma_sems[sname]
        ent[1] += 16
        self.eng[q].dma_start(out=out, in_=in_).then_inc(ent[0], 16)
        tok = (sname, ent[0], ent[1])
        self._commit(tok, reads, writes)
        self.n_ops += 1
        return tok

    def finish(self, e="sp"):
        for k, t in list(self.last_w.items()):
            self._wait(e, t)


def bc(ap, shape):
    return ap.to_broadcast(list(shape))


def build(T, TB=256):
    nc = bass.Bass("TRN2", target_bir_lowering=False)
    s = Sched(nc)
    dt_in = lambda n, sh: nc.dram_tensor(n, list(sh), F32, kind="ExternalInput").ap()
    dt_out = lambda n, sh: nc.dram_tensor(n, list(sh), F32, kind="ExternalOutput").ap()
    xp = dt_in("xp", [T, D]); xs = dt_in("xs", [64, D])
    w_in = dt_in("w_in", [D, NCOL]); w_out = dt_in("w_out", [D, D])
    nw_d = dt_in("nw", [128, 8]); cw_d = dt_in("cw", [128, 12, 4])
    alog_d = dt_in("alog", [128, HA]); dtb_d = dt_in("dtb", [128, HA])
    gnw_d = dt_in("gnw", [128, 1]); hnw_d = dt_in("hnw", [128, 1])
    lbl_d = dt_in("lbl", [128, HB, 2]); fnw_d = dt_in("fnw", [128, D])
    sc_d = dt_in("sc", [128, 12, 4, 3])
    sg_d = dt_in("sg", [4, NP, 128, 64]); sh_d = dt_in("sh", [4, HB, 128, 128])
    yp = dt_out("yp", [T, D]); ys = dt_out("ys", [64, D])
    ncp = dt_out("ncp", [128, 12, 1, 3]); ngp = dt_out("ngp", [1, NP, 128, 64]); nhp = dt_out("nhp", [1, HB, 128, 128])
    ncs = dt_out("ncs", [128, 12, 4, 3]); ngs = dt_out("ngs", [4, NP, 128, 64]); nhs = dt_out("nhs", [4, HB, 128, 128])

    sb = lambda n, sh, d=F32: nc.alloc_sbuf_tensor(n, list(sh), d)
    Wb = sb("Wb", [128, 8, NCOL], BF16)
    WOb = sb("WOb", [128, 8, D], BF16)
    nw = sb("nw_t", [128, 8]); cw = sb("cw_t", [128, 12, 4])
    alog = sb("alog_t", [128, HA]); dtb = sb("dtb_t", [128, HA]); negA = sb("negA", [128, HA])
    gnw = sb("gnw_t", [128, 1]); hnw = sb("hnw_t", [128, 1])
    lbl = sb("lbl_t", [128, HB, 2]); lb = sb("lb", [128, HB]); oml = sb("oml", [128, HB])
    fnw = sb("fnw_t", [128, D])
    identb = sb("identb", [128, 128], BF16); identf = sb("identf", [128, 128])
    ones = sb("ones", [128, 128]); ob2 = sb("ob2", [128, 128])
    fgt = sb("fgt", [128, 128]); fle = sb("fle", [128, 128])
    I_s = sb("I_s", [128, 64]); U_s = sb("U_s", [128, 64]); Tri_s = sb("Tri_s", [128, 64]); Mc_s = sb("Mc_s", [128, 64])
    halo = sb("halo", [128, 12, 4, 3])
    Sg = sb("Sg", [128, NP, 64]); Sh = sb("Sh", [128, HB, 128])
    W_ = TB
    xt = [sb(f"xt{i}", [128, D]) for i in range(2)]
    sqj = sb("sqj", [128, D], BF16)
    xb = sb("xb", [128, D], BF16)
    hT = sb("hT", [128, 8, W_], BF16)
    ss = sb("ss", [128, 1]); rr = sb("rr", [128, 1])
    raw = sb("raw", [128, 3, W_ + 12])
    cv = sb("cv", [128, 3, W_])
    tmp = [sb(f"tmp{i}", [128, W_]) for i in range(4)]
    za = [sb(f"za{i}", [128, W_]) for i in range(2)]
    cvb = [sb(f"cvb{i}", [128, 3, W_], BF16) for i in range(2)]
    tmpA = [sb(f"tmpA{i}", [128, W_]) for i in range(3)]
    sqA = [sb(f"sqA{i}", [128, W_], BF16) for i in range(2)]
    tnA = [sb(f"tnA{i}", [128, W_]) for i in range(2)]
    I_sb = sb("I_sb", [128, 64], BF16); ob2b = sb("ob2b", [128, 128], BF16); onesb = sb("onesb", [128, 128], BF16)
    Sgb = sb("Sgb", [128, NP, 64], BF16); Shb = sb("Shb", [128, HB, 128], BF16)
    sqb = sb("sqb", [128, W_], BF16); sqbH = sb("sqbH", [128, W_], BF16)
    G = sb("G", [128, 4, 16]); Gb = sb("Gb", [128, 4, HA]); Gg = sb("Gg", [128, 4, HA])
    gs = sb("gs", [128, NP, 4]); bs = sb("bs", [128, NP, 4]); nbs = sb("nbs", [128, NP, 4])
    gc = sb("gc", [128, NP, 4]); gl = sb("gl", [128, NP, 4]); egc = sb("egc", [128, NP, 4])
    dk = sb("dk", [128, NP, 4]); bge = sb("bge", [128, NP, 4])
    rhsD = sb("rhsD", [128, W_]); Dg = sb("Dg", [128, W_])
    Ee = sb("Ee", [128, W_]); Dm = sb("Dm", [128, W_]); Ds = sb("Ds", [128, W_])
    EBs = [sb(f"EBs{i}", [128, W_]) for i in range(2)]
    P0t = [sb(f"P0t{i}", [128, W_], BF16) for i in range(2)]
    PT0t = [sb(f"PT0t{i}", [128, W_], BF16) for i in range(2)]
    R0t = [sb(f"R0t{i}", [128, W_], BF16) for i in range(2)]
    P = [sb(f"P{i}", [128, W_], BF16) for i in range(2)]
    PT = [sb(f"PT{i}", [128, W_], BF16) for i in range(2)]
    R = [sb(f"R{i}", [128, W_], BF16) for i in range(2)]
    attn = sb("attn", [128, W_], BF16); attnT = [sb(f"attnT{i}", [128, W_], BF16) for i in range(2)]
    Kbe = [sb(f"Kbe{i}", [128, 4, 64], BF16) for i in range(2)]; Kd = [sb(f"Kd{i}", [128, 4, 64], BF16) for i in range(2)]; bV = [sb(f"bV{i}", [128, 4, 64], BF16) for i in range(2)]
    u = sb("u", [128, 4, 64]); wT = sb("wT", [128, W_], BF16); QeT = [sb(f"QeT{i}", [128, W_], BF16) for i in range(2)]
    vn = sb("vn", [128, 64], BF16)
    oTf = sb("oTf", [128, W_])
    oTn = sb("oTn", [128, 8, W_], BF16)
    qb = sb("qb", [128, W_]); ff = sb("ff", [128, W_]); lf = sb("lf", [128, W_]); kb = sb("kb", [128, W_])
    bb = sb("bb", [128, W_]); bl = sb("bl", [128, W_])
    Qe = sb("Qe", [128, W_], BF16); Qx = sb("Qx", [128, W_], BF16); Kdh = sb("Kdh", [128, W_], BF16)
    Qef = sb("Qef", [128, W_]); Qxf = sb("Qxf", [128, W_]); Kdf = sb("Kdf", [128, W_])
    ebl = sb("ebl", [128, 4]); zb = sb("zb", [128, W_])
    vtok = sb("vtok", [64, 4, 128], BF16); Kdt = sb("Kdt", [64, 4, 128], BF16); aTh = sb("aTh", [64, W_], BF16)
    smask = sb("smask", [128, W_])
    tmpH = [sb(f"tmpH{i}", [128, W_]) for i in range(2)]; oTfH = sb("oTfH", [128, W_])
    yo = sb("yo", [128, D]); yo2 = sb("yo2", [128, D])
    pb = [nc.alloc_psum_tensor(f"pb{i}", [128, 512], F32) for i in range(8)]
    pT = pb[7][:, :].bitcast(BF16).rearrange("p (k t) -> p k t", t=128)

    def aff(out, cmp, fill_in, step=-1, cm=1, base=0):
        s.op("pool", lambda e: e.memset(out[:], fill_in), writes=[out.name])
        s.op("pool", lambda e: e.affine_select(out=out[:], in_=out[:], pattern=[[step, 128]], compare_op=cmp,
                                               fill=0.0, base=base, channel_multiplier=cm), reads=[out.name], writes=[out.name])
    aff(identf, ALU.is_equal, 1.0)
    aff(fgt, ALU.is_gt, 1.0)
    aff(fle, ALU.is_gt, 1.0, step=1, cm=-1, base=1)
    s.op("pool", lambda e: e.memset(ones[:], 1.0), writes=["ones"])
    s.op("pool", lambda e: e.memset(ob2[:], 0.0), writes=["ob2"])
    for h in range(2):
        sl = slice(64 * h, 64 * h + 64)
        s.op("pool", lambda e: e.memset(ob2[sl, sl], 1.0), reads=["ob2"], writes=["ob2"])
    s.op("dve", lambda e: e.tensor_copy(out=identb[:], in_=identf[:]), reads=["identf"], writes=["identb"])
    for (dst, src) in ((I_s, identf), (U_s, fgt), (Tri_s, fle)):
        for h in range(2):
            sl = slice(64 * h, 64 * h + 64)
            s.op("dve", lambda e: e.tensor_copy(out=dst[sl, :], in_=src[sl, sl]), reads=[src.name], writes=[dst.name])
    s.op("dve", lambda e: e.tensor_tensor(out=Mc_s[:], in0=U_s[:], in1=I_s[:], op=ALU.add), reads=["U_s", "I_s"], writes=["Mc_s"])
    s.op("dve", lambda e: e.tensor_copy(out=I_sb[:], in_=I_s[:]), reads=["I_s"], writes=["I_sb"])
    s.op("dve", lambda e: e.tensor_copy(out=ob2b[:], in_=ob2[:]), reads=["ob2"], writes=["ob2b"])
    s.op("dve", lambda e: e.tensor_copy(out=onesb[:], in_=ones[:]), reads=["ones"], writes=["onesb"])
    for t_, d_ in ((nw, nw_d), (cw, cw_d), (alog, alog_d), (dtb, dtb_d), (gnw, gnw_d), (hnw, hnw_d), (lbl, lbl_d), (fnw, fnw_d)):
        s.dma("sp", t_[:], d_, writes=[t_.name])
    s.op("act", lambda e: e.activation(out=negA[:], in_=alog[:], func=AF.Exp), reads=["alog_t"], writes=["negA"])
    s.op("dve", lambda e: e.tensor_scalar(out=negA[:], in0=negA[:], scalar1=-1.0, scalar2=None, op0=ALU.mult), reads=["negA"], writes=["negA"])
    s.op("dve", lambda e: e.tensor_tensor(out=lb[:], in0=lbl[:, :, 1], in1=lbl[:, :, 0], op=ALU.subtract), reads=["lbl_t"], writes=["lb"])
    s.op("act", lambda e: e.activation(out=lb[:], in_=lb[:], func=AF.Exp), reads=["lb"], writes=["lb"])
    s.op("dve", lambda e: e.tensor_scalar(out=lb[:], in0=lb[:], scalar1=1.0, scalar2=None, op0=ALU.add), reads=["lb"], writes=["lb"])
    s.op("dve", lambda e: e.reciprocal(out=lb[:], in_=lb[:]), reads=["lb"], writes=["lb"])
    s.op("dve", lambda e: e.tensor_scalar(out=oml[:], in0=lb[:], scalar1=-1.0, scalar2=1.0, op0=ALU.mult, op1=ALU.add), reads=["lb"], writes=["oml"])
    w_in_v = w_in.rearrange("(k p) n -> p k n", p=128)
    stgs = [(xt[0], "xt0"), (xt[1], "xt1"), (yo, "yo"), (yo2, "yo2")]
    q = 0
    for k in range(8):
        pieces = [(i * 1024, 1024, stgs[i][0], stgs[i][1]) for i in range(4)] + [(4096, 16, Ee, "Ee")]
        for (c0, cn, tl, key) in pieces:
            s.dma("sp", tl[:, 0:cn], w_in_v[:, k, c0:c0 + cn], writes=[key])
            if q % 2 == 0:
                s.op("dve", lambda e: e.tensor_scalar(out=Wb[:, k, c0:c0 + cn], in0=tl[:, 0:cn], scalar1=nw[:, k:k + 1], scalar2=None, op0=ALU.mult),
                     reads=[key, "nw_t"], writes=["Wb"])
            else:
                s.op("act", lambda e: e.activation(out=Wb[:, k, c0:c0 + cn], in_=tl[:, 0:cn], func=AF.Copy, scale=nw[:, k:k + 1]),
                     reads=[key, "nw_t"], writes=["Wb"])
            q += 1
    w_out_v = w_out.rearrange("(k p) n -> p k n", p=128)
    for k in range(8):
        tl, key = stgs[k % 4]
        s.dma("sp", tl[:, :], w_out_v[:, k, :], writes=[key])
        if k % 2 == 0:
            s.op("dve", lambda e: e.tensor_copy(out=WOb[:, k, :], in_=tl[:, :]), reads=[key], writes=["WOb"])
        else:
            s.op("act", lambda e: e.activation(out=WOb[:, k, :], in_=tl[:, :], func=AF.Copy), reads=[key], writes=["WOb"])

    def block(x_src, y_dst, t0, nseg, seglen, c, first, is_sample):
        TBk = nseg * seglen
        nch = TBk // c
        cps = seglen // c
        TT = min(128, TBk)
        ntt = TBk // TT
        nlev = {64: 5, 16: 3}[c]
        v3 = lambda ap: ap.rearrange("p (n c) -> p n c", c=c)

        for tt in range(ntt):
            X = xt[tt]
            s.dma("sp", X[0:TT, :], x_src[t0 + tt * TT: t0 + (tt + 1) * TT, :], writes=[X.name])
            s.op("act", lambda e: e.activation(out=sqj[0:TT, :], in_=X[0:TT, :], func=AF.Square, accum_out=ss[0:TT, :]),
                 reads=[X.name], writes=["sqj", "ss"])
            s.op("act", lambda e: e.activation(out=rr[0:TT, :], in_=ss[0:TT, :], func=AF.Ln, scale=1.0 / D, bias=EPS), reads=["ss"], writes=["rr"])
            s.op("act", lambda e: e.activation(out=rr[0:TT, :], in_=rr[0:TT, :], func=AF.Exp, scale=-0.5), reads=["rr"], writes=["rr"])
            s.op("dve", lambda e: e.tensor_scalar(out=xb[0:TT, :], in0=X[0:TT, :], scalar1=rr[0:TT, :], scalar2=None, op0=ALU.mult),
                 reads=[X.name, "rr"], writes=["xb"])
            for k in range(8):
                s.op("pe", lambda e: e.transpose(out=pT[:, k, 0:TT], in_=xb[0:TT, k * 128:(k + 1) * 128], identity=identb[0:TT, 0:TT]),
                     reads=["xb", "identb"], writes=["BT"])
            s.op("dve", lambda e: e.tensor_copy(out=hT[:, :, tt * TT:(tt + 1) * TT], in_=pT[:, :, 0:TT]), reads=["BT"], writes=["hT"])

        def silu_from(src, srckeys, dst, dstkey, scr, scrkey, W, outdt_note=None):
            s.op("act", lambda e: e.activation(out=scr, in_=src, func=AF.Exp, scale=-1.0), reads=srckeys, writes=[scrkey])
            s.op("act", lambda e: e.activation(out=scr, in_=scr, func=AF.Ln, bias=1.0), reads=[scrkey], writes=[scrkey])
            s.op("act", lambda e: e.activation(out=scr, in_=scr, func=AF.Exp, scale=-1.0), reads=[scrkey], writes=[scrkey])
            s.op("dve", lambda e: e.tensor_tensor(out=dst, in0=src, in1=scr, op=ALU.mult), reads=list(srckeys) + [scrkey], writes=[dstkey])


        def inproj_fm(ct, i=0):
            key = "B0"
            out = pb[0][:, i * 256: i * 256 + TBk]
            for k in range(8):
                s.op("pe", lambda e: e.matmul(out, lhsT=Wb[:, k, ct * 128:(ct + 1) * 128], rhs=hT[:, k, 0:TBk], start=(k == 0), stop=(k == 7)),
                     reads=["Wb", "hT"], writes=[key])
            return out, key

        for ch in range(nch):
            for h in range(2):
                out = pb[2][64 * h:64 * h + c, ch * 16:(ch + 1) * 16]
                for k in range(8):
                    s.op("pe", lambda e: e.matmul(out, lhsT=hT[:, k, ch * c:(ch + 1) * c], rhs=Wb[:, k, C_G:C_G + 16], start=(k == 0), stop=(k == 7)),
                         reads=["Wb", "hT"], writes=["B2"])
        Gv = G[:, 0:nch, :]
        s.op("dve", lambda e: e.tensor_copy(out=Gv, in_=pb[2][:, 0:nch * 16].rearrange("p (n g) -> p n g", g=16)), reads=["B2"], writes=["G"])
        Gbv = Gb[:, 0:nch, :]; Ggv = Gg[:, 0:nch, :]
        s.op("act", lambda e: e.activation(out=Gbv, in_=Gv[:, :, 0:HA], func=AF.Exp, scale=-1.0), reads=["G"], writes=["Gb"])
        s.op("act", lambda e: e.activation(out=Gbv, in_=Gbv, func=AF.Ln, bias=1.0), reads=["Gb"], writes=["Gb"])
        s.op("act", lambda e: e.activation(out=Gbv, in_=Gbv, func=AF.Exp, scale=-1.0), reads=["Gb"], writes=["Gb"])
        s.op("dve", lambda e: e.tensor_tensor(out=Ggv, in0=Gv[:, :, HA:2 * HA], in1=bc(dtb[:, None, :], [128, nch, HA]), op=ALU.add),
             reads=["G", "dtb_t"], writes=["Gg"])
        s.op("act", lambda e: e.activation(out=Ggv, in_=Ggv, func=AF.Exp), reads=["Gg"], writes=["Gg"])
        s.op("act", lambda e: e.activation(out=Ggv, in_=Ggv, func=AF.Ln, bias=1.0), reads=["Gg"], writes=["Gg"])
        s.op("dve", lambda e: e.tensor_tensor(out=Ggv, in0=Ggv, in1=bc(negA[:, None, :], [128, nch, HA]), op=ALU.mult),
             reads=["Gg", "negA"], writes=["Gg"])
        gsv = gs[:, :, 0:nch]; bsv = bs[:, :, 0:nch]; nbsv = nbs[:, :, 0:nch]
        gcv = gc[:, :, 0:nch]; glv = gl[:, :, 0:nch]; egcv = egc[:, :, 0:nch]; dkv = dk[:, :, 0:nch]; bgev = bge[:, :, 0:nch]
        for h in range(2):
            sl = slice(64 * h, 64 * h + 64)
            for (dst, src, kd, ks) in ((gs, Gg, "gs", "Gg"), (bs, Gb, "bs", "Gb")):
                for p in range(NP):
                    s.op("dve", lambda e: e.tensor_copy(out=dst[sl, p, 0:nch], in_=src[sl, 0:nch, 2 * p + h]), reads=[ks], writes=[kd])
        s.op("dve", lambda e: e.tensor_scalar(out=nbsv, in0=bsv, scalar1=-1.0, scalar2=None, op0=ALU.mult), reads=["bs"], writes=["nbs"])
        for h in range(2):
            rs = slice(64 * h, 64 * h + c)
            s.op("pe", lambda e: e.matmul(pb[2][rs, 64:64 + NP * nch], lhsT=Tri_s[rs, 0:c], rhs=gs[rs, :, 0:nch], start=True, stop=True),
                 reads=["Tri_s", "gs"], writes=["B2"])
            s.op("pe", lambda e: e.matmul(pb[2][rs, 96:96 + NP * nch], lhsT=ones[rs, 0:c], rhs=gs[rs, :, 0:nch], start=True, stop=True),
                 reads=["ones", "gs"], writes=["B2"])
        s.op("dve", lambda e: e.tensor_copy(out=gcv, in_=pb[2][:, 64:64 + NP * nch].rearrange("p (a n) -> p a n", n=nch)), reads=["B2"], writes=["gc"])
        s.op("dve", lambda e: e.tensor_copy(out=glv, in_=pb[2][:, 96:96 + NP * nch].rearrange("p (a n) -> p a n", n=nch)), reads=["B2"], writes=["gl"])
        s.op("act", lambda e: e.activation(out=egcv, in_=gcv, func=AF.Exp), reads=["gc"], writes=["egc"])
        s.op("dve", lambda e: e.tensor_tensor(out=dkv, in0=glv, in1=gcv, op=ALU.subtract), reads=["gl", "gc"], writes=["dk"])
        s.op("act", lambda e: e.activation(out=dkv, in_=dkv, func=AF.Exp), reads=["dk"], writes=["dk"])
        s.op("dve", lambda e: e.tensor_tensor(out=bgev, in0=bsv, in1=egcv, op=ALU.mult), reads=["bs", "egc"], writes=["bge"])

        def front(p):
            par = p % 2
            for i3 in range(3):
                ct = 4 * i3 + p
                pp, pk = inproj_fm(ct)
                rv = raw[:, i3, 0:nseg * (seglen + 3)].rearrange("p (n c) -> p n c", c=seglen + 3)
                if first and not is_sample:
                    s.op("pool", lambda e: e.memset(rv[:, :, 0:3], 0.0), reads=[f"raw{i3}"], writes=[f"raw{i3}"])
                else:
                    s.op("pool", lambda e: e.tensor_copy(out=rv[:, :, 0:3], in_=halo[:, ct, 0:nseg, :]), reads=["halo"], writes=[f"raw{i3}"])
                s.op("act", lambda e: e.activation(out=rv[:, :, 3:3 + seglen], in_=pp.rearrange("p (n c) -> p n c", c=seglen), func=AF.Copy),
                     reads=[pk], writes=[f"raw{i3}"])
                s.op("pool", lambda e: e.tensor_copy(out=halo[:, ct, 0:nseg, :], in_=rv[:, :, seglen:seglen + 3]), reads=[f"raw{i3}"], writes=["halo"])
                cvv = cv[:, i3, 0:TBk].rearrange("p (n c) -> p n c", c=seglen)
                s.op("dve", lambda e: e.tensor_scalar(out=cvv, in0=rv[:, :, 0:seglen], scalar1=cw[:, ct, 0:1], scalar2=None, op0=ALU.mult),
                     reads=[f"raw{i3}", "cw_t"], writes=[f"cv{i3}"])
                for j in range(1, 4):
                    s.op("dve", lambda e: e.scalar_tensor_tensor(out=cvv, in0=rv[:, :, j:j + seglen], scalar=cw[:, ct, j:j + 1], in1=cvv,
                                                                 op0=ALU.mult, op1=ALU.add), reads=[f"raw{i3}", "cw_t", f"cv{i3}"], writes=[f"cv{i3}"])
                if i3 < 2:
                    silu_from(cv[:, i3, 0:TBk], [f"cv{i3}"], cv[:, i3, 0:TBk], f"cv{i3}", tmpA[i3][:, 0:TBk], f"tmpA{i3}", TBk)
                else:
                    silu_from(cv[:, i3, 0:TBk], [f"cv{i3}"], cvb[par][:, 2, 0:TBk], f"cvb{par}v", tmpA[i3][:, 0:TBk], f"tmpA{i3}", TBk)
            for i3 in range(2):
                src = cv[:, i3, 0:TBk]
                s.op("pool", lambda e: e.tensor_tensor(out=sqA[i3][:, 0:TBk], in0=src, in1=src, op=ALU.mult), reads=[f"cv{i3}"], writes=[f"sqA{i3}"])
                s.op("pe", lambda e: e.matmul(pb[7][:, i3 * 256:i3 * 256 + TBk], lhsT=ob2b[:], rhs=sqA[i3][:, 0:TBk], start=True, stop=True), reads=["ob2b", f"sqA{i3}"], writes=["BT"])
                s.op("act", lambda e: e.activation(out=tnA[i3][:, 0:TBk], in_=pb[7][:, i3 * 256:i3 * 256 + TBk], func=AF.Ln, bias=EPS), reads=["BT"], writes=[f"tnA{i3}"])
                s.op("act", lambda e: e.activation(out=tnA[i3][:, 0:TBk], in_=tnA[i3][:, 0:TBk], func=AF.Exp, scale=-0.5), reads=[f"tnA{i3}"], writes=[f"tnA{i3}"])
                if i3 == 0:
                    s.op("dve", lambda e: e.scalar_tensor_tensor(out=cvb[par][:, 0, 0:TBk], in0=src, scalar=0.125, in1=tnA[i3][:, 0:TBk], op0=ALU.mult, op1=ALU.mult),
                         reads=[f"cv{i3}", f"tnA{i3}"], writes=[f"cvb{par}q"])
                else:
                    s.op("dve", lambda e: e.tensor_tensor(out=cvb[par][:, 1, 0:TBk], in0=src, in1=tnA[i3][:, 0:TBk], op=ALU.mult), reads=[f"cv{i3}", f"tnA{i3}"], writes=[f"cvb{par}k"])
            pp, pk = inproj_fm(12 + p)
            s.op("act", lambda e: e.activation(out=za[par][:, 0:TBk], in_=pp, func=AF.Copy), reads=[pk], writes=[f"za{par}"])
            silu_from(za[par][:, 0:TBk], [f"za{par}"], za[par][:, 0:TBk], f"za{par}", tmpA[0][:, 0:TBk], "tmpA0", TBk)
            qn = cvb[par][:, 0, 0:TBk]; kn = cvb[par][:, 1, 0:TBk]; vs = cvb[par][:, 2, 0:TBk]
            ckq, ckk, ckv = f"cvb{par}q", f"cvb{par}k", f"cvb{par}v"
            s.op("dve", lambda e: e.tensor_tensor(out=v3(rhsD[:, 0:TBk]), in0=bc(gs[:, p, 0:nch, None], [128, nch, c]), in1=bc(U_s[:, None, 0:c], [128, nch, c]), op=ALU.mult),
                 reads=["gs", "U_s"], writes=["rhsD"])
            s.op("pool", lambda e: e.tensor_tensor(out=v3(Dg[:, 0:TBk]), in0=bc(egc[:, p, 0:nch, None], [128, nch, c]), in1=bc(I_s[:, None, 0:c], [128, nch, c]), op=ALU.mult),
                 reads=["egc", "I_s"], writes=["Dg"])
            for h in range(2):
                rs = slice(64 * h, 64 * h + c)
                s.op("pe", lambda e: e.matmul(pb[6][rs, 0:TBk], lhsT=Tri_s[rs, 0:c], rhs=rhsD[rs, 0:TBk], start=True, stop=True),
                     reads=["Tri_s", "rhsD"], writes=["B6"])
            s.op("act", lambda e: e.activation(out=Ee[:, 0:TBk], in_=pb[6][:, 0:TBk], func=AF.Exp), reads=["B6"], writes=["Ee"])
            s.op("pool", lambda e: e.tensor_tensor(out=v3(Dm[:, 0:TBk]), in0=v3(Ee[:, 0:TBk]), in1=bc(Mc_s[:, None, 0:c], [128, nch, c]), op=ALU.mult),
                 reads=["Ee", "Mc_s"], writes=["Dm"])
            s.op("pool", lambda e: e.tensor_tensor(out=v3(Ds[:, 0:TBk]), in0=v3(Ee[:, 0:TBk]), in1=bc(U_s[:, None, 0:c], [128, nch, c]), op=ALU.mult),
                 reads=["Ee", "U_s"], writes=["Ds"])
            for h in range(2):
                rs = slice(64 * h, 64 * h + c)
                s.op("pe", lambda e: e.matmul(pb[6][64 * h:64 * h + 64, 0:TBk], lhsT=ones[rs, 0:64], rhs=Dg[rs, 0:TBk], start=True, stop=True),
                     reads=["ones", "Dg"], writes=["B6"])
            s.op("act", lambda e: e.activation(out=EBs[par][:, 0:TBk], in_=pb[6][:, 0:TBk], func=AF.Copy), reads=["B6"], writes=[f"EBs{par}"])
            s.op("dve", lambda e: e.tensor_tensor(out=QeT[par][:, 0:TBk], in0=qn, in1=EBs[par][:, 0:TBk], op=ALU.mult), reads=[ckq, f"EBs{par}"], writes=[f"QeT{par}"])
            for ch in range(nch):
                cs = slice(ch * c, (ch + 1) * c)
                for h in range(2):
                    fs = slice(64 * h, 64 * h + 64); rs = slice(64 * h, 64 * h + c)
                    s.op("pe", lambda e: e.matmul(pb[7][rs, cs], lhsT=kn[fs, cs], rhs=kn[fs, cs], start=True, stop=True), reads=[ckk], writes=["BT"])
                    s.op("pe", lambda e: e.matmul(pb[7][rs, 256 + ch * c:256 + (ch + 1) * c], lhsT=qn[fs, cs], rhs=kn[fs, cs], start=True, stop=True),
                         reads=[ckq, ckk], writes=["BT"])
            s.op("dve", lambda e: e.tensor_tensor(out=v3(tmp[2][:, 0:TBk]), in0=v3(pb[7][:, 0:TBk]), in1=bc(nbs[:, p, 0:nch, None], [128, nch, c]), op=ALU.mult),
                 reads=["BT", "nbs"], writes=["tmp2"])
            s.op("pool", lambda e: e.tensor_tensor(out=PT0t[par][:, 0:TBk], in0=tmp[2][:, 0:TBk], in1=Ds[:, 0:TBk], op=ALU.mult), reads=["tmp2", "Ds"], writes=[f"PT0t{par}"])
            s.op("dve", lambda e: e.tensor_tensor(out=attn[:, 0:TBk], in0=pb[7][:, 256:256 + TBk], in1=Dm[:, 0:TBk], op=ALU.mult), reads=["BT", "Dm"], writes=["attn"])

            for ch in range(nch):
                cs = slice(ch * c, (ch + 1) * c)
                for h in range(2):
                    fs = slice(64 * h, 64 * h + 64); rs = slice(64 * h, 64 * h + c)
                    s.op("pe", lambda e: e.matmul(pb[6][rs, cs], lhsT=PT0t[par][rs, cs], rhs=I_sb[rs, 0:c], start=True, stop=True), reads=[f"PT0t{par}", "I_sb"], writes=["B6"])
                    s.op("pe", lambda e: e.matmul(pb[6][rs, 256 + ch * c:256 + (ch + 1) * c], lhsT=attn[rs, cs], rhs=I_sb[rs, 0:c], start=True, stop=True),
                         reads=["attn", "I_sb"], writes=["B6"])
                    s.op("pe", lambda e: e.matmul(pb[7][rs, ch * 64:(ch + 1) * 64], lhsT=kn[fs, cs], rhs=I_sb[fs, 0:64], start=True, stop=True), reads=[ckk, ckv, "I_sb"], writes=["BT"])
                    s.op("pe", lambda e: e.matmul(pb[7][rs, 256 + ch * 64:256 + (ch + 1) * 64], lhsT=vs[fs, cs], rhs=I_sb[fs, 0:64], start=True, stop=True),
                         reads=[ckk, ckv, "I_sb"], writes=["BT"])
            s.op("act", lambda e: e.activation(out=P0t[par][:, 0:TBk], in_=pb[6][:, 0:TBk], func=AF.Copy), reads=["B6"], writes=[f"P0t{par}"])
            s.op("dve", lambda e: e.tensor_tensor(out=v3(R0t[par][:, 0:TBk]), in0=v3(pb[6][:, 0:TBk]), in1=bc(I_s[:, None, 0:c], [128, nch, c]), op=ALU.add),
                 reads=["B6", "I_s"], writes=[f"R0t{par}"])
            s.op("act", lambda e: e.activation(out=attnT[par][:, 0:TBk], in_=pb[6][:, 256:256 + TBk], func=AF.Copy), reads=["B6"], writes=[f"attnT{par}"])
            k4 = pb[7][:, 0:nch * 64].rearrange("p (n d) -> p n d", d=64)
            v4 = pb[7][:, 256:256 + nch * 64].rearrange("p (n d) -> p n d", d=64)
            s.op("dve", lambda e: e.tensor_tensor(out=Kbe[par][:, 0:nch, :], in0=k4, in1=bc(bge[:, p, 0:nch, None], [128, nch, 64]), op=ALU.mult), reads=["BT", "bge"], writes=[f"Kbe{par}"])
            s.op("dve", lambda e: e.tensor_tensor(out=Kd[par][:, 0:nch, :], in0=k4, in1=bc(dk[:, p, 0:nch, None], [128, nch, 64]), op=ALU.mult), reads=["BT", "dk"], writes=[f"Kd{par}"])
            s.op("dve", lambda e: e.tensor_tensor(out=bV[par][:, 0:nch, :], in0=v4, in1=bc(bs[:, p, 0:nch, None], [128, nch, 64]), op=ALU.mult), reads=["BT", "bs"], writes=[f"bV{par}"])
        def core(p):
            par = p % 2
            Pc, kP = P0t[par], f"P0t{par}"
            PTc, kPT = PT0t[par], f"PT0t{par}"
            Rc, kR = R0t[par], f"R0t{par}"
            for lev in range(1, nlev + 2):
                do_pow = lev <= nlev
                need_P = lev < nlev
                do_R = lev >= 2
                nP, nkP = P[lev % 2], f"P{lev % 2}"
                nPT, nkPT = PT[lev % 2], f"PT{lev % 2}"
                nR, nkR = R[lev % 2], f"R{lev % 2}"
                for ch in range(nch):
                    cs = slice(ch * c, (ch + 1) * c)
                    for h in range(2):
                        rs = slice(64 * h, 64 * h + c)
                        if do_pow and need_P:
                            s.op("pe", lambda e: e.matmul(pb[5][rs, cs], lhsT=PTc[rs, cs], rhs=Pc[rs, cs], start=True, stop=True), reads=[kPT, kP], writes=["B5"])
                        if do_pow:
                            s.op("pe", lambda e: e.matmul(pb[5][rs, 256 + ch * c:256 + (ch + 1) * c], lhsT=Pc[rs, cs], rhs=PTc[rs, cs], start=True, stop=True),
                                 reads=[kPT, kP], writes=["B5"])
                        if do_R:
                            s.op("pe", lambda e: e.matmul(pb[4][rs, cs], lhsT=PTc[rs, cs], rhs=Rc[rs, cs], start=True, stop=True), reads=[kPT, kR], writes=["B4"])
                if do_pow and need_P:
                    s.op("act", lambda e: e.activation(out=nP[:, 0:TBk], in_=pb[5][:, 0:TBk], func=AF.Copy), reads=["B5"], writes=[nkP])
                if do_pow:
                    s.op("dve", lambda e: e.tensor_copy(out=nPT[:, 0:TBk], in_=pb[5][:, 256:256 + TBk]), reads=["B5"], writes=[nkPT])
                if do_R:
                    s.op("dve", lambda e: e.tensor_tensor(out=nR[:, 0:TBk], in0=pb[4][:, 0:TBk], in1=Rc[:, 0:TBk], op=ALU.add), reads=["B4", kR], writes=[nkR])
                    Rc, kR = nR, nkR
                if do_pow:
                    if need_P:
                        Pc, kP = nP, nkP
                    PTc, kPT = nPT, nkPT
            Rf, rk = Rc, kR
            for ch in range(nch):
                cs = slice(ch * c, (ch + 1) * c)
                for h in range(2):
                    rs = slice(64 * h, 64 * h + c)
                    s.op("pe", lambda e: e.matmul(pb[4][rs, 256 + ch * 64:256 + (ch + 1) * 64], lhsT=Rf[rs, cs], rhs=bV[par][rs, ch, :], start=True, stop=True),
                         reads=[rk, f"bV{par}"], writes=["B4"])
                    s.op("pe", lambda e: e.matmul(pb[5][64 * h:64 * h + 64, ch * c:(ch + 1) * c], lhsT=Kbe[par][rs, ch, :], rhs=Rf[rs, cs], start=True, stop=True),
                         reads=[rk, f"Kbe{par}"], writes=["B5"])
            s.op("act", lambda e: e.activation(out=u[:, 0:nch, :], in_=pb[4][:, 256:256 + nch * 64].rearrange("p (n d) -> p n d", d=64), func=AF.Copy), reads=["B4"], writes=["u"])
            s.op("dve", lambda e: e.tensor_copy(out=wT[:, 0:TBk], in_=pb[5][:, 0:TBk]), reads=["B5"], writes=["wT"])
            skey = f"Sg{p}"
            for ch in range(nch):
                cs = slice(ch * c, (ch + 1) * c)
                seg = ch // cps
                if ch % cps == 0:
                    if is_sample:
                        s.dma("sp", Sg[:, p, :], sg_d[seg, p], writes=[skey])
                        s.op("dve", lambda e: e.tensor_copy(out=Sgb[:, p, :], in_=Sg[:, p, :]), reads=[skey], writes=[skey + "b"])
                    elif first:
                        s.op("pool", lambda e: e.memset(Sg[:, p, :], 0.0), writes=[skey])
                        s.op("pool", lambda e: e.memset(Sgb[:, p, :], 0.0), writes=[skey + "b"])
                for h in range(2):
                    fs = slice(64 * h, 64 * h + 64); rs = slice(64 * h, 64 * h + c)
                    s.op("pe", lambda e: e.matmul(pb[1][rs, 0:64], lhsT=wT[fs, cs], rhs=Sgb[fs, p, :], start=True, stop=True), reads=["wT", skey + "b"], writes=["B1"])
                s.op("dve", lambda e: e.tensor_tensor(out=vn[:], in0=u[:, ch, :], in1=pb[1][:, 0:64], op=ALU.subtract), reads=["u", "B1"], writes=["vn"])
                for h in range(2):
                    fs = slice(64 * h, 64 * h + 64); rs = slice(64 * h, 64 * h + c)
                    s.op("pe", lambda e: e.matmul(pb[1][fs, 256 + ch * c:256 + (ch + 1) * c], lhsT=Sgb[fs, p, :], rhs=QeT[par][fs, cs], start=True, stop=False),
                         reads=[skey + "b", f"QeT{par}"], writes=["B1"])
                    s.op("pe", lambda e: e.matmul(pb[1][fs, 256 + ch * c:256 + (ch + 1) * c], lhsT=vn[rs, :], rhs=attnT[par][rs, cs], start=False, stop=True),
                         reads=["vn", f"attnT{par}"], writes=["B1"])
                for h in range(2):
                    fs = slice(64 * h, 64 * h + 64); rs = slice(64 * h, 64 * h + c)
                    s.op("pe", lambda e: e.matmul(pb[1][fs, 64:128], lhsT=Kd[par][rs, ch, :], rhs=vn[rs, :], start=True, stop=True), reads=[f"Kd{par}", "vn"], writes=["B1"])
                s.op("dve", lambda e: e.scalar_tensor_tensor(out=Sgb[:, p, :], in0=Sg[:, p, :], scalar=EBs[par][:, (ch + 1) * c - 1:(ch + 1) * c], in1=pb[1][:, 64:128],
                                                             op0=ALU.mult, op1=ALU.add), reads=[skey, f"EBs{par}", "B1"], writes=[skey + "b"])
                s.op("dve", lambda e: e.scalar_tensor_tensor(out=Sg[:, p, :], in0=Sg[:, p, :], scalar=EBs[par][:, (ch + 1) * c - 1:(ch + 1) * c], in1=pb[1][:, 64:128],
                                                             op0=ALU.mult, op1=ALU.add), reads=[skey, f"EBs{par}", "B1"], writes=[skey])
                if is_sample and (ch + 1) % cps == 0:
                    s.dma("sp", ngs[seg, p], Sg[:, p, :], reads=[skey], writes=[f"o_ngs{seg}_{p}"])
            s.op("act", lambda e: e.activation(out=oTf[:, 0:TBk], in_=pb[1][:, 256:256 + TBk], func=AF.Copy), reads=["B1"], writes=["oTf"])
            s.op("pool", lambda e: e.tensor_tensor(out=sqb[:, 0:TBk], in0=oTf[:, 0:TBk], in1=oTf[:, 0:TBk], op=ALU.mult), reads=["oTf"], writes=["sqb"])
            s.op("pe", lambda e: e.matmul(pb[3][:, 0:TBk], lhsT=ob2b[:], rhs=sqb[:, 0:TBk], start=True, stop=True), reads=["ob2b", "sqb"], writes=["B3"])
            s.op("act", lambda e: e.activation(out=tmp[1][:, 0:TBk], in_=pb[3][:, 0:TBk], func=AF.Ln, scale=1.0 / 64, bias=EPS), reads=["B3"], writes=["tmp1"])
            s.op("act", lambda e: e.activation(out=tmp[1][:, 0:TBk], in_=tmp[1][:, 0:TBk], func=AF.Exp, scale=-0.5), reads=["tmp1"], writes=["tmp1"])
            s.op("dve", lambda e: e.tensor_tensor(out=oTf[:, 0:TBk], in0=oTf[:, 0:TBk], in1=tmp[1][:, 0:TBk], op=ALU.mult), reads=["oTf", "tmp1"], writes=["oTf"])
            s.op("dve", lambda e: e.scalar_tensor_tensor(out=oTn[:, p, 0:TBk], in0=oTf[:, 0:TBk], scalar=gnw[:, 0:1], in1=za[par][:, 0:TBk], op0=ALU.mult, op1=ALU.mult),
                 reads=["oTf", "gnw_t", f"za{par}"], writes=[f"oTn{p}"])

        s.op("pool", lambda e: e.memset(smask[:, 0:TBk], 1.0), writes=["smask"])
        s.op("pool", lambda e: e.memset(v3(smask[:, 0:TBk])[:, :, 0:1], 0.0), reads=["smask"], writes=["smask"])
        def hg(h):
            pp, pk = inproj_fm(16 + h, 1)
            s.op("act", lambda e: e.activation(out=qb[:, 0:TBk], in_=pp, func=AF.Copy), reads=[pk], writes=["qb"])
            silu_from(qb[:, 0:TBk], ["qb"], qb[:, 0:TBk], "qb", bl[:, 0:TBk], "bl", TBk)
            pp, pk = inproj_fm(24 + h, 1)
            s.op("act", lambda e: e.activation(out=zb[:, 0:TBk], in_=pp, func=AF.Copy), reads=[pk], writes=["zb"])
            silu_from(zb[:, 0:TBk], ["zb"], zb[:, 0:TBk], "zb", bl[:, 0:TBk], "bl", TBk)
            pp, pk = inproj_fm(20 + h, 1)
            s.op("act", lambda e: e.activation(out=ff[:, 0:TBk], in_=pp, func=AF.Exp, scale=-1.0), reads=[pk], writes=["ff"])
            s.op("act", lambda e: e.activation(out=ff[:, 0:TBk], in_=ff[:, 0:TBk], func=AF.Ln, bias=1.0), reads=["ff"], writes=["ff"])
            s.op("act", lambda e: e.activation(out=ff[:, 0:TBk], in_=ff[:, 0:TBk], func=AF.Exp, scale=-1.0), reads=["ff"], writes=["ff"])
            s.op("dve", lambda e: e.tensor_scalar(out=ff[:, 0:TBk], in0=ff[:, 0:TBk], scalar1=oml[:, h:h + 1], scalar2=lb[:, h:h + 1], op0=ALU.mult, op1=ALU.add),
                 reads=["ff", "oml", "lb"], writes=["ff"])
            s.op("act", lambda e: e.activation(out=lf[:, 0:TBk], in_=ff[:, 0:TBk], func=AF.Ln), reads=["ff"], writes=["lf"])
            s.op("dve", lambda e: e.tensor_scalar(out=kb[:, 0:TBk], in0=ff[:, 0:TBk], scalar1=-1.0, scalar2=1.0, op0=ALU.mult, op1=ALU.add), reads=["ff"], writes=["kb"])
            s.op("dve", lambda e: e.tensor_tensor_scan(out=bb[:, 0:TBk], data0=smask[:, 0:TBk], data1=lf[:, 0:TBk], initial=0.0, op0=ALU.mult, op1=ALU.add),
                 reads=["smask", "lf"], writes=["bb"])
            b3 = v3(bb[:, 0:TBk])
            s.op("pool", lambda e: e.tensor_tensor(out=v3(bl[:, 0:TBk]), in0=b3, in1=bc(b3[:, :, c - 1:c], [128, nch, c]), op=ALU.subtract), reads=["bb"], writes=["bl"])
            s.op("act", lambda e: e.activation(out=Qef[:, 0:TBk], in_=bb[:, 0:TBk], func=AF.Exp), reads=["bb"], writes=["Qef"])
            s.op("act", lambda e: e.activation(out=Qxf[:, 0:TBk], in_=bl[:, 0:TBk], func=AF.Exp), reads=["bl"], writes=["Qxf"])
            s.op("act", lambda e: e.activation(out=Kdf[:, 0:TBk], in_=bl[:, 0:TBk], func=AF.Exp, scale=-1.0), reads=["bl"], writes=["Kdf"])
            s.op("act", lambda e: e.activation(out=ebl[:, 0:nch], in_=b3[:, :, c - 1], func=AF.Exp), reads=["bb"], writes=["ebl"])
            s.op("dve", lambda e: e.tensor_tensor(out=Qe[:, 0:TBk], in0=Qef[:, 0:TBk], in1=qb[:, 0:TBk], op=ALU.mult), reads=["Qef", "qb"], writes=["Qe"])
            s.op("pool", lambda e: e.tensor_tensor(out=Qx[:, 0:TBk], in0=Qxf[:, 0:TBk], in1=qb[:, 0:TBk], op=ALU.mult), reads=["Qxf", "qb"], writes=["Qx"])
            s.op("dve", lambda e: e.tensor_tensor(out=Kdh[:, 0:TBk], in0=Kdf[:, 0:TBk], in1=kb[:, 0:TBk], op=ALU.mult), reads=["Kdf", "kb"], writes=["Kdh"])
            for ch in range(nch):
                cs = slice(ch * c, (ch + 1) * c)
                outp = pb[2][0:c, 128:256]
                for k in range(8):
                    s.op("pe", lambda e: e.matmul(outp, lhsT=hT[:, k, cs], rhs=Wb[:, k, C_HI + 128 * h:C_HI + 128 * (h + 1)], start=(k == 0), stop=(k == 7)),
                         reads=["Wb", "hT"], writes=["B2"])
                s.op("pe", lambda e: e.matmul(pb[2][0:c, 256:384], lhsT=Kdh[:, cs], rhs=identb[:], start=True, stop=True), reads=["Kdh", "identb"], writes=["B2"])
                s.op("pe", lambda e: e.matmul(pb[2][0:c, 384:384 + c], lhsT=Kdh[:, cs], rhs=Qx[:, cs], start=True, stop=True), reads=["Kdh", "Qx"], writes=["B2"])
                s.op("act", lambda e: e.activation(out=vtok[0:c, ch, :], in_=outp, func=AF.Copy), reads=["B2"], writes=["vtok"])
                s.op("dve", lambda e: e.tensor_copy(out=Kdt[0:c, ch, :], in_=pb[2][0:c, 256:384]), reads=["B2"], writes=["Kdt"])
                s.op("dve", lambda e: e.tensor_tensor(out=aTh[0:c, cs], in0=pb[2][0:c, 384:384 + c], in1=Tri_s[0:c, 0:c], op=ALU.mult),
                     reads=["B2", "Tri_s"], writes=["aTh"])
            skey = f"Sh{h}"
            for ch in range(nch):
                cs = slice(ch * c, (ch + 1) * c)
                seg = ch // cps
                if ch % cps == 0:
                    if is_sample:
                        s.dma("sp", Sh[:, h, :], sh_d[seg, h], writes=[skey])
                        s.op("dve", lambda e: e.tensor_copy(out=Shb[:, h, :], in_=Sh[:, h, :]), reads=[skey], writes=[skey + "b"])
                    elif first:
                        s.op("pool", lambda e: e.memset(Sh[:, h, :], 0.0), writes=[skey])
                        s.op("pool", lambda e: e.memset(Shb[:, h, :], 0.0), writes=[skey + "b"])
                s.op("pe", lambda e: e.matmul(pb[3][:, 256 + ch * c:256 + (ch + 1) * c], lhsT=Shb[:, h, :], rhs=Qe[:, cs], start=True, stop=False), reads=[skey + "b", "Qe"], writes=["B3"])
                s.op("pe", lambda e: e.matmul(pb[3][:, 256 + ch * c:256 + (ch + 1) * c], lhsT=vtok[0:c, ch, :], rhs=aTh[0:c, cs], start=False, stop=True),
                     reads=["vtok", "aTh"], writes=["B3"])
                s.op("pe", lambda e: e.matmul(pb[1][:, 128:256], lhsT=Kdt[0:c, ch, :], rhs=vtok[0:c, ch, :], start=True, stop=True), reads=["Kdt", "vtok"], writes=["B1"])
                s.op("dve", lambda e: e.scalar_tensor_tensor(out=Shb[:, h, :], in0=Sh[:, h, :], scalar=ebl[:, ch:ch + 1], in1=pb[1][:, 128:256], op0=ALU.mult, op1=ALU.add),
                     reads=[skey, "ebl", "B1"], writes=[skey + "b"])
                s.op("dve", lambda e: e.scalar_tensor_tensor(out=Sh[:, h, :], in0=Sh[:, h, :], scalar=ebl[:, ch:ch + 1], in1=pb[1][:, 128:256], op0=ALU.mult, op1=ALU.add),
                     reads=[skey, "ebl", "B1"], writes=[skey])
                if is_sample and (ch + 1) % cps == 0:
                    s.dma("sp", nhs[seg, h], Sh[:, h, :], reads=[skey], writes=[f"o_nhs{seg}_{h}"])
            s.op("act", lambda e: e.activation(out=oTfH[:, 0:TBk], in_=pb[3][:, 256:256 + TBk], func=AF.Copy), reads=["B3"], writes=["oTfH"])
            s.op("pool", lambda e: e.tensor_tensor(out=sqbH[:, 0:TBk], in0=oTfH[:, 0:TBk], in1=oTfH[:, 0:TBk], op=ALU.mult), reads=["oTfH"], writes=["sqbH"])
            s.op("pe", lambda e: e.matmul(pb[3][:, 256:256 + TBk], lhsT=onesb[:], rhs=sqbH[:, 0:TBk], start=True, stop=True), reads=["onesb", "sqbH"], writes=["B3"])
            s.op("act", lambda e: e.activation(out=tmpH[1][:, 0:TBk], in_=pb[3][:, 256:256 + TBk], func=AF.Ln, scale=1.0 / 128, bias=EPS), reads=["B3"], writes=["tmpH1"])
            s.op("act", lambda e: e.activation(out=tmpH[1][:, 0:TBk], in_=tmpH[1][:, 0:TBk], func=AF.Exp, scale=-0.5), reads=["tmpH1"], writes=["tmpH1"])
            s.op("dve", lambda e: e.tensor_tensor(out=oTfH[:, 0:TBk], in0=oTfH[:, 0:TBk], in1=tmpH[1][:, 0:TBk], op=ALU.mult), reads=["oTfH", "tmpH1"], writes=["oTfH"])
            s.op("dve", lambda e: e.scalar_tensor_tensor(out=oTn[:, 4 + h, 0:TBk], in0=oTfH[:, 0:TBk], scalar=hnw[:, 0:1], in1=zb[:, 0:TBk], op0=ALU.mult, op1=ALU.mult),
                 reads=["oTfH", "hnw_t", "zb"], writes=[f"oTn{4 + h}"])

        def record(fn, arg):
            lst = []
            s.rec = lst
            fn(arg)
            s.rec = None
            return lst
        fr_ = [record(front, p) for p in range(NP)]
        co_ = [record(core, p) for p in range(NP)]
        hg_ = [record(hg, h) for h in range(HB)]
        s.merge_emit([fr_[0], hg_[0]])
        for ph in range(NP):
            st = [co_[ph]]
            if ph + 1 < NP:
                st += [fr_[ph + 1], hg_[ph + 1]]
            s.merge_emit(st)
        for tt in range(ntt):
            X = xt[tt]
            for half in range(2):
                bank = pb[0] if half == 0 else pb[2]
                key = "pi_full" if half == 0 else "pg_full"
                for k in range(8):
                    s.op("pe", lambda e: e.matmul(bank[0:TT, :], lhsT=oTn[:, k, tt * TT:(tt + 1) * TT], rhs=WOb[:, k, half * 512:(half + 1) * 512], start=(k == 0), stop=(k == 7)),
                         reads=[f"oTn{k}", "WOb"], writes=(["B0", "B0"] if half == 0 else ["B2", "B2", "B2"]))
                s.op("dve", lambda e: e.tensor_tensor(out=yo[0:TT, half * 512:(half + 1) * 512], in0=bank[0:TT, :], in1=X[0:TT, half * 512:(half + 1) * 512], op=ALU.add),
                     reads=(["B0", "B0"] if half == 0 else ["B2", "B2", "B2"]) + [X.name], writes=["yo"])
            s.op("act", lambda e: e.activation(out=sqj[0:TT, :], in_=yo[0:TT, :], func=AF.Square, accum_out=ss[0:TT, :]), reads=["yo"], writes=["sqj", "ss"])
            s.op("act", lambda e: e.activation(out=rr[0:TT, :], in_=ss[0:TT, :], func=AF.Ln, scale=1.0 / D, bias=EPS), reads=["ss"], writes=["rr"])
            s.op("act", lambda e: e.activation(out=rr[0:TT, :], in_=rr[0:TT, :], func=AF.Exp, scale=-0.5), reads=["rr"], writes=["rr"])
            s.op("dve", lambda e: e.scalar_tensor_tensor(out=yo2[0:TT, :], in0=yo[0:TT, :], scalar=rr[0:TT, :], in1=fnw[0:TT, :], op0=ALU.mult, op1=ALU.mult),
                 reads=["yo", "rr", "fnw_t"], writes=["yo2"])
            s.dma("sp", y_dst[t0 + tt * TT: t0 + (tt + 1) * TT, :], yo2[0:TT, :], reads=["yo2"], writes=[f"o_y{id(y_dst)}"], slot="yout")

    nblk = T // TB
    for b in range(nblk):
        block(xp, yp, b * TB, 1, TB, 64, b == 0, False)
    s.dma("sp", ncp[:, :, 0, :], halo[:, :, 0, :], reads=["halo"], writes=["o_ncp"])
    for p in range(NP):
        s.dma("sp", ngp[0, p], Sg[:, p, :], reads=[f"Sg{p}"], writes=[f"o_ngp{p}"])
    for h in range(HB):
        s.dma("sp", nhp[0, h], Sh[:, h, :], reads=[f"Sh{h}"], writes=[f"o_nhp{h}"])
    s.dma("sp", halo[:], sc_d, reads=["halo"], writes=["halo"])
    block(xs, ys, 0, 4, 16, 16, True, True)
    s.dma("sp", ncs, halo[:], reads=["halo"], writes=["o_ncs"])
    s.finish("sp")
    return nc, s


_PERM = np.concatenate([np.arange(0, 1536), np.arange(1536, 2048), np.arange(2064, 2576), np.arange(2576, 3088),
                        np.arange(3600, 4112), np.arange(3088, 3600), np.arange(2048, 2064)])


def _core_inputs(b, inp):
    f = lambda a: np.ascontiguousarray(a, dtype=np.float32)
    cwv = inp["conv_w"][0]
    scv = inp["state_conv"][0][4 * b:4 * b + 4]
    return {
        "xp": f(inp["x_prompt"][b]),
        "xs": f(inp["x_sample"][4 * b:4 * b + 4].reshape(64, D)),
        "w_in": f(inp["w_in"][0][:, _PERM]),
        "w_out": f(inp["w_out"][0]),
        "nw": f(inp["norm_w"][0].reshape(8, 128).T),
        "cw": f(cwv.reshape(4, 12, 128).transpose(2, 1, 0)),
        "alog": f(np.broadcast_to(inp["gdn_A_log"][0][None, :], (128, HA))),
        "dtb": f(np.broadcast_to(inp["gdn_dt_bias"][0][None, :], (128, HA))),
        "gnw": f(np.tile(inp["gdn_norm_w"][0], 2).reshape(128, 1)),
        "hnw": f(inp["hgrn_norm_w"][0].reshape(128, 1)),
        "lbl": f(inp["hgrn_lb_logits"].reshape(2, HB, 128).transpose(2, 1, 0)),
        "fnw": f(np.broadcast_to(inp["final_norm_w"][None, :], (128, D))),
        "sc": f(scv.reshape(4, 3, 12, 128).transpose(3, 2, 0, 1)),
        "sg": f(inp["state_gdn"][0][4 * b:4 * b + 4].reshape(4, NP, 128, 64)),
        "sh": f(inp["state_hgrn"][0][4 * b:4 * b + 4]),
    }


_CACHE = {}


def kernel(**inputs):
    inp = {k: np.asarray(v) for k, v in inputs.items()}
    Bp, T, _ = inp["x_prompt"].shape
    assert Bp == 4 and inp["x_sample"].shape[:2] == (16, 16)
    if T not in _CACHE:
        _CACHE[T] = build(T)[0]
    nc = _CACHE[T]
    in_maps = [_core_inputs(c % 4, inp) for c in range(8)]
    res = run_bass_kernel_spmd(nc, in_maps, core_ids=list(range(8)))
    r = res.results
    y_prompt = np.stack([r[b]["yp"] for b in range(4)]).astype(np.float32)
    y_sample = np.concatenate([r[b]["ys"].reshape(4, 16, D) for b in range(4)]).astype(np.float32)
    cvt = lambda a: a.transpose(2, 3, 1, 0).reshape(a.shape[2], 3, 1536)
    ncp_ = np.stack([cvt(r[b]["ncp"])[0] for b in range(4)])[None].astype(np.float32)
    ngp_ = np.stack([r[b]["ngp"].reshape(HA, 64, 64) for b in range(4)])[None].astype(np.float32)
    nhp_ = np.stack([r[b]["nhp"].reshape(HB, 128, 128) for b in range(4)])[None].astype(np.float32)
    ncs_ = np.concatenate([cvt(r[b]["ncs"]) for b in range(4)])[None].astype(np.float32)
    ngs_ = np.concatenate([r[b]["ngs"].reshape(4, HA, 64, 64) for b in range(4)])[None].astype(np.float32)
    nhs_ = np.concatenate([r[b]["nhs"].reshape(4, HB, 128, 128) for b in range(4)])[None].astype(np.float32)
    return (y_prompt, y_sample, ncp_, ngp_, nhp_, ncs_, ngs_, nhs_)
```

```python
import numpy as np
import concourse.bass as bass
import concourse.mybir as mybir
from concourse.bass_utils import run_bass_kernel_spmd

F32 = mybir.dt.float32
BF16 = mybir.dt.bfloat16
AF = mybir.ActivationFunctionType
ALU = mybir.AluOpType

D = 1024
HA, HB = 8, 4
NP = HA // 2
NCOL = 4112
EPS = 1e-6
C_HI = 3584
C_G = 4096


SAME_ENGINE_WAIT = True


class _Proxy:
    def __getattr__(self, name):
        return lambda *a, **k: (name, a, k)


_PROXY = _Proxy()
PSUM_KEYS = {"B0", "B1", "B2", "B3", "B4", "B5", "B6", "BT"}


class Sched:
    def __init__(self, nc):
        self.nc = nc
        self.eng = {"pe": nc.tensor, "dve": nc.vector, "act": nc.scalar, "pool": nc.gpsimd, "sp": nc.sync}
        self.sem = {k: nc.alloc_semaphore(name=f"s_{k}") for k in self.eng}
        self.cnt = {k: 0 for k in self.eng}
        self.seen = {k: {} for k in self.eng}
        self.last_w = {}
        self.readers = {}
        self.dma_sems = {}
        self.n_wait = 0
        self.n_ops = 0
        self.rec = None

    def emit(self, r):
        if r[0] == "op":
            _, e, call, reads, writes = r
            self.op(e, lambda eng: getattr(eng, call[0])(*call[1], **call[2]), reads, writes)
        else:
            _, q, out, in_, reads, writes, slot = r
            self.dma(q, out, in_, reads, writes, slot)

    def _cost(self, r):
        if r[0] == "dma":
            return 2500.0
        _, e, call, reads, writes = r
        name, args, kw = call
        def nfree(ap):
            try:
                sh = list(ap.shape)
                n = 1
                for d in sh[1:]:
                    n *= int(d)
                return n
            except Exception:
                return 256
        if e == "pe":
            ap = kw.get("rhs", None) if name == "matmul" else kw.get("in_", None)
            n = nfree(ap) if ap is not None else 64
            c = 45.0 + 0.45 * n
            try:
                if name == "matmul" and kw["rhs"].dtype == F32:
                    c *= 3.0
            except Exception:
                pass
            return c
        ap = kw.get("out", None)
        n = nfree(ap) if ap is not None else 256
        if e == "dve":
            return 70.0 + 1.0 * n
        if e == "act":
            return 130.0 + 0.9 * n
        return 110.0 + 1.8 * n

    def _est_start(self, r):
        if r[0] == "dma":
            e, reads, writes = r[1], r[4], r[5]
        else:
            e, reads, writes = r[1], r[3], r[4]
        m = self.model
        t = m["eng"].get(e, 0.0)
        ex = [k for k in reads if k in PSUM_KEYS]
        for k in reads:
            w = m["w"].get(k)
            if w is not None:
                t = max(t, w[0] + (0.0 if w[1] == e else 160.0))
        for k in list(writes) + ex:
            w = m["w"].get(k)
            if w is not None:
                t = max(t, w[0] + (0.0 if w[1] == e else 160.0))
            for (tt, ee) in m["r"].get(k, {}).values():
                t = max(t, tt + (0.0 if ee == e else 160.0))
        return t

    def _model_commit(self, r, t0):
        if r[0] == "dma":
            e, reads, writes = r[1], r[4], r[5]
            eng_busy = 60.0
        else:
            e, reads, writes = r[1], r[3], r[4]
            eng_busy = None
        m = self.model
        c = self._cost(r)
        t1 = t0 + c
        m["eng"][e] = t0 + (eng_busy if eng_busy is not None else c)
        ex = [k for k in reads if k in PSUM_KEYS]
        who = e if r[0] == "op" else "dma"
        for k in reads:
            m["r"].setdefault(k, {})[who] = (t1, who)
        for k in list(writes) + ex:
            m["w"][k] = (t1, who)
            m["r"][k] = {}
        m["t"] = max(m.get("t", 0.0), t1)

    def merge_emit(self, streams):
        if not hasattr(self, "model"):
            self.model = {"eng": {}, "w": {}, "r": {}, "t": 0.0}
        units = []
        for st in streams:
            u = []
            for r in st:
                glued = (r[0] == "op" and r[2][0] == "matmul" and r[2][2].get("start") is False)
                if glued and u:
                    u[-1].append(r)
                else:
                    u.append([r])
            units.append(u)
        pos = [0] * len(units)
        while True:
            best, bi = None, -1
            for i, u in enumerate(units):
                if pos[i] < len(u):
                    t = self._est_start(u[pos[i]][0])
                    key = (t, -(len(u) - pos[i]))
                    if best is None or key < best:
                        best, bi = key, i
            if bi < 0:
                break
            for r in units[bi][pos[bi]]:
                t0 = self._est_start(r)
                self._model_commit(r, t0)
                self.emit(r)
            pos[bi] += 1


    def _wait(self, e, tok):
        name, sem, val = tok
        if name == "pe" and e == "pe":
            return
        if name == e and not SAME_ENGINE_WAIT:
            return
        if self.seen[e].get(name, 0) >= val:
            return
        self.eng[e].wait_ge(sem, val)
        self.seen[e][name] = val
        self.n_wait += 1

    def _deps(self, e, reads, writes):
        for k in reads:
            t = self.last_w.get(k)
            if t is not None:
                self._wait(e, t)
        for k in writes:
            t = self.last_w.get(k)
            if t is not None:
                self._wait(e, t)
            for t in self.readers.get(k, {}).values():
                self._wait(e, t)

    def _commit(self, tok, reads, writes):
        for k in reads:
            self.readers.setdefault(k, {})[tok[0]] = tok
        for k in writes:
            self.last_w[k] = tok
            self.readers[k] = {}

    def op(self, e, fn, reads=(), writes=()):
        if self.rec is not None:
            self.rec.append(("op", e, fn(_PROXY), tuple(reads), tuple(writes)))
            return None
        ex = [k for k in reads if k in PSUM_KEYS]
        if ex:
            writes = list(writes) + ex
        self._deps(e, reads, writes)
        ins = fn(self.eng[e])
        self.cnt[e] += 1
        ins.then_inc(self.sem[e], 1)
        tok = (e, self.sem[e], self.cnt[e])
        self._commit(tok, reads, writes)
        self.n_ops += 1
        return tok

    def dma(self, q, out, in_, reads=(), writes=(), slot=None):
        if self.rec is not None:
            self.rec.append(("dma", q, out, in_, tuple(reads), tuple(writes), slot))
            return None
        self._deps(q, reads, writes)
        slot = slot or (writes[0] if writes else reads[0])
        sname = f"d_{slot}"
        if sname not in self.dma_sems:
            self.dma_sems[sname] = [self.nc.alloc_semaphore(name=sname), 0]
        ent = self.dma_sems[sname]
        ent[1] += 16
        self.eng[q].dma_start(out=out, in_=in_).then_inc(ent[0], 16)
        tok = (sname, ent[0], ent[1])
        self._commit(tok, reads, writes)
        self.n_ops += 1
        return tok

    def finish(self, e="sp"):
        for k, t in list(self.last_w.items()):
            self._wait(e, t)


def bc(ap, shape):
    return ap.to_broadcast(list(shape))


def build(T, TB=256):
    nc = bass.Bass("TRN2", target_bir_lowering=False)
    s = Sched(nc)
    dt_in = lambda n, sh: nc.dram_tensor(n, list(sh), F32, kind="ExternalInput").ap()
    dt_out = lambda n, sh: nc.dram_tensor(n, list(sh), F32, kind="ExternalOutput").ap()
    xp = dt_in("xp", [T, D]); xs = dt_in("xs", [64, D])
    w_in = dt_in("w_in", [D, NCOL]); w_out = dt_in("w_out", [D, D])
    nw_d = dt_in("nw", [128, 8]); cw_d = dt_in("cw", [128, 12, 4])
    alog_d = dt_in("alog", [128, HA]); dtb_d = dt_in("dtb", [128, HA])
    gnw_d = dt_in("gnw", [128, 1]); hnw_d = dt_in("hnw", [128, 1])
    lbl_d = dt_in("lbl", [128, HB, 2]); fnw_d = dt_in("fnw", [128, D])
    sc_d = dt_in("sc", [128, 12, 4, 3])
    sg_d = dt_in("sg", [4, NP, 128, 64]); sh_d = dt_in("sh", [4, HB, 128, 128])
    yp = dt_out("yp", [T, D]); ys = dt_out("ys", [64, D])
    ncp = dt_out("ncp", [128, 12, 1, 3]); ngp = dt_out("ngp", [1, NP, 128, 64]); nhp = dt_out("nhp", [1, HB, 128, 128])
    ncs = dt_out("ncs", [128, 12, 4, 3]); ngs = dt_out("ngs", [4, NP, 128, 64]); nhs = dt_out("nhs", [4, HB, 128, 128])

    sb = lambda n, sh, d=F32: nc.alloc_sbuf_tensor(n, list(sh), d)
    Wb = sb("Wb", [128, 8, NCOL], BF16)
    WOb = sb("WOb", [128, 8, D], BF16)
    nw = sb("nw_t", [128, 8]); cw = sb("cw_t", [128, 12, 4])
    alog = sb("alog_t", [128, HA]); dtb = sb("dtb_t", [128, HA]); negA = sb("negA", [128, HA])
    gnw = sb("gnw_t", [128, 1]); hnw = sb("hnw_t", [128, 1])
    lbl = sb("lbl_t", [128, HB, 2]); lb = sb("lb", [128, HB]); oml = sb("oml", [128, HB])
    fnw = sb("fnw_t", [128, D])
    identb = sb("identb", [128, 128], BF16); identf = sb("identf", [128, 128])
    ones = sb("ones", [128, 128]); ob2 = sb("ob2", [128, 128])
    fgt = sb("fgt", [128, 128]); fle = sb("fle", [128, 128])
    I_s = sb("I_s", [128, 64]); U_s = sb("U_s", [128, 64]); Tri_s = sb("Tri_s", [128, 64]); Mc_s = sb("Mc_s", [128, 64])
    halo = sb("halo", [128, 12, 4, 3])
    Sg = sb("Sg", [128, NP, 64]); Sh = sb("Sh", [128, HB, 128])
    W_ = TB
    xt = [sb(f"xt{i}", [128, D]) for i in range(4)]
    sqj = sb("sqj", [128, D], BF16)
    xb = sb("xb", [128, D], BF16)
    hT = sb("hT", [128, 8, W_], BF16)
    ss = sb("ss", [128, 1]); rr = sb("rr", [128, 1])
    raw = sb("raw", [128, 3, W_ + 12])
    cv = sb("cv", [128, 3, W_])
    tmp = [None, sb("tmp1", [128, W_]), sb("tmp2", [128, W_]), None]
    za = [sb(f"za{i}", [128, W_]) for i in range(2)]
    cvb = [sb(f"cvb{i}", [128, 3, W_], BF16) for i in range(2)]
    tmpA = [sb(f"tmpA{i}", [128, W_]) for i in range(3)]
    sqA = [sb(f"sqA{i}", [128, W_], BF16) for i in range(2)]
    tnA = [sb(f"tnA{i}", [128, W_]) for i in range(2)]
    I_sb = sb("I_sb", [128, 64], BF16); ob2b = sb("ob2b", [128, 128], BF16); onesb = sb("onesb", [128, 128], BF16)
    Sgb = sb("Sgb", [128, NP, 64], BF16); Shb = sb("Shb", [128, HB, 128], BF16)
    sqb = sb("sqb", [128, W_], BF16); sqbH = sb("sqbH", [128, W_], BF16)
    G = sb("G", [128, 4, 16]); Gb = sb("Gb", [128, 4, HA]); Gg = sb("Gg", [128, 4, HA])
    gs = sb("gs", [128, NP, 4]); bs = sb("bs", [128, NP, 4]); nbs = sb("nbs", [128, NP, 4])
    gc = sb("gc", [128, NP, 4]); gl = sb("gl", [128, NP, 4]); egc = sb("egc", [128, NP, 4])
    dk = sb("dk", [128, NP, 4]); bge = sb("bge", [128, NP, 4])
    rhsD = sb("rhsD", [128, W_]); Dg = sb("Dg", [128, W_])
    Ee = sb("Ee", [128, W_]); Dm = sb("Dm", [128, W_]); Ds = sb("Ds", [128, W_])
    EBs = [sb(f"EBs{i}", [128, W_]) for i in range(2)]
    P0t = [sb(f"P0t{i}", [128, W_], BF16) for i in range(2)]
    PT0t = [sb(f"PT0t{i}", [128, W_], BF16) for i in range(2)]
    R0t = [sb(f"R0t{i}", [128, W_], BF16) for i in range(2)]
    P = [sb(f"P{i}", [128, W_], BF16) for i in range(2)]
    PT = [sb(f"PT{i}", [128, W_], BF16) for i in range(2)]
    R = [sb(f"R{i}", [128, W_], BF16) for i in range(2)]
    attn = sb("attn", [128, W_], BF16); attnT = [sb(f"attnT{i}", [128, W_], BF16) for i in range(2)]
    Kbe = [sb(f"Kbe{i}", [128, 4, 64], BF16) for i in range(2)]; Kd = [sb(f"Kd{i}", [128, 4, 64], BF16) for i in range(2)]; bV = [sb(f"bV{i}", [128, 4, 64], BF16) for i in range(2)]
    u = sb("u", [128, 4, 64]); wT = sb("wT", [128, W_], BF16); QeT = [sb(f"QeT{i}", [128, W_], BF16) for i in range(2)]
    vn = sb("vn", [128, 64], BF16)
    oTf = sb("oTf", [128, W_])
    oTn = [sb(f"oTn_{i}", [128, 8, W_], BF16) for i in range(2)]
    qb = sb("qb", [128, W_]); ff = sb("ff", [128, W_]); lf = sb("lf", [128, W_]); kb = sb("kb", [128, W_])
    bb = sb("bb", [128, W_]); bl = sb("bl", [128, W_])
    Qe = sb("Qe", [128, W_], BF16); Qx = sb("Qx", [128, W_], BF16); Kdh = sb("Kdh", [128, W_], BF16)
    Qef = sb("Qef", [128, W_]); Qxf = sb("Qxf", [128, W_]); Kdf = sb("Kdf", [128, W_])
    ebl = sb("ebl", [128, 4]); zb = sb("zb", [128, W_])
    vtok = sb("vtok", [64, 4, 128], BF16); Kdt = sb("Kdt", [64, 4, 128], BF16); aTh = sb("aTh", [64, W_], BF16)
    smask = sb("smask", [128, W_])
    tmpH = [None, sb("tmpH1", [128, W_])]; oTfH = sb("oTfH", [128, W_])
    yo = sb("yo", [128, D]); yo2 = sb("yo2", [128, D])
    pb = [nc.alloc_psum_tensor(f"pb{i}", [128, 512], F32) for i in range(8)]
    pT = pb[7][:, :].bitcast(BF16).rearrange("p (k t) -> p k t", t=128)

    def aff(out, cmp, fill_in, step=-1, cm=1, base=0):
        s.op("pool", lambda e: e.memset(out[:], fill_in), writes=[out.name])
        s.op("pool", lambda e: e.affine_select(out=out[:], in_=out[:], pattern=[[step, 128]], compare_op=cmp,
                                               fill=0.0, base=base, channel_multiplier=cm), reads=[out.name], writes=[out.name])
    aff(identf, ALU.is_equal, 1.0)
    aff(fgt, ALU.is_gt, 1.0)
    aff(fle, ALU.is_gt, 1.0, step=1, cm=-1, base=1)
    s.op("pool", lambda e: e.memset(ones[:], 1.0), writes=["ones"])
    s.op("pool", lambda e: e.memset(ob2[:], 0.0), writes=["ob2"])
    for h in range(2):
        sl = slice(64 * h, 64 * h + 64)
        s.op("pool", lambda e: e.memset(ob2[sl, sl], 1.0), reads=["ob2"], writes=["ob2"])
    s.op("dve", lambda e: e.tensor_copy(out=identb[:], in_=identf[:]), reads=["identf"], writes=["identb"])
    for (dst, src) in ((I_s, identf), (U_s, fgt), (Tri_s, fle)):
        for h in range(2):
            sl = slice(64 * h, 64 * h + 64)
            s.op("dve", lambda e: e.tensor_copy(out=dst[sl, :], in_=src[sl, sl]), reads=[src.name], writes=[dst.name])
    s.op("dve", lambda e: e.tensor_tensor(out=Mc_s[:], in0=U_s[:], in1=I_s[:], op=ALU.add), reads=["U_s", "I_s"], writes=["Mc_s"])
    s.op("dve", lambda e: e.tensor_copy(out=I_sb[:], in_=I_s[:]), reads=["I_s"], writes=["I_sb"])
    s.op("dve", lambda e: e.tensor_copy(out=ob2b[:], in_=ob2[:]), reads=["ob2"], writes=["ob2b"])
    s.op("dve", lambda e: e.tensor_copy(out=onesb[:], in_=ones[:]), reads=["ones"], writes=["onesb"])
    for t_, d_ in ((nw, nw_d), (cw, cw_d), (alog, alog_d), (dtb, dtb_d), (gnw, gnw_d), (hnw, hnw_d), (lbl, lbl_d), (fnw, fnw_d)):
        s.dma("sp", t_[:], d_, writes=[t_.name])
    s.op("act", lambda e: e.activation(out=negA[:], in_=alog[:], func=AF.Exp), reads=["alog_t"], writes=["negA"])
    s.op("dve", lambda e: e.tensor_scalar(out=negA[:], in0=negA[:], scalar1=-1.0, scalar2=None, op0=ALU.mult), reads=["negA"], writes=["negA"])
    s.op("dve", lambda e: e.tensor_tensor(out=lb[:], in0=lbl[:, :, 1], in1=lbl[:, :, 0], op=ALU.subtract), reads=["lbl_t"], writes=["lb"])
    s.op("act", lambda e: e.activation(out=lb[:], in_=lb[:], func=AF.Exp), reads=["lb"], writes=["lb"])
    s.op("dve", lambda e: e.tensor_scalar(out=lb[:], in0=lb[:], scalar1=1.0, scalar2=None, op0=ALU.add), reads=["lb"], writes=["lb"])
    s.op("dve", lambda e: e.reciprocal(out=lb[:], in_=lb[:]), reads=["lb"], writes=["lb"])
    s.op("dve", lambda e: e.tensor_scalar(out=oml[:], in0=lb[:], scalar1=-1.0, scalar2=1.0, op0=ALU.mult, op1=ALU.add), reads=["lb"], writes=["oml"])
    w_in_v = w_in.rearrange("(k p) n -> p k n", p=128)
    stgs = [(xt[0], "xt0"), (xt[1], "xt1"), (yo, "yo"), (yo2, "yo2")]
    q = 0
    for k in range(8):
        pieces = [(i * 1024, 1024, stgs[i][0], stgs[i][1]) for i in range(4)] + [(4096, 16, Ee, "Ee")]
        for (c0, cn, tl, key) in pieces:
            s.dma("sp", tl[:, 0:cn], w_in_v[:, k, c0:c0 + cn], writes=[key])
            if q % 2 == 0:
                s.op("dve", lambda e: e.tensor_scalar(out=Wb[:, k, c0:c0 + cn], in0=tl[:, 0:cn], scalar1=nw[:, k:k + 1], scalar2=None, op0=ALU.mult),
                     reads=[key, "nw_t"], writes=["Wb"])
            else:
                s.op("act", lambda e: e.activation(out=Wb[:, k, c0:c0 + cn], in_=tl[:, 0:cn], func=AF.Copy, scale=nw[:, k:k + 1]),
                     reads=[key, "nw_t"], writes=["Wb"])
            q += 1
    w_out_v = w_out.rearrange("(k p) n -> p k n", p=128)
    for k in range(8):
        tl, key = stgs[k % 4]
        s.dma("sp", tl[:, :], w_out_v[:, k, :], writes=[key])
        if k % 2 == 0:
            s.op("dve", lambda e: e.tensor_copy(out=WOb[:, k, :], in_=tl[:, :]), reads=[key], writes=["WOb"])
        else:
            s.op("act", lambda e: e.activation(out=WOb[:, k, :], in_=tl[:, :], func=AF.Copy), reads=[key], writes=["WOb"])

    def block(x_src, y_dst, t0, nseg, seglen, c, first, is_sample, bpar=0):
        TBk = nseg * seglen
        nch = TBk // c
        cps = seglen // c
        TT = min(128, TBk)
        ntt = TBk // TT
        nlev = {64: 5, 16: 3}[c]
        v3 = lambda ap: ap.rearrange("p (n c) -> p n c", c=c)

        def ph0_x():
            for tt in range(ntt):
                X = xt[bpar * 2 + tt]
                s.dma("sp", X[0:TT, :], x_src[t0 + tt * TT: t0 + (tt + 1) * TT, :], writes=[X.name])
                s.op("act", lambda e: e.activation(out=sqj[0:TT, :], in_=X[0:TT, :], func=AF.Square, accum_out=ss[0:TT, :]),
                     reads=[X.name], writes=["sqj", "ss"])
                s.op("act", lambda e: e.activation(out=rr[0:TT, :], in_=ss[0:TT, :], func=AF.Ln, scale=1.0 / D, bias=EPS), reads=["ss"], writes=["rr"])
                s.op("act", lambda e: e.activation(out=rr[0:TT, :], in_=rr[0:TT, :], func=AF.Exp, scale=-0.5), reads=["rr"], writes=["rr"])
                s.op("dve", lambda e: e.tensor_scalar(out=xb[0:TT, :], in0=X[0:TT, :], scalar1=rr[0:TT, :], scalar2=None, op0=ALU.mult),
                     reads=[X.name, "rr"], writes=["xb"])
                for k in range(8):
                    s.op("pe", lambda e: e.transpose(out=pT[:, k, 0:TT], in_=xb[0:TT, k * 128:(k + 1) * 128], identity=identb[0:TT, 0:TT]),
                         reads=["xb", "identb"], writes=["BT"])
                s.op("dve", lambda e: e.tensor_copy(out=hT[:, :, tt * TT:(tt + 1) * TT], in_=pT[:, :, 0:TT]), reads=["BT"], writes=["hT"])

        def silu_from(src, srckeys, dst, dstkey, scr, scrkey, W, outdt_note=None):
            s.op("act", lambda e: e.activation(out=scr, in_=src, func=AF.Exp, scale=-1.0), reads=srckeys, writes=[scrkey])
            s.op("act", lambda e: e.activation(out=scr, in_=scr, func=AF.Ln, bias=1.0), reads=[scrkey], writes=[scrkey])
            s.op("act", lambda e: e.activation(out=scr, in_=scr, func=AF.Exp, scale=-1.0), reads=[scrkey], writes=[scrkey])
            s.op("dve", lambda e: e.tensor_tensor(out=dst, in0=src, in1=scr, op=ALU.mult), reads=list(srckeys) + [scrkey], writes=[dstkey])


        def inproj_fm(ct, i=0):
            key = "B0"
            out = pb[0][:, i * 256: i * 256 + TBk]
            for k in range(8):
                s.op("pe", lambda e: e.matmul(out, lhsT=Wb[:, k, ct * 128:(ct + 1) * 128], rhs=hT[:, k, 0:TBk], start=(k == 0), stop=(k == 7)),
                     reads=["Wb", "hT"], writes=[key])
            return out, key

        def ph0_g():
            for ch in range(nch):
                for h in range(2):
                    out = pb[2][64 * h:64 * h + c, ch * 16:(ch + 1) * 16]
                    for k in range(8):
                        s.op("pe", lambda e: e.matmul(out, lhsT=hT[:, k, ch * c:(ch + 1) * c], rhs=Wb[:, k, C_G:C_G + 16], start=(k == 0), stop=(k == 7)),
                             reads=["Wb", "hT"], writes=["B2"])
            Gv = G[:, 0:nch, :]
            s.op("dve", lambda e: e.tensor_copy(out=Gv, in_=pb[2][:, 0:nch * 16].rearrange("p (n g) -> p n g", g=16)), reads=["B2"], writes=["G"])
            Gbv = Gb[:, 0:nch, :]; Ggv = Gg[:, 0:nch, :]
            s.op("act", lambda e: e.activation(out=Gbv, in_=Gv[:, :, 0:HA], func=AF.Exp, scale=-1.0), reads=["G"], writes=["Gb"])
            s.op("act", lambda e: e.activation(out=Gbv, in_=Gbv, func=AF.Ln, bias=1.0), reads=["Gb"], writes=["Gb"])
            s.op("act", lambda e: e.activation(out=Gbv, in_=Gbv, func=AF.Exp, scale=-1.0), reads=["Gb"], writes=["Gb"])
            s.op("dve", lambda e: e.tensor_tensor(out=Ggv, in0=Gv[:, :, HA:2 * HA], in1=bc(dtb[:, None, :], [128, nch, HA]), op=ALU.add),
                 reads=["G", "dtb_t"], writes=["Gg"])
            s.op("act", lambda e: e.activation(out=Ggv, in_=Ggv, func=AF.Exp), reads=["Gg"], writes=["Gg"])
            s.op("act", lambda e: e.activation(out=Ggv, in_=Ggv, func=AF.Ln, bias=1.0), reads=["Gg"], writes=["Gg"])
            s.op("dve", lambda e: e.tensor_tensor(out=Ggv, in0=Ggv, in1=bc(negA[:, None, :], [128, nch, HA]), op=ALU.mult),
                 reads=["Gg", "negA"], writes=["Gg"])
            gsv = gs[:, :, 0:nch]; bsv = bs[:, :, 0:nch]; nbsv = nbs[:, :, 0:nch]
            gcv = gc[:, :, 0:nch]; glv = gl[:, :, 0:nch]; egcv = egc[:, :, 0:nch]; dkv = dk[:, :, 0:nch]; bgev = bge[:, :, 0:nch]
            for h in range(2):
                sl = slice(64 * h, 64 * h + 64)
                for (dst, src, kd, ks) in ((gs, Gg, "gs", "Gg"), (bs, Gb, "bs", "Gb")):
                    for p in range(NP):
                        s.op("dve", lambda e: e.tensor_copy(out=dst[sl, p, 0:nch], in_=src[sl, 0:nch, 2 * p + h]), reads=[ks], writes=[kd])
            s.op("dve", lambda e: e.tensor_scalar(out=nbsv, in0=bsv, scalar1=-1.0, scalar2=None, op0=ALU.mult), reads=["bs"], writes=["nbs"])
            for h in range(2):
                rs = slice(64 * h, 64 * h + c)
                s.op("pe", lambda e: e.matmul(pb[2][rs, 64:64 + NP * nch], lhsT=Tri_s[rs, 0:c], rhs=gs[rs, :, 0:nch], start=True, stop=True),
                     reads=["Tri_s", "gs"], writes=["B2"])
                s.op("pe", lambda e: e.matmul(pb[2][rs, 96:96 + NP * nch], lhsT=ones[rs, 0:c], rhs=gs[rs, :, 0:nch], start=True, stop=True),
                     reads=["ones", "gs"], writes=["B2"])
            s.op("dve", lambda e: e.tensor_copy(out=gcv, in_=pb[2][:, 64:64 + NP * nch].rearrange("p (a n) -> p a n", n=nch)), reads=["B2"], writes=["gc"])
            s.op("dve", lambda e: e.tensor_copy(out=glv, in_=pb[2][:, 96:96 + NP * nch].rearrange("p (a n) -> p a n", n=nch)), reads=["B2"], writes=["gl"])
            s.op("act", lambda e: e.activation(out=egcv, in_=gcv, func=AF.Exp), reads=["gc"], writes=["egc"])
            s.op("dve", lambda e: e.tensor_tensor(out=dkv, in0=glv, in1=gcv, op=ALU.subtract), reads=["gl", "gc"], writes=["dk"])
            s.op("act", lambda e: e.activation(out=dkv, in_=dkv, func=AF.Exp), reads=["dk"], writes=["dk"])
            s.op("dve", lambda e: e.tensor_tensor(out=bgev, in0=bsv, in1=egcv, op=ALU.mult), reads=["bs", "egc"], writes=["bge"])


        def phase0(_=None):
            ph0_x()
            ph0_g()
            ph0_m()

        def front(p):
            par = p % 2
            for i3 in range(3):
                ct = 4 * i3 + p
                pp, pk = inproj_fm(ct)
                rv = raw[:, i3, 0:nseg * (seglen + 3)].rearrange("p (n c) -> p n c", c=seglen + 3)
                if first and not is_sample:
                    s.op("pool", lambda e: e.memset(rv[:, :, 0:3], 0.0), reads=[f"raw{i3}"], writes=[f"raw{i3}"])
                else:
                    s.op("pool", lambda e: e.tensor_copy(out=rv[:, :, 0:3], in_=halo[:, ct, 0:nseg, :]), reads=["halo"], writes=[f"raw{i3}"])
                s.op("act", lambda e: e.activation(out=rv[:, :, 3:3 + seglen], in_=pp.rearrange("p (n c) -> p n c", c=seglen), func=AF.Copy),
                     reads=[pk], writes=[f"raw{i3}"])
                s.op("pool", lambda e: e.tensor_copy(out=halo[:, ct, 0:nseg, :], in_=rv[:, :, seglen:seglen + 3]), reads=[f"raw{i3}"], writes=["halo"])
                cvv = cv[:, i3, 0:TBk].rearrange("p (n c) -> p n c", c=seglen)
                s.op("dve", lambda e: e.tensor_scalar(out=cvv, in0=rv[:, :, 0:seglen], scalar1=cw[:, ct, 0:1], scalar2=None, op0=ALU.mult),
                     reads=[f"raw{i3}", "cw_t"], writes=[f"cv{i3}"])
                for j in range(1, 4):
                    s.op("dve", lambda e: e.scalar_tensor_tensor(out=cvv, in0=rv[:, :, j:j + seglen], scalar=cw[:, ct, j:j + 1], in1=cvv,
                                                                 op0=ALU.mult, op1=ALU.add), reads=[f"raw{i3}", "cw_t", f"cv{i3}"], writes=[f"cv{i3}"])
                if i3 < 2:
                    silu_from(cv[:, i3, 0:TBk], [f"cv{i3}"], cv[:, i3, 0:TBk], f"cv{i3}", tmpA[i3][:, 0:TBk], f"tmpA{i3}", TBk)
                else:
                    silu_from(cv[:, i3, 0:TBk], [f"cv{i3}"], cvb[par][:, 2, 0:TBk], f"cvb{par}v", tmpA[i3][:, 0:TBk], f"tmpA{i3}", TBk)
            for i3 in range(2):
                src = cv[:, i3, 0:TBk]
                s.op("pool", lambda e: e.tensor_tensor(out=sqA[i3][:, 0:TBk], in0=src, in1=src, op=ALU.mult), reads=[f"cv{i3}"], writes=[f"sqA{i3}"])
                s.op("pe", lambda e: e.matmul(pb[7][:, i3 * 256:i3 * 256 + TBk], lhsT=ob2b[:], rhs=sqA[i3][:, 0:TBk], start=True, stop=True), reads=["ob2b", f"sqA{i3}"], writes=["BT"])
                s.op("act", lambda e: e.activation(out=tnA[i3][:, 0:TBk], in_=pb[7][:, i3 * 256:i3 * 256 + TBk], func=AF.Ln, bias=EPS), reads=["BT"], writes=[f"tnA{i3}"])
                s.op("act", lambda e: e.activation(out=tnA[i3][:, 0:TBk], in_=tnA[i3][:, 0:TBk], func=AF.Exp, scale=-0.5), reads=[f"tnA{i3}"], writes=[f"tnA{i3}"])
                if i3 == 0:
                    s.op("dve", lambda e: e.scalar_tensor_tensor(out=cvb[par][:, 0, 0:TBk], in0=src, scalar=0.125, in1=tnA[i3][:, 0:TBk], op0=ALU.mult, op1=ALU.mult),
                         reads=[f"cv{i3}", f"tnA{i3}"], writes=[f"cvb{par}q"])
                else:
                    s.op("dve", lambda e: e.tensor_tensor(out=cvb[par][:, 1, 0:TBk], in0=src, in1=tnA[i3][:, 0:TBk], op=ALU.mult), reads=[f"cv{i3}", f"tnA{i3}"], writes=[f"cvb{par}k"])
            pp, pk = inproj_fm(12 + p)
            s.op("act", lambda e: e.activation(out=za[par][:, 0:TBk], in_=pp, func=AF.Copy), reads=[pk], writes=[f"za{par}"])
            silu_from(za[par][:, 0:TBk], [f"za{par}"], za[par][:, 0:TBk], f"za{par}", tmpA[0][:, 0:TBk], "tmpA0", TBk)
            qn = cvb[par][:, 0, 0:TBk]; kn = cvb[par][:, 1, 0:TBk]; vs = cvb[par][:, 2, 0:TBk]
            ckq, ckk, ckv = f"cvb{par}q", f"cvb{par}k", f"cvb{par}v"
            s.op("dve", lambda e: e.tensor_tensor(out=v3(rhsD[:, 0:TBk]), in0=bc(gs[:, p, 0:nch, None], [128, nch, c]), in1=bc(U_s[:, None, 0:c], [128, nch, c]), op=ALU.mult),
                 reads=["gs", "U_s"], writes=["rhsD"])
            s.op("pool", lambda e: e.tensor_tensor(out=v3(Dg[:, 0:TBk]), in0=bc(egc[:, p, 0:nch, None], [128, nch, c]), in1=bc(I_s[:, None, 0:c], [128, nch, c]), op=ALU.mult),
                 reads=["egc", "I_s"], writes=["Dg"])
            for h in range(2):
                rs = slice(64 * h, 64 * h + c)
                s.op("pe", lambda e: e.matmul(pb[6][rs, 0:TBk], lhsT=Tri_s[rs, 0:c], rhs=rhsD[rs, 0:TBk], start=True, stop=True),
                     reads=["Tri_s", "rhsD"], writes=["B6"])
            s.op("act", lambda e: e.activation(out=Ee[:, 0:TBk], in_=pb[6][:, 0:TBk], func=AF.Exp), reads=["B6"], writes=["Ee"])
            s.op("pool", lambda e: e.tensor_tensor(out=v3(Dm[:, 0:TBk]), in0=v3(Ee[:, 0:TBk]), in1=bc(Mc_s[:, None, 0:c], [128, nch, c]), op=ALU.mult),
                 reads=["Ee", "Mc_s"], writes=["Dm"])
            s.op("pool", lambda e: e.tensor_tensor(out=v3(Ds[:, 0:TBk]), in0=v3(Ee[:, 0:TBk]), in1=bc(U_s[:, None, 0:c], [128, nch, c]), op=ALU.mult),
                 reads=["Ee", "U_s"], writes=["Ds"])
            for h in range(2):
                rs = slice(64 * h, 64 * h + c)
                s.op("pe", lambda e: e.matmul(pb[6][64 * h:64 * h + 64, 0:TBk], lhsT=ones[rs, 0:64], rhs=Dg[rs, 0:TBk], start=True, stop=True),
                     reads=["ones", "Dg"], writes=["B6"])
            s.op("act", lambda e: e.activation(out=EBs[par][:, 0:TBk], in_=pb[6][:, 0:TBk], func=AF.Copy), reads=["B6"], writes=[f"EBs{par}"])
            s.op("dve", lambda e: e.tensor_tensor(out=QeT[par][:, 0:TBk], in0=qn, in1=EBs[par][:, 0:TBk], op=ALU.mult), reads=[ckq, f"EBs{par}"], writes=[f"QeT{par}"])
            for ch in range(nch):
                cs = slice(ch * c, (ch + 1) * c)
                for h in range(2):
                    fs = slice(64 * h, 64 * h + 64); rs = slice(64 * h, 64 * h + c)
                    s.op("pe", lambda e: e.matmul(pb[7][rs, cs], lhsT=kn[fs, cs], rhs=kn[fs, cs], start=True, stop=True), reads=[ckk], writes=["BT"])
                    s.op("pe", lambda e: e.matmul(pb[7][rs, 256 + ch * c:256 + (ch + 1) * c], lhsT=qn[fs, cs], rhs=kn[fs, cs], start=True, stop=True),
                         reads=[ckq, ckk], writes=["BT"])
            s.op("dve", lambda e: e.tensor_tensor(out=v3(tmp[2][:, 0:TBk]), in0=v3(pb[7][:, 0:TBk]), in1=bc(nbs[:, p, 0:nch, None], [128, nch, c]), op=ALU.mult),
                 reads=["BT", "nbs"], writes=["tmp2"])
            s.op("pool", lambda e: e.tensor_tensor(out=PT0t[par][:, 0:TBk], in0=tmp[2][:, 0:TBk], in1=Ds[:, 0:TBk], op=ALU.mult), reads=["tmp2", "Ds"], writes=[f"PT0t{par}"])
            s.op("dve", lambda e: e.tensor_tensor(out=attn[:, 0:TBk], in0=pb[7][:, 256:256 + TBk], in1=Dm[:, 0:TBk], op=ALU.mult), reads=["BT", "Dm"], writes=["attn"])

            for ch in range(nch):
                cs = slice(ch * c, (ch + 1) * c)
                for h in range(2):
                    fs = slice(64 * h, 64 * h + 64); rs = slice(64 * h, 64 * h + c)
                    s.op("pe", lambda e: e.matmul(pb[6][rs, cs], lhsT=PT0t[par][rs, cs], rhs=I_sb[rs, 0:c], start=True, stop=True), reads=[f"PT0t{par}", "I_sb"], writes=["B6"])
                    s.op("pe", lambda e: e.matmul(pb[6][rs, 256 + ch * c:256 + (ch + 1) * c], lhsT=attn[rs, cs], rhs=I_sb[rs, 0:c], start=True, stop=True),
                         reads=["attn", "I_sb"], writes=["B6"])
                    s.op("pe", lambda e: e.matmul(pb[7][rs, ch * 64:(ch + 1) * 64], lhsT=kn[fs, cs], rhs=I_sb[fs, 0:64], start=True, stop=True), reads=[ckk, ckv, "I_sb"], writes=["BT"])
                    s.op("pe", lambda e: e.matmul(pb[7][rs, 256 + ch * 64:256 + (ch + 1) * 64], lhsT=vs[fs, cs], rhs=I_sb[fs, 0:64], start=True, stop=True),
                         reads=[ckk, ckv, "I_sb"], writes=["BT"])
            s.op("act", lambda e: e.activation(out=P0t[par][:, 0:TBk], in_=pb[6][:, 0:TBk], func=AF.Copy), reads=["B6"], writes=[f"P0t{par}"])
            s.op("dve", lambda e: e.tensor_tensor(out=v3(R0t[par][:, 0:TBk]), in0=v3(pb[6][:, 0:TBk]), in1=bc(I_s[:, None, 0:c], [128, nch, c]), op=ALU.add),
                 reads=["B6", "I_s"], writes=[f"R0t{par}"])
            s.op("act", lambda e: e.activation(out=attnT[par][:, 0:TBk], in_=pb[6][:, 256:256 + TBk], func=AF.Copy), reads=["B6"], writes=[f"attnT{par}"])
            k4 = pb[7][:, 0:nch * 64].rearrange("p (n d) -> p n d", d=64)
            v4 = pb[7][:, 256:256 + nch * 64].rearrange("p (n d) -> p n d", d=64)
            s.op("dve", lambda e: e.tensor_tensor(out=Kbe[par][:, 0:nch, :], in0=k4, in1=bc(bge[:, p, 0:nch, None], [128, nch, 64]), op=ALU.mult), reads=["BT", "bge"], writes=[f"Kbe{par}"])
            s.op("dve", lambda e: e.tensor_tensor(out=Kd[par][:, 0:nch, :], in0=k4, in1=bc(dk[:, p, 0:nch, None], [128, nch, 64]), op=ALU.mult), reads=["BT", "dk"], writes=[f"Kd{par}"])
            s.op("dve", lambda e: e.tensor_tensor(out=bV[par][:, 0:nch, :], in0=v4, in1=bc(bs[:, p, 0:nch, None], [128, nch, 64]), op=ALU.mult), reads=["BT", "bs"], writes=[f"bV{par}"])
        def core(p):
            par = p % 2
            Pc, kP = P0t[par], f"P0t{par}"
            PTc, kPT = PT0t[par], f"PT0t{par}"
            Rc, kR = R0t[par], f"R0t{par}"
            for lev in range(1, nlev + 2):
                do_pow = lev <= nlev
                need_P = lev < nlev
                do_R = lev >= 2
                nP, nkP = P[lev % 2], f"P{lev % 2}"
                nPT, nkPT = PT[lev % 2], f"PT{lev % 2}"
                nR, nkR = R[lev % 2], f"R{lev % 2}"
                for ch in range(nch):
                    cs = slice(ch * c, (ch + 1) * c)
                    for h in range(2):
                        rs = slice(64 * h, 64 * h + c)
                        if do_pow and need_P:
                            s.op("pe", lambda e: e.matmul(pb[5][rs, cs], lhsT=PTc[rs, cs], rhs=Pc[rs, cs], start=True, stop=True), reads=[kPT, kP], writes=["B5"])
                        if do_pow:
                            s.op("pe", lambda e: e.matmul(pb[5][rs, 256 + ch * c:256 + (ch + 1) * c], lhsT=Pc[rs, cs], rhs=PTc[rs, cs], start=True, stop=True),
                                 reads=[kPT, kP], writes=["B5"])
                        if do_R:
                            s.op("pe", lambda e: e.matmul(pb[4][rs, cs], lhsT=PTc[rs, cs], rhs=Rc[rs, cs], start=True, stop=True), reads=[kPT, kR], writes=["B4"])
                if do_pow and need_P:
                    s.op("act", lambda e: e.activation(out=nP[:, 0:TBk], in_=pb[5][:, 0:TBk], func=AF.Copy), reads=["B5"], writes=[nkP])
                if do_pow:
                    s.op("dve", lambda e: e.tensor_copy(out=nPT[:, 0:TBk], in_=pb[5][:, 256:256 + TBk]), reads=["B5"], writes=[nkPT])
                if do_R:
                    s.op("dve", lambda e: e.tensor_tensor(out=nR[:, 0:TBk], in0=pb[4][:, 0:TBk], in1=Rc[:, 0:TBk], op=ALU.add), reads=["B4", kR], writes=[nkR])
                    Rc, kR = nR, nkR
                if do_pow:
                    if need_P:
                        Pc, kP = nP, nkP
                    PTc, kPT = nPT, nkPT
            Rf, rk = Rc, kR
            for ch in range(nch):
                cs = slice(ch * c, (ch + 1) * c)
                for h in range(2):
                    rs = slice(64 * h, 64 * h + c)
                    s.op("pe", lambda e: e.matmul(pb[4][rs, 256 + ch * 64:256 + (ch + 1) * 64], lhsT=Rf[rs, cs], rhs=bV[par][rs, ch, :], start=True, stop=True),
                         reads=[rk, f"bV{par}"], writes=["B4"])
                    s.op("pe", lambda e: e.matmul(pb[5][64 * h:64 * h + 64, ch * c:(ch + 1) * c], lhsT=Kbe[par][rs, ch, :], rhs=Rf[rs, cs], start=True, stop=True),
                         reads=[rk, f"Kbe{par}"], writes=["B5"])
            s.op("act", lambda e: e.activation(out=u[:, 0:nch, :], in_=pb[4][:, 256:256 + nch * 64].rearrange("p (n d) -> p n d", d=64), func=AF.Copy), reads=["B4"], writes=["u"])
            s.op("dve", lambda e: e.tensor_copy(out=wT[:, 0:TBk], in_=pb[5][:, 0:TBk]), reads=["B5"], writes=["wT"])
            skey = f"Sg{p}"
            for ch in range(nch):
                cs = slice(ch * c, (ch + 1) * c)
                seg = ch // cps
                if ch % cps == 0:
                    if is_sample:
                        s.dma("sp", Sg[:, p, :], sg_d[seg, p], writes=[skey])
                        s.op("dve", lambda e: e.tensor_copy(out=Sgb[:, p, :], in_=Sg[:, p, :]), reads=[skey], writes=[skey + "b"])
                    elif first:
                        s.op("pool", lambda e: e.memset(Sg[:, p, :], 0.0), writes=[skey])
                        s.op("pool", lambda e: e.memset(Sgb[:, p, :], 0.0), writes=[skey + "b"])
                for h in range(2):
                    fs = slice(64 * h, 64 * h + 64); rs = slice(64 * h, 64 * h + c)
                    s.op("pe", lambda e: e.matmul(pb[1][rs, 0:64], lhsT=wT[fs, cs], rhs=Sgb[fs, p, :], start=True, stop=True), reads=["wT", skey + "b"], writes=["B1"])
                s.op("dve", lambda e: e.tensor_tensor(out=vn[:], in0=u[:, ch, :], in1=pb[1][:, 0:64], op=ALU.subtract), reads=["u", "B1"], writes=["vn"])
                for h in range(2):
                    fs = slice(64 * h, 64 * h + 64); rs = slice(64 * h, 64 * h + c)
                    s.op("pe", lambda e: e.matmul(pb[1][fs, 256 + ch * c:256 + (ch + 1) * c], lhsT=Sgb[fs, p, :], rhs=QeT[par][fs, cs], start=True, stop=False),
                         reads=[skey + "b", f"QeT{par}"], writes=["B1"])
                    s.op("pe", lambda e: e.matmul(pb[1][fs, 256 + ch * c:256 + (ch + 1) * c], lhsT=vn[rs, :], rhs=attnT[par][rs, cs], start=False, stop=True),
                         reads=["vn", f"attnT{par}"], writes=["B1"])
                for h in range(2):
                    fs = slice(64 * h, 64 * h + 64); rs = slice(64 * h, 64 * h + c)
                    s.op("pe", lambda e: e.matmul(pb[1][fs, 64:128], lhsT=Kd[par][rs, ch, :], rhs=vn[rs, :], start=True, stop=True), reads=[f"Kd{par}", "vn"], writes=["B1"])
                s.op("dve", lambda e: e.scalar_tensor_tensor(out=Sgb[:, p, :], in0=Sg[:, p, :], scalar=EBs[par][:, (ch + 1) * c - 1:(ch + 1) * c], in1=pb[1][:, 64:128],
                                                             op0=ALU.mult, op1=ALU.add), reads=[skey, f"EBs{par}", "B1"], writes=[skey + "b"])
                s.op("dve", lambda e: e.scalar_tensor_tensor(out=Sg[:, p, :], in0=Sg[:, p, :], scalar=EBs[par][:, (ch + 1) * c - 1:(ch + 1) * c], in1=pb[1][:, 64:128],
                                                             op0=ALU.mult, op1=ALU.add), reads=[skey, f"EBs{par}", "B1"], writes=[skey])
                if is_sample and (ch + 1) % cps == 0:
                    s.dma("sp", ngs[seg, p], Sg[:, p, :], reads=[skey], writes=[f"o_ngs{seg}_{p}"])
            s.op("act", lambda e: e.activation(out=oTf[:, 0:TBk], in_=pb[1][:, 256:256 + TBk], func=AF.Copy), reads=["B1"], writes=["oTf"])
            s.op("pool", lambda e: e.tensor_tensor(out=sqb[:, 0:TBk], in0=oTf[:, 0:TBk], in1=oTf[:, 0:TBk], op=ALU.mult), reads=["oTf"], writes=["sqb"])
            s.op("pe", lambda e: e.matmul(pb[3][:, 0:TBk], lhsT=ob2b[:], rhs=sqb[:, 0:TBk], start=True, stop=True), reads=["ob2b", "sqb"], writes=["B3"])
            s.op("act", lambda e: e.activation(out=tmp[1][:, 0:TBk], in_=pb[3][:, 0:TBk], func=AF.Ln, scale=1.0 / 64, bias=EPS), reads=["B3"], writes=["tmp1"])
            s.op("act", lambda e: e.activation(out=tmp[1][:, 0:TBk], in_=tmp[1][:, 0:TBk], func=AF.Exp, scale=-0.5), reads=["tmp1"], writes=["tmp1"])
            s.op("dve", lambda e: e.tensor_tensor(out=oTf[:, 0:TBk], in0=oTf[:, 0:TBk], in1=tmp[1][:, 0:TBk], op=ALU.mult), reads=["oTf", "tmp1"], writes=["oTf"])
            s.op("dve", lambda e: e.scalar_tensor_tensor(out=oTn[bpar][:, p, 0:TBk], in0=oTf[:, 0:TBk], scalar=gnw[:, 0:1], in1=za[par][:, 0:TBk], op0=ALU.mult, op1=ALU.mult),
                 reads=["oTf", "gnw_t", f"za{par}"], writes=[f"oTn{bpar}_{p}"])

        def ph0_m():
            s.op("pool", lambda e: e.memset(smask[:, 0:TBk], 1.0), writes=["smask"])
            s.op("pool", lambda e: e.memset(v3(smask[:, 0:TBk])[:, :, 0:1], 0.0), reads=["smask"], writes=["smask"])

        def hg(h):
            pp, pk = inproj_fm(16 + h, 1)
            s.op("act", lambda e: e.activation(out=qb[:, 0:TBk], in_=pp, func=AF.Copy), reads=[pk], writes=["qb"])
            silu_from(qb[:, 0:TBk], ["qb"], qb[:, 0:TBk], "qb", bl[:, 0:TBk], "bl", TBk)
            pp, pk = inproj_fm(24 + h, 1)
            s.op("act", lambda e: e.activation(out=zb[:, 0:TBk], in_=pp, func=AF.Copy), reads=[pk], writes=["zb"])
            silu_from(zb[:, 0:TBk], ["zb"], zb[:, 0:TBk], "zb", bl[:, 0:TBk], "bl", TBk)
            pp, pk = inproj_fm(20 + h, 1)
            s.op("act", lambda e: e.activation(out=ff[:, 0:TBk], in_=pp, func=AF.Exp, scale=-1.0), reads=[pk], writes=["ff"])
            s.op("act", lambda e: e.activation(out=ff[:, 0:TBk], in_=ff[:, 0:TBk], func=AF.Ln, bias=1.0), reads=["ff"], writes=["ff"])
            s.op("act", lambda e: e.activation(out=ff[:, 0:TBk], in_=ff[:, 0:TBk], func=AF.Exp, scale=-1.0), reads=["ff"], writes=["ff"])
            s.op("dve", lambda e: e.tensor_scalar(out=ff[:, 0:TBk], in0=ff[:, 0:TBk], scalar1=oml[:, h:h + 1], scalar2=lb[:, h:h + 1], op0=ALU.mult, op1=ALU.add),
                 reads=["ff", "oml", "lb"], writes=["ff"])
            s.op("act", lambda e: e.activation(out=lf[:, 0:TBk], in_=ff[:, 0:TBk], func=AF.Ln), reads=["ff"], writes=["lf"])
            s.op("dve", lambda e: e.tensor_scalar(out=kb[:, 0:TBk], in0=ff[:, 0:TBk], scalar1=-1.0, scalar2=1.0, op0=ALU.mult, op1=ALU.add), reads=["ff"], writes=["kb"])
            s.op("dve", lambda e: e.tensor_tensor_scan(out=bb[:, 0:TBk], data0=smask[:, 0:TBk], data1=lf[:, 0:TBk], initial=0.0, op0=ALU.mult, op1=ALU.add),
                 reads=["smask", "lf"], writes=["bb"])
            b3 = v3(bb[:, 0:TBk])
            s.op("pool", lambda e: e.tensor_tensor(out=v3(bl[:, 0:TBk]), in0=b3, in1=bc(b3[:, :, c - 1:c], [128, nch, c]), op=ALU.subtract), reads=["bb"], writes=["bl"])
            s.op("act", lambda e: e.activation(out=Qef[:, 0:TBk], in_=bb[:, 0:TBk], func=AF.Exp), reads=["bb"], writes=["Qef"])
            s.op("act", lambda e: e.activation(out=Qxf[:, 0:TBk], in_=bl[:, 0:TBk], func=AF.Exp), reads=["bl"], writes=["Qxf"])
            s.op("act", lambda e: e.activation(out=Kdf[:, 0:TBk], in_=bl[:, 0:TBk], func=AF.Exp, scale=-1.0), reads=["bl"], writes=["Kdf"])
            s.op("act", lambda e: e.activation(out=ebl[:, 0:nch], in_=b3[:, :, c - 1], func=AF.Exp), reads=["bb"], writes=["ebl"])
            s.op("dve", lambda e: e.tensor_tensor(out=Qe[:, 0:TBk], in0=Qef[:, 0:TBk], in1=qb[:, 0:TBk], op=ALU.mult), reads=["Qef", "qb"], writes=["Qe"])
            s.op("pool", lambda e: e.tensor_tensor(out=Qx[:, 0:TBk], in0=Qxf[:, 0:TBk], in1=qb[:, 0:TBk], op=ALU.mult), reads=["Qxf", "qb"], writes=["Qx"])
            s.op("dve", lambda e: e.tensor_tensor(out=Kdh[:, 0:TBk], in0=Kdf[:, 0:TBk], in1=kb[:, 0:TBk], op=ALU.mult), reads=["Kdf", "kb"], writes=["Kdh"])
            for ch in range(nch):
                cs = slice(ch * c, (ch + 1) * c)
                outp = pb[2][0:c, 128:256]
                for k in range(8):
                    s.op("pe", lambda e: e.matmul(outp, lhsT=hT[:, k, cs], rhs=Wb[:, k, C_HI + 128 * h:C_HI + 128 * (h + 1)], start=(k == 0), stop=(k == 7)),
                         reads=["Wb", "hT"], writes=["B2"])
                s.op("pe", lambda e: e.matmul(pb[2][0:c, 256:384], lhsT=Kdh[:, cs], rhs=identb[:], start=True, stop=True), reads=["Kdh", "identb"], writes=["B2"])
                s.op("pe", lambda e: e.matmul(pb[2][0:c, 384:384 + c], lhsT=Kdh[:, cs], rhs=Qx[:, cs], start=True, stop=True), reads=["Kdh", "Qx"], writes=["B2"])
                s.op("act", lambda e: e.activation(out=vtok[0:c, ch, :], in_=outp, func=AF.Copy), reads=["B2"], writes=["vtok"])
                s.op("dve", lambda e: e.tensor_copy(out=Kdt[0:c, ch, :], in_=pb[2][0:c, 256:384]), reads=["B2"], writes=["Kdt"])
                s.op("dve", lambda e: e.tensor_tensor(out=aTh[0:c, cs], in0=pb[2][0:c, 384:384 + c], in1=Tri_s[0:c, 0:c], op=ALU.mult),
                     reads=["B2", "Tri_s"], writes=["aTh"])
            skey = f"Sh{h}"
            for ch in range(nch):
                cs = slice(ch * c, (ch + 1) * c)
                seg = ch // cps
                if ch % cps == 0:
                    if is_sample:
                        s.dma("sp", Sh[:, h, :], sh_d[seg, h], writes=[skey])
                        s.op("dve", lambda e: e.tensor_copy(out=Shb[:, h, :], in_=Sh[:, h, :]), reads=[skey], writes=[skey + "b"])
                    elif first:
                        s.op("pool", lambda e: e.memset(Sh[:, h, :], 0.0), writes=[skey])
                        s.op("pool", lambda e: e.memset(Shb[:, h, :], 0.0), writes=[skey + "b"])
                s.op("pe", lambda e: e.matmul(pb[3][:, 256 + ch * c:256 + (ch + 1) * c], lhsT=Shb[:, h, :], rhs=Qe[:, cs], start=True, stop=False), reads=[skey + "b", "Qe"], writes=["B3"])
                s.op("pe", lambda e: e.matmul(pb[3][:, 256 + ch * c:256 + (ch + 1) * c], lhsT=vtok[0:c, ch, :], rhs=aTh[0:c, cs], start=False, stop=True),
                     reads=["vtok", "aTh"], writes=["B3"])
                s.op("pe", lambda e: e.matmul(pb[1][:, 128:256], lhsT=Kdt[0:c, ch, :], rhs=vtok[0:c, ch, :], start=True, stop=True), reads=["Kdt", "vtok"], writes=["B1"])
                s.op("dve", lambda e: e.scalar_tensor_tensor(out=Shb[:, h, :], in0=Sh[:, h, :], scalar=ebl[:, ch:ch + 1], in1=pb[1][:, 128:256], op0=ALU.mult, op1=ALU.add),
                     reads=[skey, "ebl", "B1"], writes=[skey + "b"])
                s.op("dve", lambda e: e.scalar_tensor_tensor(out=Sh[:, h, :], in0=Sh[:, h, :], scalar=ebl[:, ch:ch + 1], in1=pb[1][:, 128:256], op0=ALU.mult, op1=ALU.add),
                     reads=[skey, "ebl", "B1"], writes=[skey])
                if is_sample and (ch + 1) % cps == 0:
                    s.dma("sp", nhs[seg, h], Sh[:, h, :], reads=[skey], writes=[f"o_nhs{seg}_{h}"])
            s.op("act", lambda e: e.activation(out=oTfH[:, 0:TBk], in_=pb[3][:, 256:256 + TBk], func=AF.Copy), reads=["B3"], writes=["oTfH"])
            s.op("pool", lambda e: e.tensor_tensor(out=sqbH[:, 0:TBk], in0=oTfH[:, 0:TBk], in1=oTfH[:, 0:TBk], op=ALU.mult), reads=["oTfH"], writes=["sqbH"])
            s.op("pe", lambda e: e.matmul(pb[3][:, 256:256 + TBk], lhsT=onesb[:], rhs=sqbH[:, 0:TBk], start=True, stop=True), reads=["onesb", "sqbH"], writes=["B3"])
            s.op("act", lambda e: e.activation(out=tmpH[1][:, 0:TBk], in_=pb[3][:, 256:256 + TBk], func=AF.Ln, scale=1.0 / 128, bias=EPS), reads=["B3"], writes=["tmpH1"])
            s.op("act", lambda e: e.activation(out=tmpH[1][:, 0:TBk], in_=tmpH[1][:, 0:TBk], func=AF.Exp, scale=-0.5), reads=["tmpH1"], writes=["tmpH1"])
            s.op("dve", lambda e: e.tensor_tensor(out=oTfH[:, 0:TBk], in0=oTfH[:, 0:TBk], in1=tmpH[1][:, 0:TBk], op=ALU.mult), reads=["oTfH", "tmpH1"], writes=["oTfH"])
            s.op("dve", lambda e: e.scalar_tensor_tensor(out=oTn[bpar][:, 4 + h, 0:TBk], in0=oTfH[:, 0:TBk], scalar=hnw[:, 0:1], in1=zb[:, 0:TBk], op0=ALU.mult, op1=ALU.mult),
                 reads=["oTfH", "hnw_t", "zb"], writes=[f"oTn{bpar}_{4 + h}"])

        def outproj(_=None):
            for tt in range(ntt):
                X = xt[bpar * 2 + tt]
                for half in range(2):
                    bank = pb[4] if half == 0 else pb[5]
                    bk = "B4" if half == 0 else "B5"
                    for k in range(8):
                        s.op("pe", lambda e: e.matmul(bank[0:TT, :], lhsT=oTn[bpar][:, k, tt * TT:(tt + 1) * TT], rhs=WOb[:, k, half * 512:(half + 1) * 512], start=(k == 0), stop=(k == 7)),
                             reads=[f"oTn{bpar}_{k}", "WOb"], writes=[bk])
                    s.op("dve", lambda e: e.tensor_tensor(out=yo[0:TT, half * 512:(half + 1) * 512], in0=bank[0:TT, :], in1=X[0:TT, half * 512:(half + 1) * 512], op=ALU.add),
                         reads=[bk, X.name], writes=["yo"])
                s.op("act", lambda e: e.activation(out=sqj[0:TT, :], in_=yo[0:TT, :], func=AF.Square, accum_out=ss[0:TT, :]), reads=["yo"], writes=["sqj", "ss"])
                s.op("act", lambda e: e.activation(out=rr[0:TT, :], in_=ss[0:TT, :], func=AF.Ln, scale=1.0 / D, bias=EPS), reads=["ss"], writes=["rr"])
                s.op("act", lambda e: e.activation(out=rr[0:TT, :], in_=rr[0:TT, :], func=AF.Exp, scale=-0.5), reads=["rr"], writes=["rr"])
                s.op("dve", lambda e: e.scalar_tensor_tensor(out=yo2[0:TT, :], in0=yo[0:TT, :], scalar=rr[0:TT, :], in1=fnw[0:TT, :], op0=ALU.mult, op1=ALU.mult),
                     reads=["yo", "rr", "fnw_t"], writes=["yo2"])
                s.dma("sp", y_dst[t0 + tt * TT: t0 + (tt + 1) * TT, :], yo2[0:TT, :], reads=["yo2"], writes=[f"o_y{id(y_dst)}"], slot="yout")

        def record(fn, arg=None):
            lst = []
            s.rec = lst
            fn(arg)
            s.rec = None
            return lst
        return dict(p0=lambda: record(phase0), op=lambda: record(outproj),
                    fr=lambda p: record(front, p), co=lambda p: record(core, p), hg=lambda h: record(hg, h))

    def run_block(parts, prev_op, next_p0):
        st = [parts["fr"](0), parts["hg"](0)]
        if prev_op is not None:
            st.append(prev_op)
        s.merge_emit(st)
        for ph in range(NP):
            st = [parts["co"](ph)]
            if ph + 1 < NP:
                st += [parts["fr"](ph + 1), parts["hg"](ph + 1)]
            elif next_p0 is not None:
                st.append(next_p0())
            s.merge_emit(st)

    nblk = T // TB
    blks = [block(xp, yp, b * TB, 1, TB, 64, b == 0, False, b % 2) for b in range(nblk)]
    s.merge_emit([blks[0]["p0"]()])
    prev_op = None
    for b in range(nblk):
        nxt = blks[b + 1]["p0"] if b + 1 < nblk else None
        run_block(blks[b], prev_op, nxt)
        prev_op = blks[b]["op"]()
    s.merge_emit([prev_op])
    s.dma("sp", ncp[:, :, 0, :], halo[:, :, 0, :], reads=["halo"], writes=["o_ncp"])
    for p in range(NP):
        s.dma("sp", ngp[0, p], Sg[:, p, :], reads=[f"Sg{p}"], writes=[f"o_ngp{p}"])
    for h in range(HB):
        s.dma("sp", nhp[0, h], Sh[:, h, :], reads=[f"Sh{h}"], writes=[f"o_nhp{h}"])
    s.dma("sp", halo[:], sc_d, reads=["halo"], writes=["halo"])
    sblk = block(xs, ys, 0, 4, 16, 16, True, True, 0)
    s.merge_emit([sblk["p0"]()])
    run_block(sblk, None, None)
    s.merge_emit([sblk["op"]()])
    s.dma("sp", ncs, halo[:], reads=["halo"], writes=["o_ncs"])
    s.finish("sp")
    return nc, s


_PERM = np.concatenate([np.arange(0, 1536), np.arange(1536, 2048), np.arange(2064, 2576), np.arange(2576, 3088),
                        np.arange(3600, 4112), np.arange(3088, 3600), np.arange(2048, 2064)])


def _core_inputs(b, inp):
    f = lambda a: np.ascontiguousarray(a, dtype=np.float32)
    cwv = inp["conv_w"][0]
    scv = inp["state_conv"][0][4 * b:4 * b + 4]
    return {
        "xp": f(inp["x_prompt"][b]),
        "xs": f(inp["x_sample"][4 * b:4 * b + 4].reshape(64, D)),
        "w_in": f(inp["w_in"][0][:, _PERM]),
        "w_out": f(inp["w_out"][0]),
        "nw": f(inp["norm_w"][0].reshape(8, 128).T),
        "cw": f(cwv.reshape(4, 12, 128).transpose(2, 1, 0)),
        "alog": f(np.broadcast_to(inp["gdn_A_log"][0][None, :], (128, HA))),
        "dtb": f(np.broadcast_to(inp["gdn_dt_bias"][0][None, :], (128, HA))),
        "gnw": f(np.tile(inp["gdn_norm_w"][0], 2).reshape(128, 1)),
        "hnw": f(inp["hgrn_norm_w"][0].reshape(128, 1)),
        "lbl": f(inp["hgrn_lb_logits"].reshape(2, HB, 128).transpose(2, 1, 0)),
        "fnw": f(np.broadcast_to(inp["final_norm_w"][None, :], (128, D))),
        "sc": f(scv.reshape(4, 3, 12, 128).transpose(3, 2, 0, 1)),
        "sg": f(inp["state_gdn"][0][4 * b:4 * b + 4].reshape(4, NP, 128, 64)),
        "sh": f(inp["state_hgrn"][0][4 * b:4 * b + 4]),
    }


_CACHE = {}


def kernel(**inputs):
    inp = {k: np.asarray(v) for k, v in inputs.items()}
    Bp, T, _ = inp["x_prompt"].shape
    assert Bp == 4 and inp["x_sample"].shape[:2] == (16, 16)
    if T not in _CACHE:
        _CACHE[T] = build(T)[0]
    nc = _CACHE[T]
    in_maps = [_core_inputs(c % 4, inp) for c in range(8)]
    res = run_bass_kernel_spmd(nc, in_maps, core_ids=list(range(8)))
    r = res.results
    y_prompt = np.stack([r[b]["yp"] for b in range(4)]).astype(np.float32)
    y_sample = np.concatenate([r[b]["ys"].reshape(4, 16, D) for b in range(4)]).astype(np.float32)
    cvt = lambda a: a.transpose(2, 3, 1, 0).reshape(a.shape[2], 3, 1536)
    ncp_ = np.stack([cvt(r[b]["ncp"])[0] for b in range(4)])[None].astype(np.float32)
    ngp_ = np.stack([r[b]["ngp"].reshape(HA, 64, 64) for b in range(4)])[None].astype(np.float32)
    nhp_ = np.stack([r[b]["nhp"].reshape(HB, 128, 128) for b in range(4)])[None].astype(np.float32)
    ncs_ = np.concatenate([cvt(r[b]["ncs"]) for b in range(4)])[None].astype(np.float32)
    ngs_ = np.concatenate([r[b]["ngs"].reshape(4, HA, 64, 64) for b in range(4)])[None].astype(np.float32)
    nhs_ = np.concatenate([r[b]["nhs"].reshape(4, HB, 128, 128) for b in range(4)])[None].astype(np.float32)
    return (y_prompt, y_sample, ncp_, ngp_, nhp_, ncs_, ngs_, nhs_)
```

```python
import numpy as np
import concourse.bass as bass
import concourse.mybir as mybir
from concourse.bass_utils import run_bass_kernel_spmd

F32 = mybir.dt.float32
BF16 = mybir.dt.bfloat16
AF = mybir.ActivationFunctionType
ALU = mybir.AluOpType

D = 1024
HA, HB = 4, 2
NP = HA // 2
NT = 4 * NP + 3 * HB
C_HI = NT * 128
C_G = C_HI + HB * 128
NCOL = C_G + 2 * HA
EPS = 1e-6
RG = [[0, 1], [2, 3], [4, 5], [6, 7]]


SAME_ENGINE_WAIT = True


class _Proxy:
    def __getattr__(self, name):
        return lambda *a, **k: (name, a, k)


_PROXY = _Proxy()
PSUM_KEYS = {"B0", "B1", "B2", "B3", "B4", "B5", "B6", "BT"}


class Sched:
    def __init__(self, nc):
        self.nc = nc
        self.eng = {"pe": nc.tensor, "dve": nc.vector, "act": nc.scalar, "pool": nc.gpsimd, "sp": nc.sync}
        self.sem = {k: nc.alloc_semaphore(name=f"s_{k}") for k in self.eng}
        self.cnt = {k: 0 for k in self.eng}
        self.seen = {k: {} for k in self.eng}
        self.last_w = {}
        self.readers = {}
        self.dma_sems = {}
        self.n_wait = 0
        self.n_ops = 0
        self.rec = None

    def coll(self, ins, outs, reads=(), writes=()):
        if self.rec is not None:
            self.rec.append(("coll", "pool", ins, outs, tuple(reads), tuple(writes)))
            return None
        self._deps("pool", reads, writes)
        if "cc" not in self.dma_sems:
            self.dma_sems["cc"] = [self.nc.alloc_semaphore(name="cc_sem"), 0]
        ent = self.dma_sems["cc"]
        ent[1] += 1
        self.nc.gpsimd.collective_compute("AllGather", ALU.bypass, replica_groups=RG, ins=ins, outs=outs).then_inc(ent[0], 1)
        tok = ("cc", ent[0], ent[1])
        self._commit(tok, reads, writes)
        self.n_ops += 1
        return tok

    def emit(self, r):
        if r[0] == "coll":
            self.coll(r[2], r[3], r[4], r[5])
            return
        if r[0] == "op":
            _, e, call, reads, writes = r
            self.op(e, lambda eng: getattr(eng, call[0])(*call[1], **call[2]), reads, writes)
        else:
            _, q, out, in_, reads, writes, slot = r
            self.dma(q, out, in_, reads, writes, slot)

    def _cost(self, r):
        if r[0] == "dma":
            return 2500.0
        if r[0] == "coll":
            return 30000.0
        _, e, call, reads, writes = r
        name, args, kw = call
        def nfree(ap):
            try:
                sh = list(ap.shape)
                n = 1
                for d in sh[1:]:
                    n *= int(d)
                return n
            except Exception:
                return 256
        if e == "pe":
            ap = kw.get("rhs", None) if name == "matmul" else kw.get("in_", None)
            n = nfree(ap) if ap is not None else 64
            c = 45.0 + 0.45 * n
            try:
                if name == "matmul" and kw["rhs"].dtype == F32:
                    c *= 3.0
            except Exception:
                pass
            return c
        ap = kw.get("out", None)
        n = nfree(ap) if ap is not None else 256
        if e == "dve":
            return 70.0 + 1.0 * n
        if e == "act":
            return 130.0 + 0.9 * n
        return 110.0 + 1.8 * n

    def _est_start(self, r):
        if r[0] in ("dma", "coll"):
            e, reads, writes = r[1], r[4], r[5]
        else:
            e, reads, writes = r[1], r[3], r[4]
        m = self.model
        t = m["eng"].get(e, 0.0)
        ex = [k for k in reads if k in PSUM_KEYS]
        for k in reads:
            w = m["w"].get(k)
            if w is not None:
                t = max(t, w[0] + (0.0 if w[1] == e else 160.0))
        for k in list(writes) + ex:
            w = m["w"].get(k)
            if w is not None:
                t = max(t, w[0] + (0.0 if w[1] == e else 160.0))
            for (tt, ee) in m["r"].get(k, {}).values():
                t = max(t, tt + (0.0 if ee == e else 160.0))
        return t

    def _model_commit(self, r, t0):
        if r[0] in ("dma", "coll"):
            e, reads, writes = r[1], r[4], r[5]
            eng_busy = 100.0
        else:
            e, reads, writes = r[1], r[3], r[4]
            eng_busy = None
        m = self.model
        c = self._cost(r)
        t1 = t0 + c
        m["eng"][e] = t0 + (eng_busy if eng_busy is not None else c)
        ex = [k for k in reads if k in PSUM_KEYS]
        who = e if r[0] == "op" else "dma"
        for k in reads:
            m["r"].setdefault(k, {})[who] = (t1, who)
        for k in list(writes) + ex:
            m["w"][k] = (t1, who)
            m["r"][k] = {}
        m["t"] = max(m.get("t", 0.0), t1)

    def merge_emit(self, streams):
        if not hasattr(self, "model"):
            self.model = {"eng": {}, "w": {}, "r": {}, "t": 0.0}
        units = []
        for st in streams:
            u = []
            for r in st:
                glued = (r[0] == "op" and r[2][0] == "matmul" and r[2][2].get("start") is False)
                if glued and u:
                    u[-1].append(r)
                else:
                    u.append([r])
            units.append(u)
        pos = [0] * len(units)
        while True:
            best, bi = None, -1
            for i, u in enumerate(units):
                if pos[i] < len(u):
                    t = self._est_start(u[pos[i]][0])
                    key = (t, -(len(u) - pos[i]))
                    if best is None or key < best:
                        best, bi = key, i
            if bi < 0:
                break
            for r in units[bi][pos[bi]]:
                t0 = self._est_start(r)
                self._model_commit(r, t0)
                self.emit(r)
            pos[bi] += 1


    def _wait(self, e, tok):
        name, sem, val = tok
        if name == "pe" and e == "pe":
            return
        if name == e and not SAME_ENGINE_WAIT:
            return
        if self.seen[e].get(name, 0) >= val:
            return
        self.eng[e].wait_ge(sem, val)
        self.seen[e][name] = val
        self.n_wait += 1

    def _deps(self, e, reads, writes):
        for k in reads:
            t = self.last_w.get(k)
            if t is not None:
                self._wait(e, t)
        for k in writes:
            t = self.last_w.get(k)
            if t is not None:
                self._wait(e, t)
            for t in self.readers.get(k, {}).values():
                self._wait(e, t)

    def _commit(self, tok, reads, writes):
        for k in reads:
            self.readers.setdefault(k, {})[tok[0]] = tok
        for k in writes:
            self.last_w[k] = tok
            self.readers[k] = {}

    def op(self, e, fn, reads=(), writes=()):
        if self.rec is not None:
            self.rec.append(("op", e, fn(_PROXY), tuple(reads), tuple(writes)))
            return None
        ex = [k for k in reads if k in PSUM_KEYS]
        if ex:
            writes = list(writes) + ex
        self._deps(e, reads, writes)
        ins = fn(self.eng[e])
        self.cnt[e] += 1
        ins.then_inc(self.sem[e], 1)
        tok = (e, self.sem[e], self.cnt[e])
        self._commit(tok, reads, writes)
        self.n_ops += 1
        return tok

    def dma(self, q, out, in_, reads=(), writes=(), slot=None):
        if self.rec is not None:
            self.rec.append(("dma", q, out, in_, tuple(reads), tuple(writes), slot))
            return None
        self._deps(q, reads, writes)
        slot = slot or (writes[0] if writes else reads[0])
        sname = f"d_{slot}"
        if sname not in self.dma_sems:
            self.dma_sems[sname] = [self.nc.alloc_semaphore(name=sname), 0]
        ent = self.dma_sems[sname]
        ent[1] += 16
        self.eng[q].dma_start(out=out, in_=in_).then_inc(ent[0], 16)
        tok = (sname, ent[0], ent[1])
        self._commit(tok, reads, writes)
        self.n_ops += 1
        return tok

    def finish(self, e="sp"):
        for k, t in list(self.last_w.items()):
            self._wait(e, t)


def bc(ap, shape):
    return ap.to_broadcast(list(shape))


def build(T, TB=256):
    nc = bass.Bass("TRN2", target_bir_lowering=False)
    s = Sched(nc)
    dt_in = lambda n, sh: nc.dram_tensor(n, list(sh), F32, kind="ExternalInput").ap()
    dt_out = lambda n, sh: nc.dram_tensor(n, list(sh), F32, kind="ExternalOutput").ap()
    xp = dt_in("xp", [T, D]); xs = dt_in("xs", [64, D])
    w_in = dt_in("w_in", [D, NCOL]); w_out = dt_in("w_out", [D, D])
    nw_d = dt_in("nw", [128, 8]); cw_d = dt_in("cw", [128, 3 * NP, 4])
    alog_d = dt_in("alog", [128, HA]); dtb_d = dt_in("dtb", [128, HA])
    gnw_d = dt_in("gnw", [128, 1]); hnw_d = dt_in("hnw", [128, 1])
    lbl_d = dt_in("lbl", [128, HB, 2]); fnw_d = dt_in("fnw", [128, D])
    sc_d = dt_in("sc", [128, 3 * NP, 4, 3])
    sg_d = dt_in("sg", [4, NP, 128, 64]); sh_d = dt_in("sh", [4, HB, 128, 128])
    yp = dt_out("yp", [T, D]); ys = dt_out("ys", [64, D])
    ncp = dt_out("ncp", [128, 3 * NP, 1, 3]); ngp = dt_out("ngp", [1, NP, 128, 64]); nhp = dt_out("nhp", [1, HB, 128, 128])
    ncs = dt_out("ncs", [128, 3 * NP, 4, 3]); ngs = dt_out("ngs", [4, NP, 128, 64]); nhs = dt_out("nhs", [4, HB, 128, 128])

    NL = NP + HB
    xsrc = [nc.dram_tensor(f"xsrc{i}", [NL * 128, TB], BF16).ap() for i in range(2)]
    xdst = [nc.dram_tensor(f"xdst{i}", [2 * NL * 128, TB], BF16).ap() for i in range(2)]
    xsrc_s = nc.dram_tensor("xsrc_s", [NL * 128, 64], BF16).ap()
    xdst_s = nc.dram_tensor("xdst_s", [2 * NL * 128, 64], BF16).ap()
    sb = lambda n, sh, d=F32: nc.alloc_sbuf_tensor(n, list(sh), d)
    Wb = sb("Wb", [128, 8, NCOL], BF16)
    WOb = sb("WOb", [128, 8, D], BF16)
    nw = sb("nw_t", [128, 8]); cw = sb("cw_t", [128, 3 * NP, 4])
    alog = sb("alog_t", [128, HA]); dtb = sb("dtb_t", [128, HA]); negA = sb("negA", [128, HA])
    gnw = sb("gnw_t", [128, 1]); hnw = sb("hnw_t", [128, 1])
    lbl = sb("lbl_t", [128, HB, 2]); lb = sb("lb", [128, HB]); oml = sb("oml", [128, HB])
    fnw = sb("fnw_t", [128, D])
    identb = sb("identb", [128, 128], BF16); identf = sb("identf", [128, 128])
    ones = sb("ones", [128, 128]); ob2 = sb("ob2", [128, 128])
    fgt = sb("fgt", [128, 128]); fle = sb("fle", [128, 128])
    I_s = sb("I_s", [128, 64]); U_s = sb("U_s", [128, 64]); Tri_s = sb("Tri_s", [128, 64]); Mc_s = sb("Mc_s", [128, 64])
    halo = sb("halo", [128, 3 * NP, 4, 3])
    Sg = sb("Sg", [128, NP, 64]); Sh = sb("Sh", [128, HB, 128])
    W_ = TB
    xt = [sb(f"xt{i}", [128, D]) for i in range(4)]
    sqj = sb("sqj", [128, D], BF16)
    xb = sb("xb", [128, D], BF16)
    hT = sb("hT", [128, 8, W_], BF16)
    ss = sb("ss", [128, 1]); rr = sb("rr", [128, 1])
    raw = sb("raw", [128, 3, W_ + 12])
    cv = sb("cv", [128, 3, W_])
    tmp = [None, sb("tmp1", [128, W_]), sb("tmp2", [128, W_]), None]
    za = [sb(f"za{i}", [128, W_]) for i in range(2)]
    cvb = [sb(f"cvb{i}", [128, 3, W_], BF16) for i in range(2)]
    tmpA = [sb(f"tmpA{i}", [128, W_]) for i in range(3)]
    sqA = [sb(f"sqA{i}", [128, W_], BF16) for i in range(2)]
    tnA = [sb(f"tnA{i}", [128, W_]) for i in range(2)]
    I_sb = sb("I_sb", [128, 64], BF16); ob2b = sb("ob2b", [128, 128], BF16); onesb = sb("onesb", [128, 128], BF16)
    Sgb = sb("Sgb", [128, NP, 64], BF16); Shb = sb("Shb", [128, HB, 128], BF16)
    sqb = sb("sqb", [128, W_], BF16); sqbH = sb("sqbH", [128, W_], BF16)
    G = sb("G", [128, 4, 2 * HA]); Gb = sb("Gb", [128, 4, HA]); Gg = sb("Gg", [128, 4, HA])
    gs = sb("gs", [128, NP, 4]); bs = sb("bs", [128, NP, 4]); nbs = sb("nbs", [128, NP, 4])
    gc = sb("gc", [128, NP, 4]); gl = sb("gl", [128, NP, 4]); egc = sb("egc", [128, NP, 4])
    dk = sb("dk", [128, NP, 4]); bge = sb("bge", [128, NP, 4])
    rhsD = sb("rhsD", [128, W_]); Dg = sb("Dg", [128, W_])
    Ee = sb("Ee", [128, W_]); Dm = sb("Dm", [128, W_]); Ds = sb("Ds", [128, W_])
    EBs = [sb(f"EBs{i}", [128, W_]) for i in range(2)]
    P0t = [sb(f"P0t{i}", [128, W_], BF16) for i in range(2)]
    PT0t = [sb(f"PT0t{i}", [128, W_], BF16) for i in range(2)]
    R0t = [sb(f"R0t{i}", [128, W_], BF16) for i in range(2)]
    P = [sb(f"P{i}", [128, W_], BF16) for i in range(2)]
    PT = [sb(f"PT{i}", [128, W_], BF16) for i in range(2)]
    R = [sb(f"R{i}", [128, W_], BF16) for i in range(2)]
    attn = sb("attn", [128, W_], BF16); attnT = [sb(f"attnT{i}", [128, W_], BF16) for i in range(2)]
    Kbe = [sb(f"Kbe{i}", [128, 4, 64], BF16) for i in range(2)]; Kd = [sb(f"Kd{i}", [128, 4, 64], BF16) for i in range(2)]; bV = [sb(f"bV{i}", [128, 4, 64], BF16) for i in range(2)]
    u = sb("u", [128, 4, 64]); wT = sb("wT", [128, W_], BF16); QeT = [sb(f"QeT{i}", [128, W_], BF16) for i in range(2)]
    vn = sb("vn", [128, 64], BF16)
    oTf = sb("oTf", [128, W_])
    oTn = [sb(f"oTn_{i}", [128, 2 * NL, W_], BF16) for i in range(2)]
    oOwn = [sb(f"oOwn_{i}", [128, NL, W_], BF16) for i in range(2)]
    qb = sb("qb", [128, W_]); ff = sb("ff", [128, W_]); lf = sb("lf", [128, W_]); kb = sb("kb", [128, W_])
    bb = sb("bb", [128, W_]); bl = sb("bl", [128, W_])
    Qe = sb("Qe", [128, W_], BF16); Qx = sb("Qx", [128, W_], BF16); Kdh = sb("Kdh", [128, W_], BF16)
    Qef = sb("Qef", [128, W_]); Qxf = sb("Qxf", [128, W_]); Kdf = sb("Kdf", [128, W_])
    ebl = sb("ebl", [128, 4]); zb = sb("zb", [128, W_])
    vtok = sb("vtok", [64, 4, 128], BF16); Kdt = sb("Kdt", [64, 4, 128], BF16); aTh = sb("aTh", [64, W_], BF16)
    smask = sb("smask", [128, W_])
    tmpH = [None, sb("tmpH1", [128, W_])]; oTfH = sb("oTfH", [128, W_])
    yo = sb("yo", [128, D]); yo2 = sb("yo2", [128, D])
    pb = [nc.alloc_psum_tensor(f"pb{i}", [128, 512], F32) for i in range(8)]
    pT = pb[7][:, :].bitcast(BF16).rearrange("p (k t) -> p k t", t=128)

    def aff(out, cmp, fill_in, step=-1, cm=1, base=0):
        s.op("pool", lambda e: e.memset(out[:], fill_in), writes=[out.name])
        s.op("pool", lambda e: e.affine_select(out=out[:], in_=out[:], pattern=[[step, 128]], compare_op=cmp,
                                               fill=0.0, base=base, channel_multiplier=cm), reads=[out.name], writes=[out.name])
    aff(identf, ALU.is_equal, 1.0)
    aff(fgt, ALU.is_gt, 1.0)
    aff(fle, ALU.is_gt, 1.0, step=1, cm=-1, base=1)
    s.op("pool", lambda e: e.memset(ones[:], 1.0), writes=["ones"])
    s.op("pool", lambda e: e.memset(ob2[:], 0.0), writes=["ob2"])
    for h in range(2):
        sl = slice(64 * h, 64 * h + 64)
        s.op("pool", lambda e: e.memset(ob2[sl, sl], 1.0), reads=["ob2"], writes=["ob2"])
    s.op("dve", lambda e: e.tensor_copy(out=identb[:], in_=identf[:]), reads=["identf"], writes=["identb"])
    for (dst, src) in ((I_s, identf), (U_s, fgt), (Tri_s, fle)):
        for h in range(2):
            sl = slice(64 * h, 64 * h + 64)
            s.op("dve", lambda e: e.tensor_copy(out=dst[sl, :], in_=src[sl, sl]), reads=[src.name], writes=[dst.name])
    s.op("dve", lambda e: e.tensor_tensor(out=Mc_s[:], in0=U_s[:], in1=I_s[:], op=ALU.add), reads=["U_s", "I_s"], writes=["Mc_s"])
    s.op("dve", lambda e: e.tensor_copy(out=I_sb[:], in_=I_s[:]), reads=["I_s"], writes=["I_sb"])
    s.op("dve", lambda e: e.tensor_copy(out=ob2b[:], in_=ob2[:]), reads=["ob2"], writes=["ob2b"])
    s.op("dve", lambda e: e.tensor_copy(out=onesb[:], in_=ones[:]), reads=["ones"], writes=["onesb"])
    for t_, d_ in ((nw, nw_d), (cw, cw_d), (alog, alog_d), (dtb, dtb_d), (gnw, gnw_d), (hnw, hnw_d), (lbl, lbl_d), (fnw, fnw_d)):
        s.dma("sp", t_[:], d_, writes=[t_.name])
    s.op("act", lambda e: e.activation(out=negA[:], in_=alog[:], func=AF.Exp), reads=["alog_t"], writes=["negA"])
    s.op("dve", lambda e: e.tensor_scalar(out=negA[:], in0=negA[:], scalar1=-1.0, scalar2=None, op0=ALU.mult), reads=["negA"], writes=["negA"])
    s.op("dve", lambda e: e.tensor_tensor(out=lb[:], in0=lbl[:, :, 1], in1=lbl[:, :, 0], op=ALU.subtract), reads=["lbl_t"], writes=["lb"])
    s.op("act", lambda e: e.activation(out=lb[:], in_=lb[:], func=AF.Exp), reads=["lb"], writes=["lb"])
    s.op("dve", lambda e: e.tensor_scalar(out=lb[:], in0=lb[:], scalar1=1.0, scalar2=None, op0=ALU.add), reads=["lb"], writes=["lb"])
    s.op("dve", lambda e: e.reciprocal(out=lb[:], in_=lb[:]), reads=["lb"], writes=["lb"])
    s.op("dve", lambda e: e.tensor_scalar(out=oml[:], in0=lb[:], scalar1=-1.0, scalar2=1.0, op0=ALU.mult, op1=ALU.add), reads=["lb"], writes=["oml"])
    w_in_v = w_in.rearrange("(k p) n -> p k n", p=128)
    stgs = [(xt[0], "xt0"), (xt[1], "xt1"), (yo, "yo"), (yo2, "yo2")]
    q = 0
    for k in range(8):
        pieces = [(i * 1024, 1024, stgs[i][0], stgs[i][1]) for i in range(NCOL // 1024)] + [((NCOL // 1024) * 1024, NCOL % 1024, Ee, "Ee")]
        for (c0, cn, tl, key) in pieces:
            s.dma("sp", tl[:, 0:cn], w_in_v[:, k, c0:c0 + cn], writes=[key])
            if q % 2 == 0:
                s.op("dve", lambda e: e.tensor_scalar(out=Wb[:, k, c0:c0 + cn], in0=tl[:, 0:cn], scalar1=nw[:, k:k + 1], scalar2=None, op0=ALU.mult),
                     reads=[key, "nw_t"], writes=["Wb"])
            else:
                s.op("act", lambda e: e.activation(out=Wb[:, k, c0:c0 + cn], in_=tl[:, 0:cn], func=AF.Copy, scale=nw[:, k:k + 1]),
                     reads=[key, "nw_t"], writes=["Wb"])
            q += 1
    w_out_v = w_out.rearrange("(k p) n -> p k n", p=128)
    for k in range(8):
        tl, key = stgs[k % 4]
        s.dma("sp", tl[:, :], w_out_v[:, k, :], writes=[key])
        if k % 2 == 0:
            s.op("dve", lambda e: e.tensor_copy(out=WOb[:, k, :], in_=tl[:, :]), reads=[key], writes=["WOb"])
        else:
            s.op("act", lambda e: e.activation(out=WOb[:, k, :], in_=tl[:, :], func=AF.Copy), reads=[key], writes=["WOb"])

    def block(x_src, y_dst, t0, nseg, seglen, c, first, is_sample, bpar=0):
        TBk = nseg * seglen
        nch = TBk // c
        cps = seglen // c
        TT = min(128, TBk)
        ntt = TBk // TT
        nlev = {64: 5, 16: 3}[c]
        v3 = lambda ap: ap.rearrange("p (n c) -> p n c", c=c)

        def ph0_x():
            for tt in range(ntt):
                X = xt[bpar * 2 + tt]
                s.dma("sp", X[0:TT, :], x_src[t0 + tt * TT: t0 + (tt + 1) * TT, :], writes=[X.name])
                s.op("act", lambda e: e.activation(out=sqj[0:TT, :], in_=X[0:TT, :], func=AF.Square, accum_out=ss[0:TT, :]),
                     reads=[X.name], writes=["sqj", "ss"])
                s.op("act", lambda e: e.activation(out=rr[0:TT, :], in_=ss[0:TT, :], func=AF.Ln, scale=1.0 / D, bias=EPS), reads=["ss"], writes=["rr"])
                s.op("act", lambda e: e.activation(out=rr[0:TT, :], in_=rr[0:TT, :], func=AF.Exp, scale=-0.5), reads=["rr"], writes=["rr"])
                s.op("dve", lambda e: e.tensor_scalar(out=xb[0:TT, :], in0=X[0:TT, :], scalar1=rr[0:TT, :], scalar2=None, op0=ALU.mult),
                     reads=[X.name, "rr"], writes=["xb"])
                for k in range(8):
                    s.op("pe", lambda e: e.transpose(out=pT[:, k, 0:TT], in_=xb[0:TT, k * 128:(k + 1) * 128], identity=identb[0:TT, 0:TT]),
                         reads=["xb", "identb"], writes=["BT"])
                s.op("dve", lambda e: e.tensor_copy(out=hT[:, :, tt * TT:(tt + 1) * TT], in_=pT[:, :, 0:TT]), reads=["BT"], writes=["hT"])

        def silu_from(src, srckeys, dst, dstkey, scr, scrkey, W, outdt_note=None):
            s.op("act", lambda e: e.activation(out=scr, in_=src, func=AF.Exp, scale=-1.0), reads=srckeys, writes=[scrkey])
            s.op("act", lambda e: e.activation(out=scr, in_=scr, func=AF.Ln, bias=1.0), reads=[scrkey], writes=[scrkey])
            s.op("act", lambda e: e.activation(out=scr, in_=scr, func=AF.Exp, scale=-1.0), reads=[scrkey], writes=[scrkey])
            s.op("dve", lambda e: e.tensor_tensor(out=dst, in0=src, in1=scr, op=ALU.mult), reads=list(srckeys) + [scrkey], writes=[dstkey])


        def inproj_fm(ct, i=0):
            key = "B0"
            out = pb[0][:, i * 256: i * 256 + TBk]
            for k in range(8):
                s.op("pe", lambda e: e.matmul(out, lhsT=Wb[:, k, ct * 128:(ct + 1) * 128], rhs=hT[:, k, 0:TBk], start=(k == 0), stop=(k == 7)),
                     reads=["Wb", "hT"], writes=[key])
            return out, key

        def ph0_g():
            for ch in range(nch):
                for h in range(2):
                    out = pb[2][64 * h:64 * h + c, ch * 2 * HA:(ch + 1) * 2 * HA]
                    for k in range(8):
                        s.op("pe", lambda e: e.matmul(out, lhsT=hT[:, k, ch * c:(ch + 1) * c], rhs=Wb[:, k, C_G:C_G + 2 * HA], start=(k == 0), stop=(k == 7)),
                             reads=["Wb", "hT"], writes=["B2"])
            Gv = G[:, 0:nch, :]
            s.op("dve", lambda e: e.tensor_copy(out=Gv, in_=pb[2][:, 0:nch * 2 * HA].rearrange("p (n g) -> p n g", g=2 * HA)), reads=["B2"], writes=["G"])
            Gbv = Gb[:, 0:nch, :]; Ggv = Gg[:, 0:nch, :]
            s.op("act", lambda e: e.activation(out=Gbv, in_=Gv[:, :, 0:HA], func=AF.Exp, scale=-1.0), reads=["G"], writes=["Gb"])
            s.op("act", lambda e: e.activation(out=Gbv, in_=Gbv, func=AF.Ln, bias=1.0), reads=["Gb"], writes=["Gb"])
            s.op("act", lambda e: e.activation(out=Gbv, in_=Gbv, func=AF.Exp, scale=-1.0), reads=["Gb"], writes=["Gb"])
            s.op("dve", lambda e: e.tensor_tensor(out=Ggv, in0=Gv[:, :, HA:2 * HA], in1=bc(dtb[:, None, :], [128, nch, HA]), op=ALU.add),
                 reads=["G", "dtb_t"], writes=["Gg"])
            s.op("act", lambda e: e.activation(out=Ggv, in_=Ggv, func=AF.Exp), reads=["Gg"], writes=["Gg"])
            s.op("act", lambda e: e.activation(out=Ggv, in_=Ggv, func=AF.Ln, bias=1.0), reads=["Gg"], writes=["Gg"])
            s.op("dve", lambda e: e.tensor_tensor(out=Ggv, in0=Ggv, in1=bc(negA[:, None, :], [128, nch, HA]), op=ALU.mult),
                 reads=["Gg", "negA"], writes=["Gg"])
            gsv = gs[:, :, 0:nch]; bsv = bs[:, :, 0:nch]; nbsv = nbs[:, :, 0:nch]
            gcv = gc[:, :, 0:nch]; glv = gl[:, :, 0:nch]; egcv = egc[:, :, 0:nch]; dkv = dk[:, :, 0:nch]; bgev = bge[:, :, 0:nch]
            for h in range(2):
                sl = slice(64 * h, 64 * h + 64)
                for (dst, src, kd, ks) in ((gs, Gg, "gs", "Gg"), (bs, Gb, "bs", "Gb")):
                    for p in range(NP):
                        s.op("dve", lambda e: e.tensor_copy(out=dst[sl, p, 0:nch], in_=src[sl, 0:nch, 2 * p + h]), reads=[ks], writes=[kd])
            s.op("dve", lambda e: e.tensor_scalar(out=nbsv, in0=bsv, scalar1=-1.0, scalar2=None, op0=ALU.mult), reads=["bs"], writes=["nbs"])
            for h in range(2):
                rs = slice(64 * h, 64 * h + c)
                s.op("pe", lambda e: e.matmul(pb[2][rs, 64:64 + NP * nch], lhsT=Tri_s[rs, 0:c], rhs=gs[rs, :, 0:nch], start=True, stop=True),
                     reads=["Tri_s", "gs"], writes=["B2"])
                s.op("pe", lambda e: e.matmul(pb[2][rs, 96:96 + NP * nch], lhsT=ones[rs, 0:c], rhs=gs[rs, :, 0:nch], start=True, stop=True),
                     reads=["ones", "gs"], writes=["B2"])
            s.op("dve", lambda e: e.tensor_copy(out=gcv, in_=pb[2][:, 64:64 + NP * nch].rearrange("p (a n) -> p a n", n=nch)), reads=["B2"], writes=["gc"])
            s.op("dve", lambda e: e.tensor_copy(out=glv, in_=pb[2][:, 96:96 + NP * nch].rearrange("p (a n) -> p a n", n=nch)), reads=["B2"], writes=["gl"])
            s.op("act", lambda e: e.activation(out=egcv, in_=gcv, func=AF.Exp), reads=["gc"], writes=["egc"])
            s.op("dve", lambda e: e.tensor_tensor(out=dkv, in0=glv, in1=gcv, op=ALU.subtract), reads=["gl", "gc"], writes=["dk"])
            s.op("act", lambda e: e.activation(out=dkv, in_=dkv, func=AF.Exp), reads=["dk"], writes=["dk"])
            s.op("dve", lambda e: e.tensor_tensor(out=bgev, in0=bsv, in1=egcv, op=ALU.mult), reads=["bs", "egc"], writes=["bge"])


        def phase0(_=None):
            ph0_x()
            ph0_g()
            ph0_m()

        def front(p):
            par = p % 2
            for i3 in range(3):
                ct = NP * i3 + p
                pp, pk = inproj_fm(ct)
                rv = raw[:, i3, 0:nseg * (seglen + 3)].rearrange("p (n c) -> p n c", c=seglen + 3)
                if first and not is_sample:
                    s.op("pool", lambda e: e.memset(rv[:, :, 0:3], 0.0), reads=[f"raw{i3}"], writes=[f"raw{i3}"])
                else:
                    s.op("pool", lambda e: e.tensor_copy(out=rv[:, :, 0:3], in_=halo[:, ct, 0:nseg, :]), reads=["halo"], writes=[f"raw{i3}"])
                s.op("act", lambda e: e.activation(out=rv[:, :, 3:3 + seglen], in_=pp.rearrange("p (n c) -> p n c", c=seglen), func=AF.Copy),
                     reads=[pk], writes=[f"raw{i3}"])
                s.op("pool", lambda e: e.tensor_copy(out=halo[:, ct, 0:nseg, :], in_=rv[:, :, seglen:seglen + 3]), reads=[f"raw{i3}"], writes=["halo"])
                cvv = cv[:, i3, 0:TBk].rearrange("p (n c) -> p n c", c=seglen)
                s.op("dve", lambda e: e.tensor_scalar(out=cvv, in0=rv[:, :, 0:seglen], scalar1=cw[:, ct, 0:1], scalar2=None, op0=ALU.mult),
                     reads=[f"raw{i3}", "cw_t"], writes=[f"cv{i3}"])
                for j in range(1, 4):
                    s.op("dve", lambda e: e.scalar_tensor_tensor(out=cvv, in0=rv[:, :, j:j + seglen], scalar=cw[:, ct, j:j + 1], in1=cvv,
                                                                 op0=ALU.mult, op1=ALU.add), reads=[f"raw{i3}", "cw_t", f"cv{i3}"], writes=[f"cv{i3}"])
                if i3 < 2:
                    silu_from(cv[:, i3, 0:TBk], [f"cv{i3}"], cv[:, i3, 0:TBk], f"cv{i3}", tmpA[i3][:, 0:TBk], f"tmpA{i3}", TBk)
                else:
                    silu_from(cv[:, i3, 0:TBk], [f"cv{i3}"], cvb[par][:, 2, 0:TBk], f"cvb{par}v", tmpA[i3][:, 0:TBk], f"tmpA{i3}", TBk)
            for i3 in range(2):
                src = cv[:, i3, 0:TBk]
                s.op("pool", lambda e: e.tensor_tensor(out=sqA[i3][:, 0:TBk], in0=src, in1=src, op=ALU.mult), reads=[f"cv{i3}"], writes=[f"sqA{i3}"])
                s.op("pe", lambda e: e.matmul(pb[7][:, i3 * 256:i3 * 256 + TBk], lhsT=ob2b[:], rhs=sqA[i3][:, 0:TBk], start=True, stop=True), reads=["ob2b", f"sqA{i3}"], writes=["BT"])
                s.op("act", lambda e: e.activation(out=tnA[i3][:, 0:TBk], in_=pb[7][:, i3 * 256:i3 * 256 + TBk], func=AF.Ln, bias=EPS), reads=["BT"], writes=[f"tnA{i3}"])
                s.op("act", lambda e: e.activation(out=tnA[i3][:, 0:TBk], in_=tnA[i3][:, 0:TBk], func=AF.Exp, scale=-0.5), reads=[f"tnA{i3}"], writes=[f"tnA{i3}"])
                if i3 == 0:
                    s.op("dve", lambda e: e.scalar_tensor_tensor(out=cvb[par][:, 0, 0:TBk], in0=src, scalar=0.125, in1=tnA[i3][:, 0:TBk], op0=ALU.mult, op1=ALU.mult),
                         reads=[f"cv{i3}", f"tnA{i3}"], writes=[f"cvb{par}q"])
                else:
                    s.op("dve", lambda e: e.tensor_tensor(out=cvb[par][:, 1, 0:TBk], in0=src, in1=tnA[i3][:, 0:TBk], op=ALU.mult), reads=[f"cv{i3}", f"tnA{i3}"], writes=[f"cvb{par}k"])
            pp, pk = inproj_fm(3 * NP + p)
            s.op("act", lambda e: e.activation(out=za[par][:, 0:TBk], in_=pp, func=AF.Copy), reads=[pk], writes=[f"za{par}"])
            silu_from(za[par][:, 0:TBk], [f"za{par}"], za[par][:, 0:TBk], f"za{par}", tmpA[0][:, 0:TBk], "tmpA0", TBk)
            qn = cvb[par][:, 0, 0:TBk]; kn = cvb[par][:, 1, 0:TBk]; vs = cvb[par][:, 2, 0:TBk]
            ckq, ckk, ckv = f"cvb{par}q", f"cvb{par}k", f"cvb{par}v"
            s.op("dve", lambda e: e.tensor_tensor(out=v3(rhsD[:, 0:TBk]), in0=bc(gs[:, p, 0:nch, None], [128, nch, c]), in1=bc(U_s[:, None, 0:c], [128, nch, c]), op=ALU.mult),
                 reads=["gs", "U_s"], writes=["rhsD"])
            s.op("pool", lambda e: e.tensor_tensor(out=v3(Dg[:, 0:TBk]), in0=bc(egc[:, p, 0:nch, None], [128, nch, c]), in1=bc(I_s[:, None, 0:c], [128, nch, c]), op=ALU.mult),
                 reads=["egc", "I_s"], writes=["Dg"])
            for h in range(2):
                rs = slice(64 * h, 64 * h + c)
                s.op("pe", lambda e: e.matmul(pb[6][rs, 0:TBk], lhsT=Tri_s[rs, 0:c], rhs=rhsD[rs, 0:TBk], start=True, stop=True),
                     reads=["Tri_s", "rhsD"], writes=["B6"])
            s.op("act", lambda e: e.activation(out=Ee[:, 0:TBk], in_=pb[6][:, 0:TBk], func=AF.Exp), reads=["B6"], writes=["Ee"])
            s.op("pool", lambda e: e.tensor_tensor(out=v3(Dm[:, 0:TBk]), in0=v3(Ee[:, 0:TBk]), in1=bc(Mc_s[:, None, 0:c], [128, nch, c]), op=ALU.mult),
                 reads=["Ee", "Mc_s"], writes=["Dm"])
            s.op("pool", lambda e: e.tensor_tensor(out=v3(Ds[:, 0:TBk]), in0=v3(Ee[:, 0:TBk]), in1=bc(U_s[:, None, 0:c], [128, nch, c]), op=ALU.mult),
                 reads=["Ee", "U_s"], writes=["Ds"])
            for h in range(2):
                rs = slice(64 * h, 64 * h + c)
                s.op("pe", lambda e: e.matmul(pb[6][64 * h:64 * h + 64, 0:TBk], lhsT=ones[rs, 0:64], rhs=Dg[rs, 0:TBk], start=True, stop=True),
                     reads=["ones", "Dg"], writes=["B6"])
            s.op("act", lambda e: e.activation(out=EBs[par][:, 0:TBk], in_=pb[6][:, 0:TBk], func=AF.Copy), reads=["B6"], writes=[f"EBs{par}"])
            s.op("dve", lambda e: e.tensor_tensor(out=QeT[par][:, 0:TBk], in0=qn, in1=EBs[par][:, 0:TBk], op=ALU.mult), reads=[ckq, f"EBs{par}"], writes=[f"QeT{par}"])
            for ch in range(nch):
                cs = slice(ch * c, (ch + 1) * c)
                for h in range(2):
                    fs = slice(64 * h, 64 * h + 64); rs = slice(64 * h, 64 * h + c)
                    s.op("pe", lambda e: e.matmul(pb[7][rs, cs], lhsT=kn[fs, cs], rhs=kn[fs, cs], start=True, stop=True), reads=[ckk], writes=["BT"])
                    s.op("pe", lambda e: e.matmul(pb[7][rs, 256 + ch * c:256 + (ch + 1) * c], lhsT=qn[fs, cs], rhs=kn[fs, cs], start=True, stop=True),
                         reads=[ckq, ckk], writes=["BT"])
            s.op("dve", lambda e: e.tensor_tensor(out=v3(tmp[2][:, 0:TBk]), in0=v3(pb[7][:, 0:TBk]), in1=bc(nbs[:, p, 0:nch, None], [128, nch, c]), op=ALU.mult),
                 reads=["BT", "nbs"], writes=["tmp2"])
            s.op("pool", lambda e: e.tensor_tensor(out=PT0t[par][:, 0:TBk], in0=tmp[2][:, 0:TBk], in1=Ds[:, 0:TBk], op=ALU.mult), reads=["tmp2", "Ds"], writes=[f"PT0t{par}"])
            s.op("dve", lambda e: e.tensor_tensor(out=attn[:, 0:TBk], in0=pb[7][:, 256:256 + TBk], in1=Dm[:, 0:TBk], op=ALU.mult), reads=["BT", "Dm"], writes=["attn"])

            for ch in range(nch):
                cs = slice(ch * c, (ch + 1) * c)
                for h in range(2):
                    fs = slice(64 * h, 64 * h + 64); rs = slice(64 * h, 64 * h + c)
                    s.op("pe", lambda e: e.matmul(pb[6][rs, cs], lhsT=PT0t[par][rs, cs], rhs=I_sb[rs, 0:c], start=True, stop=True), reads=[f"PT0t{par}", "I_sb"], writes=["B6"])
                    s.op("pe", lambda e: e.matmul(pb[6][rs, 256 + ch * c:256 + (ch + 1) * c], lhsT=attn[rs, cs], rhs=I_sb[rs, 0:c], start=True, stop=True),
                         reads=["attn", "I_sb"], writes=["B6"])
                    s.op("pe", lambda e: e.matmul(pb[7][rs, ch * 64:(ch + 1) * 64], lhsT=kn[fs, cs], rhs=I_sb[fs, 0:64], start=True, stop=True), reads=[ckk, ckv, "I_sb"], writes=["BT"])
                    s.op("pe", lambda e: e.matmul(pb[7][rs, 256 + ch * 64:256 + (ch + 1) * 64], lhsT=vs[fs, cs], rhs=I_sb[fs, 0:64], start=True, stop=True),
                         reads=[ckk, ckv, "I_sb"], writes=["BT"])
            s.op("act", lambda e: e.activation(out=P0t[par][:, 0:TBk], in_=pb[6][:, 0:TBk], func=AF.Copy), reads=["B6"], writes=[f"P0t{par}"])
            s.op("dve", lambda e: e.tensor_tensor(out=v3(R0t[par][:, 0:TBk]), in0=v3(pb[6][:, 0:TBk]), in1=bc(I_s[:, None, 0:c], [128, nch, c]), op=ALU.add),
                 reads=["B6", "I_s"], writes=[f"R0t{par}"])
            s.op("act", lambda e: e.activation(out=attnT[par][:, 0:TBk], in_=pb[6][:, 256:256 + TBk], func=AF.Copy), reads=["B6"], writes=[f"attnT{par}"])
            k4 = pb[7][:, 0:nch * 64].rearrange("p (n d) -> p n d", d=64)
            v4 = pb[7][:, 256:256 + nch * 64].rearrange("p (n d) -> p n d", d=64)
            s.op("dve", lambda e: e.tensor_tensor(out=Kbe[par][:, 0:nch, :], in0=k4, in1=bc(bge[:, p, 0:nch, None], [128, nch, 64]), op=ALU.mult), reads=["BT", "bge"], writes=[f"Kbe{par}"])
            s.op("dve", lambda e: e.tensor_tensor(out=Kd[par][:, 0:nch, :], in0=k4, in1=bc(dk[:, p, 0:nch, None], [128, nch, 64]), op=ALU.mult), reads=["BT", "dk"], writes=[f"Kd{par}"])
            s.op("dve", lambda e: e.tensor_tensor(out=bV[par][:, 0:nch, :], in0=v4, in1=bc(bs[:, p, 0:nch, None], [128, nch, 64]), op=ALU.mult), reads=["BT", "bs"], writes=[f"bV{par}"])
        def core(p):
            par = p % 2
            Pc, kP = P0t[par], f"P0t{par}"
            PTc, kPT = PT0t[par], f"PT0t{par}"
            Rc, kR = R0t[par], f"R0t{par}"
            for lev in range(1, nlev + 2):
                do_pow = lev <= nlev
                need_P = lev < nlev
                do_R = lev >= 2
                nP, nkP = P[lev % 2], f"P{lev % 2}"
                nPT, nkPT = PT[lev % 2], f"PT{lev % 2}"
                nR, nkR = R[lev % 2], f"R{lev % 2}"
                for ch in range(nch):
                    cs = slice(ch * c, (ch + 1) * c)
                    for h in range(2):
                        rs = slice(64 * h, 64 * h + c)
                        if do_pow and need_P:
                            s.op("pe", lambda e: e.matmul(pb[5][rs, cs], lhsT=PTc[rs, cs], rhs=Pc[rs, cs], start=True, stop=True), reads=[kPT, kP], writes=["B5"])
                        if do_pow:
                            s.op("pe", lambda e: e.matmul(pb[5][rs, 256 + ch * c:256 + (ch + 1) * c], lhsT=Pc[rs, cs], rhs=PTc[rs, cs], start=True, stop=True),
                                 reads=[kPT, kP], writes=["B5"])
                        if do_R:
                            s.op("pe", lambda e: e.matmul(pb[4][rs, cs], lhsT=PTc[rs, cs], rhs=Rc[rs, cs], start=True, stop=True), reads=[kPT, kR], writes=["B4"])
                if do_pow and need_P:
                    s.op("act", lambda e: e.activation(out=nP[:, 0:TBk], in_=pb[5][:, 0:TBk], func=AF.Copy), reads=["B5"], writes=[nkP])
                if do_pow:
                    s.op("dve", lambda e: e.tensor_copy(out=nPT[:, 0:TBk], in_=pb[5][:, 256:256 + TBk]), reads=["B5"], writes=[nkPT])
                if do_R:
                    s.op("dve", lambda e: e.tensor_tensor(out=nR[:, 0:TBk], in0=pb[4][:, 0:TBk], in1=Rc[:, 0:TBk], op=ALU.add), reads=["B4", kR], writes=[nkR])
                    Rc, kR = nR, nkR
                if do_pow:
                    if need_P:
                        Pc, kP = nP, nkP
                    PTc, kPT = nPT, nkPT
            Rf, rk = Rc, kR
            for ch in range(nch):
                cs = slice(ch * c, (ch + 1) * c)
                for h in range(2):
                    rs = slice(64 * h, 64 * h + c)
                    s.op("pe", lambda e: e.matmul(pb[4][rs, 256 + ch * 64:256 + (ch + 1) * 64], lhsT=Rf[rs, cs], rhs=bV[par][rs, ch, :], start=True, stop=True),
                         reads=[rk, f"bV{par}"], writes=["B4"])
                    s.op("pe", lambda e: e.matmul(pb[5][64 * h:64 * h + 64, ch * c:(ch + 1) * c], lhsT=Kbe[par][rs, ch, :], rhs=Rf[rs, cs], start=True, stop=True),
                         reads=[rk, f"Kbe{par}"], writes=["B5"])
            s.op("act", lambda e: e.activation(out=u[:, 0:nch, :], in_=pb[4][:, 256:256 + nch * 64].rearrange("p (n d) -> p n d", d=64), func=AF.Copy), reads=["B4"], writes=["u"])
            s.op("dve", lambda e: e.tensor_copy(out=wT[:, 0:TBk], in_=pb[5][:, 0:TBk]), reads=["B5"], writes=["wT"])
            skey = f"Sg{p}"
            for ch in range(nch):
                cs = slice(ch * c, (ch + 1) * c)
                seg = ch // cps
                if ch % cps == 0:
                    if is_sample:
                        s.dma("sp", Sg[:, p, :], sg_d[seg, p], writes=[skey])
                        s.op("dve", lambda e: e.tensor_copy(out=Sgb[:, p, :], in_=Sg[:, p, :]), reads=[skey], writes=[skey + "b"])
                    elif first:
                        s.op("pool", lambda e: e.memset(Sg[:, p, :], 0.0), writes=[skey])
                        s.op("pool", lambda e: e.memset(Sgb[:, p, :], 0.0), writes=[skey + "b"])
                for h in range(2):
                    fs = slice(64 * h, 64 * h + 64); rs = slice(64 * h, 64 * h + c)
                    s.op("pe", lambda e: e.matmul(pb[1][rs, 0:64], lhsT=wT[fs, cs], rhs=Sgb[fs, p, :], start=True, stop=True), reads=["wT", skey + "b"], writes=["B1"])
                s.op("dve", lambda e: e.tensor_tensor(out=vn[:], in0=u[:, ch, :], in1=pb[1][:, 0:64], op=ALU.subtract), reads=["u", "B1"], writes=["vn"])
                for h in range(2):
                    fs = slice(64 * h, 64 * h + 64); rs = slice(64 * h, 64 * h + c)
                    s.op("pe", lambda e: e.matmul(pb[1][fs, 256 + ch * c:256 + (ch + 1) * c], lhsT=Sgb[fs, p, :], rhs=QeT[par][fs, cs], start=True, stop=False),
                         reads=[skey + "b", f"QeT{par}"], writes=["B1"])
                    s.op("pe", lambda e: e.matmul(pb[1][fs, 256 + ch * c:256 + (ch + 1) * c], lhsT=vn[rs, :], rhs=attnT[par][rs, cs], start=False, stop=True),
                         reads=["vn", f"attnT{par}"], writes=["B1"])
                for h in range(2):
                    fs = slice(64 * h, 64 * h + 64); rs = slice(64 * h, 64 * h + c)
                    s.op("pe", lambda e: e.matmul(pb[1][fs, 64:128], lhsT=Kd[par][rs, ch, :], rhs=vn[rs, :], start=True, stop=True), reads=[f"Kd{par}", "vn"], writes=["B1"])
                s.op("dve", lambda e: e.scalar_tensor_tensor(out=Sgb[:, p, :], in0=Sg[:, p, :], scalar=EBs[par][:, (ch + 1) * c - 1:(ch + 1) * c], in1=pb[1][:, 64:128],
                                                             op0=ALU.mult, op1=ALU.add), reads=[skey, f"EBs{par}", "B1"], writes=[skey + "b"])
                s.op("dve", lambda e: e.scalar_tensor_tensor(out=Sg[:, p, :], in0=Sg[:, p, :], scalar=EBs[par][:, (ch + 1) * c - 1:(ch + 1) * c], in1=pb[1][:, 64:128],
                                                             op0=ALU.mult, op1=ALU.add), reads=[skey, f"EBs{par}", "B1"], writes=[skey])
                if is_sample and (ch + 1) % cps == 0:
                    s.dma("sp", ngs[seg, p], Sg[:, p, :], reads=[skey], writes=[f"o_ngs{seg}_{p}"])
            s.op("act", lambda e: e.activation(out=oTf[:, 0:TBk], in_=pb[1][:, 256:256 + TBk], func=AF.Copy), reads=["B1"], writes=["oTf"])
            s.op("pool", lambda e: e.tensor_tensor(out=sqb[:, 0:TBk], in0=oTf[:, 0:TBk], in1=oTf[:, 0:TBk], op=ALU.mult), reads=["oTf"], writes=["sqb"])
            s.op("pe", lambda e: e.matmul(pb[3][:, 0:TBk], lhsT=ob2b[:], rhs=sqb[:, 0:TBk], start=True, stop=True), reads=["ob2b", "sqb"], writes=["B3"])
            s.op("act", lambda e: e.activation(out=tmp[1][:, 0:TBk], in_=pb[3][:, 0:TBk], func=AF.Ln, scale=1.0 / 64, bias=EPS), reads=["B3"], writes=["tmp1"])
            s.op("act", lambda e: e.activation(out=tmp[1][:, 0:TBk], in_=tmp[1][:, 0:TBk], func=AF.Exp, scale=-0.5), reads=["tmp1"], writes=["tmp1"])
            s.op("dve", lambda e: e.tensor_tensor(out=oTf[:, 0:TBk], in0=oTf[:, 0:TBk], in1=tmp[1][:, 0:TBk], op=ALU.mult), reads=["oTf", "tmp1"], writes=["oTf"])
            s.op("dve", lambda e: e.scalar_tensor_tensor(out=oOwn[bpar][:, p, 0:TBk], in0=oTf[:, 0:TBk], scalar=gnw[:, 0:1], in1=za[par][:, 0:TBk], op0=ALU.mult, op1=ALU.mult),
                 reads=["oTf", "gnw_t", f"za{par}"], writes=[f"oOwn{bpar}_{p}"])

        def ph0_m():
            s.op("pool", lambda e: e.memset(smask[:, 0:TBk], 1.0), writes=["smask"])
            s.op("pool", lambda e: e.memset(v3(smask[:, 0:TBk])[:, :, 0:1], 0.0), reads=["smask"], writes=["smask"])

        def hg(h):
            pp, pk = inproj_fm(4 * NP + h, 1)
            s.op("act", lambda e: e.activation(out=qb[:, 0:TBk], in_=pp, func=AF.Copy), reads=[pk], writes=["qb"])
            silu_from(qb[:, 0:TBk], ["qb"], qb[:, 0:TBk], "qb", bl[:, 0:TBk], "bl", TBk)
            pp, pk = inproj_fm(4 * NP + 2 * HB + h, 1)
            s.op("act", lambda e: e.activation(out=zb[:, 0:TBk], in_=pp, func=AF.Copy), reads=[pk], writes=["zb"])
            silu_from(zb[:, 0:TBk], ["zb"], zb[:, 0:TBk], "zb", bl[:, 0:TBk], "bl", TBk)
            pp, pk = inproj_fm(4 * NP + HB + h, 1)
            s.op("act", lambda e: e.activation(out=ff[:, 0:TBk], in_=pp, func=AF.Exp, scale=-1.0), reads=[pk], writes=["ff"])
            s.op("act", lambda e: e.activation(out=ff[:, 0:TBk], in_=ff[:, 0:TBk], func=AF.Ln, bias=1.0), reads=["ff"], writes=["ff"])
            s.op("act", lambda e: e.activation(out=ff[:, 0:TBk], in_=ff[:, 0:TBk], func=AF.Exp, scale=-1.0), reads=["ff"], writes=["ff"])
            s.op("dve", lambda e: e.tensor_scalar(out=ff[:, 0:TBk], in0=ff[:, 0:TBk], scalar1=oml[:, h:h + 1], scalar2=lb[:, h:h + 1], op0=ALU.mult, op1=ALU.add),
                 reads=["ff", "oml", "lb"], writes=["ff"])
            s.op("act", lambda e: e.activation(out=lf[:, 0:TBk], in_=ff[:, 0:TBk], func=AF.Ln), reads=["ff"], writes=["lf"])
            s.op("dve", lambda e: e.tensor_scalar(out=kb[:, 0:TBk], in0=ff[:, 0:TBk], scalar1=-1.0, scalar2=1.0, op0=ALU.mult, op1=ALU.add), reads=["ff"], writes=["kb"])
            s.op("dve", lambda e: e.tensor_tensor_scan(out=bb[:, 0:TBk], data0=smask[:, 0:TBk], data1=lf[:, 0:TBk], initial=0.0, op0=ALU.mult, op1=ALU.add),
                 reads=["smask", "lf"], writes=["bb"])
            b3 = v3(bb[:, 0:TBk])
            s.op("pool", lambda e: e.tensor_tensor(out=v3(bl[:, 0:TBk]), in0=b3, in1=bc(b3[:, :, c - 1:c], [128, nch, c]), op=ALU.subtract), reads=["bb"], writes=["bl"])
            s.op("act", lambda e: e.activation(out=Qef[:, 0:TBk], in_=bb[:, 0:TBk], func=AF.Exp), reads=["bb"], writes=["Qef"])
            s.op("act", lambda e: e.activation(out=Qxf[:, 0:TBk], in_=bl[:, 0:TBk], func=AF.Exp), reads=["bl"], writes=["Qxf"])
            s.op("act", lambda e: e.activation(out=Kdf[:, 0:TBk], in_=bl[:, 0:TBk], func=AF.Exp, scale=-1.0), reads=["bl"], writes=["Kdf"])
            s.op("act", lambda e: e.activation(out=ebl[:, 0:nch], in_=b3[:, :, c - 1], func=AF.Exp), reads=["bb"], writes=["ebl"])
            s.op("dve", lambda e: e.tensor_tensor(out=Qe[:, 0:TBk], in0=Qef[:, 0:TBk], in1=qb[:, 0:TBk], op=ALU.mult), reads=["Qef", "qb"], writes=["Qe"])
            s.op("pool", lambda e: e.tensor_tensor(out=Qx[:, 0:TBk], in0=Qxf[:, 0:TBk], in1=qb[:, 0:TBk], op=ALU.mult), reads=["Qxf", "qb"], writes=["Qx"])
            s.op("dve", lambda e: e.tensor_tensor(out=Kdh[:, 0:TBk], in0=Kdf[:, 0:TBk], in1=kb[:, 0:TBk], op=ALU.mult), reads=["Kdf", "kb"], writes=["Kdh"])
            for ch in range(nch):
                cs = slice(ch * c, (ch + 1) * c)
                outp = pb[2][0:c, 128:256]
                for k in range(8):
                    s.op("pe", lambda e: e.matmul(outp, lhsT=hT[:, k, cs], rhs=Wb[:, k, C_HI + 128 * h:C_HI + 128 * (h + 1)], start=(k == 0), stop=(k == 7)),
                         reads=["Wb", "hT"], writes=["B2"])
                s.op("pe", lambda e: e.matmul(pb[2][0:c, 256:384], lhsT=Kdh[:, cs], rhs=identb[:], start=True, stop=True), reads=["Kdh", "identb"], writes=["B2"])
                s.op("pe", lambda e: e.matmul(pb[2][0:c, 384:384 + c], lhsT=Kdh[:, cs], rhs=Qx[:, cs], start=True, stop=True), reads=["Kdh", "Qx"], writes=["B2"])
                s.op("act", lambda e: e.activation(out=vtok[0:c, ch, :], in_=outp, func=AF.Copy), reads=["B2"], writes=["vtok"])
                s.op("dve", lambda e: e.tensor_copy(out=Kdt[0:c, ch, :], in_=pb[2][0:c, 256:384]), reads=["B2"], writes=["Kdt"])
                s.op("dve", lambda e: e.tensor_tensor(out=aTh[0:c, cs], in0=pb[2][0:c, 384:384 + c], in1=Tri_s[0:c, 0:c], op=ALU.mult),
                     reads=["B2", "Tri_s"], writes=["aTh"])
            skey = f"Sh{h}"
            for ch in range(nch):
                cs = slice(ch * c, (ch + 1) * c)
                seg = ch // cps
                if ch % cps == 0:
                    if is_sample:
                        s.dma("sp", Sh[:, h, :], sh_d[seg, h], writes=[skey])
                        s.op("dve", lambda e: e.tensor_copy(out=Shb[:, h, :], in_=Sh[:, h, :]), reads=[skey], writes=[skey + "b"])
                    elif first:
                        s.op("pool", lambda e: e.memset(Sh[:, h, :], 0.0), writes=[skey])
                        s.op("pool", lambda e: e.memset(Shb[:, h, :], 0.0), writes=[skey + "b"])
                s.op("pe", lambda e: e.matmul(pb[3][:, 256 + ch * c:256 + (ch + 1) * c], lhsT=Shb[:, h, :], rhs=Qe[:, cs], start=True, stop=False), reads=[skey + "b", "Qe"], writes=["B3"])
                s.op("pe", lambda e: e.matmul(pb[3][:, 256 + ch * c:256 + (ch + 1) * c], lhsT=vtok[0:c, ch, :], rhs=aTh[0:c, cs], start=False, stop=True),
                     reads=["vtok", "aTh"], writes=["B3"])
                s.op("pe", lambda e: e.matmul(pb[1][:, 128:256], lhsT=Kdt[0:c, ch, :], rhs=vtok[0:c, ch, :], start=True, stop=True), reads=["Kdt", "vtok"], writes=["B1"])
                s.op("dve", lambda e: e.scalar_tensor_tensor(out=Shb[:, h, :], in0=Sh[:, h, :], scalar=ebl[:, ch:ch + 1], in1=pb[1][:, 128:256], op0=ALU.mult, op1=ALU.add),
                     reads=[skey, "ebl", "B1"], writes=[skey + "b"])
                s.op("dve", lambda e: e.scalar_tensor_tensor(out=Sh[:, h, :], in0=Sh[:, h, :], scalar=ebl[:, ch:ch + 1], in1=pb[1][:, 128:256], op0=ALU.mult, op1=ALU.add),
                     reads=[skey, "ebl", "B1"], writes=[skey])
                if is_sample and (ch + 1) % cps == 0:
                    s.dma("sp", nhs[seg, h], Sh[:, h, :], reads=[skey], writes=[f"o_nhs{seg}_{h}"])
            s.op("act", lambda e: e.activation(out=oTfH[:, 0:TBk], in_=pb[3][:, 256:256 + TBk], func=AF.Copy), reads=["B3"], writes=["oTfH"])
            s.op("pool", lambda e: e.tensor_tensor(out=sqbH[:, 0:TBk], in0=oTfH[:, 0:TBk], in1=oTfH[:, 0:TBk], op=ALU.mult), reads=["oTfH"], writes=["sqbH"])
            s.op("pe", lambda e: e.matmul(pb[3][:, 256:256 + TBk], lhsT=onesb[:], rhs=sqbH[:, 0:TBk], start=True, stop=True), reads=["onesb", "sqbH"], writes=["B3"])
            s.op("act", lambda e: e.activation(out=tmpH[1][:, 0:TBk], in_=pb[3][:, 256:256 + TBk], func=AF.Ln, scale=1.0 / 128, bias=EPS), reads=["B3"], writes=["tmpH1"])
            s.op("act", lambda e: e.activation(out=tmpH[1][:, 0:TBk], in_=tmpH[1][:, 0:TBk], func=AF.Exp, scale=-0.5), reads=["tmpH1"], writes=["tmpH1"])
            s.op("dve", lambda e: e.tensor_tensor(out=oTfH[:, 0:TBk], in0=oTfH[:, 0:TBk], in1=tmpH[1][:, 0:TBk], op=ALU.mult), reads=["oTfH", "tmpH1"], writes=["oTfH"])
            s.op("dve", lambda e: e.scalar_tensor_tensor(out=oOwn[bpar][:, NP + h, 0:TBk], in0=oTfH[:, 0:TBk], scalar=hnw[:, 0:1], in1=zb[:, 0:TBk], op0=ALU.mult, op1=ALU.mult),
                 reads=["oTfH", "hnw_t", "zb"], writes=[f"oOwn{bpar}_{NP + h}"])

        def xchg(_=None):
            xs_, xd_ = (xsrc_s, xdst_s) if is_sample else (xsrc[bpar], xdst[bpar])
            s.dma("sp", xs_.rearrange("(t p) n -> p t n", p=128), oOwn[bpar][:, :, 0:TBk], reads=[f"oOwn{bpar}_{i}" for i in range(NL)], writes=[f"xsrc{bpar}"])
            s.coll([xs_], [xd_], reads=[f"xsrc{bpar}"], writes=[f"xdst{bpar}"])
            s.dma("sp", oTn[bpar][:, :, 0:TBk], xd_.rearrange("(t p) n -> p t n", p=128), reads=[f"xdst{bpar}"], writes=[f"oTn{bpar}_{k}" for k in range(2 * NL)])

        def outproj(_=None):
            for tt in range(ntt):
                X = xt[bpar * 2 + tt]
                for half in range(2):
                    bank = pb[4] if half == 0 else pb[5]
                    bk = "B4" if half == 0 else "B5"
                    for k in range(8):
                        s.op("pe", lambda e: e.matmul(bank[0:TT, :], lhsT=oTn[bpar][:, k, tt * TT:(tt + 1) * TT], rhs=WOb[:, k, half * 512:(half + 1) * 512], start=(k == 0), stop=(k == 7)),
                             reads=[f"oTn{bpar}_{k}", "WOb"], writes=[bk])
                    s.op("dve", lambda e: e.tensor_tensor(out=yo[0:TT, half * 512:(half + 1) * 512], in0=bank[0:TT, :], in1=X[0:TT, half * 512:(half + 1) * 512], op=ALU.add),
                         reads=[bk, X.name], writes=["yo"])
                s.op("act", lambda e: e.activation(out=sqj[0:TT, :], in_=yo[0:TT, :], func=AF.Square, accum_out=ss[0:TT, :]), reads=["yo"], writes=["sqj", "ss"])
                s.op("act", lambda e: e.activation(out=rr[0:TT, :], in_=ss[0:TT, :], func=AF.Ln, scale=1.0 / D, bias=EPS), reads=["ss"], writes=["rr"])
                s.op("act", lambda e: e.activation(out=rr[0:TT, :], in_=rr[0:TT, :], func=AF.Exp, scale=-0.5), reads=["rr"], writes=["rr"])
                s.op("dve", lambda e: e.scalar_tensor_tensor(out=yo2[0:TT, :], in0=yo[0:TT, :], scalar=rr[0:TT, :], in1=fnw[0:TT, :], op0=ALU.mult, op1=ALU.mult),
                     reads=["yo", "rr", "fnw_t"], writes=["yo2"])
                s.dma("sp", y_dst[t0 + tt * TT: t0 + (tt + 1) * TT, :], yo2[0:TT, :], reads=["yo2"], writes=[f"o_y{id(y_dst)}"], slot="yout")

        def record(fn, arg=None):
            lst = []
            s.rec = lst
            fn(arg)
            s.rec = None
            return lst
        return dict(p0=lambda: record(phase0), op=lambda: record(outproj), xc=lambda: record(xchg),
                    fr=lambda p: record(front, p), co=lambda p: record(core, p), hg=lambda h: record(hg, h))

    def run_block(parts, prev_op, next_p0):
        st = [parts["fr"](0), parts["hg"](0)]
        if prev_op is not None:
            st.append(prev_op)
        s.merge_emit(st)
        for ph in range(NP):
            st = [parts["co"](ph)]
            if ph + 1 < NP:
                st += [parts["fr"](ph + 1)]
            if ph + 1 < HB:
                st += [parts["hg"](ph + 1)]
            if ph == NP - 1 and next_p0 is not None:
                st.append(next_p0())
            s.merge_emit(st)
        s.merge_emit([parts["xc"]()])

    nblk = T // TB
    blks = [block(xp, yp, b * TB, 1, TB, 64, b == 0, False, b % 2) for b in range(nblk)]
    s.merge_emit([blks[0]["p0"]()])
    prev_op = None
    for b in range(nblk):
        nxt = blks[b + 1]["p0"] if b + 1 < nblk else None
        run_block(blks[b], prev_op, nxt)
        prev_op = blks[b]["op"]()
    s.merge_emit([prev_op])
    s.dma("sp", ncp[:, :, 0, :], halo[:, :, 0, :], reads=["halo"], writes=["o_ncp"])
    for p in range(NP):
        s.dma("sp", ngp[0, p], Sg[:, p, :], reads=[f"Sg{p}"], writes=[f"o_ngp{p}"])
    for h in range(HB):
        s.dma("sp", nhp[0, h], Sh[:, h, :], reads=[f"Sh{h}"], writes=[f"o_nhp{h}"])
    s.dma("sp", halo[:], sc_d, reads=["halo"], writes=["halo"])
    sblk = block(xs, ys, 0, 4, 16, 16, True, True, 0)
    s.merge_emit([sblk["p0"]()])
    run_block(sblk, None, None)
    s.merge_emit([sblk["op"]()])
    s.dma("sp", ncs, halo[:], reads=["halo"], writes=["o_ncs"])
    s.finish("sp")
    return nc, s


def _perm(hh):
    r = lambda base, w: np.arange(base + hh * w, base + (hh + 1) * w)
    return np.concatenate([r(0, 256), r(512, 256), r(1024, 256), r(1536, 256),
                           r(2064, 256), r(2576, 256), r(3600, 256), r(3088, 256),
                           r(2048, 4), r(2056, 4)])


def _chan(hh):
    r = lambda base: np.arange(base + hh * 256, base + (hh + 1) * 256)
    return np.concatenate([r(0), r(512), r(1024)])


_WOUT_ROWS = np.concatenate([np.concatenate([np.arange(r * 256, (r + 1) * 256), np.arange(512 + r * 256, 512 + (r + 1) * 256)]) for r in range(2)])


def _core_inputs(c, inp):
    b, hh = c // 2, c % 2
    f = lambda a: np.ascontiguousarray(a, dtype=np.float32)
    ch = _chan(hh)
    cwv = inp["conv_w"][0][:, ch]
    scv = inp["state_conv"][0][4 * b:4 * b + 4][:, :, ch]
    hs = slice(4 * hh, 4 * hh + 4)
    return {
        "xp": f(inp["x_prompt"][b]),
        "xs": f(inp["x_sample"][4 * b:4 * b + 4].reshape(64, D)),
        "w_in": f(inp["w_in"][0][:, _perm(hh)]),
        "w_out": f(inp["w_out"][0][_WOUT_ROWS]),
        "nw": f(inp["norm_w"][0].reshape(8, 128).T),
        "cw": f(cwv.reshape(4, 3 * NP, 128).transpose(2, 1, 0)),
        "alog": f(np.broadcast_to(inp["gdn_A_log"][0][None, hs], (128, HA))),
        "dtb": f(np.broadcast_to(inp["gdn_dt_bias"][0][None, hs], (128, HA))),
        "gnw": f(np.tile(inp["gdn_norm_w"][0], 2).reshape(128, 1)),
        "hnw": f(inp["hgrn_norm_w"][0].reshape(128, 1)),
        "lbl": f(inp["hgrn_lb_logits"][:, hh * 256:(hh + 1) * 256].reshape(2, HB, 128).transpose(2, 1, 0)),
        "fnw": f(np.broadcast_to(inp["final_norm_w"][None, :], (128, D))),
        "sc": f(scv.reshape(4, 3, 3 * NP, 128).transpose(3, 2, 0, 1)),
        "sg": f(inp["state_gdn"][0][4 * b:4 * b + 4][:, hs].reshape(4, NP, 128, 64)),
        "sh": f(inp["state_hgrn"][0][4 * b:4 * b + 4][:, 2 * hh:2 * hh + 2]),
    }


_CACHE = {}


def kernel(**inputs):
    inp = {k: np.asarray(v) for k, v in inputs.items()}
    Bp, T, _ = inp["x_prompt"].shape
    assert Bp == 4 and inp["x_sample"].shape[:2] == (16, 16)
    if T not in _CACHE:
        _CACHE[T] = build(T)[0]
    nc = _CACHE[T]
    in_maps = [_core_inputs(c, inp) for c in range(8)]
    res = run_bass_kernel_spmd(nc, in_maps, core_ids=list(range(8)))
    r = res.results
    y_prompt = np.stack([r[2 * b]["yp"] for b in range(4)]).astype(np.float32)
    y_sample = np.concatenate([r[2 * b]["ys"].reshape(4, 16, D) for b in range(4)]).astype(np.float32)
    ncp_ = np.zeros((1, 4, 3, 1536), np.float32); ncs_ = np.zeros((1, 16, 3, 1536), np.float32)
    ngp_ = np.zeros((1, 4, 8, 64, 64), np.float32); ngs_ = np.zeros((1, 16, 8, 64, 64), np.float32)
    nhp_ = np.zeros((1, 4, 4, 128, 128), np.float32); nhs_ = np.zeros((1, 16, 4, 128, 128), np.float32)
    cvt = lambda a: a.transpose(2, 3, 1, 0).reshape(a.shape[2], 3, 3 * NP * 128)
    for c in range(8):
        b, hh = c // 2, c % 2
        ch = _chan(hh)
        ncp_[0, b][:, ch] = cvt(r[c]["ncp"])[0]
        ncs_[0, 4 * b:4 * b + 4][:, :, ch] = cvt(r[c]["ncs"])
        ngp_[0, b, 4 * hh:4 * hh + 4] = r[c]["ngp"].reshape(HA, 64, 64)
        ngs_[0, 4 * b:4 * b + 4, 4 * hh:4 * hh + 4] = r[c]["ngs"].reshape(4, HA, 64, 64)
        nhp_[0, b, 2 * hh:2 * hh + 2] = r[c]["nhp"].reshape(HB, 128, 128)
        nhs_[0, 4 * b:4 * b + 4, 2 * hh:2 * hh + 2] = r[c]["nhs"].reshape(4, HB, 128, 128)
    return (y_prompt, y_sample, ncp_, ngp_, nhp_, ncs_, ngs_, nhs_)
```

```python
import numpy as np
import concourse.bass as bass
import concourse.mybir as mybir
from concourse.bass_utils import run_bass_kernel_spmd

F32 = mybir.dt.float32
BF16 = mybir.dt.bfloat16
AF = mybir.ActivationFunctionType
ALU = mybir.AluOpType

D = 1024
HA, HB = 4, 2
NP = HA // 2
NT = 4 * NP + 3 * HB
C_HI = NT * 128
C_G = C_HI + HB * 128
NCOL = C_G + 2 * HA
EPS = 1e-6
RG = [[0, 1], [2, 3], [4, 5], [6, 7]]


SAME_ENGINE_WAIT = True
SCHED_MODE = 0
SCHED_DELTA = 300.0
SCHED_WIN = 24


class _Proxy:
    def __getattr__(self, name):
        return lambda *a, **k: (name, a, k)


_PROXY = _Proxy()
PSUM_KEYS = {"B0", "B1", "B2", "B3", "B4", "B5", "B6", "BT"}


class Sched:
    def __init__(self, nc):
        self.nc = nc
        self.eng = {"pe": nc.tensor, "dve": nc.vector, "act": nc.scalar, "pool": nc.gpsimd, "sp": nc.sync}
        self.sem = {k: nc.alloc_semaphore(name=f"s_{k}") for k in self.eng}
        self.cnt = {k: 0 for k in self.eng}
        self.seen = {k: {} for k in self.eng}
        self.last_w = {}
        self.readers = {}
        self.dma_sems = {}
        self.n_wait = 0
        self.n_ops = 0
        self.rec = None

    def coll(self, ins, outs, reads=(), writes=()):
        if self.rec is not None:
            self.rec.append(("coll", "pool", ins, outs, tuple(reads), tuple(writes)))
            return None
        self._deps("pool", reads, writes)
        if "cc" not in self.dma_sems:
            self.dma_sems["cc"] = [self.nc.alloc_semaphore(name="cc_sem"), 0]
        ent = self.dma_sems["cc"]
        ent[1] += 1
        self.nc.gpsimd.collective_compute("AllGather", ALU.bypass, replica_groups=RG, ins=ins, outs=outs).then_inc(ent[0], 1)
        tok = ("cc", ent[0], ent[1])
        self._commit(tok, reads, writes)
        self.n_ops += 1
        return tok

    def emit(self, r):
        if r[0] == "coll":
            self.coll(r[2], r[3], r[4], r[5])
            return
        if r[0] == "op":
            _, e, call, reads, writes = r
            self.op(e, lambda eng: getattr(eng, call[0])(*call[1], **call[2]), reads, writes)
        else:
            _, q, out, in_, reads, writes, slot = r
            self.dma(q, out, in_, reads, writes, slot)

    def _cost(self, r):
        if r[0] == "dma":
            return 2500.0
        if r[0] == "coll":
            return 30000.0
        _, e, call, reads, writes = r
        name, args, kw = call
        def nfree(ap):
            try:
                sh = list(ap.shape)
                n = 1
                for d in sh[1:]:
                    n *= int(d)
                return n
            except Exception:
                return 256
        if e == "pe":
            ap = kw.get("rhs", None) if name == "matmul" else kw.get("in_", None)
            n = nfree(ap) if ap is not None else 64
            c = 32.0 + 0.4 * n
            try:
                if name == "matmul" and kw["rhs"].dtype == F32:
                    c *= 3.0
            except Exception:
                pass
            return c
        ap = kw.get("out", None)
        n = nfree(ap) if ap is not None else 256
        if e == "dve":
            return 70.0 + 1.0 * n
        if e == "act":
            return 130.0 + 0.9 * n
        return 110.0 + 1.8 * n

    def _est_start(self, r):
        if r[0] in ("dma", "coll"):
            e, reads, writes = r[1], r[4], r[5]
        else:
            e, reads, writes = r[1], r[3], r[4]
        m = self.model
        t = m["eng"].get(e, 0.0)
        ex = [k for k in reads if k in PSUM_KEYS]
        for k in reads:
            w = m["w"].get(k)
            if w is not None:
                t = max(t, w[0] + ((0.0 if e == "pe" else 120.0) if w[1] == e else 160.0))
        for k in list(writes) + ex:
            w = m["w"].get(k)
            if w is not None:
                t = max(t, w[0] + ((0.0 if e == "pe" else 120.0) if w[1] == e else 160.0))
            for (tt, ee) in m["r"].get(k, {}).values():
                t = max(t, tt + ((0.0 if e == "pe" else 120.0) if ee == e else 160.0))
        return t

    def _model_commit(self, r, t0):
        if r[0] in ("dma", "coll"):
            e, reads, writes = r[1], r[4], r[5]
            eng_busy = 100.0
        else:
            e, reads, writes = r[1], r[3], r[4]
            eng_busy = None
        m = self.model
        c = self._cost(r)
        t1 = t0 + c
        m["eng"][e] = t0 + (eng_busy if eng_busy is not None else c)
        ex = [k for k in reads if k in PSUM_KEYS]
        who = e if r[0] == "op" else "dma"
        for k in reads:
            m["r"].setdefault(k, {})[who] = (t1, who)
        for k in list(writes) + ex:
            m["w"][k] = (t1, who)
            m["r"][k] = {}
        m["t"] = max(m.get("t", 0.0), t1)

    def dag_emit(self, nodes):
        if not hasattr(self, "model"):
            self.model = {"eng": {}, "w": {}, "r": {}, "t": 0.0}
        names = {n[0] for n in nodes}
        units, deps, prio, preds = {}, {}, {}, {}
        def rw(r):
            if r[0] in ("dma", "coll"):
                reads, writes = r[4], r[5]
            else:
                reads, writes = r[3], r[4]
            ex = [k for k in reads if k in PSUM_KEYS]
            return list(reads), list(writes) + ex
        for (name, ops, dp, pr) in nodes:
            u = []
            for r in ops:
                glued = (r[0] == "op" and r[2][0] == "matmul" and r[2][2].get("start") is False)
                if glued and u:
                    u[-1].append(r)
                else:
                    u.append([r])
            units[name] = u
            deps[name] = {d for d in dp if d in names}
            prio[name] = pr
            lw, rd, pl = {}, {}, []
            for j, unit in enumerate(u):
                p = set()
                R, W = [], []
                for r in unit:
                    a_, b_ = rw(r)
                    R += a_; W += b_
                for k in R:
                    if k in lw:
                        p.add(lw[k])
                for k in W:
                    if k in lw:
                        p.add(lw[k])
                    p |= rd.get(k, set())
                p.discard(j)
                pl.append(p)
                for k in R:
                    rd.setdefault(k, set()).add(j)
                for k in W:
                    lw[k] = j
                    rd[k] = set()
            preds[name] = pl
        emitted = {n: [False] * len(units[n]) for n in units}
        nleft = {n: len(units[n]) for n in units}
        lo = {n: 0 for n in units}
        done = {n for n in units if not units[n]}
        waiting = [n for n in units if n not in done]
        active = []
        def refresh():
            nonlocal waiting
            still = []
            for n in waiting:
                if deps[n] <= done:
                    active.append(n)
                else:
                    still.append(n)
            waiting = still
        refresh()
        WIN = SCHED_WIN
        while active:
            best, bsel = None, None
            for n in active:
                em, pl, u = emitted[n], preds[n], units[n]
                j = lo[n]
                seen = 0
                while j < len(u) and seen < WIN:
                    if not em[j]:
                        seen += 1
                        if all(em[q] for q in pl[j]):
                            t = self._est_start(u[j][0])
                            key = (t, prio[n], j)
                            if best is None or key < best:
                                best, bsel = key, (n, j)
                    j += 1
            n, j = bsel
            for r in units[n][j]:
                t0 = self._est_start(r)
                self._model_commit(r, t0)
                self.emit(r)
            emitted[n][j] = True
            nleft[n] -= 1
            while lo[n] < len(units[n]) and emitted[n][lo[n]]:
                lo[n] += 1
            if nleft[n] == 0:
                active.remove(n)
                done.add(n)
                refresh()
        assert not waiting, ("DAG deadlock", waiting[:5])

    def merge_emit(self, streams):
        if not hasattr(self, "model"):
            self.model = {"eng": {}, "w": {}, "r": {}, "t": 0.0}
        units = []
        for st in streams:
            u = []
            for r in st:
                glued = (r[0] == "op" and r[2][0] == "matmul" and r[2][2].get("start") is False)
                if glued and u:
                    u[-1].append(r)
                else:
                    u.append([r])
            units.append(u)
        pos = [0] * len(units)
        while True:
            best, bi = None, -1
            for i, u in enumerate(units):
                if pos[i] < len(u):
                    t = self._est_start(u[pos[i]][0])
                    key = (t, -(len(u) - pos[i]))
                    if best is None or key < best:
                        best, bi = key, i
            if bi < 0:
                break
            for r in units[bi][pos[bi]]:
                t0 = self._est_start(r)
                self._model_commit(r, t0)
                self.emit(r)
            pos[bi] += 1


    def _wait(self, e, tok):
        name, sem, val = tok
        if name == "pe" and e == "pe":
            return
        if name == e and not SAME_ENGINE_WAIT:
            return
        if self.seen[e].get(name, 0) >= val:
            return
        self.eng[e].wait_ge(sem, val)
        self.seen[e][name] = val
        self.n_wait += 1

    def _deps(self, e, reads, writes):
        for k in reads:
            t = self.last_w.get(k)
            if t is not None:
                self._wait(e, t)
        for k in writes:
            t = self.last_w.get(k)
            if t is not None:
                self._wait(e, t)
            for t in self.readers.get(k, {}).values():
                self._wait(e, t)

    def _commit(self, tok, reads, writes):
        for k in reads:
            self.readers.setdefault(k, {})[tok[0]] = tok
        for k in writes:
            self.last_w[k] = tok
            self.readers[k] = {}

    def op(self, e, fn, reads=(), writes=()):
        if self.rec is not None:
            self.rec.append(("op", e, fn(_PROXY), tuple(reads), tuple(writes)))
            return None
        ex = [k for k in reads if k in PSUM_KEYS]
        if ex:
            writes = list(writes) + ex
        self._deps(e, reads, writes)
        ins = fn(self.eng[e])
        self.cnt[e] += 1
        ins.then_inc(self.sem[e], 1)
        tok = (e, self.sem[e], self.cnt[e])
        self._commit(tok, reads, writes)
        self.n_ops += 1
        return tok

    def dma(self, q, out, in_, reads=(), writes=(), slot=None):
        if self.rec is not None:
            self.rec.append(("dma", q, out, in_, tuple(reads), tuple(writes), slot))
            return None
        self._deps(q, reads, writes)
        slot = slot or (writes[0] if writes else reads[0])
        sname = f"d_{slot}"
        if sname not in self.dma_sems:
            self.dma_sems[sname] = [self.nc.alloc_semaphore(name=sname), 0]
        ent = self.dma_sems[sname]
        ent[1] += 16
        self.eng[q].dma_start(out=out, in_=in_).then_inc(ent[0], 16)
        tok = (sname, ent[0], ent[1])
        self._commit(tok, reads, writes)
        self.n_ops += 1
        return tok

    def finish(self, e="sp"):
        for k, t in list(self.last_w.items()):
            self._wait(e, t)


def bc(ap, shape):
    return ap.to_broadcast(list(shape))


def build(T, TB=256):
    nc = bass.Bass("TRN2", target_bir_lowering=False)
    s = Sched(nc)
    dt_in = lambda n, sh: nc.dram_tensor(n, list(sh), F32, kind="ExternalInput").ap()
    dt_out = lambda n, sh: nc.dram_tensor(n, list(sh), F32, kind="ExternalOutput").ap()
    xp = dt_in("xp", [T, D]); xs = dt_in("xs", [64, D])
    w_in = dt_in("w_in", [D, NCOL]); w_out = dt_in("w_out", [D, D])
    nw_d = dt_in("nw", [128, 8]); cw_d = dt_in("cw", [128, 3 * NP, 4])
    alog_d = dt_in("alog", [128, HA]); dtb_d = dt_in("dtb", [128, HA])
    gnw_d = dt_in("gnw", [128, 1]); hnw_d = dt_in("hnw", [128, 1])
    lbl_d = dt_in("lbl", [128, HB, 2]); fnw_d = dt_in("fnw", [128, D])
    sc_d = dt_in("sc", [128, 3 * NP, 4, 3])
    sg_d = dt_in("sg", [4, NP, 128, 64]); sh_d = dt_in("sh", [4, HB, 128, 128])
    yp = dt_out("yp", [T, D]); ys = dt_out("ys", [64, D])
    ncp = dt_out("ncp", [128, 3 * NP, 1, 3]); ngp = dt_out("ngp", [1, NP, 128, 64]); nhp = dt_out("nhp", [1, HB, 128, 128])
    ncs = dt_out("ncs", [128, 3 * NP, 4, 3]); ngs = dt_out("ngs", [4, NP, 128, 64]); nhs = dt_out("nhs", [4, HB, 128, 128])

    NL = NP + HB
    xsrc = [nc.dram_tensor(f"xsrc{i}", [NL * 128, TB], BF16).ap() for i in range(2)]
    xdst = [nc.dram_tensor(f"xdst{i}", [2 * NL * 128, TB], BF16).ap() for i in range(2)]
    xsrc_s = nc.dram_tensor("xsrc_s", [NL * 128, 64], BF16).ap()
    xdst_s = nc.dram_tensor("xdst_s", [2 * NL * 128, 64], BF16).ap()
    sb = lambda n, sh, d=F32: nc.alloc_sbuf_tensor(n, list(sh), d)
    Wb = sb("Wb", [128, 8, NCOL], BF16)
    WOb = sb("WOb", [128, 8, D], BF16)
    nw = sb("nw_t", [128, 8]); cw = sb("cw_t", [128, 3 * NP, 4])
    alog = sb("alog_t", [128, HA]); dtb = sb("dtb_t", [128, HA]); negA = sb("negA", [128, HA])
    gnw = sb("gnw_t", [128, 1]); hnw = sb("hnw_t", [128, 1])
    lbl = sb("lbl_t", [128, HB, 2]); lb = sb("lb", [128, HB]); oml = sb("oml", [128, HB])
    fnw = sb("fnw_t", [128, D])
    identb = sb("identb", [128, 128], BF16); identf = sb("identf", [128, 128])
    ones = sb("ones", [128, 128]); ob2 = sb("ob2", [128, 128])
    fgt = sb("fgt", [128, 128]); fle = sb("fle", [128, 128])
    I_s = sb("I_s", [128, 64]); U_s = sb("U_s", [128, 64]); Tri_s = sb("Tri_s", [128, 64]); Mc_s = sb("Mc_s", [128, 64])
    halo = sb("halo", [128, 3 * NP, 4, 3])
    Sg = sb("Sg", [128, NP, 64]); Sh = sb("Sh", [128, HB, 128])
    W_ = TB
    xt = [sb(f"xt{i}", [128, D]) for i in range(4)]
    sqj = sb("sqj", [128, D], BF16)
    xb = sb("xb", [128, D], BF16)
    hT_all = [sb(f"hT{i}", [128, 8, W_], BF16) for i in range(2)]
    sqjO = sb("sqjO", [128, D], BF16); ssO = sb("ssO", [128, 1]); rrO = sb("rrO", [128, 1])
    ss = sb("ss", [128, 1]); rr = sb("rr", [128, 1])
    raw = sb("raw", [128, 3, W_ + 12])
    cv = sb("cv", [128, 3, W_])
    tmp = [None, sb("tmp1", [128, W_]), sb("tmp2", [128, W_]), None]
    za = [sb(f"za{i}", [128, W_]) for i in range(2)]
    cvb = [sb(f"cvb{i}", [128, 3, W_], BF16) for i in range(2)]
    tmpA = [sb(f"tmpA{i}", [128, W_]) for i in range(3)]
    sqA = [sb(f"sqA{i}", [128, W_], BF16) for i in range(2)]
    tnA = [sb(f"tnA{i}", [128, W_]) for i in range(2)]
    I_sb = sb("I_sb", [128, 64], BF16); ob2b = sb("ob2b", [128, 128], BF16); onesb = sb("onesb", [128, 128], BF16)
    Sgb = sb("Sgb", [128, NP, 64], BF16); Shb = sb("Shb", [128, HB, 128], BF16)
    sqb = sb("sqb", [128, W_], BF16); sqbH = sb("sqbH", [128, W_], BF16)
    G_all = [sb(f"G{i}", [128, 4, 2 * HA]) for i in range(2)]; Gb_all = [sb(f"Gb{i}", [128, 4, HA]) for i in range(2)]; Gg_all = [sb(f"Gg{i}", [128, 4, HA]) for i in range(2)]
    gs_all = [sb(f"gs{i}", [128, NP, 4]) for i in range(2)]; bs_all = [sb(f"bs{i}", [128, NP, 4]) for i in range(2)]; nbs_all = [sb(f"nbs{i}", [128, NP, 4]) for i in range(2)]
    gc_all = [sb(f"gc{i}", [128, NP, 4]) for i in range(2)]; gl_all = [sb(f"gl{i}", [128, NP, 4]) for i in range(2)]; egc_all = [sb(f"egc{i}", [128, NP, 4]) for i in range(2)]
    dk_all = [sb(f"dk{i}", [128, NP, 4]) for i in range(2)]; bge_all = [sb(f"bge{i}", [128, NP, 4]) for i in range(2)]
    rhsD = sb("rhsD", [128, W_]); Dg = sb("Dg", [128, W_])
    Ee = sb("Ee", [128, W_]); Dm = sb("Dm", [128, W_]); Ds = sb("Ds", [128, W_])
    EBs = [sb(f"EBs{i}", [128, W_]) for i in range(2)]
    P0t = [sb(f"P0t{i}", [128, W_], BF16) for i in range(2)]
    PT0t = [sb(f"PT0t{i}", [128, W_], BF16) for i in range(2)]
    R0t = [sb(f"R0t{i}", [128, W_], BF16) for i in range(2)]
    P = [sb(f"P{i}", [128, W_], BF16) for i in range(2)]
    PT = [sb(f"PT{i}", [128, W_], BF16) for i in range(2)]
    R = [sb(f"R{i}", [128, W_], BF16) for i in range(2)]
    attn = sb("attn", [128, W_], BF16); attnT = [sb(f"attnT{i}", [128, W_], BF16) for i in range(2)]
    Kbe = [sb(f"Kbe{i}", [128, 4, 64], BF16) for i in range(2)]; Kd = [sb(f"Kd{i}", [128, 4, 64], BF16) for i in range(2)]; bV = [sb(f"bV{i}", [128, 4, 64], BF16) for i in range(2)]
    u = sb("u", [128, 4, 64]); wT = sb("wT", [128, W_], BF16); QeT = [sb(f"QeT{i}", [128, W_], BF16) for i in range(2)]
    vn = sb("vn", [128, 64], BF16)
    oTf = sb("oTf", [128, W_])
    oTn = [sb(f"oTn_{i}", [128, 2 * NL, W_], BF16) for i in range(2)]
    oOwn = [sb(f"oOwn_{i}", [128, NL, W_], BF16) for i in range(2)]
    qb = sb("qb", [128, W_]); ff = sb("ff", [128, W_]); lf = sb("lf", [128, W_]); kb = sb("kb", [128, W_])
    bb = sb("bb", [128, W_]); bl = sb("bl", [128, W_])
    Qe = sb("Qe", [128, W_], BF16); Qx = sb("Qx", [128, W_], BF16); Kdh = sb("Kdh", [128, W_], BF16)
    Qef = sb("Qef", [128, W_]); Qxf = sb("Qxf", [128, W_]); Kdf = sb("Kdf", [128, W_])
    ebl = sb("ebl", [128, 4]); zb = sb("zb", [128, W_])
    vtok = sb("vtok", [64, 4, 128], BF16); Kdt = sb("Kdt", [64, 4, 128], BF16); aTh = sb("aTh", [64, W_], BF16)
    smask_all = [sb(f"smask{i}", [128, W_]) for i in range(2)]
    tmpH = [None, sb("tmpH1", [128, W_])]; oTfH = sb("oTfH", [128, W_])
    yo = sb("yo", [128, D]); yo2 = sb("yo2", [128, D])
    pb = [nc.alloc_psum_tensor(f"pb{i}", [128, 512], F32) for i in range(8)]
    pT2 = pb[2][:, 0:128].bitcast(BF16).rearrange("p (k t) -> p k t", t=128)

    def aff(out, cmp, fill_in, step=-1, cm=1, base=0):
        s.op("pool", lambda e: e.memset(out[:], fill_in), writes=[out.name])
        s.op("pool", lambda e: e.affine_select(out=out[:], in_=out[:], pattern=[[step, 128]], compare_op=cmp,
                                               fill=0.0, base=base, channel_multiplier=cm), reads=[out.name], writes=[out.name])
    aff(identf, ALU.is_equal, 1.0)
    aff(fgt, ALU.is_gt, 1.0)
    aff(fle, ALU.is_gt, 1.0, step=1, cm=-1, base=1)
    s.op("pool", lambda e: e.memset(ones[:], 1.0), writes=["ones"])
    s.op("pool", lambda e: e.memset(ob2[:], 0.0), writes=["ob2"])
    for h in range(2):
        sl = slice(64 * h, 64 * h + 64)
        s.op("pool", lambda e: e.memset(ob2[sl, sl], 1.0), reads=["ob2"], writes=["ob2"])
    s.op("dve", lambda e: e.tensor_copy(out=identb[:], in_=identf[:]), reads=["identf"], writes=["identb"])
    for (dst, src) in ((I_s, identf), (U_s, fgt), (Tri_s, fle)):
        for h in range(2):
            sl = slice(64 * h, 64 * h + 64)
            s.op("dve", lambda e: e.tensor_copy(out=dst[sl, :], in_=src[sl, sl]), reads=[src.name], writes=[dst.name])
    s.op("dve", lambda e: e.tensor_tensor(out=Mc_s[:], in0=U_s[:], in1=I_s[:], op=ALU.add), reads=["U_s", "I_s"], writes=["Mc_s"])
    s.op("dve", lambda e: e.tensor_copy(out=I_sb[:], in_=I_s[:]), reads=["I_s"], writes=["I_sb"])
    s.op("dve", lambda e: e.tensor_copy(out=ob2b[:], in_=ob2[:]), reads=["ob2"], writes=["ob2b"])
    s.op("dve", lambda e: e.tensor_copy(out=onesb[:], in_=ones[:]), reads=["ones"], writes=["onesb"])
    for t_, d_ in ((nw, nw_d), (cw, cw_d), (alog, alog_d), (dtb, dtb_d), (gnw, gnw_d), (hnw, hnw_d), (lbl, lbl_d), (fnw, fnw_d)):
        s.dma("sp", t_[:], d_, writes=[t_.name])
    s.op("act", lambda e: e.activation(out=negA[:], in_=alog[:], func=AF.Exp), reads=["alog_t"], writes=["negA"])
    s.op("dve", lambda e: e.tensor_scalar(out=negA[:], in0=negA[:], scalar1=-1.0, scalar2=None, op0=ALU.mult), reads=["negA"], writes=["negA"])
    s.op("dve", lambda e: e.tensor_tensor(out=lb[:], in0=lbl[:, :, 1], in1=lbl[:, :, 0], op=ALU.subtract), reads=["lbl_t"], writes=["lb"])
    s.op("act", lambda e: e.activation(out=lb[:], in_=lb[:], func=AF.Exp), reads=["lb"], writes=["lb"])
    s.op("dve", lambda e: e.tensor_scalar(out=lb[:], in0=lb[:], scalar1=1.0, scalar2=None, op0=ALU.add), reads=["lb"], writes=["lb"])
    s.op("dve", lambda e: e.reciprocal(out=lb[:], in_=lb[:]), reads=["lb"], writes=["lb"])
    s.op("dve", lambda e: e.tensor_scalar(out=oml[:], in0=lb[:], scalar1=-1.0, scalar2=1.0, op0=ALU.mult, op1=ALU.add), reads=["lb"], writes=["oml"])
    w_in_v = w_in.rearrange("(k p) n -> p k n", p=128)
    stgs = [(xt[0], "xt0"), (xt[1], "xt1"), (yo, "yo"), (yo2, "yo2")]
    q = 0
    for k in range(8):
        pieces = [(i * 1024, 1024, stgs[i][0], stgs[i][1]) for i in range(NCOL // 1024)] + [((NCOL // 1024) * 1024, NCOL % 1024, Ee, "Ee")]
        for (c0, cn, tl, key) in pieces:
            s.dma("sp", tl[:, 0:cn], w_in_v[:, k, c0:c0 + cn], writes=[key])
            if q % 2 == 0:
                s.op("dve", lambda e: e.tensor_scalar(out=Wb[:, k, c0:c0 + cn], in0=tl[:, 0:cn], scalar1=nw[:, k:k + 1], scalar2=None, op0=ALU.mult),
                     reads=[key, "nw_t"], writes=["Wb"])
            else:
                s.op("act", lambda e: e.activation(out=Wb[:, k, c0:c0 + cn], in_=tl[:, 0:cn], func=AF.Copy, scale=nw[:, k:k + 1]),
                     reads=[key, "nw_t"], writes=["Wb"])
            q += 1
    w_out_v = w_out.rearrange("(k p) n -> p k n", p=128)
    for k in range(8):
        tl, key = stgs[k % 4]
        s.dma("sp", tl[:, :], w_out_v[:, k, :], writes=[key])
        if k % 2 == 0:
            s.op("dve", lambda e: e.tensor_copy(out=WOb[:, k, :], in_=tl[:, :]), reads=[key], writes=["WOb"])
        else:
            s.op("act", lambda e: e.activation(out=WOb[:, k, :], in_=tl[:, :], func=AF.Copy), reads=[key], writes=["WOb"])

    def block(x_src, y_dst, t0, nseg, seglen, c, first, is_sample, bpar=0):
        hT = hT_all[bpar]; G = G_all[bpar]; Gb = Gb_all[bpar]; Gg = Gg_all[bpar]; gs = gs_all[bpar]; bs = bs_all[bpar]; nbs = nbs_all[bpar]; gc = gc_all[bpar]; gl = gl_all[bpar]; egc = egc_all[bpar]; dk = dk_all[bpar]; bge = bge_all[bpar]; smask = smask_all[bpar]
        TBk = nseg * seglen
        nch = TBk // c
        cps = seglen // c
        TT = min(128, TBk)
        ntt = TBk // TT
        nlev = {64: 5, 16: 3}[c]
        v3 = lambda ap: ap.rearrange("p (n c) -> p n c", c=c)

        def ph0_x():
            for tt in range(ntt):
                X = xt[bpar * 2 + tt]
                s.dma("sp", X[0:TT, :], x_src[t0 + tt * TT: t0 + (tt + 1) * TT, :], writes=[X.name])
                s.op("act", lambda e: e.activation(out=sqj[0:TT, :], in_=X[0:TT, :], func=AF.Square, accum_out=ss[0:TT, :]),
                     reads=[X.name], writes=["sqj", "ss"])
                s.op("act", lambda e: e.activation(out=rr[0:TT, :], in_=ss[0:TT, :], func=AF.Ln, scale=1.0 / D, bias=EPS), reads=["ss"], writes=["rr"])
                s.op("act", lambda e: e.activation(out=rr[0:TT, :], in_=rr[0:TT, :], func=AF.Exp, scale=-0.5), reads=["rr"], writes=["rr"])
                s.op("dve", lambda e: e.tensor_scalar(out=xb[0:TT, :], in0=X[0:TT, :], scalar1=rr[0:TT, :], scalar2=None, op0=ALU.mult),
                     reads=[X.name, "rr"], writes=["xb"])
                for kk in range(4):
                    for j in range(2):
                        k = 2 * kk + j
                        s.op("pe", lambda e: e.transpose(out=pT2[:, j, 0:TT], in_=xb[0:TT, k * 128:(k + 1) * 128], identity=identb[0:TT, 0:TT]),
                             reads=["xb", "identb"], writes=["B2"])
                    if kk % 2 == 0:
                        s.op("dve", lambda e: e.tensor_copy(out=hT[:, 2 * kk:2 * kk + 2, tt * TT:(tt + 1) * TT], in_=pT2[:, :, 0:TT]), reads=["B2"], writes=[f"hT{bpar}"])
                    else:
                        s.op("act", lambda e: e.activation(out=hT[:, 2 * kk:2 * kk + 2, tt * TT:(tt + 1) * TT], in_=pT2[:, :, 0:TT], func=AF.Copy), reads=["B2"], writes=[f"hT{bpar}"])

        def silu_from(src, srckeys, dst, dstkey, scr, scrkey, W, outdt_note=None):
            s.op("act", lambda e: e.activation(out=scr, in_=src, func=AF.Exp, scale=-1.0), reads=srckeys, writes=[scrkey])
            s.op("act", lambda e: e.activation(out=scr, in_=scr, func=AF.Ln, bias=1.0), reads=[scrkey], writes=[scrkey])
            s.op("act", lambda e: e.activation(out=scr, in_=scr, func=AF.Exp, scale=-1.0), reads=[scrkey], writes=[scrkey])
            s.op("dve", lambda e: e.tensor_tensor(out=dst, in0=src, in1=scr, op=ALU.mult), reads=list(srckeys) + [scrkey], writes=[dstkey])


        def inproj_fm(ct, i=0):
            key = "B0"
            out = pb[0][:, i * 256: i * 256 + TBk]
            for k in range(8):
                s.op("pe", lambda e: e.matmul(out, lhsT=Wb[:, k, ct * 128:(ct + 1) * 128], rhs=hT[:, k, 0:TBk], start=(k == 0), stop=(k == 7)),
                     reads=["Wb", f"hT{bpar}"], writes=[key])
            return out, key

        def ph0_g():
            for ch in range(nch):
                for h in range(2):
                    out = pb[2][64 * h:64 * h + c, ch * 2 * HA:(ch + 1) * 2 * HA]
                    for k in range(8):
                        s.op("pe", lambda e: e.matmul(out, lhsT=hT[:, k, ch * c:(ch + 1) * c], rhs=Wb[:, k, C_G:C_G + 2 * HA], start=(k == 0), stop=(k == 7)),
                             reads=["Wb", f"hT{bpar}"], writes=["B2"])
            Gv = G[:, 0:nch, :]
            s.op("dve", lambda e: e.tensor_copy(out=Gv, in_=pb[2][:, 0:nch * 2 * HA].rearrange("p (n g) -> p n g", g=2 * HA)), reads=["B2"], writes=[f"G{bpar}"])
            Gbv = Gb[:, 0:nch, :]; Ggv = Gg[:, 0:nch, :]
            s.op("act", lambda e: e.activation(out=Gbv, in_=Gv[:, :, 0:HA], func=AF.Exp, scale=-1.0), reads=[f"G{bpar}"], writes=[f"Gb{bpar}"])
            s.op("act", lambda e: e.activation(out=Gbv, in_=Gbv, func=AF.Ln, bias=1.0), reads=[f"Gb{bpar}"], writes=[f"Gb{bpar}"])
            s.op("act", lambda e: e.activation(out=Gbv, in_=Gbv, func=AF.Exp, scale=-1.0), reads=[f"Gb{bpar}"], writes=[f"Gb{bpar}"])
            s.op("dve", lambda e: e.tensor_tensor(out=Ggv, in0=Gv[:, :, HA:2 * HA], in1=bc(dtb[:, None, :], [128, nch, HA]), op=ALU.add),
                 reads=[f"G{bpar}", "dtb_t"], writes=[f"Gg{bpar}"])
            s.op("act", lambda e: e.activation(out=Ggv, in_=Ggv, func=AF.Exp), reads=[f"Gg{bpar}"], writes=[f"Gg{bpar}"])
            s.op("act", lambda e: e.activation(out=Ggv, in_=Ggv, func=AF.Ln, bias=1.0), reads=[f"Gg{bpar}"], writes=[f"Gg{bpar}"])
            s.op("dve", lambda e: e.tensor_tensor(out=Ggv, in0=Ggv, in1=bc(negA[:, None, :], [128, nch, HA]), op=ALU.mult),
                 reads=[f"Gg{bpar}", "negA"], writes=[f"Gg{bpar}"])
            gsv = gs[:, :, 0:nch]; bsv = bs[:, :, 0:nch]; nbsv = nbs[:, :, 0:nch]
            gcv = gc[:, :, 0:nch]; glv = gl[:, :, 0:nch]; egcv = egc[:, :, 0:nch]; dkv = dk[:, :, 0:nch]; bgev = bge[:, :, 0:nch]
            for h in range(2):
                sl = slice(64 * h, 64 * h + 64)
                for (dst, src, kd, ks) in ((gs, Gg, f"gs{bpar}", f"Gg{bpar}"), (bs, Gb, f"bs{bpar}", f"Gb{bpar}")):
                    for p in range(NP):
                        s.op("dve", lambda e: e.tensor_copy(out=dst[sl, p, 0:nch], in_=src[sl, 0:nch, 2 * p + h]), reads=[ks], writes=[kd])
            s.op("dve", lambda e: e.tensor_scalar(out=nbsv, in0=bsv, scalar1=-1.0, scalar2=None, op0=ALU.mult), reads=[f"bs{bpar}"], writes=[f"nbs{bpar}"])
            for h in range(2):
                rs = slice(64 * h, 64 * h + c)
                s.op("pe", lambda e: e.matmul(pb[2][rs, 64:64 + NP * nch], lhsT=Tri_s[rs, 0:c], rhs=gs[rs, :, 0:nch], start=True, stop=True),
                     reads=["Tri_s", f"gs{bpar}"], writes=["B2"])
                s.op("pe", lambda e: e.matmul(pb[2][rs, 96:96 + NP * nch], lhsT=ones[rs, 0:c], rhs=gs[rs, :, 0:nch], start=True, stop=True),
                     reads=["ones", f"gs{bpar}"], writes=["B2"])
            s.op("dve", lambda e: e.tensor_copy(out=gcv, in_=pb[2][:, 64:64 + NP * nch].rearrange("p (a n) -> p a n", n=nch)), reads=["B2"], writes=[f"gc{bpar}"])
            s.op("dve", lambda e: e.tensor_copy(out=glv, in_=pb[2][:, 96:96 + NP * nch].rearrange("p (a n) -> p a n", n=nch)), reads=["B2"], writes=[f"gl{bpar}"])
            s.op("act", lambda e: e.activation(out=egcv, in_=gcv, func=AF.Exp), reads=[f"gc{bpar}"], writes=[f"egc{bpar}"])
            s.op("dve", lambda e: e.tensor_tensor(out=dkv, in0=glv, in1=gcv, op=ALU.subtract), reads=[f"gl{bpar}", f"gc{bpar}"], writes=[f"dk{bpar}"])
            s.op("act", lambda e: e.activation(out=dkv, in_=dkv, func=AF.Exp), reads=[f"dk{bpar}"], writes=[f"dk{bpar}"])
            s.op("dve", lambda e: e.tensor_tensor(out=bgev, in0=bsv, in1=egcv, op=ALU.mult), reads=[f"bs{bpar}", f"egc{bpar}"], writes=[f"bge{bpar}"])


        def phase0(_=None):
            ph0_x()
            ph0_g()
            ph0_m()

        def front(p):
            par = p % 2
            for i3 in range(3):
                ct = NP * i3 + p
                pp, pk = inproj_fm(ct)
                rv = raw[:, i3, 0:nseg * (seglen + 3)].rearrange("p (n c) -> p n c", c=seglen + 3)
                if first and not is_sample:
                    s.op("pool", lambda e: e.memset(rv[:, :, 0:3], 0.0), reads=[f"raw{i3}"], writes=[f"raw{i3}"])
                else:
                    s.op("pool", lambda e: e.tensor_copy(out=rv[:, :, 0:3], in_=halo[:, ct, 0:nseg, :]), reads=["halo"], writes=[f"raw{i3}"])
                s.op("act", lambda e: e.activation(out=rv[:, :, 3:3 + seglen], in_=pp.rearrange("p (n c) -> p n c", c=seglen), func=AF.Copy),
                     reads=[pk], writes=[f"raw{i3}"])
                s.op("pool", lambda e: e.tensor_copy(out=halo[:, ct, 0:nseg, :], in_=rv[:, :, seglen:seglen + 3]), reads=[f"raw{i3}"], writes=["halo"])
                cvv = cv[:, i3, 0:TBk].rearrange("p (n c) -> p n c", c=seglen)
                s.op("dve", lambda e: e.tensor_scalar(out=cvv, in0=rv[:, :, 0:seglen], scalar1=cw[:, ct, 0:1], scalar2=None, op0=ALU.mult),
                     reads=[f"raw{i3}", "cw_t"], writes=[f"cv{i3}"])
                for j in range(1, 4):
                    s.op("dve", lambda e: e.scalar_tensor_tensor(out=cvv, in0=rv[:, :, j:j + seglen], scalar=cw[:, ct, j:j + 1], in1=cvv,
                                                                 op0=ALU.mult, op1=ALU.add), reads=[f"raw{i3}", "cw_t", f"cv{i3}"], writes=[f"cv{i3}"])
                if i3 < 2:
                    silu_from(cv[:, i3, 0:TBk], [f"cv{i3}"], cv[:, i3, 0:TBk], f"cv{i3}", tmpA[i3][:, 0:TBk], f"tmpA{i3}", TBk)
                else:
                    silu_from(cv[:, i3, 0:TBk], [f"cv{i3}"], cvb[par][:, 2, 0:TBk], f"cvb{par}v", tmpA[i3][:, 0:TBk], f"tmpA{i3}", TBk)
            for i3 in range(2):
                src = cv[:, i3, 0:TBk]
                s.op("pool", lambda e: e.tensor_tensor(out=sqA[i3][:, 0:TBk], in0=src, in1=src, op=ALU.mult), reads=[f"cv{i3}"], writes=[f"sqA{i3}"])
                s.op("pe", lambda e: e.matmul(pb[7][:, i3 * 256:i3 * 256 + TBk], lhsT=ob2b[:], rhs=sqA[i3][:, 0:TBk], start=True, stop=True), reads=["ob2b", f"sqA{i3}"], writes=["BT"])
                s.op("act", lambda e: e.activation(out=tnA[i3][:, 0:TBk], in_=pb[7][:, i3 * 256:i3 * 256 + TBk], func=AF.Ln, bias=EPS), reads=["BT"], writes=[f"tnA{i3}"])
                s.op("act", lambda e: e.activation(out=tnA[i3][:, 0:TBk], in_=tnA[i3][:, 0:TBk], func=AF.Exp, scale=-0.5), reads=[f"tnA{i3}"], writes=[f"tnA{i3}"])
                if i3 == 0:
                    s.op("dve", lambda e: e.scalar_tensor_tensor(out=cvb[par][:, 0, 0:TBk], in0=src, scalar=0.125, in1=tnA[i3][:, 0:TBk], op0=ALU.mult, op1=ALU.mult),
                         reads=[f"cv{i3}", f"tnA{i3}"], writes=[f"cvb{par}q"])
                else:
                    s.op("dve", lambda e: e.tensor_tensor(out=cvb[par][:, 1, 0:TBk], in0=src, in1=tnA[i3][:, 0:TBk], op=ALU.mult), reads=[f"cv{i3}", f"tnA{i3}"], writes=[f"cvb{par}k"])
            pp, pk = inproj_fm(3 * NP + p)
            s.op("act", lambda e: e.activation(out=za[par][:, 0:TBk], in_=pp, func=AF.Copy), reads=[pk], writes=[f"za{par}"])
            silu_from(za[par][:, 0:TBk], [f"za{par}"], za[par][:, 0:TBk], f"za{par}", tmpA[0][:, 0:TBk], "tmpA0", TBk)
            qn = cvb[par][:, 0, 0:TBk]; kn = cvb[par][:, 1, 0:TBk]; vs = cvb[par][:, 2, 0:TBk]
            ckq, ckk, ckv = f"cvb{par}q", f"cvb{par}k", f"cvb{par}v"
            s.op("dve", lambda e: e.tensor_tensor(out=v3(rhsD[:, 0:TBk]), in0=bc(gs[:, p, 0:nch, None], [128, nch, c]), in1=bc(U_s[:, None, 0:c], [128, nch, c]), op=ALU.mult),
                 reads=[f"gs{bpar}", "U_s"], writes=["rhsD"])
            s.op("pool", lambda e: e.tensor_tensor(out=v3(Dg[:, 0:TBk]), in0=bc(egc[:, p, 0:nch, None], [128, nch, c]), in1=bc(I_s[:, None, 0:c], [128, nch, c]), op=ALU.mult),
                 reads=[f"egc{bpar}", "I_s"], writes=["Dg"])
            for h in range(2):
                rs = slice(64 * h, 64 * h + c)
                s.op("pe", lambda e: e.matmul(pb[6][rs, 0:TBk], lhsT=Tri_s[rs, 0:c], rhs=rhsD[rs, 0:TBk], start=True, stop=True),
                     reads=["Tri_s", "rhsD"], writes=["B6"])
            s.op("act", lambda e: e.activation(out=Ee[:, 0:TBk], in_=pb[6][:, 0:TBk], func=AF.Exp), reads=["B6"], writes=["Ee"])
            s.op("pool", lambda e: e.tensor_tensor(out=v3(Dm[:, 0:TBk]), in0=v3(Ee[:, 0:TBk]), in1=bc(Mc_s[:, None, 0:c], [128, nch, c]), op=ALU.mult),
                 reads=["Ee", "Mc_s"], writes=["Dm"])
            s.op("pool", lambda e: e.tensor_tensor(out=v3(Ds[:, 0:TBk]), in0=v3(Ee[:, 0:TBk]), in1=bc(U_s[:, None, 0:c], [128, nch, c]), op=ALU.mult),
                 reads=["Ee", "U_s"], writes=["Ds"])
            for h in range(2):
                rs = slice(64 * h, 64 * h + c)
                s.op("pe", lambda e: e.matmul(pb[6][64 * h:64 * h + 64, 0:TBk], lhsT=ones[rs, 0:64], rhs=Dg[rs, 0:TBk], start=True, stop=True),
                     reads=["ones", "Dg"], writes=["B6"])
            s.op("act", lambda e: e.activation(out=EBs[par][:, 0:TBk], in_=pb[6][:, 0:TBk], func=AF.Copy), reads=["B6"], writes=[f"EBs{par}"])
            s.op("dve", lambda e: e.tensor_tensor(out=QeT[par][:, 0:TBk], in0=qn, in1=EBs[par][:, 0:TBk], op=ALU.mult), reads=[ckq, f"EBs{par}"], writes=[f"QeT{par}"])
            for ch in range(nch):
                cs = slice(ch * c, (ch + 1) * c)
                for h in range(2):
                    fs = slice(64 * h, 64 * h + 64); rs = slice(64 * h, 64 * h + c)
                    s.op("pe", lambda e: e.matmul(pb[7][rs, cs], lhsT=kn[fs, cs], rhs=kn[fs, cs], start=True, stop=True), reads=[ckk], writes=["BT"])
                    s.op("pe", lambda e: e.matmul(pb[7][rs, 256 + ch * c:256 + (ch + 1) * c], lhsT=qn[fs, cs], rhs=kn[fs, cs], start=True, stop=True),
                         reads=[ckq, ckk], writes=["BT"])
            s.op("dve", lambda e: e.tensor_tensor(out=v3(tmp[2][:, 0:TBk]), in0=v3(pb[7][:, 0:TBk]), in1=bc(nbs[:, p, 0:nch, None], [128, nch, c]), op=ALU.mult),
                 reads=["BT", f"nbs{bpar}"], writes=["tmp2"])
            s.op("pool", lambda e: e.tensor_tensor(out=PT0t[par][:, 0:TBk], in0=tmp[2][:, 0:TBk], in1=Ds[:, 0:TBk], op=ALU.mult), reads=["tmp2", "Ds"], writes=[f"PT0t{par}"])
            s.op("dve", lambda e: e.tensor_tensor(out=attn[:, 0:TBk], in0=pb[7][:, 256:256 + TBk], in1=Dm[:, 0:TBk], op=ALU.mult), reads=["BT", "Dm"], writes=["attn"])

            for ch in range(nch):
                cs = slice(ch * c, (ch + 1) * c)
                for h in range(2):
                    fs = slice(64 * h, 64 * h + 64); rs = slice(64 * h, 64 * h + c)
                    s.op("pe", lambda e: e.matmul(pb[6][rs, cs], lhsT=PT0t[par][rs, cs], rhs=I_sb[rs, 0:c], start=True, stop=True), reads=[f"PT0t{par}", "I_sb"], writes=["B6"])
                    s.op("pe", lambda e: e.matmul(pb[6][rs, 256 + ch * c:256 + (ch + 1) * c], lhsT=attn[rs, cs], rhs=I_sb[rs, 0:c], start=True, stop=True),
                         reads=["attn", "I_sb"], writes=["B6"])
                    s.op("pe", lambda e: e.matmul(pb[7][rs, ch * 64:(ch + 1) * 64], lhsT=kn[fs, cs], rhs=I_sb[fs, 0:64], start=True, stop=True), reads=[ckk, ckv, "I_sb"], writes=["BT"])
                    s.op("pe", lambda e: e.matmul(pb[7][rs, 256 + ch * 64:256 + (ch + 1) * 64], lhsT=vs[fs, cs], rhs=I_sb[fs, 0:64], start=True, stop=True),
                         reads=[ckk, ckv, "I_sb"], writes=["BT"])
            s.op("act", lambda e: e.activation(out=P0t[par][:, 0:TBk], in_=pb[6][:, 0:TBk], func=AF.Copy), reads=["B6"], writes=[f"P0t{par}"])
            s.op("dve", lambda e: e.tensor_tensor(out=v3(R0t[par][:, 0:TBk]), in0=v3(pb[6][:, 0:TBk]), in1=bc(I_s[:, None, 0:c], [128, nch, c]), op=ALU.add),
                 reads=["B6", "I_s"], writes=[f"R0t{par}"])
            s.op("act", lambda e: e.activation(out=attnT[par][:, 0:TBk], in_=pb[6][:, 256:256 + TBk], func=AF.Copy), reads=["B6"], writes=[f"attnT{par}"])
            k4 = pb[7][:, 0:nch * 64].rearrange("p (n d) -> p n d", d=64)
            v4 = pb[7][:, 256:256 + nch * 64].rearrange("p (n d) -> p n d", d=64)
            s.op("dve", lambda e: e.tensor_tensor(out=Kbe[par][:, 0:nch, :], in0=k4, in1=bc(bge[:, p, 0:nch, None], [128, nch, 64]), op=ALU.mult), reads=["BT", f"bge{bpar}"], writes=[f"Kbe{par}"])
            s.op("dve", lambda e: e.tensor_tensor(out=Kd[par][:, 0:nch, :], in0=k4, in1=bc(dk[:, p, 0:nch, None], [128, nch, 64]), op=ALU.mult), reads=["BT", f"dk{bpar}"], writes=[f"Kd{par}"])
            s.op("dve", lambda e: e.tensor_tensor(out=bV[par][:, 0:nch, :], in0=v4, in1=bc(bs[:, p, 0:nch, None], [128, nch, 64]), op=ALU.mult), reads=["BT", f"bs{bpar}"], writes=[f"bV{par}"])
        def core(p):
            par = p % 2
            Pc, kP = P0t[par], f"P0t{par}"
            PTc, kPT = PT0t[par], f"PT0t{par}"
            Rc, kR = R0t[par], f"R0t{par}"
            for lev in range(1, nlev + 2):
                do_pow = lev <= nlev
                need_P = lev < nlev
                do_R = lev >= 2
                nP, nkP = P[lev % 2], f"P{lev % 2}"
                nPT, nkPT = PT[lev % 2], f"PT{lev % 2}"
                nR, nkR = R[lev % 2], f"R{lev % 2}"
                for ch in range(nch):
                    cs = slice(ch * c, (ch + 1) * c)
                    for h in range(2):
                        rs = slice(64 * h, 64 * h + c)
                        if do_pow and need_P:
                            s.op("pe", lambda e: e.matmul(pb[5][rs, cs], lhsT=PTc[rs, cs], rhs=Pc[rs, cs], start=True, stop=True), reads=[kPT, kP], writes=["B5"])
                        if do_pow:
                            s.op("pe", lambda e: e.matmul(pb[5][rs, 256 + ch * c:256 + (ch + 1) * c], lhsT=Pc[rs, cs], rhs=PTc[rs, cs], start=True, stop=True),
                                 reads=[kPT, kP], writes=["B5"])
                        if do_R:
                            s.op("pe", lambda e: e.matmul(pb[4][rs, cs], lhsT=PTc[rs, cs], rhs=Rc[rs, cs], start=True, stop=True), reads=[kPT, kR], writes=["B4"])
                if do_pow and need_P:
                    s.op("act", lambda e: e.activation(out=nP[:, 0:TBk], in_=pb[5][:, 0:TBk], func=AF.Copy), reads=["B5"], writes=[nkP])
                if do_pow:
                    s.op("dve", lambda e: e.tensor_copy(out=nPT[:, 0:TBk], in_=pb[5][:, 256:256 + TBk]), reads=["B5"], writes=[nkPT])
                if do_R:
                    s.op("dve", lambda e: e.tensor_tensor(out=nR[:, 0:TBk], in0=pb[4][:, 0:TBk], in1=Rc[:, 0:TBk], op=ALU.add), reads=["B4", kR], writes=[nkR])
                    Rc, kR = nR, nkR
                if do_pow:
                    if need_P:
                        Pc, kP = nP, nkP
                    PTc, kPT = nPT, nkPT
            Rf, rk = Rc, kR
            for ch in range(nch):
                cs = slice(ch * c, (ch + 1) * c)
                for h in range(2):
                    rs = slice(64 * h, 64 * h + c)
                    s.op("pe", lambda e: e.matmul(pb[4][rs, 256 + ch * 64:256 + (ch + 1) * 64], lhsT=Rf[rs, cs], rhs=bV[par][rs, ch, :], start=True, stop=True),
                         reads=[rk, f"bV{par}"], writes=["B4"])
                    s.op("pe", lambda e: e.matmul(pb[5][64 * h:64 * h + 64, ch * c:(ch + 1) * c], lhsT=Kbe[par][rs, ch, :], rhs=Rf[rs, cs], start=True, stop=True),
                         reads=[rk, f"Kbe{par}"], writes=["B5"])
            s.op("act", lambda e: e.activation(out=u[:, 0:nch, :], in_=pb[4][:, 256:256 + nch * 64].rearrange("p (n d) -> p n d", d=64), func=AF.Copy), reads=["B4"], writes=["u"])
            s.op("dve", lambda e: e.tensor_copy(out=wT[:, 0:TBk], in_=pb[5][:, 0:TBk]), reads=["B5"], writes=["wT"])
            skey = f"Sg{p}"
            for ch in range(nch):
                cs = slice(ch * c, (ch + 1) * c)
                seg = ch // cps
                if ch % cps == 0:
                    if is_sample:
                        s.dma("sp", Sg[:, p, :], sg_d[seg, p], writes=[skey])
                        s.op("dve", lambda e: e.tensor_copy(out=Sgb[:, p, :], in_=Sg[:, p, :]), reads=[skey], writes=[skey + "b"])
                    elif first:
                        s.op("pool", lambda e: e.memset(Sg[:, p, :], 0.0), writes=[skey])
                        s.op("pool", lambda e: e.memset(Sgb[:, p, :], 0.0), writes=[skey + "b"])
                for h in range(2):
                    fs = slice(64 * h, 64 * h + 64); rs = slice(64 * h, 64 * h + c)
                    s.op("pe", lambda e: e.matmul(pb[1][rs, 0:64], lhsT=wT[fs, cs], rhs=Sgb[fs, p, :], start=True, stop=True), reads=["wT", skey + "b"], writes=["B1"])
                s.op("dve", lambda e: e.tensor_tensor(out=vn[:], in0=u[:, ch, :], in1=pb[1][:, 0:64], op=ALU.subtract), reads=["u", "B1"], writes=["vn"])
                for h in range(2):
                    fs = slice(64 * h, 64 * h + 64); rs = slice(64 * h, 64 * h + c)
                    s.op("pe", lambda e: e.matmul(pb[1][fs, 256 + ch * c:256 + (ch + 1) * c], lhsT=Sgb[fs, p, :], rhs=QeT[par][fs, cs], start=True, stop=False),
                         reads=[skey + "b", f"QeT{par}"], writes=["B1"])
                    s.op("pe", lambda e: e.matmul(pb[1][fs, 256 + ch * c:256 + (ch + 1) * c], lhsT=vn[rs, :], rhs=attnT[par][rs, cs], start=False, stop=True),
                         reads=["vn", f"attnT{par}"], writes=["B1"])
                for h in range(2):
                    fs = slice(64 * h, 64 * h + 64); rs = slice(64 * h, 64 * h + c)
                    s.op("pe", lambda e: e.matmul(pb[1][fs, 64:128], lhsT=Kd[par][rs, ch, :], rhs=vn[rs, :], start=True, stop=True), reads=[f"Kd{par}", "vn"], writes=["B1"])
                s.op("dve", lambda e: e.scalar_tensor_tensor(out=Sgb[:, p, :], in0=Sg[:, p, :], scalar=EBs[par][:, (ch + 1) * c - 1:(ch + 1) * c], in1=pb[1][:, 64:128],
                                                             op0=ALU.mult, op1=ALU.add), reads=[skey, f"EBs{par}", "B1"], writes=[skey + "b"])
                s.op("dve", lambda e: e.scalar_tensor_tensor(out=Sg[:, p, :], in0=Sg[:, p, :], scalar=EBs[par][:, (ch + 1) * c - 1:(ch + 1) * c], in1=pb[1][:, 64:128],
                                                             op0=ALU.mult, op1=ALU.add), reads=[skey, f"EBs{par}", "B1"], writes=[skey])
                if is_sample and (ch + 1) % cps == 0:
                    s.dma("sp", ngs[seg, p], Sg[:, p, :], reads=[skey], writes=[f"o_ngs{seg}_{p}"])
            s.op("act", lambda e: e.activation(out=oTf[:, 0:TBk], in_=pb[1][:, 256:256 + TBk], func=AF.Copy), reads=["B1"], writes=["oTf"])
            s.op("pool", lambda e: e.tensor_tensor(out=sqb[:, 0:TBk], in0=oTf[:, 0:TBk], in1=oTf[:, 0:TBk], op=ALU.mult), reads=["oTf"], writes=["sqb"])
            s.op("pe", lambda e: e.matmul(pb[3][:, 0:TBk], lhsT=ob2b[:], rhs=sqb[:, 0:TBk], start=True, stop=True), reads=["ob2b", "sqb"], writes=["B3"])
            s.op("act", lambda e: e.activation(out=tmp[1][:, 0:TBk], in_=pb[3][:, 0:TBk], func=AF.Ln, scale=1.0 / 64, bias=EPS), reads=["B3"], writes=["tmp1"])
            s.op("act", lambda e: e.activation(out=tmp[1][:, 0:TBk], in_=tmp[1][:, 0:TBk], func=AF.Exp, scale=-0.5), reads=["tmp1"], writes=["tmp1"])
            s.op("dve", lambda e: e.tensor_tensor(out=oTf[:, 0:TBk], in0=oTf[:, 0:TBk], in1=tmp[1][:, 0:TBk], op=ALU.mult), reads=["oTf", "tmp1"], writes=["oTf"])
            s.op("dve", lambda e: e.scalar_tensor_tensor(out=oOwn[bpar][:, p, 0:TBk], in0=oTf[:, 0:TBk], scalar=gnw[:, 0:1], in1=za[par][:, 0:TBk], op0=ALU.mult, op1=ALU.mult),
                 reads=["oTf", "gnw_t", f"za{par}"], writes=[f"oOwn{bpar}_{p}"])

        def ph0_m():
            s.op("pool", lambda e: e.memset(smask[:, 0:TBk], 1.0), writes=[f"smask{bpar}"])
            s.op("pool", lambda e: e.memset(v3(smask[:, 0:TBk])[:, :, 0:1], 0.0), reads=[f"smask{bpar}"], writes=[f"smask{bpar}"])

        def hg(h):
            pp, pk = inproj_fm(4 * NP + h, 1)
            s.op("act", lambda e: e.activation(out=qb[:, 0:TBk], in_=pp, func=AF.Copy), reads=[pk], writes=["qb"])
            silu_from(qb[:, 0:TBk], ["qb"], qb[:, 0:TBk], "qb", bl[:, 0:TBk], "bl", TBk)
            pp, pk = inproj_fm(4 * NP + 2 * HB + h, 1)
            s.op("act", lambda e: e.activation(out=zb[:, 0:TBk], in_=pp, func=AF.Copy), reads=[pk], writes=["zb"])
            silu_from(zb[:, 0:TBk], ["zb"], zb[:, 0:TBk], "zb", bl[:, 0:TBk], "bl", TBk)
            pp, pk = inproj_fm(4 * NP + HB + h, 1)
            s.op("act", lambda e: e.activation(out=ff[:, 0:TBk], in_=pp, func=AF.Exp, scale=-1.0), reads=[pk], writes=["ff"])
            s.op("act", lambda e: e.activation(out=ff[:, 0:TBk], in_=ff[:, 0:TBk], func=AF.Ln, bias=1.0), reads=["ff"], writes=["ff"])
            s.op("act", lambda e: e.activation(out=ff[:, 0:TBk], in_=ff[:, 0:TBk], func=AF.Exp, scale=-1.0), reads=["ff"], writes=["ff"])
            s.op("dve", lambda e: e.tensor_scalar(out=ff[:, 0:TBk], in0=ff[:, 0:TBk], scalar1=oml[:, h:h + 1], scalar2=lb[:, h:h + 1], op0=ALU.mult, op1=ALU.add),
                 reads=["ff", "oml", "lb"], writes=["ff"])
            s.op("act", lambda e: e.activation(out=lf[:, 0:TBk], in_=ff[:, 0:TBk], func=AF.Ln), reads=["ff"], writes=["lf"])
            s.op("dve", lambda e: e.tensor_scalar(out=kb[:, 0:TBk], in0=ff[:, 0:TBk], scalar1=-1.0, scalar2=1.0, op0=ALU.mult, op1=ALU.add), reads=["ff"], writes=["kb"])
            s.op("dve", lambda e: e.tensor_tensor_scan(out=bb[:, 0:TBk], data0=smask[:, 0:TBk], data1=lf[:, 0:TBk], initial=0.0, op0=ALU.mult, op1=ALU.add),
                 reads=[f"smask{bpar}", "lf"], writes=["bb"])
            b3 = v3(bb[:, 0:TBk])
            s.op("pool", lambda e: e.tensor_tensor(out=v3(bl[:, 0:TBk]), in0=b3, in1=bc(b3[:, :, c - 1:c], [128, nch, c]), op=ALU.subtract), reads=["bb"], writes=["bl"])
            s.op("act", lambda e: e.activation(out=Qef[:, 0:TBk], in_=bb[:, 0:TBk], func=AF.Exp), reads=["bb"], writes=["Qef"])
            s.op("act", lambda e: e.activation(out=Qxf[:, 0:TBk], in_=bl[:, 0:TBk], func=AF.Exp), reads=["bl"], writes=["Qxf"])
            s.op("act", lambda e: e.activation(out=Kdf[:, 0:TBk], in_=bl[:, 0:TBk], func=AF.Exp, scale=-1.0), reads=["bl"], writes=["Kdf"])
            s.op("act", lambda e: e.activation(out=ebl[:, 0:nch], in_=b3[:, :, c - 1], func=AF.Exp), reads=["bb"], writes=["ebl"])
            s.op("dve", lambda e: e.tensor_tensor(out=Qe[:, 0:TBk], in0=Qef[:, 0:TBk], in1=qb[:, 0:TBk], op=ALU.mult), reads=["Qef", "qb"], writes=["Qe"])
            s.op("pool", lambda e: e.tensor_tensor(out=Qx[:, 0:TBk], in0=Qxf[:, 0:TBk], in1=qb[:, 0:TBk], op=ALU.mult), reads=["Qxf", "qb"], writes=["Qx"])
            s.op("dve", lambda e: e.tensor_tensor(out=Kdh[:, 0:TBk], in0=Kdf[:, 0:TBk], in1=kb[:, 0:TBk], op=ALU.mult), reads=["Kdf", "kb"], writes=["Kdh"])
            for ch in range(nch):
                cs = slice(ch * c, (ch + 1) * c)
                outp = pb[2][0:c, 128:256]
                for k in range(8):
                    s.op("pe", lambda e: e.matmul(outp, lhsT=hT[:, k, cs], rhs=Wb[:, k, C_HI + 128 * h:C_HI + 128 * (h + 1)], start=(k == 0), stop=(k == 7)),
                         reads=["Wb", f"hT{bpar}"], writes=["B2"])
                s.op("pe", lambda e: e.matmul(pb[2][0:c, 256:384], lhsT=Kdh[:, cs], rhs=identb[:], start=True, stop=True), reads=["Kdh", "identb"], writes=["B2"])
                s.op("pe", lambda e: e.matmul(pb[2][0:c, 384:384 + c], lhsT=Kdh[:, cs], rhs=Qx[:, cs], start=True, stop=True), reads=["Kdh", "Qx"], writes=["B2"])
                s.op("act", lambda e: e.activation(out=vtok[0:c, ch, :], in_=outp, func=AF.Copy), reads=["B2"], writes=["vtok"])
                s.op("dve", lambda e: e.tensor_copy(out=Kdt[0:c, ch, :], in_=pb[2][0:c, 256:384]), reads=["B2"], writes=["Kdt"])
                s.op("dve", lambda e: e.tensor_tensor(out=aTh[0:c, cs], in0=pb[2][0:c, 384:384 + c], in1=Tri_s[0:c, 0:c], op=ALU.mult),
                     reads=["B2", "Tri_s"], writes=["aTh"])
            skey = f"Sh{h}"
            for ch in range(nch):
                cs = slice(ch * c, (ch + 1) * c)
                seg = ch // cps
                if ch % cps == 0:
                    if is_sample:
                        s.dma("sp", Sh[:, h, :], sh_d[seg, h], writes=[skey])
                        s.op("dve", lambda e: e.tensor_copy(out=Shb[:, h, :], in_=Sh[:, h, :]), reads=[skey], writes=[skey + "b"])
                    elif first:
                        s.op("pool", lambda e: e.memset(Sh[:, h, :], 0.0), writes=[skey])
                        s.op("pool", lambda e: e.memset(Shb[:, h, :], 0.0), writes=[skey + "b"])
                s.op("pe", lambda e: e.matmul(pb[3][:, 256 + ch * c:256 + (ch + 1) * c], lhsT=Shb[:, h, :], rhs=Qe[:, cs], start=True, stop=False), reads=[skey + "b", "Qe"], writes=["B3"])
                s.op("pe", lambda e: e.matmul(pb[3][:, 256 + ch * c:256 + (ch + 1) * c], lhsT=vtok[0:c, ch, :], rhs=aTh[0:c, cs], start=False, stop=True),
                     reads=["vtok", "aTh"], writes=["B3"])
                s.op("pe", lambda e: e.matmul(pb[1][:, 128:256], lhsT=Kdt[0:c, ch, :], rhs=vtok[0:c, ch, :], start=True, stop=True), reads=["Kdt", "vtok"], writes=["B1"])
                s.op("dve", lambda e: e.scalar_tensor_tensor(out=Shb[:, h, :], in0=Sh[:, h, :], scalar=ebl[:, ch:ch + 1], in1=pb[1][:, 128:256], op0=ALU.mult, op1=ALU.add),
                     reads=[skey, "ebl", "B1"], writes=[skey + "b"])
                s.op("dve", lambda e: e.scalar_tensor_tensor(out=Sh[:, h, :], in0=Sh[:, h, :], scalar=ebl[:, ch:ch + 1], in1=pb[1][:, 128:256], op0=ALU.mult, op1=ALU.add),
                     reads=[skey, "ebl", "B1"], writes=[skey])
                if is_sample and (ch + 1) % cps == 0:
                    s.dma("sp", nhs[seg, h], Sh[:, h, :], reads=[skey], writes=[f"o_nhs{seg}_{h}"])
            s.op("act", lambda e: e.activation(out=oTfH[:, 0:TBk], in_=pb[3][:, 256:256 + TBk], func=AF.Copy), reads=["B3"], writes=["oTfH"])
            s.op("pool", lambda e: e.tensor_tensor(out=sqbH[:, 0:TBk], in0=oTfH[:, 0:TBk], in1=oTfH[:, 0:TBk], op=ALU.mult), reads=["oTfH"], writes=["sqbH"])
            s.op("pe", lambda e: e.matmul(pb[3][:, 256:256 + TBk], lhsT=onesb[:], rhs=sqbH[:, 0:TBk], start=True, stop=True), reads=["onesb", "sqbH"], writes=["B3"])
            s.op("act", lambda e: e.activation(out=tmpH[1][:, 0:TBk], in_=pb[3][:, 256:256 + TBk], func=AF.Ln, scale=1.0 / 128, bias=EPS), reads=["B3"], writes=["tmpH1"])
            s.op("act", lambda e: e.activation(out=tmpH[1][:, 0:TBk], in_=tmpH[1][:, 0:TBk], func=AF.Exp, scale=-0.5), reads=["tmpH1"], writes=["tmpH1"])
            s.op("dve", lambda e: e.tensor_tensor(out=oTfH[:, 0:TBk], in0=oTfH[:, 0:TBk], in1=tmpH[1][:, 0:TBk], op=ALU.mult), reads=["oTfH", "tmpH1"], writes=["oTfH"])
            s.op("dve", lambda e: e.scalar_tensor_tensor(out=oOwn[bpar][:, NP + h, 0:TBk], in0=oTfH[:, 0:TBk], scalar=hnw[:, 0:1], in1=zb[:, 0:TBk], op0=ALU.mult, op1=ALU.mult),
                 reads=["oTfH", "hnw_t", "zb"], writes=[f"oOwn{bpar}_{NP + h}"])

        def xchg(_=None):
            xs_, xd_ = (xsrc_s, xdst_s) if is_sample else (xsrc[bpar], xdst[bpar])
            s.dma("sp", xs_.rearrange("(t p) n -> p t n", p=128), oOwn[bpar][:, :, 0:TBk], reads=[f"oOwn{bpar}_{i}" for i in range(NL)], writes=[f"xsrc{bpar}"])
            s.coll([xs_], [xd_], reads=[f"xsrc{bpar}"], writes=[f"xdst{bpar}"])
            s.dma("sp", oTn[bpar][:, :, 0:TBk], xd_.rearrange("(t p) n -> p t n", p=128), reads=[f"xdst{bpar}"], writes=[f"oTn{bpar}_{k}" for k in range(2 * NL)])

        def outproj(_=None):
            for tt in range(ntt):
                X = xt[bpar * 2 + tt]
                for half in range(2):
                    bank = pb[6] if half == 0 else pb[7]
                    bk = "B6" if half == 0 else "BT"
                    for k in range(8):
                        s.op("pe", lambda e: e.matmul(bank[0:TT, :], lhsT=oTn[bpar][:, k, tt * TT:(tt + 1) * TT], rhs=WOb[:, k, half * 512:(half + 1) * 512], start=(k == 0), stop=(k == 7)),
                             reads=[f"oTn{bpar}_{k}", "WOb"], writes=[bk])
                    s.op("dve", lambda e: e.tensor_tensor(out=yo[0:TT, half * 512:(half + 1) * 512], in0=bank[0:TT, :], in1=X[0:TT, half * 512:(half + 1) * 512], op=ALU.add),
                         reads=[bk, X.name], writes=["yo"])
                s.op("act", lambda e: e.activation(out=sqjO[0:TT, :], in_=yo[0:TT, :], func=AF.Square, accum_out=ssO[0:TT, :]), reads=["yo"], writes=["sqjO", "ssO"])
                s.op("act", lambda e: e.activation(out=rrO[0:TT, :], in_=ssO[0:TT, :], func=AF.Ln, scale=1.0 / D, bias=EPS), reads=["ssO"], writes=["rrO"])
                s.op("act", lambda e: e.activation(out=rrO[0:TT, :], in_=rrO[0:TT, :], func=AF.Exp, scale=-0.5), reads=["rrO"], writes=["rrO"])
                s.op("dve", lambda e: e.scalar_tensor_tensor(out=yo2[0:TT, :], in0=yo[0:TT, :], scalar=rrO[0:TT, :], in1=fnw[0:TT, :], op0=ALU.mult, op1=ALU.mult),
                     reads=["yo", "rrO", "fnw_t"], writes=["yo2"])
                s.dma("sp", y_dst[t0 + tt * TT: t0 + (tt + 1) * TT, :], yo2[0:TT, :], reads=["yo2"], writes=[f"o_y{id(y_dst)}"], slot="yout")

        def record(fn, arg=None):
            lst = []
            s.rec = lst
            fn(arg)
            s.rec = None
            return lst
        return dict(p0=lambda: record(phase0), op=lambda: record(outproj), xc=lambda: record(xchg),
                    fr=lambda p: record(front, p), co=lambda p: record(core, p), hg=lambda h: record(hg, h))

    def emit_blocks(blks):
        n = len(blks)
        nodes = []
        for b, P in enumerate(blks):
            N = lambda kind, bb: f"{kind}_{bb}"
            nodes.append((N("P0", b), P["p0"](), [N("P0", b - 1), N("FR1", b - 2), N("HG1", b - 2), N("OP", b - 2)], b * 10 + 0))
            nodes.append((N("FR0", b), P["fr"](0), [N("P0", b), N("FR1", b - 1), N("CO0", b - 1), N("OP", b - 2)], b * 10 + 1))
            nodes.append((N("HG0", b), P["hg"](0), [N("P0", b), N("HG1", b - 1), N("XC", b - 2)], b * 10 + 2))
            nodes.append((N("CO0", b), P["co"](0), [N("FR0", b), N("CO1", b - 1), N("XC", b - 2)], b * 10 + 3))
            nodes.append((N("FR1", b), P["fr"](1), [N("FR0", b), N("CO1", b - 1)], b * 10 + 4))
            nodes.append((N("HG1", b), P["hg"](1), [N("HG0", b)], b * 10 + 5))
            nodes.append((N("CO1", b), P["co"](1), [N("FR1", b), N("CO0", b)], b * 10 + 6))
            nodes.append((N("XC", b), P["xc"](), [N("CO1", b), N("HG1", b), N("XC", b - 1), N("OP", b - 2)], b * 10 + 7))
            nodes.append((N("OP", b), P["op"](), [N("XC", b), N("OP", b - 1), N("FR1", b + 1) if b + 1 < n else N("FR1", b)], b * 10 + 18))
        s.dag_emit(nodes)

    assert NP == 2 and HB == 2
    nblk = T // TB
    blks = [block(xp, yp, b * TB, 1, TB, 64, b == 0, False, b % 2) for b in range(nblk)]
    emit_blocks(blks)
    s.dma("sp", ncp[:, :, 0, :], halo[:, :, 0, :], reads=["halo"], writes=["o_ncp"])
    for p in range(NP):
        s.dma("sp", ngp[0, p], Sg[:, p, :], reads=[f"Sg{p}"], writes=[f"o_ngp{p}"])
    for h in range(HB):
        s.dma("sp", nhp[0, h], Sh[:, h, :], reads=[f"Sh{h}"], writes=[f"o_nhp{h}"])
    s.dma("sp", halo[:], sc_d, reads=["halo"], writes=["halo"])
    sblk = block(xs, ys, 0, 4, 16, 16, True, True, 0)
    emit_blocks([sblk])
    s.dma("sp", ncs, halo[:], reads=["halo"], writes=["o_ncs"])
    s.finish("sp")
    return nc, s


def _perm(hh):
    r = lambda base, w: np.arange(base + hh * w, base + (hh + 1) * w)
    return np.concatenate([r(0, 256), r(512, 256), r(1024, 256), r(1536, 256),
                           r(2064, 256), r(2576, 256), r(3600, 256), r(3088, 256),
                           r(2048, 4), r(2056, 4)])


def _chan(hh):
    r = lambda base: np.arange(base + hh * 256, base + (hh + 1) * 256)
    return np.concatenate([r(0), r(512), r(1024)])


_WOUT_ROWS = np.concatenate([np.concatenate([np.arange(r * 256, (r + 1) * 256), np.arange(512 + r * 256, 512 + (r + 1) * 256)]) for r in range(2)])


def _core_inputs(c, inp):
    b, hh = c // 2, c % 2
    f = lambda a: np.ascontiguousarray(a, dtype=np.float32)
    ch = _chan(hh)
    cwv = inp["conv_w"][0][:, ch]
    scv = inp["state_conv"][0][4 * b:4 * b + 4][:, :, ch]
    hs = slice(4 * hh, 4 * hh + 4)
    return {
        "xp": f(inp["x_prompt"][b]),
        "xs": f(inp["x_sample"][4 * b:4 * b + 4].reshape(64, D)),
        "w_in": f(inp["w_in"][0][:, _perm(hh)]),
        "w_out": f(inp["w_out"][0][_WOUT_ROWS]),
        "nw": f(inp["norm_w"][0].reshape(8, 128).T),
        "cw": f(cwv.reshape(4, 3 * NP, 128).transpose(2, 1, 0)),
        "alog": f(np.broadcast_to(inp["gdn_A_log"][0][None, hs], (128, HA))),
        "dtb": f(np.broadcast_to(inp["gdn_dt_bias"][0][None, hs], (128, HA))),
        "gnw": f(np.tile(inp["gdn_norm_w"][0], 2).reshape(128, 1)),
        "hnw": f(inp["hgrn_norm_w"][0].reshape(128, 1)),
        "lbl": f(inp["hgrn_lb_logits"][:, hh * 256:(hh + 1) * 256].reshape(2, HB, 128).transpose(2, 1, 0)),
        "fnw": f(np.broadcast_to(inp["final_norm_w"][None, :], (128, D))),
        "sc": f(scv.reshape(4, 3, 3 * NP, 128).transpose(3, 2, 0, 1)),
        "sg": f(inp["state_gdn"][0][4 * b:4 * b + 4][:, hs].reshape(4, NP, 128, 64)),
        "sh": f(inp["state_hgrn"][0][4 * b:4 * b + 4][:, 2 * hh:2 * hh + 2]),
    }


_CACHE = {}


def kernel(**inputs):
    inp = {k: np.asarray(v) for k, v in inputs.items()}
    Bp, T, _ = inp["x_prompt"].shape
    assert Bp == 4 and inp["x_sample"].shape[:2] == (16, 16)
    if T not in _CACHE:
        _CACHE[T] = build(T)[0]
    nc = _CACHE[T]
    in_maps = [_core_inputs(c, inp) for c in range(8)]
    res = run_bass_kernel_spmd(nc, in_maps, core_ids=list(range(8)))
    r = res.results
    y_prompt = np.stack([r[2 * b]["yp"] for b in range(4)]).astype(np.float32)
    y_sample = np.concatenate([r[2 * b]["ys"].reshape(4, 16, D) for b in range(4)]).astype(np.float32)
    ncp_ = np.zeros((1, 4, 3, 1536), np.float32); ncs_ = np.zeros((1, 16, 3, 1536), np.float32)
    ngp_ = np.zeros((1, 4, 8, 64, 64), np.float32); ngs_ = np.zeros((1, 16, 8, 64, 64), np.float32)
    nhp_ = np.zeros((1, 4, 4, 128, 128), np.float32); nhs_ = np.zeros((1, 16, 4, 128, 128), np.float32)
    cvt = lambda a: a.transpose(2, 3, 1, 0).reshape(a.shape[2], 3, 3 * NP * 128)
    for c in range(8):
        b, hh = c // 2, c % 2
        ch = _chan(hh)
        ncp_[0, b][:, ch] = cvt(r[c]["ncp"])[0]
        ncs_[0, 4 * b:4 * b + 4][:, :, ch] = cvt(r[c]["ncs"])
        ngp_[0, b, 4 * hh:4 * hh + 4] = r[c]["ngp"].reshape(HA, 64, 64)
        ngs_[0, 4 * b:4 * b + 4, 4 * hh:4 * hh + 4] = r[c]["ngs"].reshape(4, HA, 64, 64)
        nhp_[0, b, 2 * hh:2 * hh + 2] = r[c]["nhp"].reshape(HB, 128, 128)
        nhs_[0, 4 * b:4 * b + 4, 2 * hh:2 * hh + 2] = r[c]["nhs"].reshape(4, HB, 128, 128)
    return (y_prompt, y_sample, ncp_, ngp_, nhp_, ncs_, ngs_, nhs_)
```

```python
import numpy as np
import concourse.bass as bass
import concourse.mybir as mybir
from concourse.bass_utils import run_bass_kernel_spmd

F32 = mybir.dt.float32
BF16 = mybir.dt.bfloat16
AF = mybir.ActivationFunctionType
ALU = mybir.AluOpType

D = 1024
HA, HB = 4, 2
NP = HA // 2
NT = 4 * NP + 3 * HB
C_HI = NT * 128
C_G = C_HI + HB * 128
NCOL = C_G + 2 * HA
EPS = 1e-6
RG = [[0, 1], [2, 3], [4, 5], [6, 7]]


SAME_ENGINE_WAIT = True
SCHED_MODE = 0
SCHED_DELTA = 300.0
SCHED_WIN = 24


class _Proxy:
    def __getattr__(self, name):
        return lambda *a, **k: (name, a, k)


_PROXY = _Proxy()
PSUM_KEYS = {"B0", "B1", "B2", "B3", "B4", "B5", "B6", "BT"}


class Sched:
    def __init__(self, nc):
        self.nc = nc
        self.eng = {"pe": nc.tensor, "dve": nc.vector, "act": nc.scalar, "pool": nc.gpsimd, "sp": nc.sync}
        self.sem = {k: nc.alloc_semaphore(name=f"s_{k}") for k in self.eng}
        self.cnt = {k: 0 for k in self.eng}
        self.seen = {k: {} for k in self.eng}
        self.last_w = {}
        self.readers = {}
        self.dma_sems = {}
        self.n_wait = 0
        self.n_ops = 0
        self.rec = None

    def coll(self, ins, outs, reads=(), writes=()):
        if self.rec is not None:
            self.rec.append(("coll", "pool", ins, outs, tuple(reads), tuple(writes)))
            return None
        self._deps("pool", reads, writes)
        if "cc" not in self.dma_sems:
            self.dma_sems["cc"] = [self.nc.alloc_semaphore(name="cc_sem"), 0]
        ent = self.dma_sems["cc"]
        ent[1] += 1
        self.nc.gpsimd.collective_compute("AllGather", ALU.bypass, replica_groups=RG, ins=ins, outs=outs).then_inc(ent[0], 1)
        tok = ("cc", ent[0], ent[1])
        self._commit(tok, reads, writes)
        self.n_ops += 1
        return tok

    def emit(self, r):
        if r[0] == "coll":
            self.coll(r[2], r[3], r[4], r[5])
            return
        if r[0] == "op":
            _, e, call, reads, writes = r
            self.op(e, lambda eng: getattr(eng, call[0])(*call[1], **call[2]), reads, writes)
        else:
            _, q, out, in_, reads, writes, slot = r
            self.dma(q, out, in_, reads, writes, slot)

    def _cost(self, r):
        if r[0] == "dma":
            return 2500.0
        if r[0] == "coll":
            return 30000.0
        _, e, call, reads, writes = r
        name, args, kw = call
        def nfree(ap):
            try:
                sh = list(ap.shape)
                n = 1
                for d in sh[1:]:
                    n *= int(d)
                return n
            except Exception:
                return 256
        if e == "pe":
            ap = kw.get("rhs", None) if name == "matmul" else kw.get("in_", None)
            n = nfree(ap) if ap is not None else 64
            c = 32.0 + 0.4 * n
            try:
                if name == "matmul" and kw["rhs"].dtype == F32:
                    c *= 3.0
            except Exception:
                pass
            return c
        ap = kw.get("out", None)
        n = nfree(ap) if ap is not None else 256
        if e == "dve":
            return 70.0 + 1.0 * n
        if e == "act":
            return 130.0 + 0.9 * n
        return 110.0 + 1.8 * n

    def _est_start(self, r):
        if r[0] in ("dma", "coll"):
            e, reads, writes = r[1], r[4], r[5]
        else:
            e, reads, writes = r[1], r[3], r[4]
        m = self.model
        t = m["eng"].get(e, 0.0)
        ex = [k for k in reads if k in PSUM_KEYS]
        for k in reads:
            w = m["w"].get(k)
            if w is not None:
                t = max(t, w[0] + ((0.0 if e == "pe" else 120.0) if w[1] == e else 160.0))
        for k in list(writes) + ex:
            w = m["w"].get(k)
            if w is not None:
                t = max(t, w[0] + ((0.0 if e == "pe" else 120.0) if w[1] == e else 160.0))
            for (tt, ee) in m["r"].get(k, {}).values():
                t = max(t, tt + ((0.0 if e == "pe" else 120.0) if ee == e else 160.0))
        return t

    def _model_commit(self, r, t0):
        if r[0] in ("dma", "coll"):
            e, reads, writes = r[1], r[4], r[5]
            eng_busy = 100.0
        else:
            e, reads, writes = r[1], r[3], r[4]
            eng_busy = None
        m = self.model
        c = self._cost(r)
        t1 = t0 + c
        m["eng"][e] = t0 + (eng_busy if eng_busy is not None else c)
        ex = [k for k in reads if k in PSUM_KEYS]
        who = e if r[0] == "op" else "dma"
        for k in reads:
            m["r"].setdefault(k, {})[who] = (t1, who)
        for k in list(writes) + ex:
            m["w"][k] = (t1, who)
            m["r"][k] = {}
        m["t"] = max(m.get("t", 0.0), t1)

    def dag_emit(self, nodes):
        if not hasattr(self, "model"):
            self.model = {"eng": {}, "w": {}, "r": {}, "t": 0.0}
        names = {n[0] for n in nodes}
        units, deps, prio, preds = {}, {}, {}, {}
        def rw(r):
            if r[0] in ("dma", "coll"):
                reads, writes = r[4], r[5]
            else:
                reads, writes = r[3], r[4]
            ex = [k for k in reads if k in PSUM_KEYS]
            return list(reads), list(writes) + ex
        for (name, ops, dp, pr) in nodes:
            u = []
            for r in ops:
                glued = (r[0] == "op" and r[2][0] == "matmul" and r[2][2].get("start") is False)
                if glued and u:
                    u[-1].append(r)
                else:
                    u.append([r])
            units[name] = u
            deps[name] = {d for d in dp if d in names}
            prio[name] = pr
            lw, rd, pl = {}, {}, []
            for j, unit in enumerate(u):
                p = set()
                R, W = [], []
                for r in unit:
                    a_, b_ = rw(r)
                    R += a_; W += b_
                for k in R:
                    if k in lw:
                        p.add(lw[k])
                for k in W:
                    if k in lw:
                        p.add(lw[k])
                    p |= rd.get(k, set())
                p.discard(j)
                pl.append(p)
                for k in R:
                    rd.setdefault(k, set()).add(j)
                for k in W:
                    lw[k] = j
                    rd[k] = set()
            preds[name] = pl
        emitted = {n: [False] * len(units[n]) for n in units}
        nleft = {n: len(units[n]) for n in units}
        lo = {n: 0 for n in units}
        done = {n for n in units if not units[n]}
        waiting = [n for n in units if n not in done]
        active = []
        def refresh():
            nonlocal waiting
            still = []
            for n in waiting:
                if deps[n] <= done:
                    active.append(n)
                else:
                    still.append(n)
            waiting = still
        refresh()
        WIN = SCHED_WIN
        while active:
            best, bsel = None, None
            for n in active:
                em, pl, u = emitted[n], preds[n], units[n]
                j = lo[n]
                seen = 0
                while j < len(u) and seen < WIN:
                    if not em[j]:
                        seen += 1
                        if all(em[q] for q in pl[j]):
                            t = self._est_start(u[j][0])
                            key = (t, prio[n], j)
                            if best is None or key < best:
                                best, bsel = key, (n, j)
                    j += 1
            n, j = bsel
            for r in units[n][j]:
                t0 = self._est_start(r)
                self._model_commit(r, t0)
                self.emit(r)
            emitted[n][j] = True
            nleft[n] -= 1
            while lo[n] < len(units[n]) and emitted[n][lo[n]]:
                lo[n] += 1
            if nleft[n] == 0:
                active.remove(n)
                done.add(n)
                refresh()
        assert not waiting, ("DAG deadlock", waiting[:5])

    def merge_emit(self, streams):
        if not hasattr(self, "model"):
            self.model = {"eng": {}, "w": {}, "r": {}, "t": 0.0}
        units = []
        for st in streams:
            u = []
            for r in st:
                glued = (r[0] == "op" and r[2][0] == "matmul" and r[2][2].get("start") is False)
                if glued and u:
                    u[-1].append(r)
                else:
                    u.append([r])
            units.append(u)
        pos = [0] * len(units)
        while True:
            best, bi = None, -1
            for i, u in enumerate(units):
                if pos[i] < len(u):
                    t = self._est_start(u[pos[i]][0])
                    key = (t, -(len(u) - pos[i]))
                    if best is None or key < best:
                        best, bi = key, i
            if bi < 0:
                break
            for r in units[bi][pos[bi]]:
                t0 = self._est_start(r)
                self._model_commit(r, t0)
                self.emit(r)
            pos[bi] += 1


    def _wait(self, e, tok):
        name, sem, val = tok
        if name == "pe" and e == "pe":
            return
        if name == e and not SAME_ENGINE_WAIT:
            return
        if self.seen[e].get(name, 0) >= val:
            return
        self.eng[e].wait_ge(sem, val)
        self.seen[e][name] = val
        self.n_wait += 1

    def _deps(self, e, reads, writes):
        for k in reads:
            t = self.last_w.get(k)
            if t is not None:
                self._wait(e, t)
        for k in writes:
            t = self.last_w.get(k)
            if t is not None:
                self._wait(e, t)
            for t in self.readers.get(k, {}).values():
                self._wait(e, t)

    def _commit(self, tok, reads, writes):
        for k in reads:
            self.readers.setdefault(k, {})[tok[0]] = tok
        for k in writes:
            self.last_w[k] = tok
            self.readers[k] = {}

    def op(self, e, fn, reads=(), writes=()):
        if self.rec is not None:
            self.rec.append(("op", e, fn(_PROXY), tuple(reads), tuple(writes)))
            return None
        ex = [k for k in reads if k in PSUM_KEYS]
        if ex:
            writes = list(writes) + ex
        self._deps(e, reads, writes)
        ins = fn(self.eng[e])
        self.cnt[e] += 1
        ins.then_inc(self.sem[e], 1)
        tok = (e, self.sem[e], self.cnt[e])
        self._commit(tok, reads, writes)
        self.n_ops += 1
        return tok

    def dma(self, q, out, in_, reads=(), writes=(), slot=None):
        if self.rec is not None:
            self.rec.append(("dma", q, out, in_, tuple(reads), tuple(writes), slot))
            return None
        self._deps(q, reads, writes)
        slot = slot or (writes[0] if writes else reads[0])
        sname = f"d_{slot}"
        if sname not in self.dma_sems:
            self.dma_sems[sname] = [self.nc.alloc_semaphore(name=sname), 0]
        ent = self.dma_sems[sname]
        ent[1] += 16
        self.eng[q].dma_start(out=out, in_=in_).then_inc(ent[0], 16)
        tok = (sname, ent[0], ent[1])
        self._commit(tok, reads, writes)
        self.n_ops += 1
        return tok

    def finish(self, e="sp"):
        for k, t in list(self.last_w.items()):
            self._wait(e, t)


def bc(ap, shape):
    return ap.to_broadcast(list(shape))


def build(T, TB=256):
    nc = bass.Bass("TRN2", target_bir_lowering=False)
    s = Sched(nc)
    dt_in = lambda n, sh: nc.dram_tensor(n, list(sh), F32, kind="ExternalInput").ap()
    dt_out = lambda n, sh: nc.dram_tensor(n, list(sh), F32, kind="ExternalOutput").ap()
    xp = dt_in("xp", [T, D]); xs = dt_in("xs", [64, D])
    w_in = dt_in("w_in", [D, NCOL]); w_out = dt_in("w_out", [D, D])
    nw_d = dt_in("nw", [128, 8]); cw_d = dt_in("cw", [128, 3 * NP, 4])
    alog_d = dt_in("alog", [128, HA]); dtb_d = dt_in("dtb", [128, HA])
    gnw_d = dt_in("gnw", [128, 1]); hnw_d = dt_in("hnw", [128, 1])
    lbl_d = dt_in("lbl", [128, HB, 2]); fnw_d = dt_in("fnw", [128, D])
    sc_d = dt_in("sc", [128, 3 * NP, 4, 3])
    sg_d = dt_in("sg", [4, NP, 128, 64]); sh_d = dt_in("sh", [4, HB, 128, 128])
    yp = dt_out("yp", [T, D]); ys = dt_out("ys", [64, D])
    ncp = dt_out("ncp", [128, 3 * NP, 1, 3]); ngp = dt_out("ngp", [1, NP, 128, 64]); nhp = dt_out("nhp", [1, HB, 128, 128])
    ncs = dt_out("ncs", [128, 3 * NP, 4, 3]); ngs = dt_out("ngs", [4, NP, 128, 64]); nhs = dt_out("nhs", [4, HB, 128, 128])

    NL = NP + HB
    xsrc = [nc.dram_tensor(f"xsrc{i}", [NL * 128, TB], BF16).ap() for i in range(2)]
    xdst = [nc.dram_tensor(f"xdst{i}", [2 * NL * 128, TB], BF16).ap() for i in range(2)]
    xsrc_s = nc.dram_tensor("xsrc_s", [NL * 128, 64], BF16).ap()
    xdst_s = nc.dram_tensor("xdst_s", [2 * NL * 128, 64], BF16).ap()
    sb = lambda n, sh, d=F32: nc.alloc_sbuf_tensor(n, list(sh), d)
    Wb = sb("Wb", [128, 8, NCOL], BF16)
    WOb = sb("WOb", [128, 8, D], BF16)
    nw = sb("nw_t", [128, 8]); cw = sb("cw_t", [128, 3 * NP, 4])
    alog = sb("alog_t", [128, HA]); dtb = sb("dtb_t", [128, HA]); negA = sb("negA", [128, HA])
    gnw = sb("gnw_t", [128, 1]); hnw = sb("hnw_t", [128, 1])
    lbl = sb("lbl_t", [128, HB, 2]); lb = sb("lb", [128, HB]); oml = sb("oml", [128, HB])
    fnw = sb("fnw_t", [128, D])
    identb = sb("identb", [128, 128], BF16); identf = sb("identf", [128, 128])
    ones = sb("ones", [128, 128]); ob2 = sb("ob2", [128, 128])
    fgt = sb("fgt", [128, 128]); fle = sb("fle", [128, 128])
    I_s = sb("I_s", [128, 64]); U_s = sb("U_s", [128, 64]); Tri_s = sb("Tri_s", [128, 64]); Mc_s = sb("Mc_s", [128, 64])
    halo = sb("halo", [128, 3 * NP, 4, 3])
    Sg = sb("Sg", [128, NP, 64]); Sh = sb("Sh", [128, HB, 128])
    W_ = TB
    xt = [sb(f"xt{i}", [128, D]) for i in range(4)]
    sqj = sb("sqj", [128, D], BF16)
    xb = sb("xb", [128, D], BF16)
    hT_all = [sb(f"hT{i}", [128, 8, W_], BF16) for i in range(2)]
    sqjO = sb("sqjO", [128, D], BF16); ssO = sb("ssO", [128, 1]); rrO = sb("rrO", [128, 1])
    ss = sb("ss", [128, 1]); rr = sb("rr", [128, 1])
    raw = sb("raw", [128, 3, W_ + 12])
    cv = sb("cv", [128, 3, W_])
    tmp = [None, sb("tmp1", [128, W_]), sb("tmp2", [128, W_]), None]
    za = [sb(f"za{i}", [128, W_]) for i in range(2)]
    cvb = [sb(f"cvb{i}", [128, 3, W_], BF16) for i in range(2)]
    tmpA = [sb(f"tmpA{i}", [128, W_]) for i in range(3)]
    sqA = [sb(f"sqA{i}", [128, W_], BF16) for i in range(2)]
    tnA = [sb(f"tnA{i}", [128, W_]) for i in range(2)]
    I_sb = sb("I_sb", [128, 64], BF16); ob2b = sb("ob2b", [128, 128], BF16); onesb = sb("onesb", [128, 128], BF16)
    Sgb = sb("Sgb", [128, NP, 64], BF16); Shb = sb("Shb", [128, HB, 128], BF16)
    sqb = sb("sqb", [128, W_], BF16); sqbH = sb("sqbH", [128, W_], BF16)
    G_all = [sb(f"G{i}", [128, 4, 2 * HA]) for i in range(2)]; Gb_all = [sb(f"Gb{i}", [128, 4, HA]) for i in range(2)]; Gg_all = [sb(f"Gg{i}", [128, 4, HA]) for i in range(2)]
    gs_all = [sb(f"gs{i}", [128, NP, 4]) for i in range(2)]; bs_all = [sb(f"bs{i}", [128, NP, 4]) for i in range(2)]; nbs_all = [sb(f"nbs{i}", [128, NP, 4]) for i in range(2)]
    gc_all = [sb(f"gc{i}", [128, NP, 4]) for i in range(2)]; gl_all = [sb(f"gl{i}", [128, NP, 4]) for i in range(2)]; egc_all = [sb(f"egc{i}", [128, NP, 4]) for i in range(2)]
    dk_all = [sb(f"dk{i}", [128, NP, 4]) for i in range(2)]; bge_all = [sb(f"bge{i}", [128, NP, 4]) for i in range(2)]
    rhsD = sb("rhsD", [128, W_]); Dg = sb("Dg", [128, W_])
    Ee = sb("Ee", [128, W_]); Dm = sb("Dm", [128, W_]); Ds = sb("Ds", [128, W_])
    EBs = [sb(f"EBs{i}", [128, W_]) for i in range(2)]
    P0t = [sb(f"P0t{i}", [128, W_], BF16) for i in range(2)]
    PT0t = [sb(f"PT0t{i}", [128, W_], BF16) for i in range(2)]
    R0t = [sb(f"R0t{i}", [128, W_], BF16) for i in range(2)]
    P = [sb(f"P{i}", [128, W_], BF16) for i in range(2)]
    PT = [sb(f"PT{i}", [128, W_], BF16) for i in range(2)]
    R = [sb(f"R{i}", [128, W_], BF16) for i in range(2)]
    attn = sb("attn", [128, W_], BF16); attnT = [sb(f"attnT{i}", [128, W_], BF16) for i in range(2)]
    Kbe = [sb(f"Kbe{i}", [128, 4, 64], BF16) for i in range(2)]; Kd = [sb(f"Kd{i}", [128, 4, 64], BF16) for i in range(2)]; bV = [sb(f"bV{i}", [128, 4, 64], BF16) for i in range(2)]
    u = sb("u", [128, 4, 64]); wT = sb("wT", [128, W_], BF16); QeT = [sb(f"QeT{i}", [128, W_], BF16) for i in range(2)]
    vn = sb("vn", [128, 64], BF16)
    oTf = sb("oTf", [128, W_])
    oTn = [sb(f"oTn_{i}", [128, 2 * NL, W_], BF16) for i in range(2)]
    oOwn = [sb(f"oOwn_{i}", [128, NL, W_], BF16) for i in range(2)]
    qb = sb("qb", [128, W_]); ff = sb("ff", [128, W_]); lf = sb("lf", [128, W_]); kb = sb("kb", [128, W_])
    bb = sb("bb", [128, W_]); bl = sb("bl", [128, W_])
    Qe = sb("Qe", [128, W_], BF16); Qx = sb("Qx", [128, W_], BF16); Kdh = sb("Kdh", [128, W_], BF16)
    Qef = sb("Qef", [128, W_]); Qxf = sb("Qxf", [128, W_]); Kdf = sb("Kdf", [128, W_])
    ebl = sb("ebl", [128, 4]); zb = sb("zb", [128, W_])
    vtok = sb("vtok", [64, 4, 128], BF16); Kdt = sb("Kdt", [64, 4, 128], BF16); aTh = sb("aTh", [64, W_], BF16)
    smask_all = [sb(f"smask{i}", [128, W_]) for i in range(2)]
    tmpH = [None, sb("tmpH1", [128, W_])]; oTfH = sb("oTfH", [128, W_])
    yo = sb("yo", [128, D]); yo2 = sb("yo2", [128, D])
    pb = [nc.alloc_psum_tensor(f"pb{i}", [128, 512], F32) for i in range(8)]
    pT2 = pb[2][:, 0:128].bitcast(BF16).rearrange("p (k t) -> p k t", t=128)

    def aff(out, cmp, fill_in, step=-1, cm=1, base=0):
        s.op("pool", lambda e: e.memset(out[:], fill_in), writes=[out.name])
        s.op("pool", lambda e: e.affine_select(out=out[:], in_=out[:], pattern=[[step, 128]], compare_op=cmp,
                                               fill=0.0, base=base, channel_multiplier=cm), reads=[out.name], writes=[out.name])
    aff(identf, ALU.is_equal, 1.0)
    aff(fgt, ALU.is_gt, 1.0)
    aff(fle, ALU.is_gt, 1.0, step=1, cm=-1, base=1)
    s.op("pool", lambda e: e.memset(ones[:], 1.0), writes=["ones"])
    s.op("pool", lambda e: e.memset(ob2[:], 0.0), writes=["ob2"])
    for h in range(2):
        sl = slice(64 * h, 64 * h + 64)
        s.op("pool", lambda e: e.memset(ob2[sl, sl], 1.0), reads=["ob2"], writes=["ob2"])
    s.op("dve", lambda e: e.tensor_copy(out=identb[:], in_=identf[:]), reads=["identf"], writes=["identb"])
    for (dst, src) in ((I_s, identf), (U_s, fgt), (Tri_s, fle)):
        for h in range(2):
            sl = slice(64 * h, 64 * h + 64)
            s.op("dve", lambda e: e.tensor_copy(out=dst[sl, :], in_=src[sl, sl]), reads=[src.name], writes=[dst.name])
    s.op("dve", lambda e: e.tensor_tensor(out=Mc_s[:], in0=U_s[:], in1=I_s[:], op=ALU.add), reads=["U_s", "I_s"], writes=["Mc_s"])
    s.op("dve", lambda e: e.tensor_copy(out=I_sb[:], in_=I_s[:]), reads=["I_s"], writes=["I_sb"])
    s.op("dve", lambda e: e.tensor_copy(out=ob2b[:], in_=ob2[:]), reads=["ob2"], writes=["ob2b"])
    s.op("dve", lambda e: e.tensor_copy(out=onesb[:], in_=ones[:]), reads=["ones"], writes=["onesb"])
    for t_, d_ in ((nw, nw_d), (cw, cw_d), (alog, alog_d), (dtb, dtb_d), (gnw, gnw_d), (hnw, hnw_d), (lbl, lbl_d), (fnw, fnw_d)):
        s.dma("sp", t_[:], d_, writes=[t_.name])
    s.op("act", lambda e: e.activation(out=negA[:], in_=alog[:], func=AF.Exp), reads=["alog_t"], writes=["negA"])
    s.op("dve", lambda e: e.tensor_scalar(out=negA[:], in0=negA[:], scalar1=-1.0, scalar2=None, op0=ALU.mult), reads=["negA"], writes=["negA"])
    s.op("dve", lambda e: e.tensor_tensor(out=lb[:], in0=lbl[:, :, 1], in1=lbl[:, :, 0], op=ALU.subtract), reads=["lbl_t"], writes=["lb"])
    s.op("act", lambda e: e.activation(out=lb[:], in_=lb[:], func=AF.Exp), reads=["lb"], writes=["lb"])
    s.op("dve", lambda e: e.tensor_scalar(out=lb[:], in0=lb[:], scalar1=1.0, scalar2=None, op0=ALU.add), reads=["lb"], writes=["lb"])
    s.op("dve", lambda e: e.reciprocal(out=lb[:], in_=lb[:]), reads=["lb"], writes=["lb"])
    s.op("dve", lambda e: e.tensor_scalar(out=oml[:], in0=lb[:], scalar1=-1.0, scalar2=1.0, op0=ALU.mult, op1=ALU.add), reads=["lb"], writes=["oml"])
    w_in_v = w_in.rearrange("(k p) n -> p k n", p=128)
    stgx = [sb(f"stgx{i}", [128, D]) for i in range(4)]
    stgs = [(xt[0], "xt0"), (stgx[0], "stgx0"), (xt[1], "xt1"), (stgx[1], "stgx1"), (yo, "yo"), (stgx[2], "stgx2"), (yo2, "yo2"), (stgx[3], "stgx3")]
    q = 0
    nfull = NCOL // 1024
    for k in range(8):
        pieces = [(i * 1024, 1024) for i in range(nfull)] + [(nfull * 1024, NCOL % 1024)]
        for (c0, cn) in pieces:
            if cn == 1024:
                tl, key = stgs[q % len(stgs)]
            else:
                tl, key = Ee, "Ee"
            s.dma("sp", tl[:, 0:cn], w_in_v[:, k, c0:c0 + cn], writes=[key])
            if q % 2 == 0:
                s.op("dve", lambda e: e.tensor_scalar(out=Wb[:, k, c0:c0 + cn], in0=tl[:, 0:cn], scalar1=nw[:, k:k + 1], scalar2=None, op0=ALU.mult),
                     reads=[key, "nw_t"], writes=["Wb"])
            else:
                s.op("act", lambda e: e.activation(out=Wb[:, k, c0:c0 + cn], in_=tl[:, 0:cn], func=AF.Copy, scale=nw[:, k:k + 1]),
                     reads=[key, "nw_t"], writes=["Wb"])
            q += 1
    w_out_v = w_out.rearrange("(k p) n -> p k n", p=128)
    for k in range(8):
        tl, key = stgs[q % len(stgs)]
        s.dma("sp", tl[:, :], w_out_v[:, k, :], writes=[key])
        if q % 2 == 0:
            s.op("dve", lambda e: e.tensor_copy(out=WOb[:, k, :], in_=tl[:, :]), reads=[key], writes=["WOb"])
        else:
            s.op("act", lambda e: e.activation(out=WOb[:, k, :], in_=tl[:, :], func=AF.Copy), reads=[key], writes=["WOb"])
        q += 1

    def block(x_src, y_dst, t0, nseg, seglen, c, first, is_sample, bpar=0):
        hT = hT_all[bpar]; G = G_all[bpar]; Gb = Gb_all[bpar]; Gg = Gg_all[bpar]; gs = gs_all[bpar]; bs = bs_all[bpar]; nbs = nbs_all[bpar]; gc = gc_all[bpar]; gl = gl_all[bpar]; egc = egc_all[bpar]; dk = dk_all[bpar]; bge = bge_all[bpar]; smask = smask_all[bpar]
        TBk = nseg * seglen
        nch = TBk // c
        cps = seglen // c
        TT = min(128, TBk)
        ntt = TBk // TT
        nlev = {64: 5, 16: 3}[c]
        v3 = lambda ap: ap.rearrange("p (n c) -> p n c", c=c)

        def ph0_x():
            for tt in range(ntt):
                X = xt[bpar * 2 + tt]
                s.dma("sp", X[0:TT, :], x_src[t0 + tt * TT: t0 + (tt + 1) * TT, :], writes=[X.name])
                s.op("act", lambda e: e.activation(out=sqj[0:TT, :], in_=X[0:TT, :], func=AF.Square, accum_out=ss[0:TT, :]),
                     reads=[X.name], writes=["sqj", "ss"])
                s.op("act", lambda e: e.activation(out=rr[0:TT, :], in_=ss[0:TT, :], func=AF.Ln, scale=1.0 / D, bias=EPS), reads=["ss"], writes=["rr"])
                s.op("act", lambda e: e.activation(out=rr[0:TT, :], in_=rr[0:TT, :], func=AF.Exp, scale=-0.5), reads=["rr"], writes=["rr"])
                s.op("dve", lambda e: e.tensor_scalar(out=xb[0:TT, :], in0=X[0:TT, :], scalar1=rr[0:TT, :], scalar2=None, op0=ALU.mult),
                     reads=[X.name, "rr"], writes=["xb"])
                for kk in range(4):
                    for j in range(2):
                        k = 2 * kk + j
                        s.op("pe", lambda e: e.transpose(out=pT2[:, j, 0:TT], in_=xb[0:TT, k * 128:(k + 1) * 128], identity=identb[0:TT, 0:TT]),
                             reads=["xb", "identb"], writes=["B2"])
                    if kk % 2 == 0:
                        s.op("dve", lambda e: e.tensor_copy(out=hT[:, 2 * kk:2 * kk + 2, tt * TT:(tt + 1) * TT], in_=pT2[:, :, 0:TT]), reads=["B2"], writes=[f"hT{bpar}"])
                    else:
                        s.op("act", lambda e: e.activation(out=hT[:, 2 * kk:2 * kk + 2, tt * TT:(tt + 1) * TT], in_=pT2[:, :, 0:TT], func=AF.Copy), reads=["B2"], writes=[f"hT{bpar}"])

        def silu_from(src, srckeys, dst, dstkey, scr, scrkey, W, outdt_note=None):
            s.op("act", lambda e: e.activation(out=scr, in_=src, func=AF.Exp, scale=-1.0), reads=srckeys, writes=[scrkey])
            s.op("act", lambda e: e.activation(out=scr, in_=scr, func=AF.Ln, bias=1.0), reads=[scrkey], writes=[scrkey])
            s.op("act", lambda e: e.activation(out=scr, in_=scr, func=AF.Exp, scale=-1.0), reads=[scrkey], writes=[scrkey])
            s.op("dve", lambda e: e.tensor_tensor(out=dst, in0=src, in1=scr, op=ALU.mult), reads=list(srckeys) + [scrkey], writes=[dstkey])


        def inproj_fm(ct, i=0):
            key = "B0"
            out = pb[0][:, i * 256: i * 256 + TBk]
            for k in range(8):
                s.op("pe", lambda e: e.matmul(out, lhsT=Wb[:, k, ct * 128:(ct + 1) * 128], rhs=hT[:, k, 0:TBk], start=(k == 0), stop=(k == 7)),
                     reads=["Wb", f"hT{bpar}"], writes=[key])
            return out, key

        def ph0_g():
            for ch in range(nch):
                for h in range(2):
                    out = pb[2][64 * h:64 * h + c, ch * 2 * HA:(ch + 1) * 2 * HA]
                    for k in range(8):
                        s.op("pe", lambda e: e.matmul(out, lhsT=hT[:, k, ch * c:(ch + 1) * c], rhs=Wb[:, k, C_G:C_G + 2 * HA], start=(k == 0), stop=(k == 7)),
                             reads=["Wb", f"hT{bpar}"], writes=["B2"])
            Gv = G[:, 0:nch, :]
            s.op("dve", lambda e: e.tensor_copy(out=Gv, in_=pb[2][:, 0:nch * 2 * HA].rearrange("p (n g) -> p n g", g=2 * HA)), reads=["B2"], writes=[f"G{bpar}"])
            Gbv = Gb[:, 0:nch, :]; Ggv = Gg[:, 0:nch, :]
            s.op("act", lambda e: e.activation(out=Gbv, in_=Gv[:, :, 0:HA], func=AF.Exp, scale=-1.0), reads=[f"G{bpar}"], writes=[f"Gb{bpar}"])
            s.op("act", lambda e: e.activation(out=Gbv, in_=Gbv, func=AF.Ln, bias=1.0), reads=[f"Gb{bpar}"], writes=[f"Gb{bpar}"])
            s.op("act", lambda e: e.activation(out=Gbv, in_=Gbv, func=AF.Exp, scale=-1.0), reads=[f"Gb{bpar}"], writes=[f"Gb{bpar}"])
            s.op("dve", lambda e: e.tensor_tensor(out=Ggv, in0=Gv[:, :, HA:2 * HA], in1=bc(dtb[:, None, :], [128, nch, HA]), op=ALU.add),
                 reads=[f"G{bpar}", "dtb_t"], writes=[f"Gg{bpar}"])
            s.op("act", lambda e: e.activation(out=Ggv, in_=Ggv, func=AF.Exp), reads=[f"Gg{bpar}"], writes=[f"Gg{bpar}"])
            s.op("act", lambda e: e.activation(out=Ggv, in_=Ggv, func=AF.Ln, bias=1.0), reads=[f"Gg{bpar}"], writes=[f"Gg{bpar}"])
            s.op("dve", lambda e: e.tensor_tensor(out=Ggv, in0=Ggv, in1=bc(negA[:, None, :], [128, nch, HA]), op=ALU.mult),
                 reads=[f"Gg{bpar}", "negA"], writes=[f"Gg{bpar}"])
            gsv = gs[:, :, 0:nch]; bsv = bs[:, :, 0:nch]; nbsv = nbs[:, :, 0:nch]
            gcv = gc[:, :, 0:nch]; glv = gl[:, :, 0:nch]; egcv = egc[:, :, 0:nch]; dkv = dk[:, :, 0:nch]; bgev = bge[:, :, 0:nch]
            for h in range(2):
                sl = slice(64 * h, 64 * h + 64)
                for (dst, src, kd, ks) in ((gs, Gg, f"gs{bpar}", f"Gg{bpar}"), (bs, Gb, f"bs{bpar}", f"Gb{bpar}")):
                    for p in range(NP):
                        s.op("dve", lambda e: e.tensor_copy(out=dst[sl, p, 0:nch], in_=src[sl, 0:nch, 2 * p + h]), reads=[ks], writes=[kd])
            s.op("dve", lambda e: e.tensor_scalar(out=nbsv, in0=bsv, scalar1=-1.0, scalar2=None, op0=ALU.mult), reads=[f"bs{bpar}"], writes=[f"nbs{bpar}"])
            for h in range(2):
                rs = slice(64 * h, 64 * h + c)
                s.op("pe", lambda e: e.matmul(pb[2][rs, 64:64 + NP * nch], lhsT=Tri_s[rs, 0:c], rhs=gs[rs, :, 0:nch], start=True, stop=True),
                     reads=["Tri_s", f"gs{bpar}"], writes=["B2"])
                s.op("pe", lambda e: e.matmul(pb[2][rs, 96:96 + NP * nch], lhsT=ones[rs, 0:c], rhs=gs[rs, :, 0:nch], start=True, stop=True),
                     reads=["ones", f"gs{bpar}"], writes=["B2"])
            s.op("dve", lambda e: e.tensor_copy(out=gcv, in_=pb[2][:, 64:64 + NP * nch].rearrange("p (a n) -> p a n", n=nch)), reads=["B2"], writes=[f"gc{bpar}"])
            s.op("dve", lambda e: e.tensor_copy(out=glv, in_=pb[2][:, 96:96 + NP * nch].rearrange("p (a n) -> p a n", n=nch)), reads=["B2"], writes=[f"gl{bpar}"])
            s.op("act", lambda e: e.activation(out=egcv, in_=gcv, func=AF.Exp), reads=[f"gc{bpar}"], writes=[f"egc{bpar}"])
            s.op("dve", lambda e: e.tensor_tensor(out=dkv, in0=glv, in1=gcv, op=ALU.subtract), reads=[f"gl{bpar}", f"gc{bpar}"], writes=[f"dk{bpar}"])
            s.op("act", lambda e: e.activation(out=dkv, in_=dkv, func=AF.Exp), reads=[f"dk{bpar}"], writes=[f"dk{bpar}"])
            s.op("dve", lambda e: e.tensor_tensor(out=bgev, in0=bsv, in1=egcv, op=ALU.mult), reads=[f"bs{bpar}", f"egc{bpar}"], writes=[f"bge{bpar}"])


        def phase0(_=None):
            ph0_x()
            ph0_g()
            ph0_m()

        def front(p):
            par = p % 2
            for i3 in range(3):
                ct = NP * i3 + p
                pp, pk = inproj_fm(ct)
                rv = raw[:, i3, 0:nseg * (seglen + 3)].rearrange("p (n c) -> p n c", c=seglen + 3)
                if first and not is_sample:
                    s.op("pool", lambda e: e.memset(rv[:, :, 0:3], 0.0), reads=[f"raw{i3}"], writes=[f"raw{i3}"])
                else:
                    s.op("pool", lambda e: e.tensor_copy(out=rv[:, :, 0:3], in_=halo[:, ct, 0:nseg, :]), reads=["halo"], writes=[f"raw{i3}"])
                s.op("act", lambda e: e.activation(out=rv[:, :, 3:3 + seglen], in_=pp.rearrange("p (n c) -> p n c", c=seglen), func=AF.Copy),
                     reads=[pk], writes=[f"raw{i3}"])
                s.op("pool", lambda e: e.tensor_copy(out=halo[:, ct, 0:nseg, :], in_=rv[:, :, seglen:seglen + 3]), reads=[f"raw{i3}"], writes=["halo"])
                cvv = cv[:, i3, 0:TBk].rearrange("p (n c) -> p n c", c=seglen)
                s.op("dve", lambda e: e.tensor_scalar(out=cvv, in0=rv[:, :, 0:seglen], scalar1=cw[:, ct, 0:1], scalar2=None, op0=ALU.mult),
                     reads=[f"raw{i3}", "cw_t"], writes=[f"cv{i3}"])
                for j in range(1, 4):
                    s.op("dve", lambda e: e.scalar_tensor_tensor(out=cvv, in0=rv[:, :, j:j + seglen], scalar=cw[:, ct, j:j + 1], in1=cvv,
                                                                 op0=ALU.mult, op1=ALU.add), reads=[f"raw{i3}", "cw_t", f"cv{i3}"], writes=[f"cv{i3}"])
                if i3 < 2:
                    silu_from(cv[:, i3, 0:TBk], [f"cv{i3}"], cv[:, i3, 0:TBk], f"cv{i3}", tmpA[i3][:, 0:TBk], f"tmpA{i3}", TBk)
                else:
                    silu_from(cv[:, i3, 0:TBk], [f"cv{i3}"], cvb[par][:, 2, 0:TBk], f"cvb{par}v", tmpA[i3][:, 0:TBk], f"tmpA{i3}", TBk)
            for i3 in range(2):
                src = cv[:, i3, 0:TBk]
                s.op("pool", lambda e: e.tensor_tensor(out=sqA[i3][:, 0:TBk], in0=src, in1=src, op=ALU.mult), reads=[f"cv{i3}"], writes=[f"sqA{i3}"])
                s.op("pe", lambda e: e.matmul(pb[7][:, i3 * 256:i3 * 256 + TBk], lhsT=ob2b[:], rhs=sqA[i3][:, 0:TBk], start=True, stop=True), reads=["ob2b", f"sqA{i3}"], writes=["BT"])
                s.op("act", lambda e: e.activation(out=tnA[i3][:, 0:TBk], in_=pb[7][:, i3 * 256:i3 * 256 + TBk], func=AF.Ln, bias=EPS), reads=["BT"], writes=[f"tnA{i3}"])
                s.op("act", lambda e: e.activation(out=tnA[i3][:, 0:TBk], in_=tnA[i3][:, 0:TBk], func=AF.Exp, scale=-0.5), reads=[f"tnA{i3}"], writes=[f"tnA{i3}"])
                if i3 == 0:
                    s.op("dve", lambda e: e.scalar_tensor_tensor(out=cvb[par][:, 0, 0:TBk], in0=src, scalar=0.125, in1=tnA[i3][:, 0:TBk], op0=ALU.mult, op1=ALU.mult),
                         reads=[f"cv{i3}", f"tnA{i3}"], writes=[f"cvb{par}q"])
                else:
                    s.op("dve", lambda e: e.tensor_tensor(out=cvb[par][:, 1, 0:TBk], in0=src, in1=tnA[i3][:, 0:TBk], op=ALU.mult), reads=[f"cv{i3}", f"tnA{i3}"], writes=[f"cvb{par}k"])
            pp, pk = inproj_fm(3 * NP + p)
            s.op("act", lambda e: e.activation(out=za[par][:, 0:TBk], in_=pp, func=AF.Copy), reads=[pk], writes=[f"za{par}"])
            silu_from(za[par][:, 0:TBk], [f"za{par}"], za[par][:, 0:TBk], f"za{par}", tmpA[0][:, 0:TBk], "tmpA0", TBk)
            qn = cvb[par][:, 0, 0:TBk]; kn = cvb[par][:, 1, 0:TBk]; vs = cvb[par][:, 2, 0:TBk]
            ckq, ckk, ckv = f"cvb{par}q", f"cvb{par}k", f"cvb{par}v"
            s.op("dve", lambda e: e.tensor_tensor(out=v3(rhsD[:, 0:TBk]), in0=bc(gs[:, p, 0:nch, None], [128, nch, c]), in1=bc(U_s[:, None, 0:c], [128, nch, c]), op=ALU.mult),
                 reads=[f"gs{bpar}", "U_s"], writes=["rhsD"])
            s.op("pool", lambda e: e.tensor_tensor(out=v3(Dg[:, 0:TBk]), in0=bc(egc[:, p, 0:nch, None], [128, nch, c]), in1=bc(I_s[:, None, 0:c], [128, nch, c]), op=ALU.mult),
                 reads=[f"egc{bpar}", "I_s"], writes=["Dg"])
            for h in range(2):
                rs = slice(64 * h, 64 * h + c)
                s.op("pe", lambda e: e.matmul(pb[6][rs, 0:TBk], lhsT=Tri_s[rs, 0:c], rhs=rhsD[rs, 0:TBk], start=True, stop=True),
                     reads=["Tri_s", "rhsD"], writes=["B6"])
            s.op("act", lambda e: e.activation(out=Ee[:, 0:TBk], in_=pb[6][:, 0:TBk], func=AF.Exp), reads=["B6"], writes=["Ee"])
            s.op("pool", lambda e: e.tensor_tensor(out=v3(Dm[:, 0:TBk]), in0=v3(Ee[:, 0:TBk]), in1=bc(Mc_s[:, None, 0:c], [128, nch, c]), op=ALU.mult),
                 reads=["Ee", "Mc_s"], writes=["Dm"])
            s.op("pool", lambda e: e.tensor_tensor(out=v3(Ds[:, 0:TBk]), in0=v3(Ee[:, 0:TBk]), in1=bc(U_s[:, None, 0:c], [128, nch, c]), op=ALU.mult),
                 reads=["Ee", "U_s"], writes=["Ds"])
            for h in range(2):
                rs = slice(64 * h, 64 * h + c)
                s.op("pe", lambda e: e.matmul(pb[6][64 * h:64 * h + 64, 0:TBk], lhsT=ones[rs, 0:64], rhs=Dg[rs, 0:TBk], start=True, stop=True),
                     reads=["ones", "Dg"], writes=["B6"])
            s.op("act", lambda e: e.activation(out=EBs[par][:, 0:TBk], in_=pb[6][:, 0:TBk], func=AF.Copy), reads=["B6"], writes=[f"EBs{par}"])
            s.op("dve", lambda e: e.tensor_tensor(out=QeT[par][:, 0:TBk], in0=qn, in1=EBs[par][:, 0:TBk], op=ALU.mult), reads=[ckq, f"EBs{par}"], writes=[f"QeT{par}"])
            for ch in range(nch):
                cs = slice(ch * c, (ch + 1) * c)
                for h in range(2):
                    fs = slice(64 * h, 64 * h + 64); rs = slice(64 * h, 64 * h + c)
                    s.op("pe", lambda e: e.matmul(pb[7][rs, cs], lhsT=kn[fs, cs], rhs=kn[fs, cs], start=True, stop=True), reads=[ckk], writes=["BT"])
                    s.op("pe", lambda e: e.matmul(pb[7][rs, 256 + ch * c:256 + (ch + 1) * c], lhsT=qn[fs, cs], rhs=kn[fs, cs], start=True, stop=True),
                         reads=[ckq, ckk], writes=["BT"])
            s.op("dve", lambda e: e.tensor_tensor(out=v3(tmp[2][:, 0:TBk]), in0=v3(pb[7][:, 0:TBk]), in1=bc(nbs[:, p, 0:nch, None], [128, nch, c]), op=ALU.mult),
                 reads=["BT", f"nbs{bpar}"], writes=["tmp2"])
            s.op("pool", lambda e: e.tensor_tensor(out=PT0t[par][:, 0:TBk], in0=tmp[2][:, 0:TBk], in1=Ds[:, 0:TBk], op=ALU.mult), reads=["tmp2", "Ds"], writes=[f"PT0t{par}"])
            s.op("dve", lambda e: e.tensor_tensor(out=attn[:, 0:TBk], in0=pb[7][:, 256:256 + TBk], in1=Dm[:, 0:TBk], op=ALU.mult), reads=["BT", "Dm"], writes=["attn"])

            for ch in range(nch):
                cs = slice(ch * c, (ch + 1) * c)
                for h in range(2):
                    fs = slice(64 * h, 64 * h + 64); rs = slice(64 * h, 64 * h + c)
                    s.op("pe", lambda e: e.matmul(pb[6][rs, cs], lhsT=PT0t[par][rs, cs], rhs=I_sb[rs, 0:c], start=True, stop=True), reads=[f"PT0t{par}", "I_sb"], writes=["B6"])
                    s.op("pe", lambda e: e.matmul(pb[6][rs, 256 + ch * c:256 + (ch + 1) * c], lhsT=attn[rs, cs], rhs=I_sb[rs, 0:c], start=True, stop=True),
                         reads=["attn", "I_sb"], writes=["B6"])
                    s.op("pe", lambda e: e.matmul(pb[7][rs, ch * 64:(ch + 1) * 64], lhsT=kn[fs, cs], rhs=I_sb[fs, 0:64], start=True, stop=True), reads=[ckk, ckv, "I_sb"], writes=["BT"])
                    s.op("pe", lambda e: e.matmul(pb[7][rs, 256 + ch * 64:256 + (ch + 1) * 64], lhsT=vs[fs, cs], rhs=I_sb[fs, 0:64], start=True, stop=True),
                         reads=[ckk, ckv, "I_sb"], writes=["BT"])
            s.op("act", lambda e: e.activation(out=P0t[par][:, 0:TBk], in_=pb[6][:, 0:TBk], func=AF.Copy), reads=["B6"], writes=[f"P0t{par}"])
            s.op("dve", lambda e: e.tensor_tensor(out=v3(R0t[par][:, 0:TBk]), in0=v3(pb[6][:, 0:TBk]), in1=bc(I_s[:, None, 0:c], [128, nch, c]), op=ALU.add),
                 reads=["B6", "I_s"], writes=[f"R0t{par}"])
            s.op("act", lambda e: e.activation(out=attnT[par][:, 0:TBk], in_=pb[6][:, 256:256 + TBk], func=AF.Copy), reads=["B6"], writes=[f"attnT{par}"])
            k4 = pb[7][:, 0:nch * 64].rearrange("p (n d) -> p n d", d=64)
            v4 = pb[7][:, 256:256 + nch * 64].rearrange("p (n d) -> p n d", d=64)
            s.op("dve", lambda e: e.tensor_tensor(out=Kbe[par][:, 0:nch, :], in0=k4, in1=bc(bge[:, p, 0:nch, None], [128, nch, 64]), op=ALU.mult), reads=["BT", f"bge{bpar}"], writes=[f"Kbe{par}"])
            s.op("dve", lambda e: e.tensor_tensor(out=Kd[par][:, 0:nch, :], in0=k4, in1=bc(dk[:, p, 0:nch, None], [128, nch, 64]), op=ALU.mult), reads=["BT", f"dk{bpar}"], writes=[f"Kd{par}"])
            s.op("dve", lambda e: e.tensor_tensor(out=bV[par][:, 0:nch, :], in0=v4, in1=bc(bs[:, p, 0:nch, None], [128, nch, 64]), op=ALU.mult), reads=["BT", f"bs{bpar}"], writes=[f"bV{par}"])
        def core(p):
            par = p % 2
            Pc, kP = P0t[par], f"P0t{par}"
            PTc, kPT = PT0t[par], f"PT0t{par}"
            Rc, kR = R0t[par], f"R0t{par}"
            for lev in range(1, nlev + 2):
                do_pow = lev <= nlev
                need_P = lev < nlev
                do_R = lev >= 2
                nP, nkP = P[lev % 2], f"P{lev % 2}"
                nPT, nkPT = PT[lev % 2], f"PT{lev % 2}"
                nR, nkR = R[lev % 2], f"R{lev % 2}"
                for ch in range(nch):
                    cs = slice(ch * c, (ch + 1) * c)
                    for h in range(2):
                        rs = slice(64 * h, 64 * h + c)
                        if do_pow and need_P:
                            s.op("pe", lambda e: e.matmul(pb[5][rs, cs], lhsT=PTc[rs, cs], rhs=Pc[rs, cs], start=True, stop=True), reads=[kPT, kP], writes=["B5"])
                        if do_pow:
                            s.op("pe", lambda e: e.matmul(pb[5][rs, 256 + ch * c:256 + (ch + 1) * c], lhsT=Pc[rs, cs], rhs=PTc[rs, cs], start=True, stop=True),
                                 reads=[kPT, kP], writes=["B5"])
                        if do_R:
                            s.op("pe", lambda e: e.matmul(pb[4][rs, cs], lhsT=PTc[rs, cs], rhs=Rc[rs, cs], start=True, stop=True), reads=[kPT, kR], writes=["B4"])
                if do_pow and need_P:
                    s.op("act", lambda e: e.activation(out=nP[:, 0:TBk], in_=pb[5][:, 0:TBk], func=AF.Copy), reads=["B5"], writes=[nkP])
                if do_pow:
                    s.op("dve", lambda e: e.tensor_copy(out=nPT[:, 0:TBk], in_=pb[5][:, 256:256 + TBk]), reads=["B5"], writes=[nkPT])
                if do_R:
                    s.op("dve", lambda e: e.tensor_tensor(out=nR[:, 0:TBk], in0=pb[4][:, 0:TBk], in1=Rc[:, 0:TBk], op=ALU.add), reads=["B4", kR], writes=[nkR])
                    Rc, kR = nR, nkR
                if do_pow:
                    if need_P:
                        Pc, kP = nP, nkP
                    PTc, kPT = nPT, nkPT
            Rf, rk = Rc, kR
            for ch in range(nch):
                cs = slice(ch * c, (ch + 1) * c)
                for h in range(2):
                    rs = slice(64 * h, 64 * h + c)
                    s.op("pe", lambda e: e.matmul(pb[4][rs, 256 + ch * 64:256 + (ch + 1) * 64], lhsT=Rf[rs, cs], rhs=bV[par][rs, ch, :], start=True, stop=True),
                         reads=[rk, f"bV{par}"], writes=["B4"])
                    s.op("pe", lambda e: e.matmul(pb[5][64 * h:64 * h + 64, ch * c:(ch + 1) * c], lhsT=Kbe[par][rs, ch, :], rhs=Rf[rs, cs], start=True, stop=True),
                         reads=[rk, f"Kbe{par}"], writes=["B5"])
            s.op("act", lambda e: e.activation(out=u[:, 0:nch, :], in_=pb[4][:, 256:256 + nch * 64].rearrange("p (n d) -> p n d", d=64), func=AF.Copy), reads=["B4"], writes=["u"])
            s.op("dve", lambda e: e.tensor_copy(out=wT[:, 0:TBk], in_=pb[5][:, 0:TBk]), reads=["B5"], writes=["wT"])
            skey = f"Sg{p}"
            for ch in range(nch):
                cs = slice(ch * c, (ch + 1) * c)
                seg = ch // cps
                if ch % cps == 0:
                    if is_sample:
                        s.dma("sp", Sg[:, p, :], sg_d[seg, p], writes=[skey])
                        s.op("dve", lambda e: e.tensor_copy(out=Sgb[:, p, :], in_=Sg[:, p, :]), reads=[skey], writes=[skey + "b"])
                    elif first:
                        s.op("pool", lambda e: e.memset(Sg[:, p, :], 0.0), writes=[skey])
                        s.op("pool", lambda e: e.memset(Sgb[:, p, :], 0.0), writes=[skey + "b"])
                for h in range(2):
                    fs = slice(64 * h, 64 * h + 64); rs = slice(64 * h, 64 * h + c)
                    s.op("pe", lambda e: e.matmul(pb[1][rs, 0:64], lhsT=wT[fs, cs], rhs=Sgb[fs, p, :], start=True, stop=True), reads=["wT", skey + "b"], writes=["B1"])
                s.op("dve", lambda e: e.tensor_tensor(out=vn[:], in0=u[:, ch, :], in1=pb[1][:, 0:64], op=ALU.subtract), reads=["u", "B1"], writes=["vn"])
                for h in range(2):
                    fs = slice(64 * h, 64 * h + 64); rs = slice(64 * h, 64 * h + c)
                    s.op("pe", lambda e: e.matmul(pb[1][fs, 256 + ch * c:256 + (ch + 1) * c], lhsT=Sgb[fs, p, :], rhs=QeT[par][fs, cs], start=True, stop=False),
                         reads=[skey + "b", f"QeT{par}"], writes=["B1"])
                    s.op("pe", lambda e: e.matmul(pb[1][fs, 256 + ch * c:256 + (ch + 1) * c], lhsT=vn[rs, :], rhs=attnT[par][rs, cs], start=False, stop=True),
                         reads=["vn", f"attnT{par}"], writes=["B1"])
                for h in range(2):
                    fs = slice(64 * h, 64 * h + 64); rs = slice(64 * h, 64 * h + c)
                    s.op("pe", lambda e: e.matmul(pb[1][fs, 64:128], lhsT=Kd[par][rs, ch, :], rhs=vn[rs, :], start=True, stop=True), reads=[f"Kd{par}", "vn"], writes=["B1"])
                s.op("dve", lambda e: e.scalar_tensor_tensor(out=Sgb[:, p, :], in0=Sg[:, p, :], scalar=EBs[par][:, (ch + 1) * c - 1:(ch + 1) * c], in1=pb[1][:, 64:128],
                                                             op0=ALU.mult, op1=ALU.add), reads=[skey, f"EBs{par}", "B1"], writes=[skey + "b"])
                s.op("dve", lambda e: e.scalar_tensor_tensor(out=Sg[:, p, :], in0=Sg[:, p, :], scalar=EBs[par][:, (ch + 1) * c - 1:(ch + 1) * c], in1=pb[1][:, 64:128],
                                                             op0=ALU.mult, op1=ALU.add), reads=[skey, f"EBs{par}", "B1"], writes=[skey])
                if is_sample and (ch + 1) % cps == 0:
                    s.dma("sp", ngs[seg, p], Sg[:, p, :], reads=[skey], writes=[f"o_ngs{seg}_{p}"])
            s.op("act", lambda e: e.activation(out=oTf[:, 0:TBk], in_=pb[1][:, 256:256 + TBk], func=AF.Copy), reads=["B1"], writes=["oTf"])
            s.op("pool", lambda e: e.tensor_tensor(out=sqb[:, 0:TBk], in0=oTf[:, 0:TBk], in1=oTf[:, 0:TBk], op=ALU.mult), reads=["oTf"], writes=["sqb"])
            s.op("pe", lambda e: e.matmul(pb[3][:, 0:TBk], lhsT=ob2b[:], rhs=sqb[:, 0:TBk], start=True, stop=True), reads=["ob2b", "sqb"], writes=["B3"])
            s.op("act", lambda e: e.activation(out=tmp[1][:, 0:TBk], in_=pb[3][:, 0:TBk], func=AF.Ln, scale=1.0 / 64, bias=EPS), reads=["B3"], writes=["tmp1"])
            s.op("act", lambda e: e.activation(out=tmp[1][:, 0:TBk], in_=tmp[1][:, 0:TBk], func=AF.Exp, scale=-0.5), reads=["tmp1"], writes=["tmp1"])
            s.op("dve", lambda e: e.tensor_tensor(out=oTf[:, 0:TBk], in0=oTf[:, 0:TBk], in1=tmp[1][:, 0:TBk], op=ALU.mult), reads=["oTf", "tmp1"], writes=["oTf"])
            s.op("dve", lambda e: e.scalar_tensor_tensor(out=oOwn[bpar][:, p, 0:TBk], in0=oTf[:, 0:TBk], scalar=gnw[:, 0:1], in1=za[par][:, 0:TBk], op0=ALU.mult, op1=ALU.mult),
                 reads=["oTf", "gnw_t", f"za{par}"], writes=[f"oOwn{bpar}_{p}"])

        def ph0_m():
            s.op("pool", lambda e: e.memset(smask[:, 0:TBk], 1.0), writes=[f"smask{bpar}"])
            s.op("pool", lambda e: e.memset(v3(smask[:, 0:TBk])[:, :, 0:1], 0.0), reads=[f"smask{bpar}"], writes=[f"smask{bpar}"])

        def hg(h):
            pp, pk = inproj_fm(4 * NP + h, 1)
            s.op("act", lambda e: e.activation(out=qb[:, 0:TBk], in_=pp, func=AF.Copy), reads=[pk], writes=["qb"])
            silu_from(qb[:, 0:TBk], ["qb"], qb[:, 0:TBk], "qb", bl[:, 0:TBk], "bl", TBk)
            pp, pk = inproj_fm(4 * NP + 2 * HB + h, 1)
            s.op("act", lambda e: e.activation(out=zb[:, 0:TBk], in_=pp, func=AF.Copy), reads=[pk], writes=["zb"])
            silu_from(zb[:, 0:TBk], ["zb"], zb[:, 0:TBk], "zb", bl[:, 0:TBk], "bl", TBk)
            pp, pk = inproj_fm(4 * NP + HB + h, 1)
            s.op("act", lambda e: e.activation(out=ff[:, 0:TBk], in_=pp, func=AF.Exp, scale=-1.0), reads=[pk], writes=["ff"])
            s.op("act", lambda e: e.activation(out=ff[:, 0:TBk], in_=ff[:, 0:TBk], func=AF.Ln, bias=1.0), reads=["ff"], writes=["ff"])
            s.op("act", lambda e: e.activation(out=ff[:, 0:TBk], in_=ff[:, 0:TBk], func=AF.Exp, scale=-1.0), reads=["ff"], writes=["ff"])
            s.op("dve", lambda e: e.tensor_scalar(out=ff[:, 0:TBk], in0=ff[:, 0:TBk], scalar1=oml[:, h:h + 1], scalar2=lb[:, h:h + 1], op0=ALU.mult, op1=ALU.add),
                 reads=["ff", "oml", "lb"], writes=["ff"])
            s.op("act", lambda e: e.activation(out=lf[:, 0:TBk], in_=ff[:, 0:TBk], func=AF.Ln), reads=["ff"], writes=["lf"])
            s.op("dve", lambda e: e.tensor_scalar(out=kb[:, 0:TBk], in0=ff[:, 0:TBk], scalar1=-1.0, scalar2=1.0, op0=ALU.mult, op1=ALU.add), reads=["ff"], writes=["kb"])
            s.op("dve", lambda e: e.tensor_tensor_scan(out=bb[:, 0:TBk], data0=smask[:, 0:TBk], data1=lf[:, 0:TBk], initial=0.0, op0=ALU.mult, op1=ALU.add),
                 reads=[f"smask{bpar}", "lf"], writes=["bb"])
            b3 = v3(bb[:, 0:TBk])
            s.op("pool", lambda e: e.tensor_tensor(out=v3(bl[:, 0:TBk]), in0=b3, in1=bc(b3[:, :, c - 1:c], [128, nch, c]), op=ALU.subtract), reads=["bb"], writes=["bl"])
            s.op("act", lambda e: e.activation(out=Qef[:, 0:TBk], in_=bb[:, 0:TBk], func=AF.Exp), reads=["bb"], writes=["Qef"])
            s.op("act", lambda e: e.activation(out=Qxf[:, 0:TBk], in_=bl[:, 0:TBk], func=AF.Exp), reads=["bl"], writes=["Qxf"])
            s.op("act", lambda e: e.activation(out=Kdf[:, 0:TBk], in_=bl[:, 0:TBk], func=AF.Exp, scale=-1.0), reads=["bl"], writes=["Kdf"])
            s.op("act", lambda e: e.activation(out=ebl[:, 0:nch], in_=b3[:, :, c - 1], func=AF.Exp), reads=["bb"], writes=["ebl"])
            s.op("dve", lambda e: e.tensor_tensor(out=Qe[:, 0:TBk], in0=Qef[:, 0:TBk], in1=qb[:, 0:TBk], op=ALU.mult), reads=["Qef", "qb"], writes=["Qe"])
            s.op("pool", lambda e: e.tensor_tensor(out=Qx[:, 0:TBk], in0=Qxf[:, 0:TBk], in1=qb[:, 0:TBk], op=ALU.mult), reads=["Qxf", "qb"], writes=["Qx"])
            s.op("dve", lambda e: e.tensor_tensor(out=Kdh[:, 0:TBk], in0=Kdf[:, 0:TBk], in1=kb[:, 0:TBk], op=ALU.mult), reads=["Kdf", "kb"], writes=["Kdh"])
            for ch in range(nch):
                cs = slice(ch * c, (ch + 1) * c)
                outp = pb[2][0:c, 128:256]
                for k in range(8):
                    s.op("pe", lambda e: e.matmul(outp, lhsT=hT[:, k, cs], rhs=Wb[:, k, C_HI + 128 * h:C_HI + 128 * (h + 1)], start=(k == 0), stop=(k == 7)),
                         reads=["Wb", f"hT{bpar}"], writes=["B2"])
                s.op("pe", lambda e: e.matmul(pb[2][0:c, 256:384], lhsT=Kdh[:, cs], rhs=identb[:], start=True, stop=True), reads=["Kdh", "identb"], writes=["B2"])
                s.op("pe", lambda e: e.matmul(pb[2][0:c, 384:384 + c], lhsT=Kdh[:, cs], rhs=Qx[:, cs], start=True, stop=True), reads=["Kdh", "Qx"], writes=["B2"])
                s.op("act", lambda e: e.activation(out=vtok[0:c, ch, :], in_=outp, func=AF.Copy), reads=["B2"], writes=["vtok"])
                s.op("dve", lambda e: e.tensor_copy(out=Kdt[0:c, ch, :], in_=pb[2][0:c, 256:384]), reads=["B2"], writes=["Kdt"])
                s.op("dve", lambda e: e.tensor_tensor(out=aTh[0:c, cs], in0=pb[2][0:c, 384:384 + c], in1=Tri_s[0:c, 0:c], op=ALU.mult),
                     reads=["B2", "Tri_s"], writes=["aTh"])
            skey = f"Sh{h}"
            for ch in range(nch):
                cs = slice(ch * c, (ch + 1) * c)
                seg = ch // cps
                if ch % cps == 0:
                    if is_sample:
                        s.dma("sp", Sh[:, h, :], sh_d[seg, h], writes=[skey])
                        s.op("dve", lambda e: e.tensor_copy(out=Shb[:, h, :], in_=Sh[:, h, :]), reads=[skey], writes=[skey + "b"])
                    elif first:
                        s.op("pool", lambda e: e.memset(Sh[:, h, :], 0.0), writes=[skey])
                        s.op("pool", lambda e: e.memset(Shb[:, h, :], 0.0), writes=[skey + "b"])
                s.op("pe", lambda e: e.matmul(pb[3][:, 256 + ch * c:256 + (ch + 1) * c], lhsT=Shb[:, h, :], rhs=Qe[:, cs], start=True, stop=False), reads=[skey + "b", "Qe"], writes=["B3"])
                s.op("pe", lambda e: e.matmul(pb[3][:, 256 + ch * c:256 + (ch + 1) * c], lhsT=vtok[0:c, ch, :], rhs=aTh[0:c, cs], start=False, stop=True),
                     reads=["vtok", "aTh"], writes=["B3"])
                s.op("pe", lambda e: e.matmul(pb[1][:, 128:256], lhsT=Kdt[0:c, ch, :], rhs=vtok[0:c, ch, :], start=True, stop=True), reads=["Kdt", "vtok"], writes=["B1"])
                s.op("dve", lambda e: e.scalar_tensor_tensor(out=Shb[:, h, :], in0=Sh[:, h, :], scalar=ebl[:, ch:ch + 1], in1=pb[1][:, 128:256], op0=ALU.mult, op1=ALU.add),
                     reads=[skey, "ebl", "B1"], writes=[skey + "b"])
                s.op("dve", lambda e: e.scalar_tensor_tensor(out=Sh[:, h, :], in0=Sh[:, h, :], scalar=ebl[:, ch:ch + 1], in1=pb[1][:, 128:256], op0=ALU.mult, op1=ALU.add),
                     reads=[skey, "ebl", "B1"], writes=[skey])
                if is_sample and (ch + 1) % cps == 0:
                    s.dma("sp", nhs[seg, h], Sh[:, h, :], reads=[skey], writes=[f"o_nhs{seg}_{h}"])
            s.op("act", lambda e: e.activation(out=oTfH[:, 0:TBk], in_=pb[3][:, 256:256 + TBk], func=AF.Copy), reads=["B3"], writes=["oTfH"])
            s.op("pool", lambda e: e.tensor_tensor(out=sqbH[:, 0:TBk], in0=oTfH[:, 0:TBk], in1=oTfH[:, 0:TBk], op=ALU.mult), reads=["oTfH"], writes=["sqbH"])
            s.op("pe", lambda e: e.matmul(pb[3][:, 256:256 + TBk], lhsT=onesb[:], rhs=sqbH[:, 0:TBk], start=True, stop=True), reads=["onesb", "sqbH"], writes=["B3"])
            s.op("act", lambda e: e.activation(out=tmpH[1][:, 0:TBk], in_=pb[3][:, 256:256 + TBk], func=AF.Ln, scale=1.0 / 128, bias=EPS), reads=["B3"], writes=["tmpH1"])
            s.op("act", lambda e: e.activation(out=tmpH[1][:, 0:TBk], in_=tmpH[1][:, 0:TBk], func=AF.Exp, scale=-0.5), reads=["tmpH1"], writes=["tmpH1"])
            s.op("dve", lambda e: e.tensor_tensor(out=oTfH[:, 0:TBk], in0=oTfH[:, 0:TBk], in1=tmpH[1][:, 0:TBk], op=ALU.mult), reads=["oTfH", "tmpH1"], writes=["oTfH"])
            s.op("dve", lambda e: e.scalar_tensor_tensor(out=oOwn[bpar][:, NP + h, 0:TBk], in0=oTfH[:, 0:TBk], scalar=hnw[:, 0:1], in1=zb[:, 0:TBk], op0=ALU.mult, op1=ALU.mult),
                 reads=["oTfH", "hnw_t", "zb"], writes=[f"oOwn{bpar}_{NP + h}"])

        def xchg(_=None):
            xs_, xd_ = (xsrc_s, xdst_s) if is_sample else (xsrc[bpar], xdst[bpar])
            s.dma("sp", xs_.rearrange("(t p) n -> p t n", p=128), oOwn[bpar][:, :, 0:TBk], reads=[f"oOwn{bpar}_{i}" for i in range(NL)], writes=[f"xsrc{bpar}"])
            s.coll([xs_], [xd_], reads=[f"xsrc{bpar}"], writes=[f"xdst{bpar}"])
            s.dma("sp", oTn[bpar][:, :, 0:TBk], xd_.rearrange("(t p) n -> p t n", p=128), reads=[f"xdst{bpar}"], writes=[f"oTn{bpar}_{k}" for k in range(2 * NL)])

        def outproj(_=None):
            for tt in range(ntt):
                X = xt[bpar * 2 + tt]
                for half in range(2):
                    bank = pb[6] if half == 0 else pb[7]
                    bk = "B6" if half == 0 else "BT"
                    for k in range(8):
                        s.op("pe", lambda e: e.matmul(bank[0:TT, :], lhsT=oTn[bpar][:, k, tt * TT:(tt + 1) * TT], rhs=WOb[:, k, half * 512:(half + 1) * 512], start=(k == 0), stop=(k == 7)),
                             reads=[f"oTn{bpar}_{k}", "WOb"], writes=[bk])
                    s.op("dve", lambda e: e.tensor_tensor(out=yo[0:TT, half * 512:(half + 1) * 512], in0=bank[0:TT, :], in1=X[0:TT, half * 512:(half + 1) * 512], op=ALU.add),
                         reads=[bk, X.name], writes=["yo"])
                s.op("act", lambda e: e.activation(out=sqjO[0:TT, :], in_=yo[0:TT, :], func=AF.Square, accum_out=ssO[0:TT, :]), reads=["yo"], writes=["sqjO", "ssO"])
                s.op("act", lambda e: e.activation(out=rrO[0:TT, :], in_=ssO[0:TT, :], func=AF.Ln, scale=1.0 / D, bias=EPS), reads=["ssO"], writes=["rrO"])
                s.op("act", lambda e: e.activation(out=rrO[0:TT, :], in_=rrO[0:TT, :], func=AF.Exp, scale=-0.5), reads=["rrO"], writes=["rrO"])
                s.op("dve", lambda e: e.scalar_tensor_tensor(out=yo2[0:TT, :], in0=yo[0:TT, :], scalar=rrO[0:TT, :], in1=fnw[0:TT, :], op0=ALU.mult, op1=ALU.mult),
                     reads=["yo", "rrO", "fnw_t"], writes=["yo2"])
                s.dma("sp", y_dst[t0 + tt * TT: t0 + (tt + 1) * TT, :], yo2[0:TT, :], reads=["yo2"], writes=[f"o_y{id(y_dst)}"], slot="yout")

        def record(fn, arg=None):
            lst = []
            s.rec = lst
            fn(arg)
            s.rec = None
            return lst
        return dict(p0=lambda: record(phase0), op=lambda: record(outproj), xc=lambda: record(xchg),
                    fr=lambda p: record(front, p), co=lambda p: record(core, p), hg=lambda h: record(hg, h))

    def emit_blocks(blks, extra_nodes=(), extra_deps=None):
        n = len(blks)
        extra_deps = extra_deps or {}
        nodes = []
        for b, P in enumerate(blks):
            N = lambda kind, bb: f"{kind}_{bb}"
            X = lambda name: extra_deps.get(name, [])
            nodes.append((N("P0", b), P["p0"](), [N("P0", b - 1), N("FR1", b - 2), N("HG1", b - 2), N("OP", b - 2)] + X(N("P0", b)), b * 10 + 0))
            nodes.append((N("FR0", b), P["fr"](0), [N("P0", b), N("FR1", b - 1), N("CO0", b - 1), N("OP", b - 2)] + X(N("FR0", b)), b * 10 + 1))
            nodes.append((N("HG0", b), P["hg"](0), [N("P0", b), N("HG1", b - 1), N("XC", b - 2)] + X(N("HG0", b)), b * 10 + 2))
            nodes.append((N("CO0", b), P["co"](0), [N("FR0", b), N("CO1", b - 1), N("XC", b - 2)] + X(N("CO0", b)), b * 10 + 3))
            nodes.append((N("FR1", b), P["fr"](1), [N("FR0", b), N("CO1", b - 1)], b * 10 + 4))
            nodes.append((N("HG1", b), P["hg"](1), [N("HG0", b)], b * 10 + 5))
            nodes.append((N("CO1", b), P["co"](1), [N("FR1", b), N("CO0", b)], b * 10 + 6))
            nodes.append((N("XC", b), P["xc"](), [N("CO1", b), N("HG1", b), N("XC", b - 1), N("OP", b - 2)], b * 10 + 7))
            nodes.append((N("OP", b), P["op"](), [N("XC", b), N("OP", b - 1), N("FR1", b + 1) if b + 1 < n else N("FR1", b)], b * 10 + 18))
        nodes += list(extra_nodes)
        s.dag_emit(nodes)

    assert NP == 2 and HB == 2
    nblk = T // TB
    blks = [block(xp, yp, b * TB, 1, TB, 64, b == 0, False, b % 2) for b in range(nblk)]
    sblk = block(xs, ys, 0, 4, 16, 16, True, True, nblk % 2)
    sw = []
    s.rec = sw
    s.dma("sp", ncp[:, :, 0, :], halo[:, :, 0, :], reads=["halo"], writes=["o_ncp"])
    s.dma("sp", halo[:], sc_d, reads=["halo"], writes=["halo"])
    s.rec = None
    so = []
    s.rec = so
    for p in range(NP):
        s.dma("sp", ngp[0, p], Sg[:, p, :], reads=[f"Sg{p}"], writes=[f"o_ngp{p}"])
    for h in range(HB):
        s.dma("sp", nhp[0, h], Sh[:, h, :], reads=[f"Sh{h}"], writes=[f"o_nhp{h}"])
    s.rec = None
    L = nblk - 1
    extra_nodes = [("SW", sw, [f"FR1_{L}"], L * 10 + 8), ("SO", so, [f"CO1_{L}", f"HG1_{L}"], L * 10 + 9)]
    extra_deps = {f"FR0_{nblk}": ["SW"], f"CO0_{nblk}": ["SO"], f"HG0_{nblk}": ["SO"]}
    emit_blocks(blks + [sblk], extra_nodes, extra_deps)
    s.dma("sp", ncs, halo[:], reads=["halo"], writes=["o_ncs"])
    s.finish("sp")
    return nc, s


def _perm(hh):
    r = lambda base, w: np.arange(base + hh * w, base + (hh + 1) * w)
    return np.concatenate([r(0, 256), r(512, 256), r(1024, 256), r(1536, 256),
                           r(2064, 256), r(2576, 256), r(3600, 256), r(3088, 256),
                           r(2048, 4), r(2056, 4)])


def _chan(hh):
    r = lambda base: np.arange(base + hh * 256, base + (hh + 1) * 256)
    return np.concatenate([r(0), r(512), r(1024)])


_WOUT_ROWS = np.concatenate([np.concatenate([np.arange(r * 256, (r + 1) * 256), np.arange(512 + r * 256, 512 + (r + 1) * 256)]) for r in range(2)])


def _core_inputs(c, inp):
    b, hh = c // 2, c % 2
    f = lambda a: np.ascontiguousarray(a, dtype=np.float32)
    ch = _chan(hh)
    cwv = inp["conv_w"][0][:, ch]
    scv = inp["state_conv"][0][4 * b:4 * b + 4][:, :, ch]
    hs = slice(4 * hh, 4 * hh + 4)
    return {
        "xp": f(inp["x_prompt"][b]),
        "xs": f(inp["x_sample"][4 * b:4 * b + 4].reshape(64, D)),
        "w_in": f(inp["w_in"][0][:, _perm(hh)]),
        "w_out": f(inp["w_out"][0][_WOUT_ROWS]),
        "nw": f(inp["norm_w"][0].reshape(8, 128).T),
        "cw": f(cwv.reshape(4, 3 * NP, 128).transpose(2, 1, 0)),
        "alog": f(np.broadcast_to(inp["gdn_A_log"][0][None, hs], (128, HA))),
        "dtb": f(np.broadcast_to(inp["gdn_dt_bias"][0][None, hs], (128, HA))),
        "gnw": f(np.tile(inp["gdn_norm_w"][0], 2).reshape(128, 1)),
        "hnw": f(inp["hgrn_norm_w"][0].reshape(128, 1)),
        "lbl": f(inp["hgrn_lb_logits"][:, hh * 256:(hh + 1) * 256].reshape(2, HB, 128).transpose(2, 1, 0)),
        "fnw": f(np.broadcast_to(inp["final_norm_w"][None, :], (128, D))),
        "sc": f(scv.reshape(4, 3, 3 * NP, 128).transpose(3, 2, 0, 1)),
        "sg": f(inp["state_gdn"][0][4 * b:4 * b + 4][:, hs].reshape(4, NP, 128, 64)),
        "sh": f(inp["state_hgrn"][0][4 * b:4 * b + 4][:, 2 * hh:2 * hh + 2]),
    }


_CACHE = {}


def kernel(**inputs):
    inp = {k: np.asarray(v) for k, v in inputs.items()}
    Bp, T, _ = inp["x_prompt"].shape
    assert Bp == 4 and inp["x_sample"].shape[:2] == (16, 16)
    if T not in _CACHE:
        _CACHE[T] = build(T)[0]
    nc = _CACHE[T]
    in_maps = [_core_inputs(c, inp) for c in range(8)]
    res = run_bass_kernel_spmd(nc, in_maps, core_ids=list(range(8)))
    r = res.results
    y_prompt = np.stack([r[2 * b]["yp"] for b in range(4)]).astype(np.float32)
    y_sample = np.concatenate([r[2 * b]["ys"].reshape(4, 16, D) for b in range(4)]).astype(np.float32)
    ncp_ = np.zeros((1, 4, 3, 1536), np.float32); ncs_ = np.zeros((1, 16, 3, 1536), np.float32)
    ngp_ = np.zeros((1, 4, 8, 64, 64), np.float32); ngs_ = np.zeros((1, 16, 8, 64, 64), np.float32)
    nhp_ = np.zeros((1, 4, 4, 128, 128), np.float32); nhs_ = np.zeros((1, 16, 4, 128, 128), np.float32)
    cvt = lambda a: a.transpose(2, 3, 1, 0).reshape(a.shape[2], 3, 3 * NP * 128)
    for c in range(8):
        b, hh = c // 2, c % 2
        ch = _chan(hh)
        ncp_[0, b][:, ch] = cvt(r[c]["ncp"])[0]
        ncs_[0, 4 * b:4 * b + 4][:, :, ch] = cvt(r[c]["ncs"])
        ngp_[0, b, 4 * hh:4 * hh + 4] = r[c]["ngp"].reshape(HA, 64, 64)
        ngs_[0, 4 * b:4 * b + 4, 4 * hh:4 * hh + 4] = r[c]["ngs"].reshape(4, HA, 64, 64)
        nhp_[0, b, 2 * hh:2 * hh + 2] = r[c]["nhp"].reshape(HB, 128, 128)
        nhs_[0, 4 * b:4 * b + 4, 2 * hh:2 * hh + 2] = r[c]["nhs"].reshape(4, HB, 128, 128)
    return (y_prompt, y_sample, ncp_, ngp_, nhp_, ncs_, ngs_, nhs_)
```

```python
import numpy as np
import concourse.bass as bass
import concourse.mybir as mybir
from concourse.bass_utils import run_bass_kernel_spmd

F32 = mybir.dt.float32
BF16 = mybir.dt.bfloat16
AF = mybir.ActivationFunctionType
ALU = mybir.AluOpType

D = 1024
HA, HB = 4, 2
NP = HA // 2
NT = 4 * NP + 3 * HB
C_HI = NT * 128
C_G = C_HI + HB * 128
NCOL = C_G + 2 * HA
EPS = 1e-6
RG = [[0, 1], [2, 3], [4, 5], [6, 7]]


SAME_ENGINE_WAIT = True
SCHED_MODE = 0
SCHED_DELTA = 300.0
SCHED_WIN = 24


class _Proxy:
    def __getattr__(self, name):
        return lambda *a, **k: (name, a, k)


_PROXY = _Proxy()
PSUM_KEYS = {"B0", "B1", "B2", "B3", "B4", "B5", "B6", "BT"}


class Sched:
    def __init__(self, nc):
        self.nc = nc
        self.eng = {"pe": nc.tensor, "dve": nc.vector, "act": nc.scalar, "pool": nc.gpsimd, "sp": nc.sync}
        self.sem = {k: nc.alloc_semaphore(name=f"s_{k}") for k in self.eng}
        self.cnt = {k: 0 for k in self.eng}
        self.seen = {k: {} for k in self.eng}
        self.last_w = {}
        self.readers = {}
        self.dma_sems = {}
        self.n_wait = 0
        self.n_ops = 0
        self.rec = None

    def coll(self, ins, outs, reads=(), writes=()):
        if self.rec is not None:
            self.rec.append(("coll", "pool", ins, outs, tuple(reads), tuple(writes)))
            return None
        self._deps("pool", reads, writes)
        if "cc" not in self.dma_sems:
            self.dma_sems["cc"] = [self.nc.alloc_semaphore(name="cc_sem"), 0]
        ent = self.dma_sems["cc"]
        ent[1] += 1
        self.nc.gpsimd.collective_compute("AllGather", ALU.bypass, replica_groups=RG, ins=ins, outs=outs).then_inc(ent[0], 1)
        tok = ("cc", ent[0], ent[1])
        self._commit(tok, reads, writes)
        self.n_ops += 1
        return tok

    def emit(self, r):
        if r[0] == "coll":
            self.coll(r[2], r[3], r[4], r[5])
            return
        if r[0] == "op":
            _, e, call, reads, writes = r
            self.op(e, lambda eng: getattr(eng, call[0])(*call[1], **call[2]), reads, writes)
        else:
            _, q, out, in_, reads, writes, slot = r
            self.dma(q, out, in_, reads, writes, slot)

    def _cost(self, r):
        if r[0] == "dma":
            return 2500.0
        if r[0] == "coll":
            return 30000.0
        _, e, call, reads, writes = r
        name, args, kw = call
        def nfree(ap):
            try:
                sh = list(ap.shape)
                n = 1
                for d in sh[1:]:
                    n *= int(d)
                return n
            except Exception:
                return 256
        if e == "pe":
            ap = kw.get("rhs", None) if name == "matmul" else kw.get("in_", None)
            n = nfree(ap) if ap is not None else 64
            c = 32.0 + 0.4 * n
            try:
                if name == "matmul" and kw["rhs"].dtype == F32:
                    c *= 3.0
            except Exception:
                pass
            return c
        ap = kw.get("out", None)
        n = nfree(ap) if ap is not None else 256
        if e == "dve":
            return 110.0 + 1.0 * n
        if e == "act":
            return 170.0 + 0.9 * n
        return 110.0 + 1.8 * n

    def _est_start(self, r):
        if r[0] in ("dma", "coll"):
            e, reads, writes = r[1], r[4], r[5]
        else:
            e, reads, writes = r[1], r[3], r[4]
        m = self.model
        t = m["eng"].get(e, 0.0)
        ex = [k for k in reads if k in PSUM_KEYS]
        for k in reads:
            w = m["w"].get(k)
            if w is not None:
                t = max(t, w[0] + ((0.0 if e == "pe" else 150.0) if w[1] == e else 230.0))
        for k in list(writes) + ex:
            w = m["w"].get(k)
            if w is not None:
                t = max(t, w[0] + ((0.0 if e == "pe" else 150.0) if w[1] == e else 230.0))
            for (tt, ee) in m["r"].get(k, {}).values():
                t = max(t, tt + ((0.0 if e == "pe" else 150.0) if ee == e else 230.0))
        return t

    def _model_commit(self, r, t0):
        if r[0] in ("dma", "coll"):
            e, reads, writes = r[1], r[4], r[5]
            eng_busy = 100.0
        else:
            e, reads, writes = r[1], r[3], r[4]
            eng_busy = None
        m = self.model
        c = self._cost(r)
        t1 = t0 + c
        m["eng"][e] = t0 + (eng_busy if eng_busy is not None else c)
        ex = [k for k in reads if k in PSUM_KEYS]
        who = e if r[0] == "op" else "dma"
        for k in reads:
            m["r"].setdefault(k, {})[who] = (t1, who)
        for k in list(writes) + ex:
            m["w"][k] = (t1, who)
            m["r"][k] = {}
        m["t"] = max(m.get("t", 0.0), t1)

    def dag_emit(self, nodes):
        if not hasattr(self, "model"):
            self.model = {"eng": {}, "w": {}, "r": {}, "t": 0.0}
        names = {n[0] for n in nodes}
        units, deps, prio, preds = {}, {}, {}, {}
        def rw(r):
            if r[0] in ("dma", "coll"):
                reads, writes = r[4], r[5]
            else:
                reads, writes = r[3], r[4]
            ex = [k for k in reads if k in PSUM_KEYS]
            return list(reads), list(writes) + ex
        for (name, ops, dp, pr) in nodes:
            u = []
            for r in ops:
                glued = (r[0] == "op" and r[2][0] == "matmul" and r[2][2].get("start") is False)
                if glued and u:
                    u[-1].append(r)
                else:
                    u.append([r])
            units[name] = u
            deps[name] = {d for d in dp if d in names}
            prio[name] = pr
            lw, rd, pl = {}, {}, []
            for j, unit in enumerate(u):
                p = set()
                R, W = [], []
                for r in unit:
                    a_, b_ = rw(r)
                    R += a_; W += b_
                for k in R:
                    if k in lw:
                        p.add(lw[k])
                for k in W:
                    if k in lw:
                        p.add(lw[k])
                    p |= rd.get(k, set())
                p.discard(j)
                pl.append(p)
                for k in R:
                    rd.setdefault(k, set()).add(j)
                for k in W:
                    lw[k] = j
                    rd[k] = set()
            preds[name] = pl
        emitted = {n: [False] * len(units[n]) for n in units}
        nleft = {n: len(units[n]) for n in units}
        lo = {n: 0 for n in units}
        done = {n for n in units if not units[n]}
        waiting = [n for n in units if n not in done]
        active = []
        def refresh():
            nonlocal waiting
            still = []
            for n in waiting:
                if deps[n] <= done:
                    active.append(n)
                else:
                    still.append(n)
            waiting = still
        refresh()
        WIN = SCHED_WIN
        while active:
            best, bsel = None, None
            for n in active:
                em, pl, u = emitted[n], preds[n], units[n]
                j = lo[n]
                seen = 0
                while j < len(u) and seen < WIN:
                    if not em[j]:
                        seen += 1
                        if all(em[q] for q in pl[j]):
                            t = self._est_start(u[j][0])
                            key = (t, prio[n], j)
                            if best is None or key < best:
                                best, bsel = key, (n, j)
                    j += 1
            n, j = bsel
            for r in units[n][j]:
                t0 = self._est_start(r)
                self._model_commit(r, t0)
                self.emit(r)
            emitted[n][j] = True
            nleft[n] -= 1
            while lo[n] < len(units[n]) and emitted[n][lo[n]]:
                lo[n] += 1
            if nleft[n] == 0:
                active.remove(n)
                done.add(n)
                refresh()
        assert not waiting, ("DAG deadlock", waiting[:5])

    def merge_emit(self, streams):
        if not hasattr(self, "model"):
            self.model = {"eng": {}, "w": {}, "r": {}, "t": 0.0}
        units = []
        for st in streams:
            u = []
            for r in st:
                glued = (r[0] == "op" and r[2][0] == "matmul" and r[2][2].get("start") is False)
                if glued and u:
                    u[-1].append(r)
                else:
                    u.append([r])
            units.append(u)
        pos = [0] * len(units)
        while True:
            best, bi = None, -1
            for i, u in enumerate(units):
                if pos[i] < len(u):
                    t = self._est_start(u[pos[i]][0])
                    key = (t, -(len(u) - pos[i]))
                    if best is None or key < best:
                        best, bi = key, i
            if bi < 0:
                break
            for r in units[bi][pos[bi]]:
                t0 = self._est_start(r)
                self._model_commit(r, t0)
                self.emit(r)
            pos[bi] += 1


    def _wait(self, e, tok):
        name, sem, val = tok
        if name == "pe" and e == "pe":
            return
        if name == e and not SAME_ENGINE_WAIT:
            return
        if self.seen[e].get(name, 0) >= val:
            return
        self.eng[e].wait_ge(sem, val)
        self.seen[e][name] = val
        self.n_wait += 1

    def _deps(self, e, reads, writes):
        for k in reads:
            t = self.last_w.get(k)
            if t is not None:
                self._wait(e, t)
        for k in writes:
            t = self.last_w.get(k)
            if t is not None:
                self._wait(e, t)
            for t in self.readers.get(k, {}).values():
                self._wait(e, t)

    def _commit(self, tok, reads, writes):
        for k in reads:
            self.readers.setdefault(k, {})[tok[0]] = tok
        for k in writes:
            self.last_w[k] = tok
            self.readers[k] = {}

    def op(self, e, fn, reads=(), writes=()):
        if self.rec is not None:
            self.rec.append(("op", e, fn(_PROXY), tuple(reads), tuple(writes)))
            return None
        ex = [k for k in reads if k in PSUM_KEYS]
        if ex:
            writes = list(writes) + ex
        self._deps(e, reads, writes)
        ins = fn(self.eng[e])
        self.cnt[e] += 1
        ins.then_inc(self.sem[e], 1)
        tok = (e, self.sem[e], self.cnt[e])
        self._commit(tok, reads, writes)
        self.n_ops += 1
        return tok

    def dma(self, q, out, in_, reads=(), writes=(), slot=None):
        if self.rec is not None:
            self.rec.append(("dma", q, out, in_, tuple(reads), tuple(writes), slot))
            return None
        self._deps(q, reads, writes)
        slot = slot or (writes[0] if writes else reads[0])
        sname = f"d_{slot}"
        if sname not in self.dma_sems:
            self.dma_sems[sname] = [self.nc.alloc_semaphore(name=sname), 0]
        ent = self.dma_sems[sname]
        ent[1] += 16
        self.eng[q].dma_start(out=out, in_=in_).then_inc(ent[0], 16)
        tok = (sname, ent[0], ent[1])
        self._commit(tok, reads, writes)
        self.n_ops += 1
        return tok

    def finish(self, e="sp"):
        for k, t in list(self.last_w.items()):
            self._wait(e, t)


def bc(ap, shape):
    return ap.to_broadcast(list(shape))


def build(T, TB=256):
    nc = bass.Bass("TRN2", target_bir_lowering=False)
    s = Sched(nc)
    dt_in = lambda n, sh: nc.dram_tensor(n, list(sh), F32, kind="ExternalInput").ap()
    dt_out = lambda n, sh: nc.dram_tensor(n, list(sh), F32, kind="ExternalOutput").ap()
    xp = dt_in("xp", [T, D]); xs = dt_in("xs", [64, D])
    w_in = dt_in("w_in", [D, NCOL]); w_out = dt_in("w_out", [D, D])
    nw_d = dt_in("nw", [128, 8]); cw_d = dt_in("cw", [128, 3 * NP, 4])
    alog_d = dt_in("alog", [128, HA]); dtb_d = dt_in("dtb", [128, HA])
    gnw_d = dt_in("gnw", [128, 1]); hnw_d = dt_in("hnw", [128, 1])
    lbl_d = dt_in("lbl", [128, HB, 2]); fnw_d = dt_in("fnw", [128, D])
    sc_d = dt_in("sc", [128, 3 * NP, 4, 3])
    sg_d = dt_in("sg", [4, NP, 128, 64]); sh_d = dt_in("sh", [4, HB, 128, 128])
    yp = dt_out("yp", [T, D]); ys = dt_out("ys", [64, D])
    ncp = dt_out("ncp", [128, 3 * NP, 1, 3]); ngp = dt_out("ngp", [1, NP, 128, 64]); nhp = dt_out("nhp", [1, HB, 128, 128])
    ncs = dt_out("ncs", [128, 3 * NP, 4, 3]); ngs = dt_out("ngs", [4, NP, 128, 64]); nhs = dt_out("nhs", [4, HB, 128, 128])

    NL = NP + HB
    xsrc = [nc.dram_tensor(f"xsrc{i}", [NL * 128, TB], BF16).ap() for i in range(2)]
    xdst = [nc.dram_tensor(f"xdst{i}", [2 * NL * 128, TB], BF16).ap() for i in range(2)]
    xsrc_s = nc.dram_tensor("xsrc_s", [NL * 128, 64], BF16).ap()
    xdst_s = nc.dram_tensor("xdst_s", [2 * NL * 128, 64], BF16).ap()
    sb = lambda n, sh, d=F32: nc.alloc_sbuf_tensor(n, list(sh), d)
    Wb = sb("Wb", [128, 8, NCOL], BF16)
    WOb = sb("WOb", [128, 8, D], BF16)
    nw = sb("nw_t", [128, 8]); cw = sb("cw_t", [128, 3 * NP, 4])
    alog = sb("alog_t", [128, HA]); dtb = sb("dtb_t", [128, HA]); negA = sb("negA", [128, HA])
    gnw = sb("gnw_t", [128, 1]); hnw = sb("hnw_t", [128, 1])
    lbl = sb("lbl_t", [128, HB, 2]); lb = sb("lb", [128, HB]); oml = sb("oml", [128, HB])
    fnw = sb("fnw_t", [128, D])
    identb = sb("identb", [128, 128], BF16); identf = sb("identf", [128, 128])
    ones = sb("ones", [128, 128]); ob2 = sb("ob2", [128, 128])
    fgt = sb("fgt", [128, 128]); fle = sb("fle", [128, 128])
    I_s = sb("I_s", [128, 64]); U_s = sb("U_s", [128, 64]); Tri_s = sb("Tri_s", [128, 64]); Mc_s = sb("Mc_s", [128, 64])
    halo = sb("halo", [128, 3 * NP, 4, 3])
    Sg = sb("Sg", [128, NP, 64]); Sh = sb("Sh", [128, HB, 128])
    W_ = TB
    xt = [sb(f"xt{i}", [128, D]) for i in range(4)]
    sqj = sb("sqj", [128, D], BF16)
    xb = sb("xb", [128, D], BF16)
    hT_all = [sb(f"hT{i}", [128, 8, W_], BF16) for i in range(2)]
    sqjO = sb("sqjO", [128, D], BF16); ssO = sb("ssO", [128, 1]); rrO = sb("rrO", [128, 1])
    ss = sb("ss", [128, 1]); rr = sb("rr", [128, 1])
    raw = sb("raw", [128, 3, W_ + 12])
    cv = sb("cv", [128, 3, W_])
    tmp = [None, sb("tmp1", [128, W_]), sb("tmp2", [128, W_]), None]
    za = [sb(f"za{i}", [128, W_]) for i in range(2)]
    cvb = [sb(f"cvb{i}", [128, 3, W_], BF16) for i in range(2)]
    tmpA = [sb(f"tmpA{i}", [128, W_]) for i in range(3)]
    sqA = [sb(f"sqA{i}", [128, W_], BF16) for i in range(2)]
    tnA = [sb(f"tnA{i}", [128, W_]) for i in range(2)]
    I_sb = sb("I_sb", [128, 64], BF16); ob2b = sb("ob2b", [128, 128], BF16); onesb = sb("onesb", [128, 128], BF16)
    Sgb = sb("Sgb", [128, NP, 64], BF16); Shb = sb("Shb", [128, HB, 128], BF16)
    sqb = sb("sqb", [128, W_], BF16); sqbH = sb("sqbH", [128, W_], BF16)
    G_all = [sb(f"G{i}", [128, 4, 2 * HA]) for i in range(2)]; Gb_all = [sb(f"Gb{i}", [128, 4, HA]) for i in range(2)]; Gg_all = [sb(f"Gg{i}", [128, 4, HA]) for i in range(2)]
    gs_all = [sb(f"gs{i}", [128, NP, 4]) for i in range(2)]; bs_all = [sb(f"bs{i}", [128, NP, 4]) for i in range(2)]; nbs_all = [sb(f"nbs{i}", [128, NP, 4]) for i in range(2)]
    gc_all = [sb(f"gc{i}", [128, NP, 4]) for i in range(2)]; gl_all = [sb(f"gl{i}", [128, NP, 4]) for i in range(2)]; egc_all = [sb(f"egc{i}", [128, NP, 4]) for i in range(2)]
    dk_all = [sb(f"dk{i}", [128, NP, 4]) for i in range(2)]; bge_all = [sb(f"bge{i}", [128, NP, 4]) for i in range(2)]
    rhsD = sb("rhsD", [128, W_]); Dg = sb("Dg", [128, W_])
    Ee = sb("Ee", [128, W_]); Dm = sb("Dm", [128, W_]); Ds = sb("Ds", [128, W_])
    EBs = [sb(f"EBs{i}", [128, W_]) for i in range(2)]
    P0t = [sb(f"P0t{i}", [128, W_], BF16) for i in range(2)]
    PT0t = [sb(f"PT0t{i}", [128, W_], BF16) for i in range(2)]
    R0t = [sb(f"R0t{i}", [128, W_], BF16) for i in range(2)]
    P = [sb(f"P{i}", [128, W_], BF16) for i in range(2)]
    PT = [sb(f"PT{i}", [128, W_], BF16) for i in range(2)]
    R = [sb(f"R{i}", [128, W_], BF16) for i in range(2)]
    attn = sb("attn", [128, W_], BF16); attnT = [sb(f"attnT{i}", [128, W_], BF16) for i in range(2)]
    Kbe = [sb(f"Kbe{i}", [128, 4, 64], BF16) for i in range(2)]; Kd = [sb(f"Kd{i}", [128, 4, 64], BF16) for i in range(2)]; bV = [sb(f"bV{i}", [128, 4, 64], BF16) for i in range(2)]
    u = sb("u", [128, 4, 64]); wT = sb("wT", [128, W_], BF16); QeT = [sb(f"QeT{i}", [128, W_], BF16) for i in range(2)]
    vn = sb("vn", [128, 64], BF16)
    oTf = sb("oTf", [128, W_])
    oTn = [sb(f"oTn_{i}", [128, 2 * NL, W_], BF16) for i in range(2)]
    oOwn = [sb(f"oOwn_{i}", [128, NL, W_], BF16) for i in range(2)]
    qb = sb("qb", [128, W_]); ff = sb("ff", [128, W_]); lf = sb("lf", [128, W_]); kb = sb("kb", [128, W_])
    bb = sb("bb", [128, W_]); bl = sb("bl", [128, W_])
    Qe = sb("Qe", [128, W_], BF16); Qx = sb("Qx", [128, W_], BF16); Kdh = sb("Kdh", [128, W_], BF16)
    Qef = sb("Qef", [128, W_]); Qxf = sb("Qxf", [128, W_]); Kdf = sb("Kdf", [128, W_])
    ebl = sb("ebl", [128, 4]); zb = sb("zb", [128, W_])
    vtok = sb("vtok", [64, 4, 128], BF16); Kdt = sb("Kdt", [64, 4, 128], BF16); aTh = sb("aTh", [64, W_], BF16)
    smask_all = [sb(f"smask{i}", [128, W_]) for i in range(2)]
    tmpH = [None, sb("tmpH1", [128, W_])]; oTfH = sb("oTfH", [128, W_])
    yo = sb("yo", [128, D]); yo2 = sb("yo2", [128, D])
    pb = [nc.alloc_psum_tensor(f"pb{i}", [128, 512], F32) for i in range(8)]
    pT2 = pb[2][:, 0:128].bitcast(BF16).rearrange("p (k t) -> p k t", t=128)

    def aff(out, cmp, fill_in, step=-1, cm=1, base=0):
        s.op("pool", lambda e: e.memset(out[:], fill_in), writes=[out.name])
        s.op("pool", lambda e: e.affine_select(out=out[:], in_=out[:], pattern=[[step, 128]], compare_op=cmp,
                                               fill=0.0, base=base, channel_multiplier=cm), reads=[out.name], writes=[out.name])
    aff(identf, ALU.is_equal, 1.0)
    aff(fgt, ALU.is_gt, 1.0)
    aff(fle, ALU.is_gt, 1.0, step=1, cm=-1, base=1)
    s.op("pool", lambda e: e.memset(ones[:], 1.0), writes=["ones"])
    s.op("pool", lambda e: e.memset(ob2[:], 0.0), writes=["ob2"])
    for h in range(2):
        sl = slice(64 * h, 64 * h + 64)
        s.op("pool", lambda e: e.memset(ob2[sl, sl], 1.0), reads=["ob2"], writes=["ob2"])
    s.op("dve", lambda e: e.tensor_copy(out=identb[:], in_=identf[:]), reads=["identf"], writes=["identb"])
    for (dst, src) in ((I_s, identf), (U_s, fgt), (Tri_s, fle)):
        for h in range(2):
            sl = slice(64 * h, 64 * h + 64)
            s.op("dve", lambda e: e.tensor_copy(out=dst[sl, :], in_=src[sl, sl]), reads=[src.name], writes=[dst.name])
    s.op("dve", lambda e: e.tensor_tensor(out=Mc_s[:], in0=U_s[:], in1=I_s[:], op=ALU.add), reads=["U_s", "I_s"], writes=["Mc_s"])
    s.op("dve", lambda e: e.tensor_copy(out=I_sb[:], in_=I_s[:]), reads=["I_s"], writes=["I_sb"])
    s.op("dve", lambda e: e.tensor_copy(out=ob2b[:], in_=ob2[:]), reads=["ob2"], writes=["ob2b"])
    s.op("dve", lambda e: e.tensor_copy(out=onesb[:], in_=ones[:]), reads=["ones"], writes=["onesb"])
    for t_, d_ in ((nw, nw_d), (cw, cw_d), (alog, alog_d), (dtb, dtb_d), (gnw, gnw_d), (hnw, hnw_d), (lbl, lbl_d), (fnw, fnw_d)):
        s.dma("sp", t_[:], d_, writes=[t_.name])
    s.op("act", lambda e: e.activation(out=negA[:], in_=alog[:], func=AF.Exp), reads=["alog_t"], writes=["negA"])
    s.op("dve", lambda e: e.tensor_scalar(out=negA[:], in0=negA[:], scalar1=-1.0, scalar2=None, op0=ALU.mult), reads=["negA"], writes=["negA"])
    s.op("dve", lambda e: e.tensor_tensor(out=lb[:], in0=lbl[:, :, 1], in1=lbl[:, :, 0], op=ALU.subtract), reads=["lbl_t"], writes=["lb"])
    s.op("act", lambda e: e.activation(out=lb[:], in_=lb[:], func=AF.Exp), reads=["lb"], writes=["lb"])
    s.op("dve", lambda e: e.tensor_scalar(out=lb[:], in0=lb[:], scalar1=1.0, scalar2=None, op0=ALU.add), reads=["lb"], writes=["lb"])
    s.op("dve", lambda e: e.reciprocal(out=lb[:], in_=lb[:]), reads=["lb"], writes=["lb"])
    s.op("dve", lambda e: e.tensor_scalar(out=oml[:], in0=lb[:], scalar1=-1.0, scalar2=1.0, op0=ALU.mult, op1=ALU.add), reads=["lb"], writes=["oml"])
    w_in_v = w_in.rearrange("(k p) n -> p k n", p=128)
    stgx = [sb(f"stgx{i}", [128, D]) for i in range(4)]
    stgs = [(xt[0], "xt0"), (stgx[0], "stgx0"), (xt[1], "xt1"), (stgx[1], "stgx1"), (yo, "yo"), (stgx[2], "stgx2"), (yo2, "yo2"), (stgx[3], "stgx3")]
    q = 0
    nfull = NCOL // 1024
    for k in range(8):
        pieces = [(i * 1024, 1024) for i in range(nfull)] + [(nfull * 1024, NCOL % 1024)]
        for (c0, cn) in pieces:
            if cn == 1024:
                tl, key = stgs[q % len(stgs)]
            else:
                tl, key = Ee, "Ee"
            s.dma("sp", tl[:, 0:cn], w_in_v[:, k, c0:c0 + cn], writes=[key])
            if q % 2 == 0:
                s.op("dve", lambda e: e.tensor_scalar(out=Wb[:, k, c0:c0 + cn], in0=tl[:, 0:cn], scalar1=nw[:, k:k + 1], scalar2=None, op0=ALU.mult),
                     reads=[key, "nw_t"], writes=["Wb"])
            else:
                s.op("act", lambda e: e.activation(out=Wb[:, k, c0:c0 + cn], in_=tl[:, 0:cn], func=AF.Copy, scale=nw[:, k:k + 1]),
                     reads=[key, "nw_t"], writes=["Wb"])
            q += 1
    w_out_v = w_out.rearrange("(k p) n -> p k n", p=128)
    for k in range(8):
        tl, key = stgs[q % len(stgs)]
        s.dma("sp", tl[:, :], w_out_v[:, k, :], writes=[key])
        if q % 2 == 0:
            s.op("dve", lambda e: e.tensor_copy(out=WOb[:, k, :], in_=tl[:, :]), reads=[key], writes=["WOb"])
        else:
            s.op("act", lambda e: e.activation(out=WOb[:, k, :], in_=tl[:, :], func=AF.Copy), reads=[key], writes=["WOb"])
        q += 1

    def block(x_src, y_dst, t0, nseg, seglen, c, first, is_sample, bpar=0):
        hT = hT_all[bpar]; G = G_all[bpar]; Gb = Gb_all[bpar]; Gg = Gg_all[bpar]; gs = gs_all[bpar]; bs = bs_all[bpar]; nbs = nbs_all[bpar]; gc = gc_all[bpar]; gl = gl_all[bpar]; egc = egc_all[bpar]; dk = dk_all[bpar]; bge = bge_all[bpar]; smask = smask_all[bpar]
        TBk = nseg * seglen
        nch = TBk // c
        cps = seglen // c
        TT = min(128, TBk)
        ntt = TBk // TT
        nlev = {64: 5, 16: 3}[c]
        v3 = lambda ap: ap.rearrange("p (n c) -> p n c", c=c)

        def ph0_x():
            for tt in range(ntt):
                X = xt[bpar * 2 + tt]
                s.dma("sp", X[0:TT, :], x_src[t0 + tt * TT: t0 + (tt + 1) * TT, :], writes=[X.name])
                s.op("act", lambda e: e.activation(out=sqj[0:TT, :], in_=X[0:TT, :], func=AF.Square, accum_out=ss[0:TT, :]),
                     reads=[X.name], writes=["sqj", "ss"])
                s.op("act", lambda e: e.activation(out=rr[0:TT, :], in_=ss[0:TT, :], func=AF.Ln, scale=1.0 / D, bias=EPS), reads=["ss"], writes=["rr"])
                s.op("act", lambda e: e.activation(out=rr[0:TT, :], in_=rr[0:TT, :], func=AF.Exp, scale=-0.5), reads=["rr"], writes=["rr"])
                s.op("dve", lambda e: e.tensor_scalar(out=xb[0:TT, :], in0=X[0:TT, :], scalar1=rr[0:TT, :], scalar2=None, op0=ALU.mult),
                     reads=[X.name, "rr"], writes=["xb"])
                for kk in range(4):
                    for j in range(2):
                        k = 2 * kk + j
                        s.op("pe", lambda e: e.transpose(out=pT2[:, j, 0:TT], in_=xb[0:TT, k * 128:(k + 1) * 128], identity=identb[0:TT, 0:TT]),
                             reads=["xb", "identb"], writes=["B2"])
                    if kk % 2 == 0:
                        s.op("dve", lambda e: e.tensor_copy(out=hT[:, 2 * kk:2 * kk + 2, tt * TT:(tt + 1) * TT], in_=pT2[:, :, 0:TT]), reads=["B2"], writes=[f"hT{bpar}"])
                    else:
                        s.op("act", lambda e: e.activation(out=hT[:, 2 * kk:2 * kk + 2, tt * TT:(tt + 1) * TT], in_=pT2[:, :, 0:TT], func=AF.Copy), reads=["B2"], writes=[f"hT{bpar}"])

        def silu_from(src, srckeys, dst, dstkey, scr, scrkey, W, outdt_note=None):
            s.op("act", lambda e: e.activation(out=scr, in_=src, func=AF.Exp, scale=-1.0), reads=srckeys, writes=[scrkey])
            s.op("act", lambda e: e.activation(out=scr, in_=scr, func=AF.Ln, bias=1.0), reads=[scrkey], writes=[scrkey])
            s.op("act", lambda e: e.activation(out=scr, in_=scr, func=AF.Exp, scale=-1.0), reads=[scrkey], writes=[scrkey])
            s.op("dve", lambda e: e.tensor_tensor(out=dst, in0=src, in1=scr, op=ALU.mult), reads=list(srckeys) + [scrkey], writes=[dstkey])


        def inproj_fm(ct, i=0):
            key = "B0"
            out = pb[0][:, i * 256: i * 256 + TBk]
            for k in range(8):
                s.op("pe", lambda e: e.matmul(out, lhsT=Wb[:, k, ct * 128:(ct + 1) * 128], rhs=hT[:, k, 0:TBk], start=(k == 0), stop=(k == 7)),
                     reads=["Wb", f"hT{bpar}"], writes=[key])
            return out, key

        def ph0_g():
            for ch in range(nch):
                for h in range(2):
                    out = pb[2][64 * h:64 * h + c, ch * 2 * HA:(ch + 1) * 2 * HA]
                    for k in range(8):
                        s.op("pe", lambda e: e.matmul(out, lhsT=hT[:, k, ch * c:(ch + 1) * c], rhs=Wb[:, k, C_G:C_G + 2 * HA], start=(k == 0), stop=(k == 7)),
                             reads=["Wb", f"hT{bpar}"], writes=["B2"])
            Gv = G[:, 0:nch, :]
            s.op("dve", lambda e: e.tensor_copy(out=Gv, in_=pb[2][:, 0:nch * 2 * HA].rearrange("p (n g) -> p n g", g=2 * HA)), reads=["B2"], writes=[f"G{bpar}"])
            Gbv = Gb[:, 0:nch, :]; Ggv = Gg[:, 0:nch, :]
            s.op("act", lambda e: e.activation(out=Gbv, in_=Gv[:, :, 0:HA], func=AF.Exp, scale=-1.0), reads=[f"G{bpar}"], writes=[f"Gb{bpar}"])
            s.op("act", lambda e: e.activation(out=Gbv, in_=Gbv, func=AF.Ln, bias=1.0), reads=[f"Gb{bpar}"], writes=[f"Gb{bpar}"])
            s.op("act", lambda e: e.activation(out=Gbv, in_=Gbv, func=AF.Exp, scale=-1.0), reads=[f"Gb{bpar}"], writes=[f"Gb{bpar}"])
            s.op("dve", lambda e: e.tensor_tensor(out=Ggv, in0=Gv[:, :, HA:2 * HA], in1=bc(dtb[:, None, :], [128, nch, HA]), op=ALU.add),
                 reads=[f"G{bpar}", "dtb_t"], writes=[f"Gg{bpar}"])
            s.op("act", lambda e: e.activation(out=Ggv, in_=Ggv, func=AF.Exp), reads=[f"Gg{bpar}"], writes=[f"Gg{bpar}"])
            s.op("act", lambda e: e.activation(out=Ggv, in_=Ggv, func=AF.Ln, bias=1.0), reads=[f"Gg{bpar}"], writes=[f"Gg{bpar}"])
            s.op("dve", lambda e: e.tensor_tensor(out=Ggv, in0=Ggv, in1=bc(negA[:, None, :], [128, nch, HA]), op=ALU.mult),
                 reads=[f"Gg{bpar}", "negA"], writes=[f"Gg{bpar}"])
            gsv = gs[:, :, 0:nch]; bsv = bs[:, :, 0:nch]; nbsv = nbs[:, :, 0:nch]
            gcv = gc[:, :, 0:nch]; glv = gl[:, :, 0:nch]; egcv = egc[:, :, 0:nch]; dkv = dk[:, :, 0:nch]; bgev = bge[:, :, 0:nch]
            for h in range(2):
                sl = slice(64 * h, 64 * h + 64)
                for (dst, src, kd, ks) in ((gs, Gg, f"gs{bpar}", f"Gg{bpar}"), (bs, Gb, f"bs{bpar}", f"Gb{bpar}")):
                    for p in range(NP):
                        s.op("dve", lambda e: e.tensor_copy(out=dst[sl, p, 0:nch], in_=src[sl, 0:nch, 2 * p + h]), reads=[ks], writes=[kd])
            s.op("dve", lambda e: e.tensor_scalar(out=nbsv, in0=bsv, scalar1=-1.0, scalar2=None, op0=ALU.mult), reads=[f"bs{bpar}"], writes=[f"nbs{bpar}"])
            for h in range(2):
                rs = slice(64 * h, 64 * h + c)
                s.op("pe", lambda e: e.matmul(pb[2][rs, 64:64 + NP * nch], lhsT=Tri_s[rs, 0:c], rhs=gs[rs, :, 0:nch], start=True, stop=True),
                     reads=["Tri_s", f"gs{bpar}"], writes=["B2"])
                s.op("pe", lambda e: e.matmul(pb[2][rs, 96:96 + NP * nch], lhsT=ones[rs, 0:c], rhs=gs[rs, :, 0:nch], start=True, stop=True),
                     reads=["ones", f"gs{bpar}"], writes=["B2"])
            s.op("dve", lambda e: e.tensor_copy(out=gcv, in_=pb[2][:, 64:64 + NP * nch].rearrange("p (a n) -> p a n", n=nch)), reads=["B2"], writes=[f"gc{bpar}"])
            s.op("dve", lambda e: e.tensor_copy(out=glv, in_=pb[2][:, 96:96 + NP * nch].rearrange("p (a n) -> p a n", n=nch)), reads=["B2"], writes=[f"gl{bpar}"])
            s.op("act", lambda e: e.activation(out=egcv, in_=gcv, func=AF.Exp), reads=[f"gc{bpar}"], writes=[f"egc{bpar}"])
            s.op("dve", lambda e: e.tensor_tensor(out=dkv, in0=glv, in1=gcv, op=ALU.subtract), reads=[f"gl{bpar}", f"gc{bpar}"], writes=[f"dk{bpar}"])
            s.op("act", lambda e: e.activation(out=dkv, in_=dkv, func=AF.Exp), reads=[f"dk{bpar}"], writes=[f"dk{bpar}"])
            s.op("dve", lambda e: e.tensor_tensor(out=bgev, in0=bsv, in1=egcv, op=ALU.mult), reads=[f"bs{bpar}", f"egc{bpar}"], writes=[f"bge{bpar}"])


        def phase0(_=None):
            ph0_x()
            ph0_g()
            ph0_m()

        def front(p):
            par = p % 2
            for i3 in range(3):
                ct = NP * i3 + p
                pp, pk = inproj_fm(ct)
                rv = raw[:, i3, 0:nseg * (seglen + 3)].rearrange("p (n c) -> p n c", c=seglen + 3)
                if first and not is_sample:
                    s.op("pool", lambda e: e.memset(rv[:, :, 0:3], 0.0), reads=[f"raw{i3}"], writes=[f"raw{i3}"])
                else:
                    s.op("pool", lambda e: e.tensor_copy(out=rv[:, :, 0:3], in_=halo[:, ct, 0:nseg, :]), reads=["halo"], writes=[f"raw{i3}"])
                s.op("act", lambda e: e.activation(out=rv[:, :, 3:3 + seglen], in_=pp.rearrange("p (n c) -> p n c", c=seglen), func=AF.Copy),
                     reads=[pk], writes=[f"raw{i3}"])
                s.op("pool", lambda e: e.tensor_copy(out=halo[:, ct, 0:nseg, :], in_=rv[:, :, seglen:seglen + 3]), reads=[f"raw{i3}"], writes=["halo"])
                cvv = cv[:, i3, 0:TBk].rearrange("p (n c) -> p n c", c=seglen)
                s.op("dve", lambda e: e.tensor_scalar(out=cvv, in0=rv[:, :, 0:seglen], scalar1=cw[:, ct, 0:1], scalar2=None, op0=ALU.mult),
                     reads=[f"raw{i3}", "cw_t"], writes=[f"cv{i3}"])
                for j in range(1, 4):
                    s.op("dve", lambda e: e.scalar_tensor_tensor(out=cvv, in0=rv[:, :, j:j + seglen], scalar=cw[:, ct, j:j + 1], in1=cvv,
                                                                 op0=ALU.mult, op1=ALU.add), reads=[f"raw{i3}", "cw_t", f"cv{i3}"], writes=[f"cv{i3}"])
                if i3 < 2:
                    silu_from(cv[:, i3, 0:TBk], [f"cv{i3}"], cv[:, i3, 0:TBk], f"cv{i3}", tmpA[i3][:, 0:TBk], f"tmpA{i3}", TBk)
                else:
                    silu_from(cv[:, i3, 0:TBk], [f"cv{i3}"], cvb[par][:, 2, 0:TBk], f"cvb{par}v", tmpA[i3][:, 0:TBk], f"tmpA{i3}", TBk)
            for i3 in range(2):
                src = cv[:, i3, 0:TBk]
                s.op("pool", lambda e: e.tensor_tensor(out=sqA[i3][:, 0:TBk], in0=src, in1=src, op=ALU.mult), reads=[f"cv{i3}"], writes=[f"sqA{i3}"])
                s.op("pe", lambda e: e.matmul(pb[7][:, i3 * 256:i3 * 256 + TBk], lhsT=ob2b[:], rhs=sqA[i3][:, 0:TBk], start=True, stop=True), reads=["ob2b", f"sqA{i3}"], writes=["BT"])
                s.op("act", lambda e: e.activation(out=tnA[i3][:, 0:TBk], in_=pb[7][:, i3 * 256:i3 * 256 + TBk], func=AF.Ln, bias=EPS), reads=["BT"], writes=[f"tnA{i3}"])
                s.op("act", lambda e: e.activation(out=tnA[i3][:, 0:TBk], in_=tnA[i3][:, 0:TBk], func=AF.Exp, scale=-0.5), reads=[f"tnA{i3}"], writes=[f"tnA{i3}"])
                if i3 == 0:
                    s.op("dve", lambda e: e.scalar_tensor_tensor(out=cvb[par][:, 0, 0:TBk], in0=src, scalar=0.125, in1=tnA[i3][:, 0:TBk], op0=ALU.mult, op1=ALU.mult),
                         reads=[f"cv{i3}", f"tnA{i3}"], writes=[f"cvb{par}q"])
                else:
                    s.op("dve", lambda e: e.tensor_tensor(out=cvb[par][:, 1, 0:TBk], in0=src, in1=tnA[i3][:, 0:TBk], op=ALU.mult), reads=[f"cv{i3}", f"tnA{i3}"], writes=[f"cvb{par}k"])
            pp, pk = inproj_fm(3 * NP + p)
            s.op("act", lambda e: e.activation(out=za[par][:, 0:TBk], in_=pp, func=AF.Copy), reads=[pk], writes=[f"za{par}"])
            silu_from(za[par][:, 0:TBk], [f"za{par}"], za[par][:, 0:TBk], f"za{par}", tmpA[0][:, 0:TBk], "tmpA0", TBk)
            qn = cvb[par][:, 0, 0:TBk]; kn = cvb[par][:, 1, 0:TBk]; vs = cvb[par][:, 2, 0:TBk]
            ckq, ckk, ckv = f"cvb{par}q", f"cvb{par}k", f"cvb{par}v"
            s.op("dve", lambda e: e.tensor_tensor(out=v3(rhsD[:, 0:TBk]), in0=bc(gs[:, p, 0:nch, None], [128, nch, c]), in1=bc(U_s[:, None, 0:c], [128, nch, c]), op=ALU.mult),
                 reads=[f"gs{bpar}", "U_s"], writes=["rhsD"])
            s.op("pool", lambda e: e.tensor_tensor(out=v3(Dg[:, 0:TBk]), in0=bc(egc[:, p, 0:nch, None], [128, nch, c]), in1=bc(I_s[:, None, 0:c], [128, nch, c]), op=ALU.mult),
                 reads=[f"egc{bpar}", "I_s"], writes=["Dg"])
            for h in range(2):
                rs = slice(64 * h, 64 * h + c)
                s.op("pe", lambda e: e.matmul(pb[6][rs, 0:TBk], lhsT=Tri_s[rs, 0:c], rhs=rhsD[rs, 0:TBk], start=True, stop=True),
                     reads=["Tri_s", "rhsD"], writes=["B6"])
            s.op("act", lambda e: e.activation(out=Ee[:, 0:TBk], in_=pb[6][:, 0:TBk], func=AF.Exp), reads=["B6"], writes=["Ee"])
            s.op("pool", lambda e: e.tensor_tensor(out=v3(Dm[:, 0:TBk]), in0=v3(Ee[:, 0:TBk]), in1=bc(Mc_s[:, None, 0:c], [128, nch, c]), op=ALU.mult),
                 reads=["Ee", "Mc_s"], writes=["Dm"])
            s.op("pool", lambda e: e.tensor_tensor(out=v3(Ds[:, 0:TBk]), in0=v3(Ee[:, 0:TBk]), in1=bc(U_s[:, None, 0:c], [128, nch, c]), op=ALU.mult),
                 reads=["Ee", "U_s"], writes=["Ds"])
            for h in range(2):
                rs = slice(64 * h, 64 * h + c)
                s.op("pe", lambda e: e.matmul(pb[6][64 * h:64 * h + 64, 0:TBk], lhsT=ones[rs, 0:64], rhs=Dg[rs, 0:TBk], start=True, stop=True),
                     reads=["ones", "Dg"], writes=["B6"])
            s.op("act", lambda e: e.activation(out=EBs[par][:, 0:TBk], in_=pb[6][:, 0:TBk], func=AF.Copy), reads=["B6"], writes=[f"EBs{par}"])
            s.op("dve", lambda e: e.tensor_tensor(out=QeT[par][:, 0:TBk], in0=qn, in1=EBs[par][:, 0:TBk], op=ALU.mult), reads=[ckq, f"EBs{par}"], writes=[f"QeT{par}"])
            for ch in range(nch):
                cs = slice(ch * c, (ch + 1) * c)
                for h in range(2):
                    fs = slice(64 * h, 64 * h + 64); rs = slice(64 * h, 64 * h + c)
                    s.op("pe", lambda e: e.matmul(pb[7][rs, cs], lhsT=kn[fs, cs], rhs=kn[fs, cs], start=True, stop=True), reads=[ckk], writes=["BT"])
                    s.op("pe", lambda e: e.matmul(pb[7][rs, 256 + ch * c:256 + (ch + 1) * c], lhsT=qn[fs, cs], rhs=kn[fs, cs], start=True, stop=True),
                         reads=[ckq, ckk], writes=["BT"])
            s.op("dve", lambda e: e.tensor_tensor(out=v3(tmp[2][:, 0:TBk]), in0=v3(pb[7][:, 0:TBk]), in1=bc(nbs[:, p, 0:nch, None], [128, nch, c]), op=ALU.mult),
                 reads=["BT", f"nbs{bpar}"], writes=["tmp2"])
            s.op("pool", lambda e: e.tensor_tensor(out=PT0t[par][:, 0:TBk], in0=tmp[2][:, 0:TBk], in1=Ds[:, 0:TBk], op=ALU.mult), reads=["tmp2", "Ds"], writes=[f"PT0t{par}"])
            s.op("dve", lambda e: e.tensor_tensor(out=attn[:, 0:TBk], in0=pb[7][:, 256:256 + TBk], in1=Dm[:, 0:TBk], op=ALU.mult), reads=["BT", "Dm"], writes=["attn"])

            for ch in range(nch):
                cs = slice(ch * c, (ch + 1) * c)
                for h in range(2):
                    fs = slice(64 * h, 64 * h + 64); rs = slice(64 * h, 64 * h + c)
                    s.op("pe", lambda e: e.matmul(pb[6][rs, cs], lhsT=PT0t[par][rs, cs], rhs=I_sb[rs, 0:c], start=True, stop=True), reads=[f"PT0t{par}", "I_sb"], writes=["B6"])
                    s.op("pe", lambda e: e.matmul(pb[6][rs, 256 + ch * c:256 + (ch + 1) * c], lhsT=attn[rs, cs], rhs=I_sb[rs, 0:c], start=True, stop=True),
                         reads=["attn", "I_sb"], writes=["B6"])
                    s.op("pe", lambda e: e.matmul(pb[7][rs, ch * 64:(ch + 1) * 64], lhsT=kn[fs, cs], rhs=I_sb[fs, 0:64], start=True, stop=True), reads=[ckk, ckv, "I_sb"], writes=["BT"])
                    s.op("pe", lambda e: e.matmul(pb[7][rs, 256 + ch * 64:256 + (ch + 1) * 64], lhsT=vs[fs, cs], rhs=I_sb[fs, 0:64], start=True, stop=True),
                         reads=[ckk, ckv, "I_sb"], writes=["BT"])
            s.op("act", lambda e: e.activation(out=P0t[par][:, 0:TBk], in_=pb[6][:, 0:TBk], func=AF.Copy), reads=["B6"], writes=[f"P0t{par}"])
            s.op("dve", lambda e: e.tensor_tensor(out=v3(R0t[par][:, 0:TBk]), in0=v3(pb[6][:, 0:TBk]), in1=bc(I_s[:, None, 0:c], [128, nch, c]), op=ALU.add),
                 reads=["B6", "I_s"], writes=[f"R0t{par}"])
            s.op("act", lambda e: e.activation(out=attnT[par][:, 0:TBk], in_=pb[6][:, 256:256 + TBk], func=AF.Copy), reads=["B6"], writes=[f"attnT{par}"])
            k4 = pb[7][:, 0:nch * 64].rearrange("p (n d) -> p n d", d=64)
            v4 = pb[7][:, 256:256 + nch * 64].rearrange("p (n d) -> p n d", d=64)
            s.op("dve", lambda e: e.tensor_tensor(out=Kbe[par][:, 0:nch, :], in0=k4, in1=bc(bge[:, p, 0:nch, None], [128, nch, 64]), op=ALU.mult), reads=["BT", f"bge{bpar}"], writes=[f"Kbe{par}"])
            s.op("dve", lambda e: e.tensor_tensor(out=Kd[par][:, 0:nch, :], in0=k4, in1=bc(dk[:, p, 0:nch, None], [128, nch, 64]), op=ALU.mult), reads=["BT", f"dk{bpar}"], writes=[f"Kd{par}"])
            s.op("dve", lambda e: e.tensor_tensor(out=bV[par][:, 0:nch, :], in0=v4, in1=bc(bs[:, p, 0:nch, None], [128, nch, 64]), op=ALU.mult), reads=["BT", f"bs{bpar}"], writes=[f"bV{par}"])
        def core(p):
            par = p % 2
            Pc, kP = P0t[par], f"P0t{par}"
            PTc, kPT = PT0t[par], f"PT0t{par}"
            Rc, kR = R0t[par], f"R0t{par}"
            for lev in range(1, nlev + 2):
                do_pow = lev <= nlev
                need_P = lev < nlev
                do_R = lev >= 2
                nP, nkP = P[lev % 2], f"P{lev % 2}"
                nPT, nkPT = PT[lev % 2], f"PT{lev % 2}"
                nR, nkR = R[lev % 2], f"R{lev % 2}"
                for ch in range(nch):
                    cs = slice(ch * c, (ch + 1) * c)
                    for h in range(2):
                        rs = slice(64 * h, 64 * h + c)
                        if do_pow and need_P:
                            s.op("pe", lambda e: e.matmul(pb[5][rs, cs], lhsT=PTc[rs, cs], rhs=Pc[rs, cs], start=True, stop=True), reads=[kPT, kP], writes=["B5"])
                        if do_pow:
                            s.op("pe", lambda e: e.matmul(pb[5][rs, 256 + ch * c:256 + (ch + 1) * c], lhsT=Pc[rs, cs], rhs=PTc[rs, cs], start=True, stop=True),
                                 reads=[kPT, kP], writes=["B5"])
                        if do_R:
                            s.op("pe", lambda e: e.matmul(pb[4][rs, cs], lhsT=PTc[rs, cs], rhs=Rc[rs, cs], start=True, stop=True), reads=[kPT, kR], writes=["B4"])
                if do_pow and need_P:
                    s.op("act", lambda e: e.activation(out=nP[:, 0:TBk], in_=pb[5][:, 0:TBk], func=AF.Copy), reads=["B5"], writes=[nkP])
                if do_pow:
                    s.op("dve", lambda e: e.tensor_copy(out=nPT[:, 0:TBk], in_=pb[5][:, 256:256 + TBk]), reads=["B5"], writes=[nkPT])
                if do_R:
                    s.op("dve", lambda e: e.tensor_tensor(out=nR[:, 0:TBk], in0=pb[4][:, 0:TBk], in1=Rc[:, 0:TBk], op=ALU.add), reads=["B4", kR], writes=[nkR])
                    Rc, kR = nR, nkR
                if do_pow:
                    if need_P:
                        Pc, kP = nP, nkP
                    PTc, kPT = nPT, nkPT
            Rf, rk = Rc, kR
            for ch in range(nch):
                cs = slice(ch * c, (ch + 1) * c)
                for h in range(2):
                    rs = slice(64 * h, 64 * h + c)
                    s.op("pe", lambda e: e.matmul(pb[4][rs, 256 + ch * 64:256 + (ch + 1) * 64], lhsT=Rf[rs, cs], rhs=bV[par][rs, ch, :], start=True, stop=True),
                         reads=[rk, f"bV{par}"], writes=["B4"])
                    s.op("pe", lambda e: e.matmul(pb[5][64 * h:64 * h + 64, ch * c:(ch + 1) * c], lhsT=Kbe[par][rs, ch, :], rhs=Rf[rs, cs], start=True, stop=True),
                         reads=[rk, f"Kbe{par}"], writes=["B5"])
            s.op("act", lambda e: e.activation(out=u[:, 0:nch, :], in_=pb[4][:, 256:256 + nch * 64].rearrange("p (n d) -> p n d", d=64), func=AF.Copy), reads=["B4"], writes=["u"])
            s.op("dve", lambda e: e.tensor_copy(out=wT[:, 0:TBk], in_=pb[5][:, 0:TBk]), reads=["B5"], writes=["wT"])
            skey = f"Sg{p}"
            for ch in range(nch):
                cs = slice(ch * c, (ch + 1) * c)
                seg = ch // cps
                if ch % cps == 0:
                    if is_sample:
                        s.dma("sp", Sg[:, p, :], sg_d[seg, p], writes=[skey])
                        s.op("dve", lambda e: e.tensor_copy(out=Sgb[:, p, :], in_=Sg[:, p, :]), reads=[skey], writes=[skey + "b"])
                    elif first:
                        s.op("pool", lambda e: e.memset(Sg[:, p, :], 0.0), writes=[skey])
                        s.op("pool", lambda e: e.memset(Sgb[:, p, :], 0.0), writes=[skey + "b"])
                for h in range(2):
                    fs = slice(64 * h, 64 * h + 64); rs = slice(64 * h, 64 * h + c)
                    s.op("pe", lambda e: e.matmul(pb[1][rs, 0:64], lhsT=wT[fs, cs], rhs=Sgb[fs, p, :], start=True, stop=True), reads=["wT", skey + "b"], writes=["B1"])
                s.op("dve", lambda e: e.tensor_tensor(out=vn[:], in0=u[:, ch, :], in1=pb[1][:, 0:64], op=ALU.subtract), reads=["u", "B1"], writes=["vn"])
                for h in range(2):
                    fs = slice(64 * h, 64 * h + 64); rs = slice(64 * h, 64 * h + c)
                    s.op("pe", lambda e: e.matmul(pb[1][fs, 256 + ch * c:256 + (ch + 1) * c], lhsT=Sgb[fs, p, :], rhs=QeT[par][fs, cs], start=True, stop=False),
                         reads=[skey + "b", f"QeT{par}"], writes=["B1"])
                    s.op("pe", lambda e: e.matmul(pb[1][fs, 256 + ch * c:256 + (ch + 1) * c], lhsT=vn[rs, :], rhs=attnT[par][rs, cs], start=False, stop=True),
                         reads=["vn", f"attnT{par}"], writes=["B1"])
                for h in range(2):
                    fs = slice(64 * h, 64 * h + 64); rs = slice(64 * h, 64 * h + c)
                    s.op("pe", lambda e: e.matmul(pb[1][fs, 64:128], lhsT=Kd[par][rs, ch, :], rhs=vn[rs, :], start=True, stop=True), reads=[f"Kd{par}", "vn"], writes=["B1"])
                s.op("dve", lambda e: e.scalar_tensor_tensor(out=Sgb[:, p, :], in0=Sg[:, p, :], scalar=EBs[par][:, (ch + 1) * c - 1:(ch + 1) * c], in1=pb[1][:, 64:128],
                                                             op0=ALU.mult, op1=ALU.add), reads=[skey, f"EBs{par}", "B1"], writes=[skey + "b"])
                s.op("dve", lambda e: e.scalar_tensor_tensor(out=Sg[:, p, :], in0=Sg[:, p, :], scalar=EBs[par][:, (ch + 1) * c - 1:(ch + 1) * c], in1=pb[1][:, 64:128],
                                                             op0=ALU.mult, op1=ALU.add), reads=[skey, f"EBs{par}", "B1"], writes=[skey])
                if is_sample and (ch + 1) % cps == 0:
                    s.dma("sp", ngs[seg, p], Sg[:, p, :], reads=[skey], writes=[f"o_ngs{seg}_{p}"])
            s.op("act", lambda e: e.activation(out=oTf[:, 0:TBk], in_=pb[1][:, 256:256 + TBk], func=AF.Copy), reads=["B1"], writes=["oTf"])
            s.op("pool", lambda e: e.tensor_tensor(out=sqb[:, 0:TBk], in0=oTf[:, 0:TBk], in1=oTf[:, 0:TBk], op=ALU.mult), reads=["oTf"], writes=["sqb"])
            s.op("pe", lambda e: e.matmul(pb[3][:, 0:TBk], lhsT=ob2b[:], rhs=sqb[:, 0:TBk], start=True, stop=True), reads=["ob2b", "sqb"], writes=["B3"])
            s.op("act", lambda e: e.activation(out=tmp[1][:, 0:TBk], in_=pb[3][:, 0:TBk], func=AF.Ln, scale=1.0 / 64, bias=EPS), reads=["B3"], writes=["tmp1"])
            s.op("act", lambda e: e.activation(out=tmp[1][:, 0:TBk], in_=tmp[1][:, 0:TBk], func=AF.Exp, scale=-0.5), reads=["tmp1"], writes=["tmp1"])
            s.op("dve", lambda e: e.tensor_tensor(out=oTf[:, 0:TBk], in0=oTf[:, 0:TBk], in1=tmp[1][:, 0:TBk], op=ALU.mult), reads=["oTf", "tmp1"], writes=["oTf"])
            s.op("dve", lambda e: e.scalar_tensor_tensor(out=oOwn[bpar][:, p, 0:TBk], in0=oTf[:, 0:TBk], scalar=gnw[:, 0:1], in1=za[par][:, 0:TBk], op0=ALU.mult, op1=ALU.mult),
                 reads=["oTf", "gnw_t", f"za{par}"], writes=[f"oOwn{bpar}_{p}"])

        def ph0_m():
            s.op("pool", lambda e: e.memset(smask[:, 0:TBk], 1.0), writes=[f"smask{bpar}"])
            s.op("pool", lambda e: e.memset(v3(smask[:, 0:TBk])[:, :, 0:1], 0.0), reads=[f"smask{bpar}"], writes=[f"smask{bpar}"])

        def hg(h):
            pp, pk = inproj_fm(4 * NP + h, 1)
            s.op("act", lambda e: e.activation(out=qb[:, 0:TBk], in_=pp, func=AF.Copy), reads=[pk], writes=["qb"])
            silu_from(qb[:, 0:TBk], ["qb"], qb[:, 0:TBk], "qb", bl[:, 0:TBk], "bl", TBk)
            pp, pk = inproj_fm(4 * NP + 2 * HB + h, 1)
            s.op("act", lambda e: e.activation(out=zb[:, 0:TBk], in_=pp, func=AF.Copy), reads=[pk], writes=["zb"])
            silu_from(zb[:, 0:TBk], ["zb"], zb[:, 0:TBk], "zb", bl[:, 0:TBk], "bl", TBk)
            pp, pk = inproj_fm(4 * NP + HB + h, 1)
            s.op("act", lambda e: e.activation(out=ff[:, 0:TBk], in_=pp, func=AF.Exp, scale=-1.0), reads=[pk], writes=["ff"])
            s.op("act", lambda e: e.activation(out=ff[:, 0:TBk], in_=ff[:, 0:TBk], func=AF.Ln, bias=1.0), reads=["ff"], writes=["ff"])
            s.op("act", lambda e: e.activation(out=ff[:, 0:TBk], in_=ff[:, 0:TBk], func=AF.Exp, scale=-1.0), reads=["ff"], writes=["ff"])
            s.op("dve", lambda e: e.tensor_scalar(out=ff[:, 0:TBk], in0=ff[:, 0:TBk], scalar1=oml[:, h:h + 1], scalar2=lb[:, h:h + 1], op0=ALU.mult, op1=ALU.add),
                 reads=["ff", "oml", "lb"], writes=["ff"])
            s.op("act", lambda e: e.activation(out=lf[:, 0:TBk], in_=ff[:, 0:TBk], func=AF.Ln), reads=["ff"], writes=["lf"])
            s.op("dve", lambda e: e.tensor_scalar(out=kb[:, 0:TBk], in0=ff[:, 0:TBk], scalar1=-1.0, scalar2=1.0, op0=ALU.mult, op1=ALU.add), reads=["ff"], writes=["kb"])
            s.op("dve", lambda e: e.tensor_tensor_scan(out=bb[:, 0:TBk], data0=smask[:, 0:TBk], data1=lf[:, 0:TBk], initial=0.0, op0=ALU.mult, op1=ALU.add),
                 reads=[f"smask{bpar}", "lf"], writes=["bb"])
            b3 = v3(bb[:, 0:TBk])
            s.op("pool", lambda e: e.tensor_tensor(out=v3(bl[:, 0:TBk]), in0=b3, in1=bc(b3[:, :, c - 1:c], [128, nch, c]), op=ALU.subtract), reads=["bb"], writes=["bl"])
            s.op("act", lambda e: e.activation(out=Qef[:, 0:TBk], in_=bb[:, 0:TBk], func=AF.Exp), reads=["bb"], writes=["Qef"])
            s.op("act", lambda e: e.activation(out=Qxf[:, 0:TBk], in_=bl[:, 0:TBk], func=AF.Exp), reads=["bl"], writes=["Qxf"])
            s.op("act", lambda e: e.activation(out=Kdf[:, 0:TBk], in_=bl[:, 0:TBk], func=AF.Exp, scale=-1.0), reads=["bl"], writes=["Kdf"])
            s.op("act", lambda e: e.activation(out=ebl[:, 0:nch], in_=b3[:, :, c - 1], func=AF.Exp), reads=["bb"], writes=["ebl"])
            s.op("dve", lambda e: e.tensor_tensor(out=Qe[:, 0:TBk], in0=Qef[:, 0:TBk], in1=qb[:, 0:TBk], op=ALU.mult), reads=["Qef", "qb"], writes=["Qe"])
            s.op("pool", lambda e: e.tensor_tensor(out=Qx[:, 0:TBk], in0=Qxf[:, 0:TBk], in1=qb[:, 0:TBk], op=ALU.mult), reads=["Qxf", "qb"], writes=["Qx"])
            s.op("dve", lambda e: e.tensor_tensor(out=Kdh[:, 0:TBk], in0=Kdf[:, 0:TBk], in1=kb[:, 0:TBk], op=ALU.mult), reads=["Kdf", "kb"], writes=["Kdh"])
            for ch in range(nch):
                cs = slice(ch * c, (ch + 1) * c)
                outp = pb[2][0:c, 128:256]
                for k in range(8):
                    s.op("pe", lambda e: e.matmul(outp, lhsT=hT[:, k, cs], rhs=Wb[:, k, C_HI + 128 * h:C_HI + 128 * (h + 1)], start=(k == 0), stop=(k == 7)),
                         reads=["Wb", f"hT{bpar}"], writes=["B2"])
                s.op("pe", lambda e: e.matmul(pb[2][0:c, 256:384], lhsT=Kdh[:, cs], rhs=identb[:], start=True, stop=True), reads=["Kdh", "identb"], writes=["B2"])
                s.op("pe", lambda e: e.matmul(pb[2][0:c, 384:384 + c], lhsT=Kdh[:, cs], rhs=Qx[:, cs], start=True, stop=True), reads=["Kdh", "Qx"], writes=["B2"])
                s.op("act", lambda e: e.activation(out=vtok[0:c, ch, :], in_=outp, func=AF.Copy), reads=["B2"], writes=["vtok"])
                s.op("dve", lambda e: e.tensor_copy(out=Kdt[0:c, ch, :], in_=pb[2][0:c, 256:384]), reads=["B2"], writes=["Kdt"])
                s.op("dve", lambda e: e.tensor_tensor(out=aTh[0:c, cs], in0=pb[2][0:c, 384:384 + c], in1=Tri_s[0:c, 0:c], op=ALU.mult),
                     reads=["B2", "Tri_s"], writes=["aTh"])
            skey = f"Sh{h}"
            for ch in range(nch):
                cs = slice(ch * c, (ch + 1) * c)
                seg = ch // cps
                if ch % cps == 0:
                    if is_sample:
                        s.dma("sp", Sh[:, h, :], sh_d[seg, h], writes=[skey])
                        s.op("dve", lambda e: e.tensor_copy(out=Shb[:, h, :], in_=Sh[:, h, :]), reads=[skey], writes=[skey + "b"])
                    elif first:
                        s.op("pool", lambda e: e.memset(Sh[:, h, :], 0.0), writes=[skey])
                        s.op("pool", lambda e: e.memset(Shb[:, h, :], 0.0), writes=[skey + "b"])
                s.op("pe", lambda e: e.matmul(pb[3][:, 256 + ch * c:256 + (ch + 1) * c], lhsT=Shb[:, h, :], rhs=Qe[:, cs], start=True, stop=False), reads=[skey + "b", "Qe"], writes=["B3"])
                s.op("pe", lambda e: e.matmul(pb[3][:, 256 + ch * c:256 + (ch + 1) * c], lhsT=vtok[0:c, ch, :], rhs=aTh[0:c, cs], start=False, stop=True),
                     reads=["vtok", "aTh"], writes=["B3"])
                s.op("pe", lambda e: e.matmul(pb[1][:, 128:256], lhsT=Kdt[0:c, ch, :], rhs=vtok[0:c, ch, :], start=True, stop=True), reads=["Kdt", "vtok"], writes=["B1"])
                s.op("dve", lambda e: e.scalar_tensor_tensor(out=Shb[:, h, :], in0=Sh[:, h, :], scalar=ebl[:, ch:ch + 1], in1=pb[1][:, 128:256], op0=ALU.mult, op1=ALU.add),
                     reads=[skey, "ebl", "B1"], writes=[skey + "b"])
                s.op("dve", lambda e: e.scalar_tensor_tensor(out=Sh[:, h, :], in0=Sh[:, h, :], scalar=ebl[:, ch:ch + 1], in1=pb[1][:, 128:256], op0=ALU.mult, op1=ALU.add),
                     reads=[skey, "ebl", "B1"], writes=[skey])
                if is_sample and (ch + 1) % cps == 0:
                    s.dma("sp", nhs[seg, h], Sh[:, h, :], reads=[skey], writes=[f"o_nhs{seg}_{h}"])
            s.op("act", lambda e: e.activation(out=oTfH[:, 0:TBk], in_=pb[3][:, 256:256 + TBk], func=AF.Copy), reads=["B3"], writes=["oTfH"])
            s.op("pool", lambda e: e.tensor_tensor(out=sqbH[:, 0:TBk], in0=oTfH[:, 0:TBk], in1=oTfH[:, 0:TBk], op=ALU.mult), reads=["oTfH"], writes=["sqbH"])
            s.op("pe", lambda e: e.matmul(pb[3][:, 256:256 + TBk], lhsT=onesb[:], rhs=sqbH[:, 0:TBk], start=True, stop=True), reads=["onesb", "sqbH"], writes=["B3"])
            s.op("act", lambda e: e.activation(out=tmpH[1][:, 0:TBk], in_=pb[3][:, 256:256 + TBk], func=AF.Ln, scale=1.0 / 128, bias=EPS), reads=["B3"], writes=["tmpH1"])
            s.op("act", lambda e: e.activation(out=tmpH[1][:, 0:TBk], in_=tmpH[1][:, 0:TBk], func=AF.Exp, scale=-0.5), reads=["tmpH1"], writes=["tmpH1"])
            s.op("dve", lambda e: e.tensor_tensor(out=oTfH[:, 0:TBk], in0=oTfH[:, 0:TBk], in1=tmpH[1][:, 0:TBk], op=ALU.mult), reads=["oTfH", "tmpH1"], writes=["oTfH"])
            s.op("dve", lambda e: e.scalar_tensor_tensor(out=oOwn[bpar][:, NP + h, 0:TBk], in0=oTfH[:, 0:TBk], scalar=hnw[:, 0:1], in1=zb[:, 0:TBk], op0=ALU.mult, op1=ALU.mult),
                 reads=["oTfH", "hnw_t", "zb"], writes=[f"oOwn{bpar}_{NP + h}"])

        def xchg(_=None):
            xs_, xd_ = (xsrc_s, xdst_s) if is_sample else (xsrc[bpar], xdst[bpar])
            s.dma("sp", xs_.rearrange("(t p) n -> p t n", p=128), oOwn[bpar][:, :, 0:TBk], reads=[f"oOwn{bpar}_{i}" for i in range(NL)], writes=[f"xsrc{bpar}"])
            s.coll([xs_], [xd_], reads=[f"xsrc{bpar}"], writes=[f"xdst{bpar}"])
            s.dma("sp", oTn[bpar][:, :, 0:TBk], xd_.rearrange("(t p) n -> p t n", p=128), reads=[f"xdst{bpar}"], writes=[f"oTn{bpar}_{k}" for k in range(2 * NL)])

        def outproj(_=None):
            for tt in range(ntt):
                X = xt[bpar * 2 + tt]
                for half in range(2):
                    bank = pb[6] if half == 0 else pb[7]
                    bk = "B6" if half == 0 else "BT"
                    for k in range(8):
                        s.op("pe", lambda e: e.matmul(bank[0:TT, :], lhsT=oTn[bpar][:, k, tt * TT:(tt + 1) * TT], rhs=WOb[:, k, half * 512:(half + 1) * 512], start=(k == 0), stop=(k == 7)),
                             reads=[f"oTn{bpar}_{k}", "WOb"], writes=[bk])
                    s.op("dve", lambda e: e.tensor_tensor(out=yo[0:TT, half * 512:(half + 1) * 512], in0=bank[0:TT, :], in1=X[0:TT, half * 512:(half + 1) * 512], op=ALU.add),
                         reads=[bk, X.name], writes=["yo"])
                s.op("act", lambda e: e.activation(out=sqjO[0:TT, :], in_=yo[0:TT, :], func=AF.Square, accum_out=ssO[0:TT, :]), reads=["yo"], writes=["sqjO", "ssO"])
                s.op("act", lambda e: e.activation(out=rrO[0:TT, :], in_=ssO[0:TT, :], func=AF.Ln, scale=1.0 / D, bias=EPS), reads=["ssO"], writes=["rrO"])
                s.op("act", lambda e: e.activation(out=rrO[0:TT, :], in_=rrO[0:TT, :], func=AF.Exp, scale=-0.5), reads=["rrO"], writes=["rrO"])
                s.op("dve", lambda e: e.scalar_tensor_tensor(out=yo2[0:TT, :], in0=yo[0:TT, :], scalar=rrO[0:TT, :], in1=fnw[0:TT, :], op0=ALU.mult, op1=ALU.mult),
                     reads=["yo", "rrO", "fnw_t"], writes=["yo2"])
                s.dma("sp", y_dst[t0 + tt * TT: t0 + (tt + 1) * TT, :], yo2[0:TT, :], reads=["yo2"], writes=[f"o_y{id(y_dst)}"], slot="yout")

        def record(fn, arg=None):
            lst = []
            s.rec = lst
            fn(arg)
            s.rec = None
            return lst
        return dict(p0=lambda: record(phase0), op=lambda: record(outproj), xc=lambda: record(xchg),
                    fr=lambda p: record(front, p), co=lambda p: record(core, p), hg=lambda h: record(hg, h))

    def emit_blocks(blks, extra_nodes=(), extra_deps=None):
        n = len(blks)
        extra_deps = extra_deps or {}
        nodes = []
        for b, P in enumerate(blks):
            N = lambda kind, bb: f"{kind}_{bb}"
            X = lambda name: extra_deps.get(name, [])
            nodes.append((N("P0", b), P["p0"](), [N("P0", b - 1), N("FR1", b - 2), N("HG1", b - 2), N("OP", b - 2)] + X(N("P0", b)), b * 10 + 0))
            nodes.append((N("FR0", b), P["fr"](0), [N("P0", b), N("FR1", b - 1), N("CO0", b - 1), N("OP", b - 2)] + X(N("FR0", b)), b * 10 + 1))
            nodes.append((N("HG0", b), P["hg"](0), [N("P0", b), N("HG1", b - 1), N("XC", b - 2)] + X(N("HG0", b)), b * 10 + 2))
            nodes.append((N("CO0", b), P["co"](0), [N("FR0", b), N("CO1", b - 1), N("XC", b - 2)] + X(N("CO0", b)), b * 10 + 3))
            nodes.append((N("FR1", b), P["fr"](1), [N("FR0", b), N("CO1", b - 1)], b * 10 + 4))
            nodes.append((N("HG1", b), P["hg"](1), [N("HG0", b)], b * 10 + 5))
            nodes.append((N("CO1", b), P["co"](1), [N("FR1", b), N("CO0", b)], b * 10 + 6))
            nodes.append((N("XC", b), P["xc"](), [N("CO1", b), N("HG1", b), N("XC", b - 1), N("OP", b - 2)], b * 10 + 7))
            nodes.append((N("OP", b), P["op"](), [N("XC", b), N("OP", b - 1), N("FR1", b + 1) if b + 1 < n else N("FR1", b)], b * 10 + 18))
        nodes += list(extra_nodes)
        s.dag_emit(nodes)

    assert NP == 2 and HB == 2
    nblk = T // TB
    blks = [block(xp, yp, b * TB, 1, TB, 64, b == 0, False, b % 2) for b in range(nblk)]
    sblk = block(xs, ys, 0, 4, 16, 16, True, True, nblk % 2)
    sw = []
    s.rec = sw
    s.dma("sp", ncp[:, :, 0, :], halo[:, :, 0, :], reads=["halo"], writes=["o_ncp"])
    s.dma("sp", halo[:], sc_d, reads=["halo"], writes=["halo"])
    s.rec = None
    so = []
    s.rec = so
    for p in range(NP):
        s.dma("sp", ngp[0, p], Sg[:, p, :], reads=[f"Sg{p}"], writes=[f"o_ngp{p}"])
    for h in range(HB):
        s.dma("sp", nhp[0, h], Sh[:, h, :], reads=[f"Sh{h}"], writes=[f"o_nhp{h}"])
    s.rec = None
    L = nblk - 1
    extra_nodes = [("SW", sw, [f"FR1_{L}"], L * 10 + 8), ("SO", so, [f"CO1_{L}", f"HG1_{L}"], L * 10 + 9)]
    extra_deps = {f"FR0_{nblk}": ["SW"], f"CO0_{nblk}": ["SO"], f"HG0_{nblk}": ["SO"]}
    emit_blocks(blks + [sblk], extra_nodes, extra_deps)
    s.dma("sp", ncs, halo[:], reads=["halo"], writes=["o_ncs"])
    s.finish("sp")
    return nc, s


def _perm(hh):
    r = lambda base, w: np.arange(base + hh * w, base + (hh + 1) * w)
    return np.concatenate([r(0, 256), r(512, 256), r(1024, 256), r(1536, 256),
                           r(2064, 256), r(2576, 256), r(3600, 256), r(3088, 256),
                           r(2048, 4), r(2056, 4)])


def _chan(hh):
    r = lambda base: np.arange(base + hh * 256, base + (hh + 1) * 256)
    return np.concatenate([r(0), r(512), r(1024)])


_WOUT_ROWS = np.concatenate([np.concatenate([np.arange(r * 256, (r + 1) * 256), np.arange(512 + r * 256, 512 + (r + 1) * 256)]) for r in range(2)])


def _core_inputs(c, inp):
    b, hh = c // 2, c % 2
    f = lambda a: np.ascontiguousarray(a, dtype=np.float32)
    ch = _chan(hh)
    cwv = inp["conv_w"][0][:, ch]
    scv = inp["state_conv"][0][4 * b:4 * b + 4][:, :, ch]
    hs = slice(4 * hh, 4 * hh + 4)
    return {
        "xp": f(inp["x_prompt"][b]),
        "xs": f(inp["x_sample"][4 * b:4 * b + 4].reshape(64, D)),
        "w_in": f(inp["w_in"][0][:, _perm(hh)]),
        "w_out": f(inp["w_out"][0][_WOUT_ROWS]),
        "nw": f(inp["norm_w"][0].reshape(8, 128).T),
        "cw": f(cwv.reshape(4, 3 * NP, 128).transpose(2, 1, 0)),
        "alog": f(np.broadcast_to(inp["gdn_A_log"][0][None, hs], (128, HA))),
        "dtb": f(np.broadcast_to(inp["gdn_dt_bias"][0][None, hs], (128, HA))),
        "gnw": f(np.tile(inp["gdn_norm_w"][0], 2).reshape(128, 1)),
        "hnw": f(inp["hgrn_norm_w"][0].reshape(128, 1)),
        "lbl": f(inp["hgrn_lb_logits"][:, hh * 256:(hh + 1) * 256].reshape(2, HB, 128).transpose(2, 1, 0)),
        "fnw": f(np.broadcast_to(inp["final_norm_w"][None, :], (128, D))),
        "sc": f(scv.reshape(4, 3, 3 * NP, 128).transpose(3, 2, 0, 1)),
        "sg": f(inp["state_gdn"][0][4 * b:4 * b + 4][:, hs].reshape(4, NP, 128, 64)),
        "sh": f(inp["state_hgrn"][0][4 * b:4 * b + 4][:, 2 * hh:2 * hh + 2]),
    }


_CACHE = {}


def kernel(**inputs):
    inp = {k: np.asarray(v) for k, v in inputs.items()}
    Bp, T, _ = inp["x_prompt"].shape
    assert Bp == 4 and inp["x_sample"].shape[:2] == (16, 16)
    if T not in _CACHE:
        _CACHE[T] = build(T)[0]
    nc = _CACHE[T]
    in_maps = [_core_inputs(c, inp) for c in range(8)]
    res = run_bass_kernel_spmd(nc, in_maps, core_ids=list(range(8)))
    r = res.results
    y_prompt = np.stack([r[2 * b]["yp"] for b in range(4)]).astype(np.float32)
    y_sample = np.concatenate([r[2 * b]["ys"].reshape(4, 16, D) for b in range(4)]).astype(np.float32)
    ncp_ = np.zeros((1, 4, 3, 1536), np.float32); ncs_ = np.zeros((1, 16, 3, 1536), np.float32)
    ngp_ = np.zeros((1, 4, 8, 64, 64), np.float32); ngs_ = np.zeros((1, 16, 8, 64, 64), np.float32)
    nhp_ = np.zeros((1, 4, 4, 128, 128), np.float32); nhs_ = np.zeros((1, 16, 4, 128, 128), np.float32)
    cvt = lambda a: a.transpose(2, 3, 1, 0).reshape(a.shape[2], 3, 3 * NP * 128)
    for c in range(8):
        b, hh = c // 2, c % 2
        ch = _chan(hh)
        ncp_[0, b][:, ch] = cvt(r[c]["ncp"])[0]
        ncs_[0, 4 * b:4 * b + 4][:, :, ch] = cvt(r[c]["ncs"])
        ngp_[0, b, 4 * hh:4 * hh + 4] = r[c]["ngp"].reshape(HA, 64, 64)
        ngs_[0, 4 * b:4 * b + 4, 4 * hh:4 * hh + 4] = r[c]["ngs"].reshape(4, HA, 64, 64)
        nhp_[0, b, 2 * hh:2 * hh + 2] = r[c]["nhp"].reshape(HB, 128, 128)
        nhs_[0, 4 * b:4 * b + 4, 2 * hh:2 * hh + 2] = r[c]["nhs"].reshape(4, HB, 128, 128)
    return (y_prompt, y_sample, ncp_, ngp_, nhp_, ncs_, ngs_, nhs_)
```

```python
import numpy as np
import concourse.bass as bass
import concourse.mybir as mybir
from concourse.bass_utils import run_bass_kernel_spmd

F32 = mybir.dt.float32
BF16 = mybir.dt.bfloat16
AF = mybir.ActivationFunctionType
ALU = mybir.AluOpType

D = 1024
HA, HB = 4, 2
NP = HA // 2
NT = 4 * NP + 3 * HB
C_HI = NT * 128
C_G = C_HI + HB * 128
NCOL = C_G + 2 * HA
EPS = 1e-6
RG = [[0, 1], [2, 3], [4, 5], [6, 7]]


SAME_ENGINE_WAIT = True
INPROJ_ALT = True
HG_ALT = False
SCHED_MODE = 0
SCHED_DELTA = 300.0
SCHED_WIN = 24


class _Proxy:
    def __getattr__(self, name):
        return lambda *a, **k: (name, a, k)


_PROXY = _Proxy()
PSUM_KEYS = {"B0", "B1", "B2", "B3", "B4", "B5", "B6", "BT"}


class Sched:
    def __init__(self, nc):
        self.nc = nc
        self.eng = {"pe": nc.tensor, "dve": nc.vector, "act": nc.scalar, "pool": nc.gpsimd, "sp": nc.sync}
        self.sem = {k: nc.alloc_semaphore(name=f"s_{k}") for k in self.eng}
        self.cnt = {k: 0 for k in self.eng}
        self.seen = {k: {} for k in self.eng}
        self.last_w = {}
        self.readers = {}
        self.dma_sems = {}
        self.n_wait = 0
        self.n_ops = 0
        self.rec = None

    def coll(self, ins, outs, reads=(), writes=()):
        if self.rec is not None:
            self.rec.append(("coll", "pool", ins, outs, tuple(reads), tuple(writes)))
            return None
        self._deps("pool", reads, writes)
        if "cc" not in self.dma_sems:
            self.dma_sems["cc"] = [self.nc.alloc_semaphore(name="cc_sem"), 0]
        ent = self.dma_sems["cc"]
        ent[1] += 1
        self.nc.gpsimd.collective_compute("AllGather", ALU.bypass, replica_groups=RG, ins=ins, outs=outs).then_inc(ent[0], 1)
        tok = ("cc", ent[0], ent[1])
        self._commit(tok, reads, writes)
        self.n_ops += 1
        return tok

    def emit(self, r):
        if r[0] == "coll":
            self.coll(r[2], r[3], r[4], r[5])
            return
        if r[0] == "op":
            _, e, call, reads, writes = r
            self.op(e, lambda eng: getattr(eng, call[0])(*call[1], **call[2]), reads, writes)
        else:
            _, q, out, in_, reads, writes, slot = r
            self.dma(q, out, in_, reads, writes, slot)

    def _cost(self, r):
        if r[0] == "dma":
            return 2500.0
        if r[0] == "coll":
            return 30000.0
        _, e, call, reads, writes = r
        name, args, kw = call
        def nfree(ap):
            try:
                sh = list(ap.shape)
                n = 1
                for d in sh[1:]:
                    n *= int(d)
                return n
            except Exception:
                return 256
        if e == "pe":
            ap = kw.get("rhs", None) if name == "matmul" else kw.get("in_", None)
            n = nfree(ap) if ap is not None else 64
            c = 32.0 + 0.4 * n
            try:
                if name == "matmul" and kw["rhs"].dtype == F32:
                    c *= 3.0
            except Exception:
                pass
            return c
        ap = kw.get("out", None)
        n = nfree(ap) if ap is not None else 256
        if e == "dve":
            return 110.0 + 1.0 * n
        if e == "act":
            return 170.0 + 0.9 * n
        return 110.0 + 1.8 * n

    def _est_start(self, r):
        if r[0] in ("dma", "coll"):
            e, reads, writes = r[1], r[4], r[5]
        else:
            e, reads, writes = r[1], r[3], r[4]
        m = self.model
        t = m["eng"].get(e, 0.0)
        ex = [k for k in reads if k in PSUM_KEYS]
        for k in reads:
            w = m["w"].get(k)
            if w is not None:
                t = max(t, w[0] + ((0.0 if e == "pe" else 150.0) if w[1] == e else 230.0))
        for k in list(writes) + ex:
            w = m["w"].get(k)
            if w is not None:
                t = max(t, w[0] + ((0.0 if e == "pe" else 150.0) if w[1] == e else 230.0))
            for (tt, ee) in m["r"].get(k, {}).values():
                t = max(t, tt + ((0.0 if e == "pe" else 150.0) if ee == e else 230.0))
        return t

    def _model_commit(self, r, t0):
        if r[0] in ("dma", "coll"):
            e, reads, writes = r[1], r[4], r[5]
            eng_busy = 100.0
        else:
            e, reads, writes = r[1], r[3], r[4]
            eng_busy = None
        m = self.model
        c = self._cost(r)
        t1 = t0 + c
        m["eng"][e] = t0 + (eng_busy if eng_busy is not None else c)
        ex = [k for k in reads if k in PSUM_KEYS]
        who = e if r[0] == "op" else "dma"
        for k in reads:
            m["r"].setdefault(k, {})[who] = (t1, who)
        for k in list(writes) + ex:
            m["w"][k] = (t1, who)
            m["r"][k] = {}
        m["t"] = max(m.get("t", 0.0), t1)

    def dag_emit(self, nodes):
        if not hasattr(self, "model"):
            self.model = {"eng": {}, "w": {}, "r": {}, "t": 0.0}
        names = {n[0] for n in nodes}
        units, deps, prio, preds = {}, {}, {}, {}
        def rw(r):
            if r[0] in ("dma", "coll"):
                reads, writes = r[4], r[5]
            else:
                reads, writes = r[3], r[4]
            ex = [k for k in reads if k in PSUM_KEYS]
            return list(reads), list(writes) + ex
        for (name, ops, dp, pr) in nodes:
            u = []
            for r in ops:
                glued = (r[0] == "op" and r[2][0] == "matmul" and r[2][2].get("start") is False)
                if glued and u:
                    u[-1].append(r)
                else:
                    u.append([r])
            units[name] = u
            deps[name] = {d for d in dp if d in names}
            prio[name] = pr
            lw, rd, pl = {}, {}, []
            for j, unit in enumerate(u):
                p = set()
                R, W = [], []
                for r in unit:
                    a_, b_ = rw(r)
                    R += a_; W += b_
                for k in R:
                    if k in lw:
                        p.add(lw[k])
                for k in W:
                    if k in lw:
                        p.add(lw[k])
                    p |= rd.get(k, set())
                p.discard(j)
                pl.append(p)
                for k in R:
                    rd.setdefault(k, set()).add(j)
                for k in W:
                    lw[k] = j
                    rd[k] = set()
            preds[name] = pl
        emitted = {n: [False] * len(units[n]) for n in units}
        nleft = {n: len(units[n]) for n in units}
        lo = {n: 0 for n in units}
        done = {n for n in units if not units[n]}
        waiting = [n for n in units if n not in done]
        active = []
        def refresh():
            nonlocal waiting
            still = []
            for n in waiting:
                if deps[n] <= done:
                    active.append(n)
                else:
                    still.append(n)
            waiting = still
        refresh()
        WIN = SCHED_WIN
        while active:
            best, bsel = None, None
            for n in active:
                em, pl, u = emitted[n], preds[n], units[n]
                j = lo[n]
                seen = 0
                while j < len(u) and seen < WIN:
                    if not em[j]:
                        seen += 1
                        if all(em[q] for q in pl[j]):
                            t = self._est_start(u[j][0])
                            key = (t, prio[n], j)
                            if best is None or key < best:
                                best, bsel = key, (n, j)
                    j += 1
            n, j = bsel
            for r in units[n][j]:
                t0 = self._est_start(r)
                self._model_commit(r, t0)
                self.emit(r)
            emitted[n][j] = True
            nleft[n] -= 1
            while lo[n] < len(units[n]) and emitted[n][lo[n]]:
                lo[n] += 1
            if nleft[n] == 0:
                active.remove(n)
                done.add(n)
                refresh()
        assert not waiting, ("DAG deadlock", waiting[:5])

    def merge_emit(self, streams):
        if not hasattr(self, "model"):
            self.model = {"eng": {}, "w": {}, "r": {}, "t": 0.0}
        units = []
        for st in streams:
            u = []
            for r in st:
                glued = (r[0] == "op" and r[2][0] == "matmul" and r[2][2].get("start") is False)
                if glued and u:
                    u[-1].append(r)
                else:
                    u.append([r])
            units.append(u)
        pos = [0] * len(units)
        while True:
            best, bi = None, -1
            for i, u in enumerate(units):
                if pos[i] < len(u):
                    t = self._est_start(u[pos[i]][0])
                    key = (t, -(len(u) - pos[i]))
                    if best is None or key < best:
                        best, bi = key, i
            if bi < 0:
                break
            for r in units[bi][pos[bi]]:
                t0 = self._est_start(r)
                self._model_commit(r, t0)
                self.emit(r)
            pos[bi] += 1


    def _wait(self, e, tok):
        name, sem, val = tok
        if name == "pe" and e == "pe":
            return
        if name == e and not SAME_ENGINE_WAIT:
            return
        if self.seen[e].get(name, 0) >= val:
            return
        self.eng[e].wait_ge(sem, val)
        self.seen[e][name] = val
        self.n_wait += 1

    def _deps(self, e, reads, writes):
        for k in reads:
            t = self.last_w.get(k)
            if t is not None:
                self._wait(e, t)
        for k in writes:
            t = self.last_w.get(k)
            if t is not None:
                self._wait(e, t)
            for t in self.readers.get(k, {}).values():
                self._wait(e, t)

    def _commit(self, tok, reads, writes):
        for k in reads:
            self.readers.setdefault(k, {})[tok[0]] = tok
        for k in writes:
            self.last_w[k] = tok
            self.readers[k] = {}

    def op(self, e, fn, reads=(), writes=()):
        if self.rec is not None:
            self.rec.append(("op", e, fn(_PROXY), tuple(reads), tuple(writes)))
            return None
        ex = [k for k in reads if k in PSUM_KEYS]
        if ex:
            writes = list(writes) + ex
        self._deps(e, reads, writes)
        ins = fn(self.eng[e])
        self.cnt[e] += 1
        ins.then_inc(self.sem[e], 1)
        tok = (e, self.sem[e], self.cnt[e])
        self._commit(tok, reads, writes)
        self.n_ops += 1
        return tok

    def dma(self, q, out, in_, reads=(), writes=(), slot=None):
        if self.rec is not None:
            self.rec.append(("dma", q, out, in_, tuple(reads), tuple(writes), slot))
            return None
        self._deps(q, reads, writes)
        slot = slot or (writes[0] if writes else reads[0])
        sname = f"d_{slot}"
        if sname not in self.dma_sems:
            self.dma_sems[sname] = [self.nc.alloc_semaphore(name=sname), 0]
        ent = self.dma_sems[sname]
        ent[1] += 16
        self.eng[q].dma_start(out=out, in_=in_).then_inc(ent[0], 16)
        tok = (sname, ent[0], ent[1])
        self._commit(tok, reads, writes)
        self.n_ops += 1
        return tok

    def finish(self, e="sp"):
        for k, t in list(self.last_w.items()):
            self._wait(e, t)


def bc(ap, shape):
    return ap.to_broadcast(list(shape))


def build(T, TB=256):
    nc = bass.Bass("TRN2", target_bir_lowering=False)
    s = Sched(nc)
    dt_in = lambda n, sh: nc.dram_tensor(n, list(sh), F32, kind="ExternalInput").ap()
    dt_out = lambda n, sh: nc.dram_tensor(n, list(sh), F32, kind="ExternalOutput").ap()
    xp = dt_in("xp", [T, D]); xs = dt_in("xs", [64, D])
    w_in = dt_in("w_in", [D, NCOL]); w_out = dt_in("w_out", [D, D])
    nw_d = dt_in("nw", [128, 8]); cw_d = dt_in("cw", [128, 3 * NP, 4])
    alog_d = dt_in("alog", [128, HA]); dtb_d = dt_in("dtb", [128, HA])
    gnw_d = dt_in("gnw", [128, 1]); hnw_d = dt_in("hnw", [128, 1])
    lbl_d = dt_in("lbl", [128, HB, 2]); fnw_d = dt_in("fnw", [128, D])
    sc_d = dt_in("sc", [128, 3 * NP, 4, 3])
    sg_d = dt_in("sg", [4, NP, 128, 64]); sh_d = dt_in("sh", [4, HB, 128, 128])
    yp = dt_out("yp", [T, D]); ys = dt_out("ys", [64, D])
    ncp = dt_out("ncp", [128, 3 * NP, 1, 3]); ngp = dt_out("ngp", [1, NP, 128, 64]); nhp = dt_out("nhp", [1, HB, 128, 128])
    ncs = dt_out("ncs", [128, 3 * NP, 4, 3]); ngs = dt_out("ngs", [4, NP, 128, 64]); nhs = dt_out("nhs", [4, HB, 128, 128])

    NL = NP + HB
    xsrc = [nc.dram_tensor(f"xsrc{i}", [NL * 128, TB], BF16).ap() for i in range(2)]
    xdst = [nc.dram_tensor(f"xdst{i}", [2 * NL * 128, TB], BF16).ap() for i in range(2)]
    xsrc_s = nc.dram_tensor("xsrc_s", [NL * 128, 64], BF16).ap()
    xdst_s = nc.dram_tensor("xdst_s", [2 * NL * 128, 64], BF16).ap()
    sb = lambda n, sh, d=F32: nc.alloc_sbuf_tensor(n, list(sh), d)
    Wb = sb("Wb", [128, 8, NCOL], BF16)
    WOb = sb("WOb", [128, 8, D], BF16)
    nw = sb("nw_t", [128, 8]); cw = sb("cw_t", [128, 3 * NP, 4])
    alog = sb("alog_t", [128, HA]); dtb = sb("dtb_t", [128, HA]); negA = sb("negA", [128, HA])
    gnw = sb("gnw_t", [128, 1]); hnw = sb("hnw_t", [128, 1])
    lbl = sb("lbl_t", [128, HB, 2]); lb = sb("lb", [128, HB]); oml = sb("oml", [128, HB])
    fnw = sb("fnw_t", [128, D])
    identb = sb("identb", [128, 128], BF16); identf = sb("identf", [128, 128])
    ones = sb("ones", [128, 128]); ob2 = sb("ob2", [128, 128])
    fgt = sb("fgt", [128, 128]); fle = sb("fle", [128, 128])
    I_s = sb("I_s", [128, 64]); U_s = sb("U_s", [128, 64]); Tri_s = sb("Tri_s", [128, 64]); Mc_s = sb("Mc_s", [128, 64])
    halo = sb("halo", [128, 3 * NP, 4, 3])
    Sg = sb("Sg", [128, NP, 64]); Sh = sb("Sh", [128, HB, 128])
    W_ = TB
    xt = [sb(f"xt{i}", [128, D]) for i in range(4)]
    sqj = sb("sqj", [128, D], BF16)
    xb = sb("xb", [128, D], BF16)
    hT_all = [sb(f"hT{i}", [128, 8, W_], BF16) for i in range(2)]
    sqjO = sb("sqjO", [128, D], BF16); ssO = sb("ssO", [128, 1]); rrO = sb("rrO", [128, 1])
    ss = sb("ss", [128, 1]); rr = sb("rr", [128, 1])
    raw = sb("raw", [128, 3, W_ + 12])
    cv = sb("cv", [128, 3, W_])
    tmp = [None, sb("tmp1", [128, W_]), sb("tmp2", [128, W_]), None]
    za = [sb(f"za{i}", [128, W_]) for i in range(2)]
    cvb = [sb(f"cvb{i}", [128, 3, W_], BF16) for i in range(2)]
    tmpA = [sb(f"tmpA{i}", [128, W_]) for i in range(3)]
    sqA = [sb(f"sqA{i}", [128, W_], BF16) for i in range(2)]
    tnA = [sb(f"tnA{i}", [128, W_]) for i in range(2)]
    I_sb = sb("I_sb", [128, 64], BF16); ob2b = sb("ob2b", [128, 128], BF16); onesb = sb("onesb", [128, 128], BF16)
    Sgb = sb("Sgb", [128, NP, 64], BF16); Shb = sb("Shb", [128, HB, 128], BF16)
    sqb = sb("sqb", [128, W_], BF16); sqbH = sb("sqbH", [128, W_], BF16)
    G_all = [sb(f"G{i}", [128, 4, 2 * HA]) for i in range(2)]; Gb_all = [sb(f"Gb{i}", [128, 4, HA]) for i in range(2)]; Gg_all = [sb(f"Gg{i}", [128, 4, HA]) for i in range(2)]
    gs_all = [sb(f"gs{i}", [128, NP, 4]) for i in range(2)]; bs_all = [sb(f"bs{i}", [128, NP, 4]) for i in range(2)]; nbs_all = [sb(f"nbs{i}", [128, NP, 4]) for i in range(2)]
    gc_all = [sb(f"gc{i}", [128, NP, 4]) for i in range(2)]; gl_all = [sb(f"gl{i}", [128, NP, 4]) for i in range(2)]; egc_all = [sb(f"egc{i}", [128, NP, 4]) for i in range(2)]
    dk_all = [sb(f"dk{i}", [128, NP, 4]) for i in range(2)]; bge_all = [sb(f"bge{i}", [128, NP, 4]) for i in range(2)]
    rhsD = sb("rhsD", [128, W_]); Dg = sb("Dg", [128, W_])
    Ee = sb("Ee", [128, W_]); Dm = sb("Dm", [128, W_]); Ds = sb("Ds", [128, W_])
    EBs = [sb(f"EBs{i}", [128, W_]) for i in range(2)]
    P0t = [sb(f"P0t{i}", [128, W_], BF16) for i in range(2)]
    PT0t = [sb(f"PT0t{i}", [128, W_], BF16) for i in range(2)]
    R0t = [sb(f"R0t{i}", [128, W_], BF16) for i in range(2)]
    P = [sb(f"P{i}", [128, W_], BF16) for i in range(2)]
    PT = [sb(f"PT{i}", [128, W_], BF16) for i in range(2)]
    R = [sb(f"R{i}", [128, W_], BF16) for i in range(2)]
    attn = sb("attn", [128, W_], BF16); attnT = [sb(f"attnT{i}", [128, W_], BF16) for i in range(2)]
    Kbe = [sb(f"Kbe{i}", [128, 4, 64], BF16) for i in range(2)]; Kd = [sb(f"Kd{i}", [128, 4, 64], BF16) for i in range(2)]; bV = [sb(f"bV{i}", [128, 4, 64], BF16) for i in range(2)]
    u = sb("u", [128, 4, 64]); wT = sb("wT", [128, W_], BF16); QeT = [sb(f"QeT{i}", [128, W_], BF16) for i in range(2)]
    vn = sb("vn", [128, 64], BF16)
    oTf = sb("oTf", [128, W_])
    oTn = [sb(f"oTn_{i}", [128, 2 * NL, W_], BF16) for i in range(2)]
    oOwn = [sb(f"oOwn_{i}", [128, NL, W_], BF16) for i in range(2)]
    qb = sb("qb", [128, W_]); ff = sb("ff", [128, W_]); lf = sb("lf", [128, W_]); kb = sb("kb", [128, W_])
    bb = sb("bb", [128, W_]); bl = sb("bl", [128, W_])
    Qe = sb("Qe", [128, W_], BF16); Qx = sb("Qx", [128, W_], BF16); Kdh = sb("Kdh", [128, W_], BF16)
    Qef = sb("Qef", [128, W_]); Qxf = sb("Qxf", [128, W_]); Kdf = sb("Kdf", [128, W_])
    ebl = sb("ebl", [128, 4]); zb = sb("zb", [128, W_])
    vtok = sb("vtok", [64, 4, 128], BF16); Kdt = sb("Kdt", [64, 4, 128], BF16); aTh = sb("aTh", [64, W_], BF16)
    smask_all = [sb(f"smask{i}", [128, W_]) for i in range(2)]
    tmpH = [None, sb("tmpH1", [128, W_])]; oTfH = sb("oTfH", [128, W_])
    yo = sb("yo", [128, D]); yo2 = sb("yo2", [128, D])
    pb = [nc.alloc_psum_tensor(f"pb{i}", [128, 512], F32) for i in range(8)]
    pT2 = pb[2][:, 0:128].bitcast(BF16).rearrange("p (k t) -> p k t", t=128)

    def aff(out, cmp, fill_in, step=-1, cm=1, base=0):
        s.op("pool", lambda e: e.memset(out[:], fill_in), writes=[out.name])
        s.op("pool", lambda e: e.affine_select(out=out[:], in_=out[:], pattern=[[step, 128]], compare_op=cmp,
                                               fill=0.0, base=base, channel_multiplier=cm), reads=[out.name], writes=[out.name])
    aff(identf, ALU.is_equal, 1.0)
    aff(fgt, ALU.is_gt, 1.0)
    aff(fle, ALU.is_gt, 1.0, step=1, cm=-1, base=1)
    s.op("pool", lambda e: e.memset(ones[:], 1.0), writes=["ones"])
    s.op("pool", lambda e: e.memset(ob2[:], 0.0), writes=["ob2"])
    for h in range(2):
        sl = slice(64 * h, 64 * h + 64)
        s.op("pool", lambda e: e.memset(ob2[sl, sl], 1.0), reads=["ob2"], writes=["ob2"])
    s.op("dve", lambda e: e.tensor_copy(out=identb[:], in_=identf[:]), reads=["identf"], writes=["identb"])
    for (dst, src) in ((I_s, identf), (U_s, fgt), (Tri_s, fle)):
        for h in range(2):
            sl = slice(64 * h, 64 * h + 64)
            s.op("dve", lambda e: e.tensor_copy(out=dst[sl, :], in_=src[sl, sl]), reads=[src.name], writes=[dst.name])
    s.op("dve", lambda e: e.tensor_tensor(out=Mc_s[:], in0=U_s[:], in1=I_s[:], op=ALU.add), reads=["U_s", "I_s"], writes=["Mc_s"])
    s.op("dve", lambda e: e.tensor_copy(out=I_sb[:], in_=I_s[:]), reads=["I_s"], writes=["I_sb"])
    s.op("dve", lambda e: e.tensor_copy(out=ob2b[:], in_=ob2[:]), reads=["ob2"], writes=["ob2b"])
    s.op("dve", lambda e: e.tensor_copy(out=onesb[:], in_=ones[:]), reads=["ones"], writes=["onesb"])
    for t_, d_ in ((nw, nw_d), (cw, cw_d), (alog, alog_d), (dtb, dtb_d), (gnw, gnw_d), (hnw, hnw_d), (lbl, lbl_d), (fnw, fnw_d)):
        s.dma("sp", t_[:], d_, writes=[t_.name])
    s.op("act", lambda e: e.activation(out=negA[:], in_=alog[:], func=AF.Exp), reads=["alog_t"], writes=["negA"])
    s.op("dve", lambda e: e.tensor_scalar(out=negA[:], in0=negA[:], scalar1=-1.0, scalar2=None, op0=ALU.mult), reads=["negA"], writes=["negA"])
    s.op("dve", lambda e: e.tensor_tensor(out=lb[:], in0=lbl[:, :, 1], in1=lbl[:, :, 0], op=ALU.subtract), reads=["lbl_t"], writes=["lb"])
    s.op("act", lambda e: e.activation(out=lb[:], in_=lb[:], func=AF.Exp), reads=["lb"], writes=["lb"])
    s.op("dve", lambda e: e.tensor_scalar(out=lb[:], in0=lb[:], scalar1=1.0, scalar2=None, op0=ALU.add), reads=["lb"], writes=["lb"])
    s.op("dve", lambda e: e.reciprocal(out=lb[:], in_=lb[:]), reads=["lb"], writes=["lb"])
    s.op("dve", lambda e: e.tensor_scalar(out=oml[:], in0=lb[:], scalar1=-1.0, scalar2=1.0, op0=ALU.mult, op1=ALU.add), reads=["lb"], writes=["oml"])
    w_in_v = w_in.rearrange("(k p) n -> p k n", p=128)
    stgx = [sb(f"stgx{i}", [128, D]) for i in range(4)]
    stgs = [(xt[0], "xt0"), (stgx[0], "stgx0"), (xt[1], "xt1"), (stgx[1], "stgx1"), (yo, "yo"), (stgx[2], "stgx2"), (yo2, "yo2"), (stgx[3], "stgx3")]
    q = 0
    nfull = NCOL // 1024
    for k in range(8):
        pieces = [(i * 1024, 1024) for i in range(nfull)] + [(nfull * 1024, NCOL % 1024)]
        for (c0, cn) in pieces:
            if cn == 1024:
                tl, key = stgs[q % len(stgs)]
            else:
                tl, key = Ee, "Ee"
            s.dma("sp", tl[:, 0:cn], w_in_v[:, k, c0:c0 + cn], writes=[key])
            if q % 2 == 0:
                s.op("dve", lambda e: e.tensor_scalar(out=Wb[:, k, c0:c0 + cn], in0=tl[:, 0:cn], scalar1=nw[:, k:k + 1], scalar2=None, op0=ALU.mult),
                     reads=[key, "nw_t"], writes=["Wb"])
            else:
                s.op("act", lambda e: e.activation(out=Wb[:, k, c0:c0 + cn], in_=tl[:, 0:cn], func=AF.Copy, scale=nw[:, k:k + 1]),
                     reads=[key, "nw_t"], writes=["Wb"])
            q += 1
    w_out_v = w_out.rearrange("(k p) n -> p k n", p=128)
    for k in range(8):
        tl, key = stgs[q % len(stgs)]
        s.dma("sp", tl[:, :], w_out_v[:, k, :], writes=[key])
        if q % 2 == 0:
            s.op("dve", lambda e: e.tensor_copy(out=WOb[:, k, :], in_=tl[:, :]), reads=[key], writes=["WOb"])
        else:
            s.op("act", lambda e: e.activation(out=WOb[:, k, :], in_=tl[:, :], func=AF.Copy), reads=[key], writes=["WOb"])
        q += 1

    def block(x_src, y_dst, t0, nseg, seglen, c, first, is_sample, bpar=0):
        hT = hT_all[bpar]; G = G_all[bpar]; Gb = Gb_all[bpar]; Gg = Gg_all[bpar]; gs = gs_all[bpar]; bs = bs_all[bpar]; nbs = nbs_all[bpar]; gc = gc_all[bpar]; gl = gl_all[bpar]; egc = egc_all[bpar]; dk = dk_all[bpar]; bge = bge_all[bpar]; smask = smask_all[bpar]
        TBk = nseg * seglen
        nch = TBk // c
        cps = seglen // c
        TT = min(128, TBk)
        ntt = TBk // TT
        nlev = {64: 5, 16: 3}[c]
        v3 = lambda ap: ap.rearrange("p (n c) -> p n c", c=c)

        def ph0_x():
            for tt in range(ntt):
                X = xt[bpar * 2 + tt]
                s.dma("sp", X[0:TT, :], x_src[t0 + tt * TT: t0 + (tt + 1) * TT, :], writes=[X.name])
                s.op("act", lambda e: e.activation(out=sqj[0:TT, :], in_=X[0:TT, :], func=AF.Square, accum_out=ss[0:TT, :]),
                     reads=[X.name], writes=["sqj", "ss"])
                s.op("act", lambda e: e.activation(out=rr[0:TT, :], in_=ss[0:TT, :], func=AF.Ln, scale=1.0 / D, bias=EPS), reads=["ss"], writes=["rr"])
                s.op("act", lambda e: e.activation(out=rr[0:TT, :], in_=rr[0:TT, :], func=AF.Exp, scale=-0.5), reads=["rr"], writes=["rr"])
                s.op("dve", lambda e: e.tensor_scalar(out=xb[0:TT, :], in0=X[0:TT, :], scalar1=rr[0:TT, :], scalar2=None, op0=ALU.mult),
                     reads=[X.name, "rr"], writes=["xb"])
                for kk in range(4):
                    for j in range(2):
                        k = 2 * kk + j
                        s.op("pe", lambda e: e.transpose(out=pT2[:, j, 0:TT], in_=xb[0:TT, k * 128:(k + 1) * 128], identity=identb[0:TT, 0:TT]),
                             reads=["xb", "identb"], writes=["B2"])
                    if kk % 2 == 0:
                        s.op("dve", lambda e: e.tensor_copy(out=hT[:, 2 * kk:2 * kk + 2, tt * TT:(tt + 1) * TT], in_=pT2[:, :, 0:TT]), reads=["B2"], writes=[f"hT{bpar}"])
                    else:
                        s.op("act", lambda e: e.activation(out=hT[:, 2 * kk:2 * kk + 2, tt * TT:(tt + 1) * TT], in_=pT2[:, :, 0:TT], func=AF.Copy), reads=["B2"], writes=[f"hT{bpar}"])

        def silu_from(src, srckeys, dst, dstkey, scr, scrkey, W, outdt_note=None):
            s.op("act", lambda e: e.activation(out=scr, in_=src, func=AF.Exp, scale=-1.0), reads=srckeys, writes=[scrkey])
            s.op("act", lambda e: e.activation(out=scr, in_=scr, func=AF.Ln, bias=1.0), reads=[scrkey], writes=[scrkey])
            s.op("act", lambda e: e.activation(out=scr, in_=scr, func=AF.Exp, scale=-1.0), reads=[scrkey], writes=[scrkey])
            s.op("dve", lambda e: e.tensor_tensor(out=dst, in0=src, in1=scr, op=ALU.mult), reads=list(srckeys) + [scrkey], writes=[dstkey])


        def inproj_fm(ct, i=0, alt=False):
            key = "B6" if alt is True else ("B3" if alt == 3 else "B0")
            out = pb[6][:, 256:256 + TBk] if alt is True else (pb[3][:, 256:256 + TBk] if alt == 3 else pb[0][:, i * 256: i * 256 + TBk])
            for k in range(8):
                s.op("pe", lambda e: e.matmul(out, lhsT=Wb[:, k, ct * 128:(ct + 1) * 128], rhs=hT[:, k, 0:TBk], start=(k == 0), stop=(k == 7)),
                     reads=["Wb", f"hT{bpar}"], writes=[key])
            return out, key

        def ph0_g():
            for ch in range(nch):
                for h in range(2):
                    out = pb[2][64 * h:64 * h + c, ch * 2 * HA:(ch + 1) * 2 * HA]
                    for k in range(8):
                        s.op("pe", lambda e: e.matmul(out, lhsT=hT[:, k, ch * c:(ch + 1) * c], rhs=Wb[:, k, C_G:C_G + 2 * HA], start=(k == 0), stop=(k == 7)),
                             reads=["Wb", f"hT{bpar}"], writes=["B2"])
            Gv = G[:, 0:nch, :]
            s.op("dve", lambda e: e.tensor_copy(out=Gv, in_=pb[2][:, 0:nch * 2 * HA].rearrange("p (n g) -> p n g", g=2 * HA)), reads=["B2"], writes=[f"G{bpar}"])
            Gbv = Gb[:, 0:nch, :]; Ggv = Gg[:, 0:nch, :]
            s.op("act", lambda e: e.activation(out=Gbv, in_=Gv[:, :, 0:HA], func=AF.Exp, scale=-1.0), reads=[f"G{bpar}"], writes=[f"Gb{bpar}"])
            s.op("act", lambda e: e.activation(out=Gbv, in_=Gbv, func=AF.Ln, bias=1.0), reads=[f"Gb{bpar}"], writes=[f"Gb{bpar}"])
            s.op("act", lambda e: e.activation(out=Gbv, in_=Gbv, func=AF.Exp, scale=-1.0), reads=[f"Gb{bpar}"], writes=[f"Gb{bpar}"])
            s.op("dve", lambda e: e.tensor_tensor(out=Ggv, in0=Gv[:, :, HA:2 * HA], in1=bc(dtb[:, None, :], [128, nch, HA]), op=ALU.add),
                 reads=[f"G{bpar}", "dtb_t"], writes=[f"Gg{bpar}"])
            s.op("act", lambda e: e.activation(out=Ggv, in_=Ggv, func=AF.Exp), reads=[f"Gg{bpar}"], writes=[f"Gg{bpar}"])
            s.op("act", lambda e: e.activation(out=Ggv, in_=Ggv, func=AF.Ln, bias=1.0), reads=[f"Gg{bpar}"], writes=[f"Gg{bpar}"])
            s.op("dve", lambda e: e.tensor_tensor(out=Ggv, in0=Ggv, in1=bc(negA[:, None, :], [128, nch, HA]), op=ALU.mult),
                 reads=[f"Gg{bpar}", "negA"], writes=[f"Gg{bpar}"])
            gsv = gs[:, :, 0:nch]; bsv = bs[:, :, 0:nch]; nbsv = nbs[:, :, 0:nch]
            gcv = gc[:, :, 0:nch]; glv = gl[:, :, 0:nch]; egcv = egc[:, :, 0:nch]; dkv = dk[:, :, 0:nch]; bgev = bge[:, :, 0:nch]
            for h in range(2):
                sl = slice(64 * h, 64 * h + 64)
                for (dst, src, kd, ks) in ((gs, Gg, f"gs{bpar}", f"Gg{bpar}"), (bs, Gb, f"bs{bpar}", f"Gb{bpar}")):
                    for p in range(NP):
                        s.op("dve", lambda e: e.tensor_copy(out=dst[sl, p, 0:nch], in_=src[sl, 0:nch, 2 * p + h]), reads=[ks], writes=[kd])
            s.op("dve", lambda e: e.tensor_scalar(out=nbsv, in0=bsv, scalar1=-1.0, scalar2=None, op0=ALU.mult), reads=[f"bs{bpar}"], writes=[f"nbs{bpar}"])
            for h in range(2):
                rs = slice(64 * h, 64 * h + c)
                s.op("pe", lambda e: e.matmul(pb[2][rs, 64:64 + NP * nch], lhsT=Tri_s[rs, 0:c], rhs=gs[rs, :, 0:nch], start=True, stop=True),
                     reads=["Tri_s", f"gs{bpar}"], writes=["B2"])
                s.op("pe", lambda e: e.matmul(pb[2][rs, 96:96 + NP * nch], lhsT=ones[rs, 0:c], rhs=gs[rs, :, 0:nch], start=True, stop=True),
                     reads=["ones", f"gs{bpar}"], writes=["B2"])
            s.op("dve", lambda e: e.tensor_copy(out=gcv, in_=pb[2][:, 64:64 + NP * nch].rearrange("p (a n) -> p a n", n=nch)), reads=["B2"], writes=[f"gc{bpar}"])
            s.op("dve", lambda e: e.tensor_copy(out=glv, in_=pb[2][:, 96:96 + NP * nch].rearrange("p (a n) -> p a n", n=nch)), reads=["B2"], writes=[f"gl{bpar}"])
            s.op("act", lambda e: e.activation(out=egcv, in_=gcv, func=AF.Exp), reads=[f"gc{bpar}"], writes=[f"egc{bpar}"])
            s.op("dve", lambda e: e.tensor_tensor(out=dkv, in0=glv, in1=gcv, op=ALU.subtract), reads=[f"gl{bpar}", f"gc{bpar}"], writes=[f"dk{bpar}"])
            s.op("act", lambda e: e.activation(out=dkv, in_=dkv, func=AF.Exp), reads=[f"dk{bpar}"], writes=[f"dk{bpar}"])
            s.op("dve", lambda e: e.tensor_tensor(out=bgev, in0=bsv, in1=egcv, op=ALU.mult), reads=[f"bs{bpar}", f"egc{bpar}"], writes=[f"bge{bpar}"])


        def phase0(_=None):
            ph0_x()
            ph0_g()
            ph0_m()

        def front(p):
            par = p % 2
            for i3 in range(3):
                ct = NP * i3 + p
                pp, pk = inproj_fm(ct, 0, INPROJ_ALT and i3 == 1)
                rv = raw[:, i3, 0:nseg * (seglen + 3)].rearrange("p (n c) -> p n c", c=seglen + 3)
                if first and not is_sample:
                    s.op("pool", lambda e: e.memset(rv[:, :, 0:3], 0.0), reads=[f"raw{i3}"], writes=[f"raw{i3}"])
                else:
                    s.op("pool", lambda e: e.tensor_copy(out=rv[:, :, 0:3], in_=halo[:, ct, 0:nseg, :]), reads=["halo"], writes=[f"raw{i3}"])
                s.op("act", lambda e: e.activation(out=rv[:, :, 3:3 + seglen], in_=pp.rearrange("p (n c) -> p n c", c=seglen), func=AF.Copy),
                     reads=[pk], writes=[f"raw{i3}"])
                s.op("pool", lambda e: e.tensor_copy(out=halo[:, ct, 0:nseg, :], in_=rv[:, :, seglen:seglen + 3]), reads=[f"raw{i3}"], writes=["halo"])
                cvv = cv[:, i3, 0:TBk].rearrange("p (n c) -> p n c", c=seglen)
                s.op("dve", lambda e: e.tensor_scalar(out=cvv, in0=rv[:, :, 0:seglen], scalar1=cw[:, ct, 0:1], scalar2=None, op0=ALU.mult),
                     reads=[f"raw{i3}", "cw_t"], writes=[f"cv{i3}"])
                for j in range(1, 4):
                    s.op("dve", lambda e: e.scalar_tensor_tensor(out=cvv, in0=rv[:, :, j:j + seglen], scalar=cw[:, ct, j:j + 1], in1=cvv,
                                                                 op0=ALU.mult, op1=ALU.add), reads=[f"raw{i3}", "cw_t", f"cv{i3}"], writes=[f"cv{i3}"])
                if i3 < 2:
                    silu_from(cv[:, i3, 0:TBk], [f"cv{i3}"], cv[:, i3, 0:TBk], f"cv{i3}", tmpA[i3][:, 0:TBk], f"tmpA{i3}", TBk)
                else:
                    silu_from(cv[:, i3, 0:TBk], [f"cv{i3}"], cvb[par][:, 2, 0:TBk], f"cvb{par}v", tmpA[i3][:, 0:TBk], f"tmpA{i3}", TBk)
            for i3 in range(2):
                src = cv[:, i3, 0:TBk]
                s.op("pool", lambda e: e.tensor_tensor(out=sqA[i3][:, 0:TBk], in0=src, in1=src, op=ALU.mult), reads=[f"cv{i3}"], writes=[f"sqA{i3}"])
                s.op("pe", lambda e: e.matmul(pb[7][:, i3 * 256:i3 * 256 + TBk], lhsT=ob2b[:], rhs=sqA[i3][:, 0:TBk], start=True, stop=True), reads=["ob2b", f"sqA{i3}"], writes=["BT"])
                s.op("act", lambda e: e.activation(out=tnA[i3][:, 0:TBk], in_=pb[7][:, i3 * 256:i3 * 256 + TBk], func=AF.Ln, bias=EPS), reads=["BT"], writes=[f"tnA{i3}"])
                s.op("act", lambda e: e.activation(out=tnA[i3][:, 0:TBk], in_=tnA[i3][:, 0:TBk], func=AF.Exp, scale=-0.5), reads=[f"tnA{i3}"], writes=[f"tnA{i3}"])
                if i3 == 0:
                    s.op("dve", lambda e: e.scalar_tensor_tensor(out=cvb[par][:, 0, 0:TBk], in0=src, scalar=0.125, in1=tnA[i3][:, 0:TBk], op0=ALU.mult, op1=ALU.mult),
                         reads=[f"cv{i3}", f"tnA{i3}"], writes=[f"cvb{par}q"])
                else:
                    s.op("dve", lambda e: e.tensor_tensor(out=cvb[par][:, 1, 0:TBk], in0=src, in1=tnA[i3][:, 0:TBk], op=ALU.mult), reads=[f"cv{i3}", f"tnA{i3}"], writes=[f"cvb{par}k"])
            pp, pk = inproj_fm(3 * NP + p, 0, INPROJ_ALT)
            s.op("act", lambda e: e.activation(out=za[par][:, 0:TBk], in_=pp, func=AF.Copy), reads=[pk], writes=[f"za{par}"])
            silu_from(za[par][:, 0:TBk], [f"za{par}"], za[par][:, 0:TBk], f"za{par}", tmpA[0][:, 0:TBk], "tmpA0", TBk)
            qn = cvb[par][:, 0, 0:TBk]; kn = cvb[par][:, 1, 0:TBk]; vs = cvb[par][:, 2, 0:TBk]
            ckq, ckk, ckv = f"cvb{par}q", f"cvb{par}k", f"cvb{par}v"
            s.op("dve", lambda e: e.tensor_tensor(out=v3(rhsD[:, 0:TBk]), in0=bc(gs[:, p, 0:nch, None], [128, nch, c]), in1=bc(U_s[:, None, 0:c], [128, nch, c]), op=ALU.mult),
                 reads=[f"gs{bpar}", "U_s"], writes=["rhsD"])
            s.op("pool", lambda e: e.tensor_tensor(out=v3(Dg[:, 0:TBk]), in0=bc(egc[:, p, 0:nch, None], [128, nch, c]), in1=bc(I_s[:, None, 0:c], [128, nch, c]), op=ALU.mult),
                 reads=[f"egc{bpar}", "I_s"], writes=["Dg"])
            for h in range(2):
                rs = slice(64 * h, 64 * h + c)
                s.op("pe", lambda e: e.matmul(pb[6][rs, 0:TBk], lhsT=Tri_s[rs, 0:c], rhs=rhsD[rs, 0:TBk], start=True, stop=True),
                     reads=["Tri_s", "rhsD"], writes=["B6"])
            s.op("act", lambda e: e.activation(out=Ee[:, 0:TBk], in_=pb[6][:, 0:TBk], func=AF.Exp), reads=["B6"], writes=["Ee"])
            s.op("pool", lambda e: e.tensor_tensor(out=v3(Dm[:, 0:TBk]), in0=v3(Ee[:, 0:TBk]), in1=bc(Mc_s[:, None, 0:c], [128, nch, c]), op=ALU.mult),
                 reads=["Ee", "Mc_s"], writes=["Dm"])
            s.op("pool", lambda e: e.tensor_tensor(out=v3(Ds[:, 0:TBk]), in0=v3(Ee[:, 0:TBk]), in1=bc(U_s[:, None, 0:c], [128, nch, c]), op=ALU.mult),
                 reads=["Ee", "U_s"], writes=["Ds"])
            for h in range(2):
                rs = slice(64 * h, 64 * h + c)
                s.op("pe", lambda e: e.matmul(pb[6][64 * h:64 * h + 64, 0:TBk], lhsT=ones[rs, 0:64], rhs=Dg[rs, 0:TBk], start=True, stop=True),
                     reads=["ones", "Dg"], writes=["B6"])
            s.op("act", lambda e: e.activation(out=EBs[par][:, 0:TBk], in_=pb[6][:, 0:TBk], func=AF.Copy), reads=["B6"], writes=[f"EBs{par}"])
            s.op("dve", lambda e: e.tensor_tensor(out=QeT[par][:, 0:TBk], in0=qn, in1=EBs[par][:, 0:TBk], op=ALU.mult), reads=[ckq, f"EBs{par}"], writes=[f"QeT{par}"])
            for ch in range(nch):
                cs = slice(ch * c, (ch + 1) * c)
                for h in range(2):
                    fs = slice(64 * h, 64 * h + 64); rs = slice(64 * h, 64 * h + c)
                    s.op("pe", lambda e: e.matmul(pb[7][rs, cs], lhsT=kn[fs, cs], rhs=kn[fs, cs], start=True, stop=True), reads=[ckk], writes=["BT"])
                    s.op("pe", lambda e: e.matmul(pb[7][rs, 256 + ch * c:256 + (ch + 1) * c], lhsT=qn[fs, cs], rhs=kn[fs, cs], start=True, stop=True),
                         reads=[ckq, ckk], writes=["BT"])
            s.op("dve", lambda e: e.tensor_tensor(out=v3(tmp[2][:, 0:TBk]), in0=v3(pb[7][:, 0:TBk]), in1=bc(nbs[:, p, 0:nch, None], [128, nch, c]), op=ALU.mult),
                 reads=["BT", f"nbs{bpar}"], writes=["tmp2"])
            s.op("pool", lambda e: e.tensor_tensor(out=PT0t[par][:, 0:TBk], in0=tmp[2][:, 0:TBk], in1=Ds[:, 0:TBk], op=ALU.mult), reads=["tmp2", "Ds"], writes=[f"PT0t{par}"])
            s.op("dve", lambda e: e.tensor_tensor(out=attn[:, 0:TBk], in0=pb[7][:, 256:256 + TBk], in1=Dm[:, 0:TBk], op=ALU.mult), reads=["BT", "Dm"], writes=["attn"])

            for ch in range(nch):
                cs = slice(ch * c, (ch + 1) * c)
                for h in range(2):
                    fs = slice(64 * h, 64 * h + 64); rs = slice(64 * h, 64 * h + c)
                    s.op("pe", lambda e: e.matmul(pb[6][rs, cs], lhsT=PT0t[par][rs, cs], rhs=I_sb[rs, 0:c], start=True, stop=True), reads=[f"PT0t{par}", "I_sb"], writes=["B6"])
                    s.op("pe", lambda e: e.matmul(pb[6][rs, 256 + ch * c:256 + (ch + 1) * c], lhsT=attn[rs, cs], rhs=I_sb[rs, 0:c], start=True, stop=True),
                         reads=["attn", "I_sb"], writes=["B6"])
                    s.op("pe", lambda e: e.matmul(pb[7][rs, ch * 64:(ch + 1) * 64], lhsT=kn[fs, cs], rhs=I_sb[fs, 0:64], start=True, stop=True), reads=[ckk, ckv, "I_sb"], writes=["BT"])
                    s.op("pe", lambda e: e.matmul(pb[7][rs, 256 + ch * 64:256 + (ch + 1) * 64], lhsT=vs[fs, cs], rhs=I_sb[fs, 0:64], start=True, stop=True),
                         reads=[ckk, ckv, "I_sb"], writes=["BT"])
            s.op("act", lambda e: e.activation(out=P0t[par][:, 0:TBk], in_=pb[6][:, 0:TBk], func=AF.Copy), reads=["B6"], writes=[f"P0t{par}"])
            s.op("dve", lambda e: e.tensor_tensor(out=v3(R0t[par][:, 0:TBk]), in0=v3(pb[6][:, 0:TBk]), in1=bc(I_s[:, None, 0:c], [128, nch, c]), op=ALU.add),
                 reads=["B6", "I_s"], writes=[f"R0t{par}"])
            s.op("act", lambda e: e.activation(out=attnT[par][:, 0:TBk], in_=pb[6][:, 256:256 + TBk], func=AF.Copy), reads=["B6"], writes=[f"attnT{par}"])
            k4 = pb[7][:, 0:nch * 64].rearrange("p (n d) -> p n d", d=64)
            v4 = pb[7][:, 256:256 + nch * 64].rearrange("p (n d) -> p n d", d=64)
            s.op("dve", lambda e: e.tensor_tensor(out=Kbe[par][:, 0:nch, :], in0=k4, in1=bc(bge[:, p, 0:nch, None], [128, nch, 64]), op=ALU.mult), reads=["BT", f"bge{bpar}"], writes=[f"Kbe{par}"])
            s.op("dve", lambda e: e.tensor_tensor(out=Kd[par][:, 0:nch, :], in0=k4, in1=bc(dk[:, p, 0:nch, None], [128, nch, 64]), op=ALU.mult), reads=["BT", f"dk{bpar}"], writes=[f"Kd{par}"])
            s.op("dve", lambda e: e.tensor_tensor(out=bV[par][:, 0:nch, :], in0=v4, in1=bc(bs[:, p, 0:nch, None], [128, nch, 64]), op=ALU.mult), reads=["BT", f"bs{bpar}"], writes=[f"bV{par}"])
        def core(p):
            par = p % 2
            Pc, kP = P0t[par], f"P0t{par}"
            PTc, kPT = PT0t[par], f"PT0t{par}"
            Rc, kR = R0t[par], f"R0t{par}"
            for lev in range(1, nlev + 2):
                do_pow = lev <= nlev
                need_P = lev < nlev
                do_R = lev >= 2
                nP, nkP = P[lev % 2], f"P{lev % 2}"
                nPT, nkPT = PT[lev % 2], f"PT{lev % 2}"
                nR, nkR = R[lev % 2], f"R{lev % 2}"
                for ch in range(nch):
                    cs = slice(ch * c, (ch + 1) * c)
                    for h in range(2):
                        rs = slice(64 * h, 64 * h + c)
                        if do_pow and need_P:
                            s.op("pe", lambda e: e.matmul(pb[5][rs, cs], lhsT=PTc[rs, cs], rhs=Pc[rs, cs], start=True, stop=True), reads=[kPT, kP], writes=["B5"])
                        if do_pow:
                            s.op("pe", lambda e: e.matmul(pb[5][rs, 256 + ch * c:256 + (ch + 1) * c], lhsT=Pc[rs, cs], rhs=PTc[rs, cs], start=True, stop=True),
                                 reads=[kPT, kP], writes=["B5"])
                        if do_R:
                            s.op("pe", lambda e: e.matmul(pb[4][rs, cs], lhsT=PTc[rs, cs], rhs=Rc[rs, cs], start=True, stop=True), reads=[kPT, kR], writes=["B4"])
                if do_pow and need_P:
                    s.op("act", lambda e: e.activation(out=nP[:, 0:TBk], in_=pb[5][:, 0:TBk], func=AF.Copy), reads=["B5"], writes=[nkP])
                if do_pow:
                    s.op("dve", lambda e: e.tensor_copy(out=nPT[:, 0:TBk], in_=pb[5][:, 256:256 + TBk]), reads=["B5"], writes=[nkPT])
                if do_R:
                    s.op("dve", lambda e: e.tensor_tensor(out=nR[:, 0:TBk], in0=pb[4][:, 0:TBk], in1=Rc[:, 0:TBk], op=ALU.add), reads=["B4", kR], writes=[nkR])
                    Rc, kR = nR, nkR
                if do_pow:
                    if need_P:
                        Pc, kP = nP, nkP
                    PTc, kPT = nPT, nkPT
            Rf, rk = Rc, kR
            for ch in range(nch):
                cs = slice(ch * c, (ch + 1) * c)
                for h in range(2):
                    rs = slice(64 * h, 64 * h + c)
                    s.op("pe", lambda e: e.matmul(pb[4][rs, 256 + ch * 64:256 + (ch + 1) * 64], lhsT=Rf[rs, cs], rhs=bV[par][rs, ch, :], start=True, stop=True),
                         reads=[rk, f"bV{par}"], writes=["B4"])
                    s.op("pe", lambda e: e.matmul(pb[5][64 * h:64 * h + 64, ch * c:(ch + 1) * c], lhsT=Kbe[par][rs, ch, :], rhs=Rf[rs, cs], start=True, stop=True),
                         reads=[rk, f"Kbe{par}"], writes=["B5"])
            s.op("act", lambda e: e.activation(out=u[:, 0:nch, :], in_=pb[4][:, 256:256 + nch * 64].rearrange("p (n d) -> p n d", d=64), func=AF.Copy), reads=["B4"], writes=["u"])
            s.op("dve", lambda e: e.tensor_copy(out=wT[:, 0:TBk], in_=pb[5][:, 0:TBk]), reads=["B5"], writes=["wT"])
            skey = f"Sg{p}"
            for ch in range(nch):
                cs = slice(ch * c, (ch + 1) * c)
                seg = ch // cps
                if ch % cps == 0:
                    if is_sample:
                        s.dma("sp", Sg[:, p, :], sg_d[seg, p], writes=[skey])
                        s.op("dve", lambda e: e.tensor_copy(out=Sgb[:, p, :], in_=Sg[:, p, :]), reads=[skey], writes=[skey + "b"])
                    elif first:
                        s.op("pool", lambda e: e.memset(Sg[:, p, :], 0.0), writes=[skey])
                        s.op("pool", lambda e: e.memset(Sgb[:, p, :], 0.0), writes=[skey + "b"])
                for h in range(2):
                    fs = slice(64 * h, 64 * h + 64); rs = slice(64 * h, 64 * h + c)
                    s.op("pe", lambda e: e.matmul(pb[1][rs, 0:64], lhsT=wT[fs, cs], rhs=Sgb[fs, p, :], start=True, stop=True), reads=["wT", skey + "b"], writes=["B1"])
                s.op("dve", lambda e: e.tensor_tensor(out=vn[:], in0=u[:, ch, :], in1=pb[1][:, 0:64], op=ALU.subtract), reads=["u", "B1"], writes=["vn"])
                for h in range(2):
                    fs = slice(64 * h, 64 * h + 64); rs = slice(64 * h, 64 * h + c)
                    s.op("pe", lambda e: e.matmul(pb[1][fs, 256 + ch * c:256 + (ch + 1) * c], lhsT=Sgb[fs, p, :], rhs=QeT[par][fs, cs], start=True, stop=False),
                         reads=[skey + "b", f"QeT{par}"], writes=["B1"])
                    s.op("pe", lambda e: e.matmul(pb[1][fs, 256 + ch * c:256 + (ch + 1) * c], lhsT=vn[rs, :], rhs=attnT[par][rs, cs], start=False, stop=True),
                         reads=["vn", f"attnT{par}"], writes=["B1"])
                for h in range(2):
                    fs = slice(64 * h, 64 * h + 64); rs = slice(64 * h, 64 * h + c)
                    s.op("pe", lambda e: e.matmul(pb[1][fs, 64:128], lhsT=Kd[par][rs, ch, :], rhs=vn[rs, :], start=True, stop=True), reads=[f"Kd{par}", "vn"], writes=["B1"])
                s.op("dve", lambda e: e.scalar_tensor_tensor(out=Sgb[:, p, :], in0=Sg[:, p, :], scalar=EBs[par][:, (ch + 1) * c - 1:(ch + 1) * c], in1=pb[1][:, 64:128],
                                                             op0=ALU.mult, op1=ALU.add), reads=[skey, f"EBs{par}", "B1"], writes=[skey + "b"])
                s.op("dve", lambda e: e.scalar_tensor_tensor(out=Sg[:, p, :], in0=Sg[:, p, :], scalar=EBs[par][:, (ch + 1) * c - 1:(ch + 1) * c], in1=pb[1][:, 64:128],
                                                             op0=ALU.mult, op1=ALU.add), reads=[skey, f"EBs{par}", "B1"], writes=[skey])
                if is_sample and (ch + 1) % cps == 0:
                    s.dma("sp", ngs[seg, p], Sg[:, p, :], reads=[skey], writes=[f"o_ngs{seg}_{p}"])
            s.op("act", lambda e: e.activation(out=oTf[:, 0:TBk], in_=pb[1][:, 256:256 + TBk], func=AF.Copy), reads=["B1"], writes=["oTf"])
            s.op("pool", lambda e: e.tensor_tensor(out=sqb[:, 0:TBk], in0=oTf[:, 0:TBk], in1=oTf[:, 0:TBk], op=ALU.mult), reads=["oTf"], writes=["sqb"])
            s.op("pe", lambda e: e.matmul(pb[3][:, 0:TBk], lhsT=ob2b[:], rhs=sqb[:, 0:TBk], start=True, stop=True), reads=["ob2b", "sqb"], writes=["B3"])
            s.op("act", lambda e: e.activation(out=tmp[1][:, 0:TBk], in_=pb[3][:, 0:TBk], func=AF.Ln, scale=1.0 / 64, bias=EPS), reads=["B3"], writes=["tmp1"])
            s.op("act", lambda e: e.activation(out=tmp[1][:, 0:TBk], in_=tmp[1][:, 0:TBk], func=AF.Exp, scale=-0.5), reads=["tmp1"], writes=["tmp1"])
            s.op("dve", lambda e: e.tensor_tensor(out=oTf[:, 0:TBk], in0=oTf[:, 0:TBk], in1=tmp[1][:, 0:TBk], op=ALU.mult), reads=["oTf", "tmp1"], writes=["oTf"])
            s.op("dve", lambda e: e.scalar_tensor_tensor(out=oOwn[bpar][:, p, 0:TBk], in0=oTf[:, 0:TBk], scalar=gnw[:, 0:1], in1=za[par][:, 0:TBk], op0=ALU.mult, op1=ALU.mult),
                 reads=["oTf", "gnw_t", f"za{par}"], writes=[f"oOwn{bpar}_{p}"])

        def ph0_m():
            s.op("pool", lambda e: e.memset(smask[:, 0:TBk], 1.0), writes=[f"smask{bpar}"])
            s.op("pool", lambda e: e.memset(v3(smask[:, 0:TBk])[:, :, 0:1], 0.0), reads=[f"smask{bpar}"], writes=[f"smask{bpar}"])

        def hg(h):
            pp, pk = inproj_fm(4 * NP + h, 1)
            s.op("act", lambda e: e.activation(out=qb[:, 0:TBk], in_=pp, func=AF.Copy), reads=[pk], writes=["qb"])
            silu_from(qb[:, 0:TBk], ["qb"], qb[:, 0:TBk], "qb", bl[:, 0:TBk], "bl", TBk)
            pp, pk = inproj_fm(4 * NP + 2 * HB + h, 1, 3 if HG_ALT else False)
            s.op("act", lambda e: e.activation(out=zb[:, 0:TBk], in_=pp, func=AF.Copy), reads=[pk], writes=["zb"])
            silu_from(zb[:, 0:TBk], ["zb"], zb[:, 0:TBk], "zb", bl[:, 0:TBk], "bl", TBk)
            pp, pk = inproj_fm(4 * NP + HB + h, 1)
            s.op("act", lambda e: e.activation(out=ff[:, 0:TBk], in_=pp, func=AF.Exp, scale=-1.0), reads=[pk], writes=["ff"])
            s.op("act", lambda e: e.activation(out=ff[:, 0:TBk], in_=ff[:, 0:TBk], func=AF.Ln, bias=1.0), reads=["ff"], writes=["ff"])
            s.op("act", lambda e: e.activation(out=ff[:, 0:TBk], in_=ff[:, 0:TBk], func=AF.Exp, scale=-1.0), reads=["ff"], writes=["ff"])
            s.op("dve", lambda e: e.tensor_scalar(out=ff[:, 0:TBk], in0=ff[:, 0:TBk], scalar1=oml[:, h:h + 1], scalar2=lb[:, h:h + 1], op0=ALU.mult, op1=ALU.add),
                 reads=["ff", "oml", "lb"], writes=["ff"])
            s.op("act", lambda e: e.activation(out=lf[:, 0:TBk], in_=ff[:, 0:TBk], func=AF.Ln), reads=["ff"], writes=["lf"])
            s.op("dve", lambda e: e.tensor_scalar(out=kb[:, 0:TBk], in0=ff[:, 0:TBk], scalar1=-1.0, scalar2=1.0, op0=ALU.mult, op1=ALU.add), reads=["ff"], writes=["kb"])
            s.op("dve", lambda e: e.tensor_tensor_scan(out=bb[:, 0:TBk], data0=smask[:, 0:TBk], data1=lf[:, 0:TBk], initial=0.0, op0=ALU.mult, op1=ALU.add),
                 reads=[f"smask{bpar}", "lf"], writes=["bb"])
            b3 = v3(bb[:, 0:TBk])
            s.op("pool", lambda e: e.tensor_tensor(out=v3(bl[:, 0:TBk]), in0=b3, in1=bc(b3[:, :, c - 1:c], [128, nch, c]), op=ALU.subtract), reads=["bb"], writes=["bl"])
            s.op("act", lambda e: e.activation(out=Qef[:, 0:TBk], in_=bb[:, 0:TBk], func=AF.Exp), reads=["bb"], writes=["Qef"])
            s.op("act", lambda e: e.activation(out=Qxf[:, 0:TBk], in_=bl[:, 0:TBk], func=AF.Exp), reads=["bl"], writes=["Qxf"])
            s.op("act", lambda e: e.activation(out=Kdf[:, 0:TBk], in_=bl[:, 0:TBk], func=AF.Exp, scale=-1.0), reads=["bl"], writes=["Kdf"])
            s.op("act", lambda e: e.activation(out=ebl[:, 0:nch], in_=b3[:, :, c - 1], func=AF.Exp), reads=["bb"], writes=["ebl"])
            s.op("dve", lambda e: e.tensor_tensor(out=Qe[:, 0:TBk], in0=Qef[:, 0:TBk], in1=qb[:, 0:TBk], op=ALU.mult), reads=["Qef", "qb"], writes=["Qe"])
            s.op("pool", lambda e: e.tensor_tensor(out=Qx[:, 0:TBk], in0=Qxf[:, 0:TBk], in1=qb[:, 0:TBk], op=ALU.mult), reads=["Qxf", "qb"], writes=["Qx"])
            s.op("dve", lambda e: e.tensor_tensor(out=Kdh[:, 0:TBk], in0=Kdf[:, 0:TBk], in1=kb[:, 0:TBk], op=ALU.mult), reads=["Kdf", "kb"], writes=["Kdh"])
            for ch in range(nch):
                cs = slice(ch * c, (ch + 1) * c)
                outp = pb[2][0:c, 128:256]
                for k in range(8):
                    s.op("pe", lambda e: e.matmul(outp, lhsT=hT[:, k, cs], rhs=Wb[:, k, C_HI + 128 * h:C_HI + 128 * (h + 1)], start=(k == 0), stop=(k == 7)),
                         reads=["Wb", f"hT{bpar}"], writes=["B2"])
                s.op("pe", lambda e: e.matmul(pb[2][0:c, 256:384], lhsT=Kdh[:, cs], rhs=identb[:], start=True, stop=True), reads=["Kdh", "identb"], writes=["B2"])
                s.op("pe", lambda e: e.matmul(pb[2][0:c, 384:384 + c], lhsT=Kdh[:, cs], rhs=Qx[:, cs], start=True, stop=True), reads=["Kdh", "Qx"], writes=["B2"])
                s.op("act", lambda e: e.activation(out=vtok[0:c, ch, :], in_=outp, func=AF.Copy), reads=["B2"], writes=["vtok"])
                s.op("dve", lambda e: e.tensor_copy(out=Kdt[0:c, ch, :], in_=pb[2][0:c, 256:384]), reads=["B2"], writes=["Kdt"])
                s.op("dve", lambda e: e.tensor_tensor(out=aTh[0:c, cs], in0=pb[2][0:c, 384:384 + c], in1=Tri_s[0:c, 0:c], op=ALU.mult),
                     reads=["B2", "Tri_s"], writes=["aTh"])
            skey = f"Sh{h}"
            for ch in range(nch):
                cs = slice(ch * c, (ch + 1) * c)
                seg = ch // cps
                if ch % cps == 0:
                    if is_sample:
                        s.dma("sp", Sh[:, h, :], sh_d[seg, h], writes=[skey])
                        s.op("dve", lambda e: e.tensor_copy(out=Shb[:, h, :], in_=Sh[:, h, :]), reads=[skey], writes=[skey + "b"])
                    elif first:
                        s.op("pool", lambda e: e.memset(Sh[:, h, :], 0.0), writes=[skey])
                        s.op("pool", lambda e: e.memset(Shb[:, h, :], 0.0), writes=[skey + "b"])
                s.op("pe", lambda e: e.matmul(pb[3][:, 256 + ch * c:256 + (ch + 1) * c], lhsT=Shb[:, h, :], rhs=Qe[:, cs], start=True, stop=False), reads=[skey + "b", "Qe"], writes=["B3"])
                s.op("pe", lambda e: e.matmul(pb[3][:, 256 + ch * c:256 + (ch + 1) * c], lhsT=vtok[0:c, ch, :], rhs=aTh[0:c, cs], start=False, stop=True),
                     reads=["vtok", "aTh"], writes=["B3"])
                s.op("pe", lambda e: e.matmul(pb[1][:, 128:256], lhsT=Kdt[0:c, ch, :], rhs=vtok[0:c, ch, :], start=True, stop=True), reads=["Kdt", "vtok"], writes=["B1"])
                s.op("dve", lambda e: e.scalar_tensor_tensor(out=Shb[:, h, :], in0=Sh[:, h, :], scalar=ebl[:, ch:ch + 1], in1=pb[1][:, 128:256], op0=ALU.mult, op1=ALU.add),
                     reads=[skey, "ebl", "B1"], writes=[skey + "b"])
                s.op("dve", lambda e: e.scalar_tensor_tensor(out=Sh[:, h, :], in0=Sh[:, h, :], scalar=ebl[:, ch:ch + 1], in1=pb[1][:, 128:256], op0=ALU.mult, op1=ALU.add),
                     reads=[skey, "ebl", "B1"], writes=[skey])
                if is_sample and (ch + 1) % cps == 0:
                    s.dma("sp", nhs[seg, h], Sh[:, h, :], reads=[skey], writes=[f"o_nhs{seg}_{h}"])
            s.op("act", lambda e: e.activation(out=oTfH[:, 0:TBk], in_=pb[3][:, 256:256 + TBk], func=AF.Copy), reads=["B3"], writes=["oTfH"])
            s.op("pool", lambda e: e.tensor_tensor(out=sqbH[:, 0:TBk], in0=oTfH[:, 0:TBk], in1=oTfH[:, 0:TBk], op=ALU.mult), reads=["oTfH"], writes=["sqbH"])
            s.op("pe", lambda e: e.matmul(pb[3][:, 256:256 + TBk], lhsT=onesb[:], rhs=sqbH[:, 0:TBk], start=True, stop=True), reads=["onesb", "sqbH"], writes=["B3"])
            s.op("act", lambda e: e.activation(out=tmpH[1][:, 0:TBk], in_=pb[3][:, 256:256 + TBk], func=AF.Ln, scale=1.0 / 128, bias=EPS), reads=["B3"], writes=["tmpH1"])
            s.op("act", lambda e: e.activation(out=tmpH[1][:, 0:TBk], in_=tmpH[1][:, 0:TBk], func=AF.Exp, scale=-0.5), reads=["tmpH1"], writes=["tmpH1"])
            s.op("dve", lambda e: e.tensor_tensor(out=oTfH[:, 0:TBk], in0=oTfH[:, 0:TBk], in1=tmpH[1][:, 0:TBk], op=ALU.mult), reads=["oTfH", "tmpH1"], writes=["oTfH"])
            s.op("dve", lambda e: e.scalar_tensor_tensor(out=oOwn[bpar][:, NP + h, 0:TBk], in0=oTfH[:, 0:TBk], scalar=hnw[:, 0:1], in1=zb[:, 0:TBk], op0=ALU.mult, op1=ALU.mult),
                 reads=["oTfH", "hnw_t", "zb"], writes=[f"oOwn{bpar}_{NP + h}"])

        def xchg(_=None):
            xs_, xd_ = (xsrc_s, xdst_s) if is_sample else (xsrc[bpar], xdst[bpar])
            s.dma("sp", xs_.rearrange("(t p) n -> p t n", p=128), oOwn[bpar][:, :, 0:TBk], reads=[f"oOwn{bpar}_{i}" for i in range(NL)], writes=[f"xsrc{bpar}"])
            s.coll([xs_], [xd_], reads=[f"xsrc{bpar}"], writes=[f"xdst{bpar}"])
            s.dma("sp", oTn[bpar][:, :, 0:TBk], xd_.rearrange("(t p) n -> p t n", p=128), reads=[f"xdst{bpar}"], writes=[f"oTn{bpar}_{k}" for k in range(2 * NL)])

        def outproj(_=None):
            for tt in range(ntt):
                X = xt[bpar * 2 + tt]
                for half in range(2):
                    bank = pb[6] if half == 0 else pb[7]
                    bk = "B6" if half == 0 else "BT"
                    for k in range(8):
                        s.op("pe", lambda e: e.matmul(bank[0:TT, :], lhsT=oTn[bpar][:, k, tt * TT:(tt + 1) * TT], rhs=WOb[:, k, half * 512:(half + 1) * 512], start=(k == 0), stop=(k == 7)),
                             reads=[f"oTn{bpar}_{k}", "WOb"], writes=[bk])
                    s.op("dve", lambda e: e.tensor_tensor(out=yo[0:TT, half * 512:(half + 1) * 512], in0=bank[0:TT, :], in1=X[0:TT, half * 512:(half + 1) * 512], op=ALU.add),
                         reads=[bk, X.name], writes=["yo"])
                s.op("act", lambda e: e.activation(out=sqjO[0:TT, :], in_=yo[0:TT, :], func=AF.Square, accum_out=ssO[0:TT, :]), reads=["yo"], writes=["sqjO", "ssO"])
                s.op("act", lambda e: e.activation(out=rrO[0:TT, :], in_=ssO[0:TT, :], func=AF.Ln, scale=1.0 / D, bias=EPS), reads=["ssO"], writes=["rrO"])
                s.op("act", lambda e: e.activation(out=rrO[0:TT, :], in_=rrO[0:TT, :], func=AF.Exp, scale=-0.5), reads=["rrO"], writes=["rrO"])
                s.op("dve", lambda e: e.scalar_tensor_tensor(out=yo2[0:TT, :], in0=yo[0:TT, :], scalar=rrO[0:TT, :], in1=fnw[0:TT, :], op0=ALU.mult, op1=ALU.mult),
                     reads=["yo", "rrO", "fnw_t"], writes=["yo2"])
                s.dma("sp", y_dst[t0 + tt * TT: t0 + (tt + 1) * TT, :], yo2[0:TT, :], reads=["yo2"], writes=[f"o_y{id(y_dst)}"], slot="yout")

        def record(fn, arg=None):
            lst = []
            s.rec = lst
            fn(arg)
            s.rec = None
            return lst
        return dict(p0=lambda: record(phase0), op=lambda: record(outproj), xc=lambda: record(xchg),
                    fr=lambda p: record(front, p), co=lambda p: record(core, p), hg=lambda h: record(hg, h))

    def emit_blocks(blks, extra_nodes=(), extra_deps=None):
        n = len(blks)
        extra_deps = extra_deps or {}
        nodes = []
        for b, P in enumerate(blks):
            N = lambda kind, bb: f"{kind}_{bb}"
            X = lambda name: extra_deps.get(name, [])
            nodes.append((N("P0", b), P["p0"](), [N("P0", b - 1), N("FR1", b - 2), N("HG1", b - 2), N("OP", b - 2)] + X(N("P0", b)), b * 10 + 0))
            nodes.append((N("FR0", b), P["fr"](0), [N("P0", b), N("FR1", b - 1), N("CO0", b - 1), N("OP", b - 2)] + X(N("FR0", b)), b * 10 + 1))
            nodes.append((N("HG0", b), P["hg"](0), [N("P0", b), N("HG1", b - 1), N("XC", b - 2)] + X(N("HG0", b)), b * 10 + 2))
            nodes.append((N("CO0", b), P["co"](0), [N("FR0", b), N("CO1", b - 1), N("XC", b - 2)] + X(N("CO0", b)), b * 10 + 3))
            nodes.append((N("FR1", b), P["fr"](1), [N("FR0", b), N("CO1", b - 1)], b * 10 + 4))
            nodes.append((N("HG1", b), P["hg"](1), [N("HG0", b)], b * 10 + 5))
            nodes.append((N("CO1", b), P["co"](1), [N("FR1", b), N("CO0", b)], b * 10 + 6))
            nodes.append((N("XC", b), P["xc"](), [N("CO1", b), N("HG1", b), N("XC", b - 1), N("OP", b - 2)], b * 10 + 7))
            nodes.append((N("OP", b), P["op"](), [N("XC", b), N("OP", b - 1), N("FR1", b + 1) if b + 1 < n else N("FR1", b)], b * 10 + 18))
        nodes += list(extra_nodes)
        s.dag_emit(nodes)

    assert NP == 2 and HB == 2
    nblk = T // TB
    blks = [block(xp, yp, b * TB, 1, TB, 64, b == 0, False, b % 2) for b in range(nblk)]
    sblk = block(xs, ys, 0, 4, 16, 16, True, True, nblk % 2)
    sw = []
    s.rec = sw
    s.dma("sp", ncp[:, :, 0, :], halo[:, :, 0, :], reads=["halo"], writes=["o_ncp"])
    s.dma("sp", halo[:], sc_d, reads=["halo"], writes=["halo"])
    s.rec = None
    so = []
    s.rec = so
    for p in range(NP):
        s.dma("sp", ngp[0, p], Sg[:, p, :], reads=[f"Sg{p}"], writes=[f"o_ngp{p}"])
    for h in range(HB):
        s.dma("sp", nhp[0, h], Sh[:, h, :], reads=[f"Sh{h}"], writes=[f"o_nhp{h}"])
    s.rec = None
    L = nblk - 1
    extra_nodes = [("SW", sw, [f"FR1_{L}"], L * 10 + 8), ("SO", so, [f"CO1_{L}", f"HG1_{L}"], L * 10 + 9)]
    extra_deps = {f"FR0_{nblk}": ["SW"], f"CO0_{nblk}": ["SO"], f"HG0_{nblk}": ["SO"]}
    emit_blocks(blks + [sblk], extra_nodes, extra_deps)
    s.dma("sp", ncs, halo[:], reads=["halo"], writes=["o_ncs"])
    s.finish("sp")
    return nc, s


def _perm(hh):
    r = lambda base, w: np.arange(base + hh * w, base + (hh + 1) * w)
    return np.concatenate([r(0, 256), r(512, 256), r(1024, 256), r(1536, 256),
                           r(2064, 256), r(2576, 256), r(3600, 256), r(3088, 256),
                           r(2048, 4), r(2056, 4)])


def _chan(hh):
    r = lambda base: np.arange(base + hh * 256, base + (hh + 1) * 256)
    return np.concatenate([r(0), r(512), r(1024)])


_WOUT_ROWS = np.concatenate([np.concatenate([np.arange(r * 256, (r + 1) * 256), np.arange(512 + r * 256, 512 + (r + 1) * 256)]) for r in range(2)])


def _core_inputs(c, inp):
    b, hh = c // 2, c % 2
    f = lambda a: np.ascontiguousarray(a, dtype=np.float32)
    ch = _chan(hh)
    cwv = inp["conv_w"][0][:, ch]
    scv = inp["state_conv"][0][4 * b:4 * b + 4][:, :, ch]
    hs = slice(4 * hh, 4 * hh + 4)
    return {
        "xp": f(inp["x_prompt"][b]),
        "xs": f(inp["x_sample"][4 * b:4 * b + 4].reshape(64, D)),
        "w_in": f(inp["w_in"][0][:, _perm(hh)]),
        "w_out": f(inp["w_out"][0][_WOUT_ROWS]),
        "nw": f(inp["norm_w"][0].reshape(8, 128).T),
        "cw": f(cwv.reshape(4, 3 * NP, 128).transpose(2, 1, 0)),
        "alog": f(np.broadcast_to(inp["gdn_A_log"][0][None, hs], (128, HA))),
        "dtb": f(np.broadcast_to(inp["gdn_dt_bias"][0][None, hs], (128, HA))),
        "gnw": f(np.tile(inp["gdn_norm_w"][0], 2).reshape(128, 1)),
        "hnw": f(inp["hgrn_norm_w"][0].reshape(128, 1)),
        "lbl": f(inp["hgrn_lb_logits"][:, hh * 256:(hh + 1) * 256].reshape(2, HB, 128).transpose(2, 1, 0)),
        "fnw": f(np.broadcast_to(inp["final_norm_w"][None, :], (128, D))),
        "sc": f(scv.reshape(4, 3, 3 * NP, 128).transpose(3, 2, 0, 1)),
        "sg": f(inp["state_gdn"][0][4 * b:4 * b + 4][:, hs].reshape(4, NP, 128, 64)),
        "sh": f(inp["state_hgrn"][0][4 * b:4 * b + 4][:, 2 * hh:2 * hh + 2]),
    }


_CACHE = {}


def kernel(**inputs):
    inp = {k: np.asarray(v) for k, v in inputs.items()}
    Bp, T, _ = inp["x_prompt"].shape
    assert Bp == 4 and inp["x_sample"].shape[:2] == (16, 16)
    if T not in _CACHE:
        _CACHE[T] = build(T)[0]
    nc = _CACHE[T]
    in_maps = [_core_inputs(c, inp) for c in range(8)]
    res = run_bass_kernel_spmd(nc, in_maps, core_ids=list(range(8)))
    r = res.results
    y_prompt = np.stack([r[2 * b]["yp"] for b in range(4)]).astype(np.float32)
    y_sample = np.concatenate([r[2 * b]["ys"].reshape(4, 16, D) for b in range(4)]).astype(np.float32)
    ncp_ = np.zeros((1, 4, 3, 1536), np.float32); ncs_ = np.zeros((1, 16, 3, 1536), np.float32)
    ngp_ = np.zeros((1, 4, 8, 64, 64), np.float32); ngs_ = np.zeros((1, 16, 8, 64, 64), np.float32)
    nhp_ = np.zeros((1, 4, 4, 128, 128), np.float32); nhs_ = np.zeros((1, 16, 4, 128, 128), np.float32)
    cvt = lambda a: a.transpose(2, 3, 1, 0).reshape(a.shape[2], 3, 3 * NP * 128)
    for c in range(8):
        b, hh = c // 2, c % 2
        ch = _chan(hh)
        ncp_[0, b][:, ch] = cvt(r[c]["ncp"])[0]
        ncs_[0, 4 * b:4 * b + 4][:, :, ch] = cvt(r[c]["ncs"])
        ngp_[0, b, 4 * hh:4 * hh + 4] = r[c]["ngp"].reshape(HA, 64, 64)
        ngs_[0, 4 * b:4 * b + 4, 4 * hh:4 * hh + 4] = r[c]["ngs"].reshape(4, HA, 64, 64)
        nhp_[0, b, 2 * hh:2 * hh + 2] = r[c]["nhp"].reshape(HB, 128, 128)
        nhs_[0, 4 * b:4 * b + 4, 2 * hh:2 * hh + 2] = r[c]["nhs"].reshape(4, HB, 128, 128)
    return (y_prompt, y_sample, ncp_, ngp_, nhp_, ncs_, ngs_, nhs_)
```

```python
import numpy as np
import concourse.bass as bass
import concourse.mybir as mybir
from concourse.bass_utils import run_bass_kernel_spmd

F32 = mybir.dt.float32
BF16 = mybir.dt.bfloat16
AF = mybir.ActivationFunctionType
ALU = mybir.AluOpType

D = 1024
HA, HB = 4, 2
NP = HA // 2
NT = 4 * NP + 3 * HB
C_HI = NT * 128
C_G = C_HI + HB * 128
NCOL = C_G + 2 * HA
EPS = 1e-6
RG = [[0, 1], [2, 3], [4, 5], [6, 7]]


SAME_ENGINE_WAIT = True
INPROJ_ALT = True
HG_ALT = True
SCHED_MODE = 0
SCHED_DELTA = 300.0
SCHED_WIN = 24


class _Proxy:
    def __getattr__(self, name):
        return lambda *a, **k: (name, a, k)


_PROXY = _Proxy()
PSUM_KEYS = {"B0", "B1", "B2", "B3", "B4", "B5", "B6", "BT"}


class Sched:
    def __init__(self, nc):
        self.nc = nc
        self.eng = {"pe": nc.tensor, "dve": nc.vector, "act": nc.scalar, "pool": nc.gpsimd, "sp": nc.sync}
        self.sem = {k: nc.alloc_semaphore(name=f"s_{k}") for k in self.eng}
        self.cnt = {k: 0 for k in self.eng}
        self.seen = {k: {} for k in self.eng}
        self.last_w = {}
        self.readers = {}
        self.dma_sems = {}
        self.n_wait = 0
        self.n_ops = 0
        self.rec = None

    def coll(self, ins, outs, reads=(), writes=()):
        if self.rec is not None:
            self.rec.append(("coll", "pool", ins, outs, tuple(reads), tuple(writes)))
            return None
        self._deps("pool", reads, writes)
        if "cc" not in self.dma_sems:
            self.dma_sems["cc"] = [self.nc.alloc_semaphore(name="cc_sem"), 0]
        ent = self.dma_sems["cc"]
        ent[1] += 1
        self.nc.gpsimd.collective_compute("AllGather", ALU.bypass, replica_groups=RG, ins=ins, outs=outs).then_inc(ent[0], 1)
        tok = ("cc", ent[0], ent[1])
        self._commit(tok, reads, writes)
        self.n_ops += 1
        return tok

    def emit(self, r):
        if r[0] == "coll":
            self.coll(r[2], r[3], r[4], r[5])
            return
        if r[0] == "op":
            _, e, call, reads, writes = r
            self.op(e, lambda eng: getattr(eng, call[0])(*call[1], **call[2]), reads, writes)
        else:
            _, q, out, in_, reads, writes, slot = r
            self.dma(q, out, in_, reads, writes, slot)

    def _cost(self, r):
        if r[0] == "dma":
            return 2500.0
        if r[0] == "coll":
            return 30000.0
        _, e, call, reads, writes = r
        name, args, kw = call
        def nfree(ap):
            try:
                sh = list(ap.shape)
                n = 1
                for d in sh[1:]:
                    n *= int(d)
                return n
            except Exception:
                return 256
        if e == "pe":
            ap = kw.get("rhs", None) if name == "matmul" else kw.get("in_", None)
            n = nfree(ap) if ap is not None else 64
            c = 32.0 + 0.4 * n
            try:
                if name == "matmul" and kw["rhs"].dtype == F32:
                    c *= 3.0
            except Exception:
                pass
            return c
        ap = kw.get("out", None)
        n = nfree(ap) if ap is not None else 256
        if e == "dve":
            return 110.0 + 1.0 * n
        if e == "act":
            return 170.0 + 0.9 * n
        return 110.0 + 1.8 * n

    def _est_start(self, r):
        if r[0] in ("dma", "coll"):
            e, reads, writes = r[1], r[4], r[5]
        else:
            e, reads, writes = r[1], r[3], r[4]
        m = self.model
        t = m["eng"].get(e, 0.0)
        ex = [k for k in reads if k in PSUM_KEYS]
        for k in reads:
            w = m["w"].get(k)
            if w is not None:
                t = max(t, w[0] + ((0.0 if e == "pe" else 150.0) if w[1] == e else 230.0))
        for k in list(writes) + ex:
            w = m["w"].get(k)
            if w is not None:
                t = max(t, w[0] + ((0.0 if e == "pe" else 150.0) if w[1] == e else 230.0))
            for (tt, ee) in m["r"].get(k, {}).values():
                t = max(t, tt + ((0.0 if e == "pe" else 150.0) if ee == e else 230.0))
        return t

    def _model_commit(self, r, t0):
        if r[0] in ("dma", "coll"):
            e, reads, writes = r[1], r[4], r[5]
            eng_busy = 100.0
        else:
            e, reads, writes = r[1], r[3], r[4]
            eng_busy = None
        m = self.model
        c = self._cost(r)
        t1 = t0 + c
        m["eng"][e] = t0 + (eng_busy if eng_busy is not None else c)
        ex = [k for k in reads if k in PSUM_KEYS]
        who = e if r[0] == "op" else "dma"
        for k in reads:
            m["r"].setdefault(k, {})[who] = (t1, who)
        for k in list(writes) + ex:
            m["w"][k] = (t1, who)
            m["r"][k] = {}
        m["t"] = max(m.get("t", 0.0), t1)

    def dag_emit(self, nodes):
        if not hasattr(self, "model"):
            self.model = {"eng": {}, "w": {}, "r": {}, "t": 0.0}
        names = {n[0] for n in nodes}
        units, deps, prio, preds = {}, {}, {}, {}
        def rw(r):
            if r[0] in ("dma", "coll"):
                reads, writes = r[4], r[5]
            else:
                reads, writes = r[3], r[4]
            ex = [k for k in reads if k in PSUM_KEYS]
            return list(reads), list(writes) + ex
        for (name, ops, dp, pr) in nodes:
            u = []
            for r in ops:
                glued = (r[0] == "op" and r[2][0] == "matmul" and r[2][2].get("start") is False)
                if glued and u:
                    u[-1].append(r)
                else:
                    u.append([r])
            units[name] = u
            deps[name] = {d for d in dp if d in names}
            prio[name] = pr
            lw, rd, pl = {}, {}, []
            for j, unit in enumerate(u):
                p = set()
                R, W = [], []
                for r in unit:
                    a_, b_ = rw(r)
                    R += a_; W += b_
                for k in R:
                    if k in lw:
                        p.add(lw[k])
                for k in W:
                    if k in lw:
                        p.add(lw[k])
                    p |= rd.get(k, set())
                p.discard(j)
                pl.append(p)
                for k in R:
                    rd.setdefault(k, set()).add(j)
                for k in W:
                    lw[k] = j
                    rd[k] = set()
            preds[name] = pl
        emitted = {n: [False] * len(units[n]) for n in units}
        nleft = {n: len(units[n]) for n in units}
        lo = {n: 0 for n in units}
        done = {n for n in units if not units[n]}
        waiting = [n for n in units if n not in done]
        active = []
        def refresh():
            nonlocal waiting
            still = []
            for n in waiting:
                if deps[n] <= done:
                    active.append(n)
                else:
                    still.append(n)
            waiting = still
        refresh()
        WIN = SCHED_WIN
        while active:
            best, bsel = None, None
            for n in active:
                em, pl, u = emitted[n], preds[n], units[n]
                j = lo[n]
                seen = 0
                while j < len(u) and seen < WIN:
                    if not em[j]:
                        seen += 1
                        if all(em[q] for q in pl[j]):
                            t = self._est_start(u[j][0])
                            key = (t, prio[n], j)
                            if best is None or key < best:
                                best, bsel = key, (n, j)
                    j += 1
            n, j = bsel
            for r in units[n][j]:
                t0 = self._est_start(r)
                self._model_commit(r, t0)
                self.emit(r)
            emitted[n][j] = True
            nleft[n] -= 1
            while lo[n] < len(units[n]) and emitted[n][lo[n]]:
                lo[n] += 1
            if nleft[n] == 0:
                active.remove(n)
                done.add(n)
                refresh()
        assert not waiting, ("DAG deadlock", waiting[:5])

    def merge_emit(self, streams):
        if not hasattr(self, "model"):
            self.model = {"eng": {}, "w": {}, "r": {}, "t": 0.0}
        units = []
        for st in streams:
            u = []
            for r in st:
                glued = (r[0] == "op" and r[2][0] == "matmul" and r[2][2].get("start") is False)
                if glued and u:
                    u[-1].append(r)
                else:
                    u.append([r])
            units.append(u)
        pos = [0] * len(units)
        while True:
            best, bi = None, -1
            for i, u in enumerate(units):
                if pos[i] < len(u):
                    t = self._est_start(u[pos[i]][0])
                    key = (t, -(len(u) - pos[i]))
                    if best is None or key < best:
                        best, bi = key, i
            if bi < 0:
                break
            for r in units[bi][pos[bi]]:
                t0 = self._est_start(r)
                self._model_commit(r, t0)
                self.emit(r)
            pos[bi] += 1


    def _wait(self, e, tok):
        name, sem, val = tok
        if name == "pe" and e == "pe":
            return
        if name == e and not SAME_ENGINE_WAIT:
            return
        if self.seen[e].get(name, 0) >= val:
            return
        self.eng[e].wait_ge(sem, val)
        self.seen[e][name] = val
        self.n_wait += 1

    def _deps(self, e, reads, writes):
        for k in reads:
            t = self.last_w.get(k)
            if t is not None:
                self._wait(e, t)
        for k in writes:
            t = self.last_w.get(k)
            if t is not None:
                self._wait(e, t)
            for t in self.readers.get(k, {}).values():
                self._wait(e, t)

    def _commit(self, tok, reads, writes):
        for k in reads:
            self.readers.setdefault(k, {})[tok[0]] = tok
        for k in writes:
            self.last_w[k] = tok
            self.readers[k] = {}

    def op(self, e, fn, reads=(), writes=()):
        if self.rec is not None:
            self.rec.append(("op", e, fn(_PROXY), tuple(reads), tuple(writes)))
            return None
        ex = [k for k in reads if k in PSUM_KEYS]
        if ex:
            writes = list(writes) + ex
        self._deps(e, reads, writes)
        ins = fn(self.eng[e])
        self.cnt[e] += 1
        ins.then_inc(self.sem[e], 1)
        tok = (e, self.sem[e], self.cnt[e])
        self._commit(tok, reads, writes)
        self.n_ops += 1
        return tok

    def dma(self, q, out, in_, reads=(), writes=(), slot=None):
        if self.rec is not None:
            self.rec.append(("dma", q, out, in_, tuple(reads), tuple(writes), slot))
            return None
        self._deps(q, reads, writes)
        slot = slot or (writes[0] if writes else reads[0])
        sname = f"d_{slot}"
        if sname not in self.dma_sems:
            self.dma_sems[sname] = [self.nc.alloc_semaphore(name=sname), 0]
        ent = self.dma_sems[sname]
        ent[1] += 16
        self.eng[q].dma_start(out=out, in_=in_).then_inc(ent[0], 16)
        tok = (sname, ent[0], ent[1])
        self._commit(tok, reads, writes)
        self.n_ops += 1
        return tok

    def finish(self, e="sp"):
        for k, t in list(self.last_w.items()):
            self._wait(e, t)


def bc(ap, shape):
    return ap.to_broadcast(list(shape))


def build(T, TB=256):
    nc = bass.Bass("TRN2", target_bir_lowering=False)
    s = Sched(nc)
    dt_in = lambda n, sh: nc.dram_tensor(n, list(sh), F32, kind="ExternalInput").ap()
    dt_out = lambda n, sh: nc.dram_tensor(n, list(sh), F32, kind="ExternalOutput").ap()
    xp = dt_in("xp", [T, D]); xs = dt_in("xs", [64, D])
    w_in = dt_in("w_in", [D, NCOL]); w_out = dt_in("w_out", [D, D])
    nw_d = dt_in("nw", [128, 8]); cw_d = dt_in("cw", [128, 3 * NP, 4])
    alog_d = dt_in("alog", [128, HA]); dtb_d = dt_in("dtb", [128, HA])
    gnw_d = dt_in("gnw", [128, 1]); hnw_d = dt_in("hnw", [128, 1])
    lbl_d = dt_in("lbl", [128, HB, 2]); fnw_d = dt_in("fnw", [128, D])
    sc_d = dt_in("sc", [128, 3 * NP, 4, 3])
    sg_d = dt_in("sg", [4, NP, 128, 64]); sh_d = dt_in("sh", [4, HB, 128, 128])
    yp = dt_out("yp", [T, D]); ys = dt_out("ys", [64, D])
    ncp = dt_out("ncp", [128, 3 * NP, 1, 3]); ngp = dt_out("ngp", [1, NP, 128, 64]); nhp = dt_out("nhp", [1, HB, 128, 128])
    ncs = dt_out("ncs", [128, 3 * NP, 4, 3]); ngs = dt_out("ngs", [4, NP, 128, 64]); nhs = dt_out("nhs", [4, HB, 128, 128])

    NL = NP + HB
    xsrc = [nc.dram_tensor(f"xsrc{i}", [NL * 128, TB], BF16).ap() for i in range(2)]
    xdst = [nc.dram_tensor(f"xdst{i}", [2 * NL * 128, TB], BF16).ap() for i in range(2)]
    xsrc_s = nc.dram_tensor("xsrc_s", [NL * 128, 64], BF16).ap()
    xdst_s = nc.dram_tensor("xdst_s", [2 * NL * 128, 64], BF16).ap()
    sb = lambda n, sh, d=F32: nc.alloc_sbuf_tensor(n, list(sh), d)
    Wb = sb("Wb", [128, 8, NCOL], BF16)
    WOb = sb("WOb", [128, 8, D], BF16)
    nw = sb("nw_t", [128, 8]); cw = sb("cw_t", [128, 3 * NP, 4])
    alog = sb("alog_t", [128, HA]); dtb = sb("dtb_t", [128, HA]); negA = sb("negA", [128, HA])
    gnw = sb("gnw_t", [128, 1]); hnw = sb("hnw_t", [128, 1])
    lbl = sb("lbl_t", [128, HB, 2]); lb = sb("lb", [128, HB]); oml = sb("oml", [128, HB])
    fnw = sb("fnw_t", [128, D])
    identb = sb("identb", [128, 128], BF16); identf = sb("identf", [128, 128])
    ones = sb("ones", [128, 128]); ob2 = sb("ob2", [128, 128])
    fgt = sb("fgt", [128, 128]); fle = sb("fle", [128, 128])
    I_s = sb("I_s", [128, 64]); U_s = sb("U_s", [128, 64]); Tri_s = sb("Tri_s", [128, 64]); Mc_s = sb("Mc_s", [128, 64])
    halo = sb("halo", [128, 3 * NP, 4, 3])
    Sg = sb("Sg", [128, NP, 64]); Sh = sb("Sh", [128, HB, 128])
    W_ = TB
    xt = [sb(f"xt{i}", [128, D]) for i in range(4)]
    sqj = sb("sqj", [128, D], BF16)
    xb = sb("xb", [128, D], BF16)
    hT_all = [sb(f"hT{i}", [128, 8, W_], BF16) for i in range(2)]
    sqjO = sb("sqjO", [128, D], BF16); ssO = sb("ssO", [128, 1]); rrO = sb("rrO", [128, 1])
    ss = sb("ss", [128, 1]); rr = sb("rr", [128, 1])
    raw = sb("raw", [128, 3, W_ + 12])
    cv = sb("cv", [128, 3, W_])
    tmp = [None, sb("tmp1", [128, W_]), sb("tmp2", [128, W_]), None]
    za = [sb(f"za{i}", [128, W_]) for i in range(2)]
    cvb = [sb(f"cvb{i}", [128, 3, W_], BF16) for i in range(2)]
    tmpA = [sb(f"tmpA{i}", [128, W_]) for i in range(3)]
    sqA = [sb(f"sqA{i}", [128, W_], BF16) for i in range(2)]
    tnA = [sb(f"tnA{i}", [128, W_]) for i in range(2)]
    I_sb = sb("I_sb", [128, 64], BF16); ob2b = sb("ob2b", [128, 128], BF16); onesb = sb("onesb", [128, 128], BF16)
    Sgb = sb("Sgb", [128, NP, 64], BF16); Shb = sb("Shb", [128, HB, 128], BF16)
    sqb = sb("sqb", [128, W_], BF16); sqbH = sb("sqbH", [128, W_], BF16)
    G_all = [sb(f"G{i}", [128, 4, 2 * HA]) for i in range(2)]; Gb_all = [sb(f"Gb{i}", [128, 4, HA]) for i in range(2)]; Gg_all = [sb(f"Gg{i}", [128, 4, HA]) for i in range(2)]
    gs_all = [sb(f"gs{i}", [128, NP, 4]) for i in range(2)]; bs_all = [sb(f"bs{i}", [128, NP, 4]) for i in range(2)]; nbs_all = [sb(f"nbs{i}", [128, NP, 4]) for i in range(2)]
    gc_all = [sb(f"gc{i}", [128, NP, 4]) for i in range(2)]; gl_all = [sb(f"gl{i}", [128, NP, 4]) for i in range(2)]; egc_all = [sb(f"egc{i}", [128, NP, 4]) for i in range(2)]
    dk_all = [sb(f"dk{i}", [128, NP, 4]) for i in range(2)]; bge_all = [sb(f"bge{i}", [128, NP, 4]) for i in range(2)]
    rhsD = sb("rhsD", [128, W_]); Dg = sb("Dg", [128, W_])
    Ee = sb("Ee", [128, W_]); Dm = sb("Dm", [128, W_]); Ds = sb("Ds", [128, W_])
    EBs = [sb(f"EBs{i}", [128, W_]) for i in range(2)]
    P0t = [sb(f"P0t{i}", [128, W_], BF16) for i in range(2)]
    PT0t = [sb(f"PT0t{i}", [128, W_], BF16) for i in range(2)]
    R0t = [sb(f"R0t{i}", [128, W_], BF16) for i in range(2)]
    P = [sb(f"P{i}", [128, W_], BF16) for i in range(2)]
    PT = [sb(f"PT{i}", [128, W_], BF16) for i in range(2)]
    R = [sb(f"R{i}", [128, W_], BF16) for i in range(2)]
    attn = sb("attn", [128, W_], BF16); attnT = [sb(f"attnT{i}", [128, W_], BF16) for i in range(2)]
    Kbe = [sb(f"Kbe{i}", [128, 4, 64], BF16) for i in range(2)]; Kd = [sb(f"Kd{i}", [128, 4, 64], BF16) for i in range(2)]; bV = [sb(f"bV{i}", [128, 4, 64], BF16) for i in range(2)]
    u = sb("u", [128, 4, 64]); wT = sb("wT", [128, W_], BF16); QeT = [sb(f"QeT{i}", [128, W_], BF16) for i in range(2)]
    vn = sb("vn", [128, 64], BF16)
    oTf = sb("oTf", [128, W_])
    oTn = [sb(f"oTn_{i}", [128, 2 * NL, W_], BF16) for i in range(2)]
    oOwn = [sb(f"oOwn_{i}", [128, NL, W_], BF16) for i in range(2)]
    qb = sb("qb", [128, W_]); ff = sb("ff", [128, W_]); lf = sb("lf", [128, W_]); kb = sb("kb", [128, W_])
    bb = sb("bb", [128, W_]); bl = sb("bl", [128, W_])
    Qe = sb("Qe", [128, W_], BF16); Qx = sb("Qx", [128, W_], BF16); Kdh = sb("Kdh", [128, W_], BF16)
    Qef = sb("Qef", [128, W_]); Qxf = sb("Qxf", [128, W_]); Kdf = sb("Kdf", [128, W_])
    ebl = sb("ebl", [128, 4]); zb = sb("zb", [128, W_])
    vtok = sb("vtok", [64, 4, 128], BF16); Kdt = sb("Kdt", [64, 4, 128], BF16); aTh = sb("aTh", [64, W_], BF16)
    smask_all = [sb(f"smask{i}", [128, W_]) for i in range(2)]
    tmpH = [None, sb("tmpH1", [128, W_])]; oTfH = sb("oTfH", [128, W_])
    yo = sb("yo", [128, D]); yo2 = sb("yo2", [128, D])
    pb = [nc.alloc_psum_tensor(f"pb{i}", [128, 512], F32) for i in range(8)]
    pT2 = pb[2][:, 0:128].bitcast(BF16).rearrange("p (k t) -> p k t", t=128)

    def aff(out, cmp, fill_in, step=-1, cm=1, base=0):
        s.op("pool", lambda e: e.memset(out[:], fill_in), writes=[out.name])
        s.op("pool", lambda e: e.affine_select(out=out[:], in_=out[:], pattern=[[step, 128]], compare_op=cmp,
                                               fill=0.0, base=base, channel_multiplier=cm), reads=[out.name], writes=[out.name])
    aff(identf, ALU.is_equal, 1.0)
    aff(fgt, ALU.is_gt, 1.0)
    aff(fle, ALU.is_gt, 1.0, step=1, cm=-1, base=1)
    s.op("pool", lambda e: e.memset(ones[:], 1.0), writes=["ones"])
    s.op("pool", lambda e: e.memset(ob2[:], 0.0), writes=["ob2"])
    for h in range(2):
        sl = slice(64 * h, 64 * h + 64)
        s.op("pool", lambda e: e.memset(ob2[sl, sl], 1.0), reads=["ob2"], writes=["ob2"])
    s.op("dve", lambda e: e.tensor_copy(out=identb[:], in_=identf[:]), reads=["identf"], writes=["identb"])
    for (dst, src) in ((I_s, identf), (U_s, fgt), (Tri_s, fle)):
        for h in range(2):
            sl = slice(64 * h, 64 * h + 64)
            s.op("dve", lambda e: e.tensor_copy(out=dst[sl, :], in_=src[sl, sl]), reads=[src.name], writes=[dst.name])
    s.op("dve", lambda e: e.tensor_tensor(out=Mc_s[:], in0=U_s[:], in1=I_s[:], op=ALU.add), reads=["U_s", "I_s"], writes=["Mc_s"])
    s.op("dve", lambda e: e.tensor_copy(out=I_sb[:], in_=I_s[:]), reads=["I_s"], writes=["I_sb"])
    s.op("dve", lambda e: e.tensor_copy(out=ob2b[:], in_=ob2[:]), reads=["ob2"], writes=["ob2b"])
    s.op("dve", lambda e: e.tensor_copy(out=onesb[:], in_=ones[:]), reads=["ones"], writes=["onesb"])
    for t_, d_ in ((nw, nw_d), (cw, cw_d), (alog, alog_d), (dtb, dtb_d), (gnw, gnw_d), (hnw, hnw_d), (lbl, lbl_d), (fnw, fnw_d)):
        s.dma("sp", t_[:], d_, writes=[t_.name])
    s.op("act", lambda e: e.activation(out=negA[:], in_=alog[:], func=AF.Exp), reads=["alog_t"], writes=["negA"])
    s.op("dve", lambda e: e.tensor_scalar(out=negA[:], in0=negA[:], scalar1=-1.0, scalar2=None, op0=ALU.mult), reads=["negA"], writes=["negA"])
    s.op("dve", lambda e: e.tensor_tensor(out=lb[:], in0=lbl[:, :, 1], in1=lbl[:, :, 0], op=ALU.subtract), reads=["lbl_t"], writes=["lb"])
    s.op("act", lambda e: e.activation(out=lb[:], in_=lb[:], func=AF.Exp), reads=["lb"], writes=["lb"])
    s.op("dve", lambda e: e.tensor_scalar(out=lb[:], in0=lb[:], scalar1=1.0, scalar2=None, op0=ALU.add), reads=["lb"], writes=["lb"])
    s.op("dve", lambda e: e.reciprocal(out=lb[:], in_=lb[:]), reads=["lb"], writes=["lb"])
    s.op("dve", lambda e: e.tensor_scalar(out=oml[:], in0=lb[:], scalar1=-1.0, scalar2=1.0, op0=ALU.mult, op1=ALU.add), reads=["lb"], writes=["oml"])
    w_in_v = w_in.rearrange("(k p) n -> p k n", p=128)
    stgx = [sb(f"stgx{i}", [128, D]) for i in range(4)]
    stgs = [(xt[0], "xt0"), (stgx[0], "stgx0"), (xt[1], "xt1"), (stgx[1], "stgx1"), (yo, "yo"), (stgx[2], "stgx2"), (yo2, "yo2"), (stgx[3], "stgx3")]
    q = 0
    nfull = NCOL // 1024
    for k in range(8):
        pieces = [(i * 1024, 1024) for i in range(nfull)] + [(nfull * 1024, NCOL % 1024)]
        for (c0, cn) in pieces:
            if cn == 1024:
                tl, key = stgs[q % len(stgs)]
            else:
                tl, key = Ee, "Ee"
            s.dma("sp", tl[:, 0:cn], w_in_v[:, k, c0:c0 + cn], writes=[key])
            if q % 2 == 0:
                s.op("dve", lambda e: e.tensor_scalar(out=Wb[:, k, c0:c0 + cn], in0=tl[:, 0:cn], scalar1=nw[:, k:k + 1], scalar2=None, op0=ALU.mult),
                     reads=[key, "nw_t"], writes=["Wb"])
            else:
                s.op("act", lambda e: e.activation(out=Wb[:, k, c0:c0 + cn], in_=tl[:, 0:cn], func=AF.Copy, scale=nw[:, k:k + 1]),
                     reads=[key, "nw_t"], writes=["Wb"])
            q += 1
    w_out_v = w_out.rearrange("(k p) n -> p k n", p=128)
    for k in range(8):
        tl, key = stgs[q % len(stgs)]
        s.dma("sp", tl[:, :], w_out_v[:, k, :], writes=[key])
        if q % 2 == 0:
            s.op("dve", lambda e: e.tensor_copy(out=WOb[:, k, :], in_=tl[:, :]), reads=[key], writes=["WOb"])
        else:
            s.op("act", lambda e: e.activation(out=WOb[:, k, :], in_=tl[:, :], func=AF.Copy), reads=[key], writes=["WOb"])
        q += 1

    def block(x_src, y_dst, t0, nseg, seglen, c, first, is_sample, bpar=0):
        hT = hT_all[bpar]; G = G_all[bpar]; Gb = Gb_all[bpar]; Gg = Gg_all[bpar]; gs = gs_all[bpar]; bs = bs_all[bpar]; nbs = nbs_all[bpar]; gc = gc_all[bpar]; gl = gl_all[bpar]; egc = egc_all[bpar]; dk = dk_all[bpar]; bge = bge_all[bpar]; smask = smask_all[bpar]
        TBk = nseg * seglen
        nch = TBk // c
        cps = seglen // c
        TT = min(128, TBk)
        ntt = TBk // TT
        nlev = {64: 5, 16: 3}[c]
        v3 = lambda ap: ap.rearrange("p (n c) -> p n c", c=c)

        def ph0_x():
            for tt in range(ntt):
                X = xt[bpar * 2 + tt]
                s.dma("sp", X[0:TT, :], x_src[t0 + tt * TT: t0 + (tt + 1) * TT, :], writes=[X.name])
                s.op("act", lambda e: e.activation(out=sqj[0:TT, :], in_=X[0:TT, :], func=AF.Square, accum_out=ss[0:TT, :]),
                     reads=[X.name], writes=["sqj", "ss"])
                s.op("act", lambda e: e.activation(out=rr[0:TT, :], in_=ss[0:TT, :], func=AF.Ln, scale=1.0 / D, bias=EPS), reads=["ss"], writes=["rr"])
                s.op("act", lambda e: e.activation(out=rr[0:TT, :], in_=rr[0:TT, :], func=AF.Exp, scale=-0.5), reads=["rr"], writes=["rr"])
                s.op("dve", lambda e: e.tensor_scalar(out=xb[0:TT, :], in0=X[0:TT, :], scalar1=rr[0:TT, :], scalar2=None, op0=ALU.mult),
                     reads=[X.name, "rr"], writes=["xb"])
                for kk in range(4):
                    for j in range(2):
                        k = 2 * kk + j
                        s.op("pe", lambda e: e.transpose(out=pT2[:, j, 0:TT], in_=xb[0:TT, k * 128:(k + 1) * 128], identity=identb[0:TT, 0:TT]),
                             reads=["xb", "identb"], writes=["B2"])
                    if kk % 2 == 0:
                        s.op("dve", lambda e: e.tensor_copy(out=hT[:, 2 * kk:2 * kk + 2, tt * TT:(tt + 1) * TT], in_=pT2[:, :, 0:TT]), reads=["B2"], writes=[f"hT{bpar}"])
                    else:
                        s.op("act", lambda e: e.activation(out=hT[:, 2 * kk:2 * kk + 2, tt * TT:(tt + 1) * TT], in_=pT2[:, :, 0:TT], func=AF.Copy), reads=["B2"], writes=[f"hT{bpar}"])

        def silu_from(src, srckeys, dst, dstkey, scr, scrkey, W, outdt_note=None):
            s.op("act", lambda e: e.activation(out=scr, in_=src, func=AF.Exp, scale=-1.0), reads=srckeys, writes=[scrkey])
            s.op("act", lambda e: e.activation(out=scr, in_=scr, func=AF.Ln, bias=1.0), reads=[scrkey], writes=[scrkey])
            s.op("act", lambda e: e.activation(out=scr, in_=scr, func=AF.Exp, scale=-1.0), reads=[scrkey], writes=[scrkey])
            s.op("dve", lambda e: e.tensor_tensor(out=dst, in0=src, in1=scr, op=ALU.mult), reads=list(srckeys) + [scrkey], writes=[dstkey])


        def inproj_fm(ct, i=0, alt=False):
            key = "B6" if alt is True else ("B3" if alt == 3 else "B0")
            out = pb[6][:, 256:256 + TBk] if alt is True else (pb[3][:, 256:256 + TBk] if alt == 3 else pb[0][:, i * 256: i * 256 + TBk])
            for k in range(8):
                s.op("pe", lambda e: e.matmul(out, lhsT=Wb[:, k, ct * 128:(ct + 1) * 128], rhs=hT[:, k, 0:TBk], start=(k == 0), stop=(k == 7)),
                     reads=["Wb", f"hT{bpar}"], writes=[key])
            return out, key

        def ph0_g():
            for ch in range(nch):
                for h in range(2):
                    out = pb[2][64 * h:64 * h + c, ch * 2 * HA:(ch + 1) * 2 * HA]
                    for k in range(8):
                        s.op("pe", lambda e: e.matmul(out, lhsT=hT[:, k, ch * c:(ch + 1) * c], rhs=Wb[:, k, C_G:C_G + 2 * HA], start=(k == 0), stop=(k == 7)),
                             reads=["Wb", f"hT{bpar}"], writes=["B2"])
            Gv = G[:, 0:nch, :]
            s.op("dve", lambda e: e.tensor_copy(out=Gv, in_=pb[2][:, 0:nch * 2 * HA].rearrange("p (n g) -> p n g", g=2 * HA)), reads=["B2"], writes=[f"G{bpar}"])
            Gbv = Gb[:, 0:nch, :]; Ggv = Gg[:, 0:nch, :]
            s.op("act", lambda e: e.activation(out=Gbv, in_=Gv[:, :, 0:HA], func=AF.Exp, scale=-1.0), reads=[f"G{bpar}"], writes=[f"Gb{bpar}"])
            s.op("act", lambda e: e.activation(out=Gbv, in_=Gbv, func=AF.Ln, bias=1.0), reads=[f"Gb{bpar}"], writes=[f"Gb{bpar}"])
            s.op("act", lambda e: e.activation(out=Gbv, in_=Gbv, func=AF.Exp, scale=-1.0), reads=[f"Gb{bpar}"], writes=[f"Gb{bpar}"])
            s.op("dve", lambda e: e.tensor_tensor(out=Ggv, in0=Gv[:, :, HA:2 * HA], in1=bc(dtb[:, None, :], [128, nch, HA]), op=ALU.add),
                 reads=[f"G{bpar}", "dtb_t"], writes=[f"Gg{bpar}"])
            s.op("act", lambda e: e.activation(out=Ggv, in_=Ggv, func=AF.Exp), reads=[f"Gg{bpar}"], writes=[f"Gg{bpar}"])
            s.op("act", lambda e: e.activation(out=Ggv, in_=Ggv, func=AF.Ln, bias=1.0), reads=[f"Gg{bpar}"], writes=[f"Gg{bpar}"])
            s.op("dve", lambda e: e.tensor_tensor(out=Ggv, in0=Ggv, in1=bc(negA[:, None, :], [128, nch, HA]), op=ALU.mult),
                 reads=[f"Gg{bpar}", "negA"], writes=[f"Gg{bpar}"])
            gsv = gs[:, :, 0:nch]; bsv = bs[:, :, 0:nch]; nbsv = nbs[:, :, 0:nch]
            gcv = gc[:, :, 0:nch]; glv = gl[:, :, 0:nch]; egcv = egc[:, :, 0:nch]; dkv = dk[:, :, 0:nch]; bgev = bge[:, :, 0:nch]
            for h in range(2):
                sl = slice(64 * h, 64 * h + 64)
                for (dst, src, kd, ks) in ((gs, Gg, f"gs{bpar}", f"Gg{bpar}"), (bs, Gb, f"bs{bpar}", f"Gb{bpar}")):
                    for p in range(NP):
                        s.op("dve", lambda e: e.tensor_copy(out=dst[sl, p, 0:nch], in_=src[sl, 0:nch, 2 * p + h]), reads=[ks], writes=[kd])
            s.op("dve", lambda e: e.tensor_scalar(out=nbsv, in0=bsv, scalar1=-1.0, scalar2=None, op0=ALU.mult), reads=[f"bs{bpar}"], writes=[f"nbs{bpar}"])
            for h in range(2):
                rs = slice(64 * h, 64 * h + c)
                s.op("pe", lambda e: e.matmul(pb[2][rs, 64:64 + NP * nch], lhsT=Tri_s[rs, 0:c], rhs=gs[rs, :, 0:nch], start=True, stop=True),
                     reads=["Tri_s", f"gs{bpar}"], writes=["B2"])
                s.op("pe", lambda e: e.matmul(pb[2][rs, 96:96 + NP * nch], lhsT=ones[rs, 0:c], rhs=gs[rs, :, 0:nch], start=True, stop=True),
                     reads=["ones", f"gs{bpar}"], writes=["B2"])
            s.op("dve", lambda e: e.tensor_copy(out=gcv, in_=pb[2][:, 64:64 + NP * nch].rearrange("p (a n) -> p a n", n=nch)), reads=["B2"], writes=[f"gc{bpar}"])
            s.op("dve", lambda e: e.tensor_copy(out=glv, in_=pb[2][:, 96:96 + NP * nch].rearrange("p (a n) -> p a n", n=nch)), reads=["B2"], writes=[f"gl{bpar}"])
            s.op("act", lambda e: e.activation(out=egcv, in_=gcv, func=AF.Exp), reads=[f"gc{bpar}"], writes=[f"egc{bpar}"])
            s.op("dve", lambda e: e.tensor_tensor(out=dkv, in0=glv, in1=gcv, op=ALU.subtract), reads=[f"gl{bpar}", f"gc{bpar}"], writes=[f"dk{bpar}"])
            s.op("act", lambda e: e.activation(out=dkv, in_=dkv, func=AF.Exp), reads=[f"dk{bpar}"], writes=[f"dk{bpar}"])
            s.op("dve", lambda e: e.tensor_tensor(out=bgev, in0=bsv, in1=egcv, op=ALU.mult), reads=[f"bs{bpar}", f"egc{bpar}"], writes=[f"bge{bpar}"])


        def phase0(_=None):
            ph0_x()
            ph0_g()
            ph0_m()

        def front(p):
            par = p % 2
            for i3 in range(3):
                ct = NP * i3 + p
                pp, pk = inproj_fm(ct, 0, INPROJ_ALT and i3 == 1)
                rv = raw[:, i3, 0:nseg * (seglen + 3)].rearrange("p (n c) -> p n c", c=seglen + 3)
                if first and not is_sample:
                    s.op("pool", lambda e: e.memset(rv[:, :, 0:3], 0.0), reads=[f"raw{i3}"], writes=[f"raw{i3}"])
                else:
                    s.op("pool", lambda e: e.tensor_copy(out=rv[:, :, 0:3], in_=halo[:, ct, 0:nseg, :]), reads=["halo"], writes=[f"raw{i3}"])
                s.op("act", lambda e: e.activation(out=rv[:, :, 3:3 + seglen], in_=pp.rearrange("p (n c) -> p n c", c=seglen), func=AF.Copy),
                     reads=[pk], writes=[f"raw{i3}"])
                s.op("pool", lambda e: e.tensor_copy(out=halo[:, ct, 0:nseg, :], in_=rv[:, :, seglen:seglen + 3]), reads=[f"raw{i3}"], writes=["halo"])
                cvv = cv[:, i3, 0:TBk].rearrange("p (n c) -> p n c", c=seglen)
                s.op("dve", lambda e: e.tensor_scalar(out=cvv, in0=rv[:, :, 0:seglen], scalar1=cw[:, ct, 0:1], scalar2=None, op0=ALU.mult),
                     reads=[f"raw{i3}", "cw_t"], writes=[f"cv{i3}"])
                for j in range(1, 4):
                    s.op("dve", lambda e: e.scalar_tensor_tensor(out=cvv, in0=rv[:, :, j:j + seglen], scalar=cw[:, ct, j:j + 1], in1=cvv,
                                                                 op0=ALU.mult, op1=ALU.add), reads=[f"raw{i3}", "cw_t", f"cv{i3}"], writes=[f"cv{i3}"])
                if i3 < 2:
                    silu_from(cv[:, i3, 0:TBk], [f"cv{i3}"], cv[:, i3, 0:TBk], f"cv{i3}", tmpA[i3][:, 0:TBk], f"tmpA{i3}", TBk)
                else:
                    silu_from(cv[:, i3, 0:TBk], [f"cv{i3}"], cvb[par][:, 2, 0:TBk], f"cvb{par}v", tmpA[i3][:, 0:TBk], f"tmpA{i3}", TBk)
            for i3 in range(2):
                src = cv[:, i3, 0:TBk]
                s.op("pool", lambda e: e.tensor_tensor(out=sqA[i3][:, 0:TBk], in0=src, in1=src, op=ALU.mult), reads=[f"cv{i3}"], writes=[f"sqA{i3}"])
                s.op("pe", lambda e: e.matmul(pb[7][:, i3 * 256:i3 * 256 + TBk], lhsT=ob2b[:], rhs=sqA[i3][:, 0:TBk], start=True, stop=True), reads=["ob2b", f"sqA{i3}"], writes=["BT"])
                s.op("act", lambda e: e.activation(out=tnA[i3][:, 0:TBk], in_=pb[7][:, i3 * 256:i3 * 256 + TBk], func=AF.Ln, bias=EPS), reads=["BT"], writes=[f"tnA{i3}"])
                s.op("act", lambda e: e.activation(out=tnA[i3][:, 0:TBk], in_=tnA[i3][:, 0:TBk], func=AF.Exp, scale=-0.5), reads=[f"tnA{i3}"], writes=[f"tnA{i3}"])
                if i3 == 0:
                    s.op("dve", lambda e: e.scalar_tensor_tensor(out=cvb[par][:, 0, 0:TBk], in0=src, scalar=0.125, in1=tnA[i3][:, 0:TBk], op0=ALU.mult, op1=ALU.mult),
                         reads=[f"cv{i3}", f"tnA{i3}"], writes=[f"cvb{par}q"])
                else:
                    s.op("dve", lambda e: e.tensor_tensor(out=cvb[par][:, 1, 0:TBk], in0=src, in1=tnA[i3][:, 0:TBk], op=ALU.mult), reads=[f"cv{i3}", f"tnA{i3}"], writes=[f"cvb{par}k"])
            pp, pk = inproj_fm(3 * NP + p, 0, INPROJ_ALT)
            s.op("act", lambda e: e.activation(out=za[par][:, 0:TBk], in_=pp, func=AF.Copy), reads=[pk], writes=[f"za{par}"])
            silu_from(za[par][:, 0:TBk], [f"za{par}"], za[par][:, 0:TBk], f"za{par}", tmpA[0][:, 0:TBk], "tmpA0", TBk)
            qn = cvb[par][:, 0, 0:TBk]; kn = cvb[par][:, 1, 0:TBk]; vs = cvb[par][:, 2, 0:TBk]
            ckq, ckk, ckv = f"cvb{par}q", f"cvb{par}k", f"cvb{par}v"
            s.op("dve", lambda e: e.tensor_tensor(out=v3(rhsD[:, 0:TBk]), in0=bc(gs[:, p, 0:nch, None], [128, nch, c]), in1=bc(U_s[:, None, 0:c], [128, nch, c]), op=ALU.mult),
                 reads=[f"gs{bpar}", "U_s"], writes=["rhsD"])
            s.op("pool", lambda e: e.tensor_tensor(out=v3(Dg[:, 0:TBk]), in0=bc(egc[:, p, 0:nch, None], [128, nch, c]), in1=bc(I_s[:, None, 0:c], [128, nch, c]), op=ALU.mult),
                 reads=[f"egc{bpar}", "I_s"], writes=["Dg"])
            for h in range(2):
                rs = slice(64 * h, 64 * h + c)
                s.op("pe", lambda e: e.matmul(pb[6][rs, 0:TBk], lhsT=Tri_s[rs, 0:c], rhs=rhsD[rs, 0:TBk], start=True, stop=True),
                     reads=["Tri_s", "rhsD"], writes=["B6"])
            s.op("act", lambda e: e.activation(out=Ee[:, 0:TBk], in_=pb[6][:, 0:TBk], func=AF.Exp), reads=["B6"], writes=["Ee"])
            s.op("pool", lambda e: e.tensor_tensor(out=v3(Dm[:, 0:TBk]), in0=v3(Ee[:, 0:TBk]), in1=bc(Mc_s[:, None, 0:c], [128, nch, c]), op=ALU.mult),
                 reads=["Ee", "Mc_s"], writes=["Dm"])
            s.op("pool", lambda e: e.tensor_tensor(out=v3(Ds[:, 0:TBk]), in0=v3(Ee[:, 0:TBk]), in1=bc(U_s[:, None, 0:c], [128, nch, c]), op=ALU.mult),
                 reads=["Ee", "U_s"], writes=["Ds"])
            for h in range(2):
                rs = slice(64 * h, 64 * h + c)
                s.op("pe", lambda e: e.matmul(pb[6][64 * h:64 * h + 64, 0:TBk], lhsT=ones[rs, 0:64], rhs=Dg[rs, 0:TBk], start=True, stop=True),
                     reads=["ones", "Dg"], writes=["B6"])
            s.op("act", lambda e: e.activation(out=EBs[par][:, 0:TBk], in_=pb[6][:, 0:TBk], func=AF.Copy), reads=["B6"], writes=[f"EBs{par}"])
            s.op("dve", lambda e: e.tensor_tensor(out=QeT[par][:, 0:TBk], in0=qn, in1=EBs[par][:, 0:TBk], op=ALU.mult), reads=[ckq, f"EBs{par}"], writes=[f"QeT{par}"])
            for ch in range(nch):
                cs = slice(ch * c, (ch + 1) * c)
                for h in range(2):
                    fs = slice(64 * h, 64 * h + 64); rs = slice(64 * h, 64 * h + c)
                    s.op("pe", lambda e: e.matmul(pb[7][rs, cs], lhsT=kn[fs, cs], rhs=kn[fs, cs], start=True, stop=True), reads=[ckk], writes=["BT"])
                    s.op("pe", lambda e: e.matmul(pb[7][rs, 256 + ch * c:256 + (ch + 1) * c], lhsT=qn[fs, cs], rhs=kn[fs, cs], start=True, stop=True),
                         reads=[ckq, ckk], writes=["BT"])
            s.op("dve", lambda e: e.tensor_tensor(out=v3(tmp[2][:, 0:TBk]), in0=v3(pb[7][:, 0:TBk]), in1=bc(nbs[:, p, 0:nch, None], [128, nch, c]), op=ALU.mult),
                 reads=["BT", f"nbs{bpar}"], writes=["tmp2"])
            s.op("pool", lambda e: e.tensor_tensor(out=PT0t[par][:, 0:TBk], in0=tmp[2][:, 0:TBk], in1=Ds[:, 0:TBk], op=ALU.mult), reads=["tmp2", "Ds"], writes=[f"PT0t{par}"])
            s.op("dve", lambda e: e.tensor_tensor(out=attn[:, 0:TBk], in0=pb[7][:, 256:256 + TBk], in1=Dm[:, 0:TBk], op=ALU.mult), reads=["BT", "Dm"], writes=["attn"])

            for ch in range(nch):
                cs = slice(ch * c, (ch + 1) * c)
                for h in range(2):
                    fs = slice(64 * h, 64 * h + 64); rs = slice(64 * h, 64 * h + c)
                    s.op("pe", lambda e: e.matmul(pb[6][rs, cs], lhsT=PT0t[par][rs, cs], rhs=I_sb[rs, 0:c], start=True, stop=True), reads=[f"PT0t{par}", "I_sb"], writes=["B6"])
                    s.op("pe", lambda e: e.matmul(pb[6][rs, 256 + ch * c:256 + (ch + 1) * c], lhsT=attn[rs, cs], rhs=I_sb[rs, 0:c], start=True, stop=True),
                         reads=["attn", "I_sb"], writes=["B6"])
                    s.op("pe", lambda e: e.matmul(pb[7][rs, ch * 64:(ch + 1) * 64], lhsT=kn[fs, cs], rhs=I_sb[fs, 0:64], start=True, stop=True), reads=[ckk, ckv, "I_sb"], writes=["BT"])
                    s.op("pe", lambda e: e.matmul(pb[7][rs, 256 + ch * 64:256 + (ch + 1) * 64], lhsT=vs[fs, cs], rhs=I_sb[fs, 0:64], start=True, stop=True),
                         reads=[ckk, ckv, "I_sb"], writes=["BT"])
            s.op("act", lambda e: e.activation(out=P0t[par][:, 0:TBk], in_=pb[6][:, 0:TBk], func=AF.Copy), reads=["B6"], writes=[f"P0t{par}"])
            s.op("dve", lambda e: e.tensor_tensor(out=v3(R0t[par][:, 0:TBk]), in0=v3(pb[6][:, 0:TBk]), in1=bc(I_s[:, None, 0:c], [128, nch, c]), op=ALU.add),
                 reads=["B6", "I_s"], writes=[f"R0t{par}"])
            s.op("act", lambda e: e.activation(out=attnT[par][:, 0:TBk], in_=pb[6][:, 256:256 + TBk], func=AF.Copy), reads=["B6"], writes=[f"attnT{par}"])
            k4 = pb[7][:, 0:nch * 64].rearrange("p (n d) -> p n d", d=64)
            v4 = pb[7][:, 256:256 + nch * 64].rearrange("p (n d) -> p n d", d=64)
            s.op("dve", lambda e: e.tensor_tensor(out=Kbe[par][:, 0:nch, :], in0=k4, in1=bc(bge[:, p, 0:nch, None], [128, nch, 64]), op=ALU.mult), reads=["BT", f"bge{bpar}"], writes=[f"Kbe{par}"])
            s.op("dve", lambda e: e.tensor_tensor(out=Kd[par][:, 0:nch, :], in0=k4, in1=bc(dk[:, p, 0:nch, None], [128, nch, 64]), op=ALU.mult), reads=["BT", f"dk{bpar}"], writes=[f"Kd{par}"])
            s.op("dve", lambda e: e.tensor_tensor(out=bV[par][:, 0:nch, :], in0=v4, in1=bc(bs[:, p, 0:nch, None], [128, nch, 64]), op=ALU.mult), reads=["BT", f"bs{bpar}"], writes=[f"bV{par}"])
        def core(p):
            par = p % 2
            Pc, kP = P0t[par], f"P0t{par}"
            PTc, kPT = PT0t[par], f"PT0t{par}"
            Rc, kR = R0t[par], f"R0t{par}"
            for lev in range(1, nlev + 2):
                do_pow = lev <= nlev
                need_P = lev < nlev
                do_R = lev >= 2
                nP, nkP = P[lev % 2], f"P{lev % 2}"
                nPT, nkPT = PT[lev % 2], f"PT{lev % 2}"
                nR, nkR = R[lev % 2], f"R{lev % 2}"
                for ch in range(nch):
                    cs = slice(ch * c, (ch + 1) * c)
                    for h in range(2):
                        rs = slice(64 * h, 64 * h + c)
                        if do_pow and need_P:
                            s.op("pe", lambda e: e.matmul(pb[5][rs, cs], lhsT=PTc[rs, cs], rhs=Pc[rs, cs], start=True, stop=True), reads=[kPT, kP], writes=["B5"])
                        if do_pow:
                            s.op("pe", lambda e: e.matmul(pb[5][rs, 256 + ch * c:256 + (ch + 1) * c], lhsT=Pc[rs, cs], rhs=PTc[rs, cs], start=True, stop=True),
                                 reads=[kPT, kP], writes=["B5"])
                        if do_R:
                            s.op("pe", lambda e: e.matmul(pb[4][rs, cs], lhsT=PTc[rs, cs], rhs=Rc[rs, cs], start=True, stop=True), reads=[kPT, kR], writes=["B4"])
                if do_pow and need_P:
                    s.op("act", lambda e: e.activation(out=nP[:, 0:TBk], in_=pb[5][:, 0:TBk], func=AF.Copy), reads=["B5"], writes=[nkP])
                if do_pow:
                    s.op("dve", lambda e: e.tensor_copy(out=nPT[:, 0:TBk], in_=pb[5][:, 256:256 + TBk]), reads=["B5"], writes=[nkPT])
                if do_R:
                    s.op("dve", lambda e: e.tensor_tensor(out=nR[:, 0:TBk], in0=pb[4][:, 0:TBk], in1=Rc[:, 0:TBk], op=ALU.add), reads=["B4", kR], writes=[nkR])
                    Rc, kR = nR, nkR
                if do_pow:
                    if need_P:
                        Pc, kP = nP, nkP
                    PTc, kPT = nPT, nkPT
            Rf, rk = Rc, kR
            for ch in range(nch):
                cs = slice(ch * c, (ch + 1) * c)
                for h in range(2):
                    rs = slice(64 * h, 64 * h + c)
                    s.op("pe", lambda e: e.matmul(pb[4][rs, 256 + ch * 64:256 + (ch + 1) * 64], lhsT=Rf[rs, cs], rhs=bV[par][rs, ch, :], start=True, stop=True),
                         reads=[rk, f"bV{par}"], writes=["B4"])
                    s.op("pe", lambda e: e.matmul(pb[5][64 * h:64 * h + 64, ch * c:(ch + 1) * c], lhsT=Kbe[par][rs, ch, :], rhs=Rf[rs, cs], start=True, stop=True),
                         reads=[rk, f"Kbe{par}"], writes=["B5"])
            s.op("act", lambda e: e.activation(out=u[:, 0:nch, :], in_=pb[4][:, 256:256 + nch * 64].rearrange("p (n d) -> p n d", d=64), func=AF.Copy), reads=["B4"], writes=["u"])
            s.op("dve", lambda e: e.tensor_copy(out=wT[:, 0:TBk], in_=pb[5][:, 0:TBk]), reads=["B5"], writes=["wT"])
            skey = f"Sg{p}"
            for ch in range(nch):
                cs = slice(ch * c, (ch + 1) * c)
                seg = ch // cps
                if ch % cps == 0:
                    if is_sample:
                        s.dma("sp", Sg[:, p, :], sg_d[seg, p], writes=[skey])
                        s.op("dve", lambda e: e.tensor_copy(out=Sgb[:, p, :], in_=Sg[:, p, :]), reads=[skey], writes=[skey + "b"])
                    elif first:
                        s.op("pool", lambda e: e.memset(Sg[:, p, :], 0.0), writes=[skey])
                        s.op("pool", lambda e: e.memset(Sgb[:, p, :], 0.0), writes=[skey + "b"])
                for h in range(2):
                    fs = slice(64 * h, 64 * h + 64); rs = slice(64 * h, 64 * h + c)
                    s.op("pe", lambda e: e.matmul(pb[1][rs, 0:64], lhsT=wT[fs, cs], rhs=Sgb[fs, p, :], start=True, stop=True), reads=["wT", skey + "b"], writes=["B1"])
                s.op("dve", lambda e: e.tensor_tensor(out=vn[:], in0=u[:, ch, :], in1=pb[1][:, 0:64], op=ALU.subtract), reads=["u", "B1"], writes=["vn"])
                for h in range(2):
                    fs = slice(64 * h, 64 * h + 64); rs = slice(64 * h, 64 * h + c)
                    s.op("pe", lambda e: e.matmul(pb[1][fs, 256 + ch * c:256 + (ch + 1) * c], lhsT=Sgb[fs, p, :], rhs=QeT[par][fs, cs], start=True, stop=False),
                         reads=[skey + "b", f"QeT{par}"], writes=["B1"])
                    s.op("pe", lambda e: e.matmul(pb[1][fs, 256 + ch * c:256 + (ch + 1) * c], lhsT=vn[rs, :], rhs=attnT[par][rs, cs], start=False, stop=True),
                         reads=["vn", f"attnT{par}"], writes=["B1"])
                for h in range(2):
                    fs = slice(64 * h, 64 * h + 64); rs = slice(64 * h, 64 * h + c)
                    s.op("pe", lambda e: e.matmul(pb[1][fs, 64:128], lhsT=Kd[par][rs, ch, :], rhs=vn[rs, :], start=True, stop=True), reads=[f"Kd{par}", "vn"], writes=["B1"])
                s.op("dve", lambda e: e.scalar_tensor_tensor(out=Sgb[:, p, :], in0=Sg[:, p, :], scalar=EBs[par][:, (ch + 1) * c - 1:(ch + 1) * c], in1=pb[1][:, 64:128],
                                                             op0=ALU.mult, op1=ALU.add), reads=[skey, f"EBs{par}", "B1"], writes=[skey + "b"])
                s.op("dve", lambda e: e.scalar_tensor_tensor(out=Sg[:, p, :], in0=Sg[:, p, :], scalar=EBs[par][:, (ch + 1) * c - 1:(ch + 1) * c], in1=pb[1][:, 64:128],
                                                             op0=ALU.mult, op1=ALU.add), reads=[skey, f"EBs{par}", "B1"], writes=[skey])
                if is_sample and (ch + 1) % cps == 0:
                    s.dma("sp", ngs[seg, p], Sg[:, p, :], reads=[skey], writes=[f"o_ngs{seg}_{p}"])
            s.op("act", lambda e: e.activation(out=oTf[:, 0:TBk], in_=pb[1][:, 256:256 + TBk], func=AF.Copy), reads=["B1"], writes=["oTf"])
            s.op("pool", lambda e: e.tensor_tensor(out=sqb[:, 0:TBk], in0=oTf[:, 0:TBk], in1=oTf[:, 0:TBk], op=ALU.mult), reads=["oTf"], writes=["sqb"])
            s.op("pe", lambda e: e.matmul(pb[3][:, 0:TBk], lhsT=ob2b[:], rhs=sqb[:, 0:TBk], start=True, stop=True), reads=["ob2b", "sqb"], writes=["B3"])
            s.op("act", lambda e: e.activation(out=tmp[1][:, 0:TBk], in_=pb[3][:, 0:TBk], func=AF.Ln, scale=1.0 / 64, bias=EPS), reads=["B3"], writes=["tmp1"])
            s.op("act", lambda e: e.activation(out=tmp[1][:, 0:TBk], in_=tmp[1][:, 0:TBk], func=AF.Exp, scale=-0.5), reads=["tmp1"], writes=["tmp1"])
            s.op("dve", lambda e: e.tensor_tensor(out=oTf[:, 0:TBk], in0=oTf[:, 0:TBk], in1=tmp[1][:, 0:TBk], op=ALU.mult), reads=["oTf", "tmp1"], writes=["oTf"])
            s.op("dve", lambda e: e.scalar_tensor_tensor(out=oOwn[bpar][:, p, 0:TBk], in0=oTf[:, 0:TBk], scalar=gnw[:, 0:1], in1=za[par][:, 0:TBk], op0=ALU.mult, op1=ALU.mult),
                 reads=["oTf", "gnw_t", f"za{par}"], writes=[f"oOwn{bpar}_{p}"])

        def ph0_m():
            s.op("pool", lambda e: e.memset(smask[:, 0:TBk], 1.0), writes=[f"smask{bpar}"])
            s.op("pool", lambda e: e.memset(v3(smask[:, 0:TBk])[:, :, 0:1], 0.0), reads=[f"smask{bpar}"], writes=[f"smask{bpar}"])

        def hg(h):
            pp, pk = inproj_fm(4 * NP + h, 1)
            s.op("act", lambda e: e.activation(out=qb[:, 0:TBk], in_=pp, func=AF.Copy), reads=[pk], writes=["qb"])
            silu_from(qb[:, 0:TBk], ["qb"], qb[:, 0:TBk], "qb", bl[:, 0:TBk], "bl", TBk)
            pp, pk = inproj_fm(4 * NP + 2 * HB + h, 1, 3 if HG_ALT else False)
            s.op("act", lambda e: e.activation(out=zb[:, 0:TBk], in_=pp, func=AF.Copy), reads=[pk], writes=["zb"])
            silu_from(zb[:, 0:TBk], ["zb"], zb[:, 0:TBk], "zb", bl[:, 0:TBk], "bl", TBk)
            pp, pk = inproj_fm(4 * NP + HB + h, 1)
            s.op("act", lambda e: e.activation(out=ff[:, 0:TBk], in_=pp, func=AF.Exp, scale=-1.0), reads=[pk], writes=["ff"])
            s.op("act", lambda e: e.activation(out=ff[:, 0:TBk], in_=ff[:, 0:TBk], func=AF.Ln, bias=1.0), reads=["ff"], writes=["ff"])
            s.op("act", lambda e: e.activation(out=ff[:, 0:TBk], in_=ff[:, 0:TBk], func=AF.Exp, scale=-1.0), reads=["ff"], writes=["ff"])
            s.op("dve", lambda e: e.tensor_scalar(out=ff[:, 0:TBk], in0=ff[:, 0:TBk], scalar1=oml[:, h:h + 1], scalar2=lb[:, h:h + 1], op0=ALU.mult, op1=ALU.add),
                 reads=["ff", "oml", "lb"], writes=["ff"])
            s.op("act", lambda e: e.activation(out=lf[:, 0:TBk], in_=ff[:, 0:TBk], func=AF.Ln), reads=["ff"], writes=["lf"])
            s.op("dve", lambda e: e.tensor_scalar(out=kb[:, 0:TBk], in0=ff[:, 0:TBk], scalar1=-1.0, scalar2=1.0, op0=ALU.mult, op1=ALU.add), reads=["ff"], writes=["kb"])
            s.op("dve", lambda e: e.tensor_tensor_scan(out=bb[:, 0:TBk], data0=smask[:, 0:TBk], data1=lf[:, 0:TBk], initial=0.0, op0=ALU.mult, op1=ALU.add),
                 reads=[f"smask{bpar}", "lf"], writes=["bb"])
            b3 = v3(bb[:, 0:TBk])
            s.op("pool", lambda e: e.tensor_tensor(out=v3(bl[:, 0:TBk]), in0=b3, in1=bc(b3[:, :, c - 1:c], [128, nch, c]), op=ALU.subtract), reads=["bb"], writes=["bl"])
            s.op("act", lambda e: e.activation(out=Qef[:, 0:TBk], in_=bb[:, 0:TBk], func=AF.Exp), reads=["bb"], writes=["Qef"])
            s.op("act", lambda e: e.activation(out=Qxf[:, 0:TBk], in_=bl[:, 0:TBk], func=AF.Exp), reads=["bl"], writes=["Qxf"])
            s.op("act", lambda e: e.activation(out=Kdf[:, 0:TBk], in_=bl[:, 0:TBk], func=AF.Exp, scale=-1.0), reads=["bl"], writes=["Kdf"])
            s.op("act", lambda e: e.activation(out=ebl[:, 0:nch], in_=b3[:, :, c - 1], func=AF.Exp), reads=["bb"], writes=["ebl"])
            s.op("dve", lambda e: e.tensor_tensor(out=Qe[:, 0:TBk], in0=Qef[:, 0:TBk], in1=qb[:, 0:TBk], op=ALU.mult), reads=["Qef", "qb"], writes=["Qe"])
            s.op("pool", lambda e: e.tensor_tensor(out=Qx[:, 0:TBk], in0=Qxf[:, 0:TBk], in1=qb[:, 0:TBk], op=ALU.mult), reads=["Qxf", "qb"], writes=["Qx"])
            s.op("dve", lambda e: e.tensor_tensor(out=Kdh[:, 0:TBk], in0=Kdf[:, 0:TBk], in1=kb[:, 0:TBk], op=ALU.mult), reads=["Kdf", "kb"], writes=["Kdh"])
            for ch in range(nch):
                cs = slice(ch * c, (ch + 1) * c)
                outp = pb[2][0:c, 128:256]
                for k in range(8):
                    s.op("pe", lambda e: e.matmul(outp, lhsT=hT[:, k, cs], rhs=Wb[:, k, C_HI + 128 * h:C_HI + 128 * (h + 1)], start=(k == 0), stop=(k == 7)),
                         reads=["Wb", f"hT{bpar}"], writes=["B2"])
                s.op("pe", lambda e: e.matmul(pb[2][0:c, 256:384], lhsT=Kdh[:, cs], rhs=identb[:], start=True, stop=True), reads=["Kdh", "identb"], writes=["B2"])
                s.op("pe", lambda e: e.matmul(pb[2][0:c, 384:384 + c], lhsT=Kdh[:, cs], rhs=Qx[:, cs], start=True, stop=True), reads=["Kdh", "Qx"], writes=["B2"])
                s.op("act", lambda e: e.activation(out=vtok[0:c, ch, :], in_=outp, func=AF.Copy), reads=["B2"], writes=["vtok"])
                s.op("dve", lambda e: e.tensor_copy(out=Kdt[0:c, ch, :], in_=pb[2][0:c, 256:384]), reads=["B2"], writes=["Kdt"])
                s.op("dve", lambda e: e.tensor_tensor(out=aTh[0:c, cs], in0=pb[2][0:c, 384:384 + c], in1=Tri_s[0:c, 0:c], op=ALU.mult),
                     reads=["B2", "Tri_s"], writes=["aTh"])
            skey = f"Sh{h}"
            for ch in range(nch):
                cs = slice(ch * c, (ch + 1) * c)
                seg = ch // cps
                if ch % cps == 0:
                    if is_sample:
                        s.dma("sp", Sh[:, h, :], sh_d[seg, h], writes=[skey])
                        s.op("dve", lambda e: e.tensor_copy(out=Shb[:, h, :], in_=Sh[:, h, :]), reads=[skey], writes=[skey + "b"])
                    elif first:
                        s.op("pool", lambda e: e.memset(Sh[:, h, :], 0.0), writes=[skey])
                        s.op("pool", lambda e: e.memset(Shb[:, h, :], 0.0), writes=[skey + "b"])
                s.op("pe", lambda e: e.matmul(pb[3][:, 256 + ch * c:256 + (ch + 1) * c], lhsT=Shb[:, h, :], rhs=Qe[:, cs], start=True, stop=False), reads=[skey + "b", "Qe"], writes=["B3"])
                s.op("pe", lambda e: e.matmul(pb[3][:, 256 + ch * c:256 + (ch + 1) * c], lhsT=vtok[0:c, ch, :], rhs=aTh[0:c, cs], start=False, stop=True),
                     reads=["vtok", "aTh"], writes=["B3"])
                s.op("pe", lambda e: e.matmul(pb[1][:, 128:256], lhsT=Kdt[0:c, ch, :], rhs=vtok[0:c, ch, :], start=True, stop=True), reads=["Kdt", "vtok"], writes=["B1"])
                s.op("dve", lambda e: e.scalar_tensor_tensor(out=Shb[:, h, :], in0=Sh[:, h, :], scalar=ebl[:, ch:ch + 1], in1=pb[1][:, 128:256], op0=ALU.mult, op1=ALU.add),
                     reads=[skey, "ebl", "B1"], writes=[skey + "b"])
                s.op("dve", lambda e: e.scalar_tensor_tensor(out=Sh[:, h, :], in0=Sh[:, h, :], scalar=ebl[:, ch:ch + 1], in1=pb[1][:, 128:256], op0=ALU.mult, op1=ALU.add),
                     reads=[skey, "ebl", "B1"], writes=[skey])
                if is_sample and (ch + 1) % cps == 0:
                    s.dma("sp", nhs[seg, h], Sh[:, h, :], reads=[skey], writes=[f"o_nhs{seg}_{h}"])
            s.op("act", lambda e: e.activation(out=oTfH[:, 0:TBk], in_=pb[3][:, 256:256 + TBk], func=AF.Copy), reads=["B3"], writes=["oTfH"])
            s.op("pool", lambda e: e.tensor_tensor(out=sqbH[:, 0:TBk], in0=oTfH[:, 0:TBk], in1=oTfH[:, 0:TBk], op=ALU.mult), reads=["oTfH"], writes=["sqbH"])
            s.op("pe", lambda e: e.matmul(pb[3][:, 256:256 + TBk], lhsT=onesb[:], rhs=sqbH[:, 0:TBk], start=True, stop=True), reads=["onesb", "sqbH"], writes=["B3"])
            s.op("act", lambda e: e.activation(out=tmpH[1][:, 0:TBk], in_=pb[3][:, 256:256 + TBk], func=AF.Ln, scale=1.0 / 128, bias=EPS), reads=["B3"], writes=["tmpH1"])
            s.op("act", lambda e: e.activation(out=tmpH[1][:, 0:TBk], in_=tmpH[1][:, 0:TBk], func=AF.Exp, scale=-0.5), reads=["tmpH1"], writes=["tmpH1"])
            s.op("dve", lambda e: e.tensor_tensor(out=oTfH[:, 0:TBk], in0=oTfH[:, 0:TBk], in1=tmpH[1][:, 0:TBk], op=ALU.mult), reads=["oTfH", "tmpH1"], writes=["oTfH"])
            s.op("dve", lambda e: e.scalar_tensor_tensor(out=oOwn[bpar][:, NP + h, 0:TBk], in0=oTfH[:, 0:TBk], scalar=hnw[:, 0:1], in1=zb[:, 0:TBk], op0=ALU.mult, op1=ALU.mult),
                 reads=["oTfH", "hnw_t", "zb"], writes=[f"oOwn{bpar}_{NP + h}"])

        def xchg(_=None):
            xs_, xd_ = (xsrc_s, xdst_s) if is_sample else (xsrc[bpar], xdst[bpar])
            s.dma("sp", xs_.rearrange("(t p) n -> p t n", p=128), oOwn[bpar][:, :, 0:TBk], reads=[f"oOwn{bpar}_{i}" for i in range(NL)], writes=[f"xsrc{bpar}"])
            s.coll([xs_], [xd_], reads=[f"xsrc{bpar}"], writes=[f"xdst{bpar}"])
            s.dma("sp", oTn[bpar][:, :, 0:TBk], xd_.rearrange("(t p) n -> p t n", p=128), reads=[f"xdst{bpar}"], writes=[f"oTn{bpar}_{k}" for k in range(2 * NL)])

        def outproj(_=None):
            for tt in range(ntt):
                X = xt[bpar * 2 + tt]
                for half in range(2):
                    bank = pb[6] if half == 0 else pb[7]
                    bk = "B6" if half == 0 else "BT"
                    for k in range(8):
                        s.op("pe", lambda e: e.matmul(bank[0:TT, :], lhsT=oTn[bpar][:, k, tt * TT:(tt + 1) * TT], rhs=WOb[:, k, half * 512:(half + 1) * 512], start=(k == 0), stop=(k == 7)),
                             reads=[f"oTn{bpar}_{k}", "WOb"], writes=[bk])
                    s.op("dve", lambda e: e.tensor_tensor(out=yo[0:TT, half * 512:(half + 1) * 512], in0=bank[0:TT, :], in1=X[0:TT, half * 512:(half + 1) * 512], op=ALU.add),
                         reads=[bk, X.name], writes=["yo"])
                s.op("act", lambda e: e.activation(out=sqjO[0:TT, :], in_=yo[0:TT, :], func=AF.Square, accum_out=ssO[0:TT, :]), reads=["yo"], writes=["sqjO", "ssO"])
                s.op("act", lambda e: e.activation(out=rrO[0:TT, :], in_=ssO[0:TT, :], func=AF.Ln, scale=1.0 / D, bias=EPS), reads=["ssO"], writes=["rrO"])
                s.op("act", lambda e: e.activation(out=rrO[0:TT, :], in_=rrO[0:TT, :], func=AF.Exp, scale=-0.5), reads=["rrO"], writes=["rrO"])
                s.op("dve", lambda e: e.scalar_tensor_tensor(out=yo2[0:TT, :], in0=yo[0:TT, :], scalar=rrO[0:TT, :], in1=fnw[0:TT, :], op0=ALU.mult, op1=ALU.mult),
                     reads=["yo", "rrO", "fnw_t"], writes=["yo2"])
                s.dma("sp", y_dst[t0 + tt * TT: t0 + (tt + 1) * TT, :], yo2[0:TT, :], reads=["yo2"], writes=[f"o_y{id(y_dst)}"], slot="yout")

        def record(fn, arg=None):
            lst = []
            s.rec = lst
            fn(arg)
            s.rec = None
            return lst
        return dict(p0=lambda: record(phase0), op=lambda: record(outproj), xc=lambda: record(xchg),
                    fr=lambda p: record(front, p), co=lambda p: record(core, p), hg=lambda h: record(hg, h))

    def emit_blocks(blks, extra_nodes=(), extra_deps=None):
        n = len(blks)
        extra_deps = extra_deps or {}
        nodes = []
        for b, P in enumerate(blks):
            N = lambda kind, bb: f"{kind}_{bb}"
            X = lambda name: extra_deps.get(name, [])
            nodes.append((N("P0", b), P["p0"](), [N("P0", b - 1), N("FR1", b - 2), N("HG1", b - 2), N("OP", b - 2)] + X(N("P0", b)), b * 10 + 0))
            nodes.append((N("FR0", b), P["fr"](0), [N("P0", b), N("FR1", b - 1), N("CO0", b - 1), N("OP", b - 2)] + X(N("FR0", b)), b * 10 + 1))
            nodes.append((N("HG0", b), P["hg"](0), [N("P0", b), N("HG1", b - 1), N("XC", b - 2)] + X(N("HG0", b)), b * 10 + 2))
            nodes.append((N("CO0", b), P["co"](0), [N("FR0", b), N("CO1", b - 1), N("XC", b - 2)] + X(N("CO0", b)), b * 10 + 3))
            nodes.append((N("FR1", b), P["fr"](1), [N("FR0", b), N("CO1", b - 1)], b * 10 + 4))
            nodes.append((N("HG1", b), P["hg"](1), [N("HG0", b)], b * 10 + 5))
            nodes.append((N("CO1", b), P["co"](1), [N("FR1", b), N("CO0", b)], b * 10 + 6))
            nodes.append((N("XC", b), P["xc"](), [N("CO1", b), N("HG1", b), N("XC", b - 1), N("OP", b - 2)], b * 10 + 7))
            nodes.append((N("OP", b), P["op"](), [N("XC", b), N("OP", b - 1), N("FR1", b + 1) if b + 1 < n else N("FR1", b)], b * 10 + 18))
        nodes += list(extra_nodes)
        s.dag_emit(nodes)

    assert NP == 2 and HB == 2
    nblk = T // TB
    blks = [block(xp, yp, b * TB, 1, TB, 64, b == 0, False, b % 2) for b in range(nblk)]
    sblk = block(xs, ys, 0, 4, 16, 16, True, True, nblk % 2)
    sw = []
    s.rec = sw
    s.dma("sp", ncp[:, :, 0, :], halo[:, :, 0, :], reads=["halo"], writes=["o_ncp"])
    s.dma("sp", halo[:], sc_d, reads=["halo"], writes=["halo"])
    s.rec = None
    so = []
    s.rec = so
    for p in range(NP):
        s.dma("sp", ngp[0, p], Sg[:, p, :], reads=[f"Sg{p}"], writes=[f"o_ngp{p}"])
    for h in range(HB):
        s.dma("sp", nhp[0, h], Sh[:, h, :], reads=[f"Sh{h}"], writes=[f"o_nhp{h}"])
    s.rec = None
    L = nblk - 1
    extra_nodes = [("SW", sw, [f"FR1_{L}"], L * 10 + 8), ("SO", so, [f"CO1_{L}", f"HG1_{L}"], L * 10 + 9)]
    extra_deps = {f"FR0_{nblk}": ["SW"], f"CO0_{nblk}": ["SO"], f"HG0_{nblk}": ["SO"]}
    emit_blocks(blks + [sblk], extra_nodes, extra_deps)
    s.dma("sp", ncs, halo[:], reads=["halo"], writes=["o_ncs"])
    s.finish("sp")
    return nc, s


def _perm(hh):
    r = lambda base, w: np.arange(base + hh * w, base + (hh + 1) * w)
    return np.concatenate([r(0, 256), r(512, 256), r(1024, 256), r(1536, 256),
                           r(2064, 256), r(2576, 256), r(3600, 256), r(3088, 256),
                           r(2048, 4), r(2056, 4)])


def _chan(hh):
    r = lambda base: np.arange(base + hh * 256, base + (hh + 1) * 256)
    return np.concatenate([r(0), r(512), r(1024)])


_WOUT_ROWS = np.concatenate([np.concatenate([np.arange(r * 256, (r + 1) * 256), np.arange(512 + r * 256, 512 + (r + 1) * 256)]) for r in range(2)])


def _core_inputs(c, inp):
    b, hh = c // 2, c % 2
    f = lambda a: np.ascontiguousarray(a, dtype=np.float32)
    ch = _chan(hh)
    cwv = inp["conv_w"][0][:, ch]
    scv = inp["state_conv"][0][4 * b:4 * b + 4][:, :, ch]
    hs = slice(4 * hh, 4 * hh + 4)
    return {
        "xp": f(inp["x_prompt"][b]),
        "xs": f(inp["x_sample"][4 * b:4 * b + 4].reshape(64, D)),
        "w_in": f(inp["w_in"][0][:, _perm(hh)]),
        "w_out": f(inp["w_out"][0][_WOUT_ROWS]),
        "nw": f(inp["norm_w"][0].reshape(8, 128).T),
        "cw": f(cwv.reshape(4, 3 * NP, 128).transpose(2, 1, 0)),
        "alog": f(np.broadcast_to(inp["gdn_A_log"][0][None, hs], (128, HA))),
        "dtb": f(np.broadcast_to(inp["gdn_dt_bias"][0][None, hs], (128, HA))),
        "gnw": f(np.tile(inp["gdn_norm_w"][0], 2).reshape(128, 1)),
        "hnw": f(inp["hgrn_norm_w"][0].reshape(128, 1)),
        "lbl": f(inp["hgrn_lb_logits"][:, hh * 256:(hh + 1) * 256].reshape(2, HB, 128).transpose(2, 1, 0)),
        "fnw": f(np.broadcast_to(inp["final_norm_w"][None, :], (128, D))),
        "sc": f(scv.reshape(4, 3, 3 * NP, 128).transpose(3, 2, 0, 1)),
        "sg": f(inp["state_gdn"][0][4 * b:4 * b + 4][:, hs].reshape(4, NP, 128, 64)),
        "sh": f(inp["state_hgrn"][0][4 * b:4 * b + 4][:, 2 * hh:2 * hh + 2]),
    }


_CACHE = {}


def kernel(**inputs):
    inp = {k: np.asarray(v) for k, v in inputs.items()}
    Bp, T, _ = inp["x_prompt"].shape
    assert Bp == 4 and inp["x_sample"].shape[:2] == (16, 16)
    if T not in _CACHE:
        _CACHE[T] = build(T)[0]
    nc = _CACHE[T]
    in_maps = [_core_inputs(c, inp) for c in range(8)]
    res = run_bass_kernel_spmd(nc, in_maps, core_ids=list(range(8)))
    r = res.results
    y_prompt = np.stack([r[2 * b]["yp"] for b in range(4)]).astype(np.float32)
    y_sample = np.concatenate([r[2 * b]["ys"].reshape(4, 16, D) for b in range(4)]).astype(np.float32)
    ncp_ = np.zeros((1, 4, 3, 1536), np.float32); ncs_ = np.zeros((1, 16, 3, 1536), np.float32)
    ngp_ = np.zeros((1, 4, 8, 64, 64), np.float32); ngs_ = np.zeros((1, 16, 8, 64, 64), np.float32)
    nhp_ = np.zeros((1, 4, 4, 128, 128), np.float32); nhs_ = np.zeros((1, 16, 4, 128, 128), np.float32)
    cvt = lambda a: a.transpose(2, 3, 1, 0).reshape(a.shape[2], 3, 3 * NP * 128)
    for c in range(8):
        b, hh = c // 2, c % 2
        ch = _chan(hh)
        ncp_[0, b][:, ch] = cvt(r[c]["ncp"])[0]
        ncs_[0, 4 * b:4 * b + 4][:, :, ch] = cvt(r[c]["ncs"])
        ngp_[0, b, 4 * hh:4 * hh + 4] = r[c]["ngp"].reshape(HA, 64, 64)
        ngs_[0, 4 * b:4 * b + 4, 4 * hh:4 * hh + 4] = r[c]["ngs"].reshape(4, HA, 64, 64)
        nhp_[0, b, 2 * hh:2 * hh + 2] = r[c]["nhp"].reshape(HB, 128, 128)
        nhs_[0, 4 * b:4 * b + 4, 2 * hh:2 * hh + 2] = r[c]["nhs"].reshape(4, HB, 128, 128)
    return (y_prompt, y_sample, ncp_, ngp_, nhp_, ncs_, ngs_, nhs_)
```
